# Optimizing a Trainium2 kernel written in Bass

```python
import math, functools
import jax, jax.numpy as jnp
from jax import lax
import numpy as np

D_MODEL = 1024
BATCH = 2
SEQ = 16384
DEPTH = 4

N_MIXERS = 3
HEAD_DIM = 64
N_HEADS = D_MODEL // HEAD_DIM
REL_BUCKETS = 32
REL_MAX_DIST = 2048
Q_BLOCK = 128
LN_EPS = 1e-5
GLA_HEADS = 4
GLA_DK = D_MODEL // 2 // GLA_HEADS
GLA_DV = D_MODEL // GLA_HEADS
GLA_GATE_RANK = 16
GLA_TAU = 16.0
GLA_CHUNK = 64
GLA_IN = 2 * GLA_HEADS * GLA_DK + 2 * GLA_HEADS * GLA_DV + GLA_GATE_RANK
NSA_KV_GROUPS = 4
NSA_Q_PER_KV = N_HEADS // NSA_KV_GROUPS
CMP_LEN = 32
CMP_STRIDE = 16
CMP_HIDDEN = 256
SEL_LEN = 64
SEL_TOP = 16
WIN_LEN = 512
NSA_IN = N_HEADS * HEAD_DIM + 6 * NSA_KV_GROUPS * HEAD_DIM + 3 * N_HEADS
DIL_GROUPS = ((128, 1), (512, 4), (2048, 16))
DIL_IN = len(DIL_GROUPS) * 3 * N_HEADS * HEAD_DIM
D_FF = 2816
CONV_W = 3
DN_ALPHA = (2 * DEPTH) ** 0.25
DN_BETA = (8 * DEPTH) ** -0.25

kernel_name = "hybrid_gla_nsa_dilated_deepnorm"


def layer_norm(x, g, b):
    xf = x.astype(jnp.float32)
    mu = jnp.mean(xf, -1, keepdims=True)
    var = jnp.mean(jnp.square(xf - mu), -1, keepdims=True)
    return ((xf - mu) * lax.rsqrt(var + LN_EPS) * g + b).astype(x.dtype)


def rel_bucket(dist):
    n = jnp.maximum(dist, 0)
    exact = REL_BUCKETS // 2
    logn = jnp.log(jnp.maximum(n, 1).astype(jnp.float32) / exact)
    large = exact + (logn / math.log(REL_MAX_DIST / exact) * (REL_BUCKETS - exact)).astype(jnp.int32)
    large = jnp.minimum(large, REL_BUCKETS - 1)
    return jnp.where(n < exact, n, large)


def masked_softmax(s, mask):
    s = jnp.where(mask, s.astype(jnp.float32), -jnp.inf)
    m = jnp.max(s, axis=-1, keepdims=True)
    m = jnp.where(jnp.isfinite(m), m, 0.0)
    e = jnp.where(mask, jnp.exp(s - m), 0.0)
    den = jnp.sum(e, axis=-1, keepdims=True)
    p = e / jnp.where(den > 0, den, 1.0)
    return p, (m + jnp.log(den))[..., 0]


def banded_attention(q, k, v, max_dist, dist_scale, rel_table):
    n, l, g, r, dh = q.shape
    nblk = -(-l // Q_BLOCK)
    lp = nblk * Q_BLOCK
    nb = -(-max_dist // Q_BLOCK)
    span = (nb + 1) * Q_BLOCK
    qp = jnp.pad(q, ((0, 0), (0, lp - l), (0, 0), (0, 0), (0, 0)))
    kv_pad = ((0, 0), (nb * Q_BLOCK, lp - l), (0, 0), (0, 0))
    kp = jnp.pad(k, kv_pad)
    vp = jnp.pad(v, kv_pad)
    dist = jnp.arange(Q_BLOCK)[:, None] - jnp.arange(span)[None, :] + nb * Q_BLOCK
    band = (dist >= 0) & (dist <= max_dist)
    bias = rel_table[rel_bucket(dist * dist_scale)].reshape(Q_BLOCK, span, g, r)
    bias = bias.transpose(2, 3, 0, 1).astype(jnp.float32)
    scale = dh ** -0.5

    def block(i):
        q0 = i * Q_BLOCK
        qi = lax.dynamic_slice_in_dim(qp, q0, Q_BLOCK, axis=1)
        ki = lax.dynamic_slice_in_dim(kp, q0, span, axis=1)
        vi = lax.dynamic_slice_in_dim(vp, q0, span, axis=1)
        kpos = q0 - nb * Q_BLOCK + jnp.arange(span)
        mask = band & (kpos >= 0)[None, :]
        s = jnp.einsum('nqgrd,nkgd->ngrqk', qi, ki).astype(jnp.float32) * scale + bias
        p, lse = masked_softmax(s, mask)
        o = jnp.einsum('ngrqk,nkgd->nqgrd', p.astype(vi.dtype), vi)
        return o, lse

    o, lse = lax.map(block, jnp.arange(nblk))
    o = jnp.moveaxis(o, 0, 1).reshape(n, lp, g, r, dh)[:, :l]
    lse = lse.transpose(1, 0, 4, 2, 3).reshape(n, lp, g, r)[:, :l]
    return o, lse


def gla_mixer(h, w_in, w_a2, b_a, norm_g, w_o):
    b, s, _ = h.shape
    qk, dv = GLA_HEADS * GLA_DK, GLA_HEADS * GLA_DV
    q, k, v, r, a_lr = jnp.split(h @ w_in, [qk, 2 * qk, 2 * qk + dv, 2 * qk + 2 * dv], axis=-1)
    log_a = jax.nn.log_sigmoid((a_lr @ w_a2 + b_a).astype(jnp.float32)) / GLA_TAU
    nc = s // GLA_CHUNK

    def chunks(t, d):
        return t.astype(jnp.float32).reshape(b, nc, GLA_CHUNK, GLA_HEADS, d).transpose(0, 3, 1, 2, 4)

    qf = chunks(q, GLA_DK) * GLA_DK ** -0.5
    kf = chunks(k, GLA_DK)
    vf = chunks(v, GLA_DV)
    bcum = jnp.cumsum(chunks(log_a, GLA_DK), axis=3)
    blast = bcum[:, :, :, -1:, :]
    q_dec = qf * jnp.exp(bcum)
    k_inv = kf * jnp.exp(-bcum)
    k_dec = kf * jnp.exp(blast - bcum)
    causal = jnp.tril(jnp.ones((GLA_CHUNK, GLA_CHUNK), dtype=bool))
    att = jnp.where(causal, jnp.einsum('bhnid,bhnjd->bhnij', q_dec, k_inv), 0.0)
    o_intra = jnp.einsum('bhnij,bhnje->bhnie', att, vf)

    def step(state, xs):
        qc, kc, vc, dl = xs
        o = jnp.einsum('bhcd,bhde->bhce', qc, state)
        state = state * dl[..., None] + jnp.einsum('bhcd,bhce->bhde', kc, vc)
        return state, o

    xs = (jnp.moveaxis(q_dec, 2, 0), jnp.moveaxis(k_dec, 2, 0), jnp.moveaxis(vf, 2, 0),
          jnp.moveaxis(jnp.exp(blast[:, :, :, 0]), 2, 0))
    state0 = jnp.zeros((b, GLA_HEADS, GLA_DK, GLA_DV), jnp.float32)
    _, o_inter = lax.scan(step, state0, xs)
    o = o_intra + jnp.moveaxis(o_inter, 0, 2)
    o = o * lax.rsqrt(jnp.mean(o * o, -1, keepdims=True) + LN_EPS) * norm_g
    o = o.transpose(0, 2, 3, 1, 4).reshape(b, s, dv).astype(h.dtype)
    return (o * jax.nn.silu(r)) @ w_o


def compress(x, pe, w1, b1, w2):
    b, s, g, dh = x.shape
    ratio = CMP_LEN // CMP_STRIDE
    pieces = x.reshape(b, s // CMP_STRIDE, CMP_STRIDE, g, dh)
    nc = s // CMP_STRIDE - ratio + 1
    blocks = jnp.concatenate([pieces[:, j:j + nc] for j in range(ratio)], axis=2)
    blocks = (blocks + pe[:, None, :]).transpose(0, 1, 3, 2, 4).reshape(b, nc, g, CMP_LEN * dh)
    return jax.nn.silu(blocks @ w1 + b1) @ w2


def nsa_cmp_sel(q, kcmp, vcmp, ks, vs, rel_table):
    b, s, G, R, dh = q.shape
    nc = kcmp.shape[1]
    nsel = s // SEL_LEN
    top = min(SEL_TOP, nsel)
    ratio_sel = SEL_LEN // CMP_STRIDE
    ratio_cmp = CMP_LEN // CMP_STRIDE
    scale = dh ** -0.5
    cmp_end = jnp.arange(nc) * CMP_STRIDE + CMP_LEN - 1
    ks_blk = ks.reshape(b, nsel, SEL_LEN, G, dh).transpose(0, 3, 1, 2, 4)
    vs_blk = vs.reshape(b, nsel, SEL_LEN, G, dh).transpose(0, 3, 1, 2, 4)
    tab_g = rel_table.reshape(REL_BUCKETS, G, R).transpose(1, 0, 2).reshape(G * REL_BUCKETS, R)
    b_ix = jnp.arange(b)[:, None, None, None]
    g_ix = jnp.arange(G)[None, :, None, None]
    blk_ids = jnp.arange(nsel)[None, :]

    def block(i):
        t = i * Q_BLOCK + jnp.arange(Q_BLOCK)
        qi = lax.dynamic_slice_in_dim(q, i * Q_BLOCK, Q_BLOCK, axis=1)
        dist_c = t[:, None] - cmp_end[None, :]
        bias_c = rel_table[rel_bucket(dist_c)].reshape(Q_BLOCK, nc, G, R).transpose(2, 3, 0, 1)
        s_c = jnp.einsum('bqgrd,bjgd->bgrqj', qi, kcmp).astype(jnp.float32) * scale + bias_c
        p_c, _ = masked_softmax(s_c, dist_c >= 0)
        o_c = jnp.einsum('bgrqj,bjgd->bqgrd', p_c.astype(vcmp.dtype), vcmp)
        p_g = jnp.pad(p_c.sum(axis=2), ((0, 0), (0, 0), (0, 0), (ratio_cmp - 1, ratio_sel * nsel - nc)))
        imp = jnp.stack([p_g[..., m - n + ratio_cmp - 1: m - n + ratio_cmp - 1 + ratio_sel * nsel: ratio_sel]
                         for m in range(ratio_sel) for n in range(ratio_cmp)]).sum(0)
        cur = (t // SEL_LEN)[:, None]
        valid = blk_ids * SEL_LEN <= t[:, None]
        forced = (blk_ids == 0) | (blk_ids == cur) | (blk_ids == cur - 1)
        score = jnp.where(valid, jnp.where(forced, jnp.inf, imp), -jnp.inf)
        top_s, idx = lax.top_k(score, top)
        k_sel = ks_blk[b_ix, g_ix, idx].reshape(b, G, Q_BLOCK, top * SEL_LEN, dh)
        v_sel = vs_blk[b_ix, g_ix, idx].reshape(b, G, Q_BLOCK, top * SEL_LEN, dh)
        pos = (idx[..., None] * SEL_LEN + jnp.arange(SEL_LEN)).reshape(b, G, Q_BLOCK, top * SEL_LEN)
        dist_s = t[None, None, :, None] - pos
        mask_s = jnp.repeat(top_s > -jnp.inf, SEL_LEN, axis=-1) & (dist_s >= 0)
        bias_s = tab_g[g_ix * REL_BUCKETS + rel_bucket(dist_s)].transpose(0, 1, 4, 2, 3)
        s_s = jnp.einsum('bqgrd,bgqkd->bgrqk', qi, k_sel).astype(jnp.float32) * scale + bias_s
        p_s, _ = masked_softmax(s_s, mask_s[:, :, None])
        o_s = jnp.einsum('bgrqk,bgqkd->bqgrd', p_s.astype(v_sel.dtype), v_sel)
        return o_c, o_s

    o_c, o_s = lax.map(block, jnp.arange(s // Q_BLOCK))
    o_c = jnp.moveaxis(o_c, 0, 1).reshape(b, s, G, R, dh)
    o_s = jnp.moveaxis(o_s, 0, 1).reshape(b, s, G, R, dh)
    return o_c, o_s


def nsa_mixer(h, w_in, cmp_pe, cmp_w1, cmp_b1, cmp_w2, rel_table, w_o):
    b, s, _ = h.shape
    G, R, dh = NSA_KV_GROUPS, NSA_Q_PER_KV, HEAD_DIM
    splits = np.cumsum([N_HEADS * dh] + [G * dh] * 6).tolist()
    q, kc, vc, ks, vs, kw, vw, gates = jnp.split(h @ w_in, splits, axis=-1)
    q = q.reshape(b, s, G, R, dh)
    kc, vc, ks, vs, kw, vw = (t.reshape(b, s, G, dh) for t in (kc, vc, ks, vs, kw, vw))
    kcmp = compress(kc, cmp_pe[0], cmp_w1[0], cmp_b1[0], cmp_w2[0])
    vcmp = compress(vc, cmp_pe[1], cmp_w1[1], cmp_b1[1], cmp_w2[1])
    o_cmp, o_sel = nsa_cmp_sel(q, kcmp, vcmp, ks, vs, rel_table)
    o_win, _ = banded_attention(q, kw, vw, WIN_LEN - 1, 1, rel_table)
    gt = jax.nn.sigmoid(gates.astype(jnp.float32)).reshape(b, s, 3, G, R, 1).astype(h.dtype)
    o = gt[:, :, 0] * o_cmp + gt[:, :, 1] * o_sel + gt[:, :, 2] * o_win
    return o.reshape(b, s, N_HEADS * dh) @ w_o


def dilated_mixer(h, w_in, rel_table, w_o):
    b, s, d_model = h.shape
    hd = N_HEADS * HEAD_DIM
    w_g = w_in.reshape(d_model, len(DIL_GROUPS), 3, hd)
    outs, lses = [], []
    for gi, (window, dil) in enumerate(DIL_GROUPS):
        qkv = jnp.einsum('bsd,dce->bsce', h, w_g[:, gi])
        lsub = s // dil

        def by_residue(t):
            return t.reshape(b, lsub, dil, N_HEADS, HEAD_DIM).transpose(0, 2, 1, 3, 4).reshape(b * dil, lsub, N_HEADS, HEAD_DIM)

        o, lse = banded_attention(by_residue(qkv[:, :, 0])[:, :, :, None], by_residue(qkv[:, :, 1]),
                                  by_residue(qkv[:, :, 2]), window // dil, dil, rel_table)
        o = o[:, :, :, 0].reshape(b, dil, lsub, N_HEADS, HEAD_DIM).transpose(0, 2, 1, 3, 4).reshape(b, s, N_HEADS, HEAD_DIM)
        lse = lse[..., 0].reshape(b, dil, lsub, N_HEADS).transpose(0, 2, 1, 3).reshape(b, s, N_HEADS)
        outs.append(o)
        lses.append(lse)
    wts = jax.nn.softmax(jnp.stack(lses, axis=-1), axis=-1).astype(h.dtype)
    o = jnp.einsum('bshg,bshgd->bshd', wts, jnp.stack(outs, axis=3))
    return o.reshape(b, s, hd) @ w_o


def conv_ffn(h, w_up, conv_w, conv_b, w_down):
    u, g = jnp.split(h @ w_up, 2, axis=-1)
    u = lax.conv_general_dilated(u, conv_w[:, None, :], window_strides=(1,), padding=((CONV_W - 1, 0),),
                                 dimension_numbers=('NWC', 'WIO', 'NWC'), feature_group_count=D_FF) + conv_b
    return (jax.nn.silu(u) * g) @ w_down


def post_norm_residual(x, cs, w_mod, b_mod, ln_gain, ln_bias, sublayer):
    shift, scale, gate = jnp.split((cs @ w_mod + b_mod)[:, None, :], 3, axis=-1)
    y = sublayer(x * (1 + scale) + shift)
    return layer_norm(DN_ALPHA * x + (1 + gate) * y, ln_gain, ln_bias)


def setup_inputs(seed: int = 0) -> dict:
    key = jax.random.key(seed)
    keys = iter(jax.random.split(key, 32))

    def nrm(shape, std):
        return std * jax.random.normal(next(keys), shape, jnp.float32)

    n_a, n_b, n_c = (len(range(m, DEPTH, N_MIXERS)) for m in range(N_MIXERS))
    D = D_MODEL
    hd = N_HEADS * HEAD_DIM
    return {
        'x': nrm((BATCH, SEQ, D), 1.0),
        'c': nrm((BATCH, D), 1.0),
        'rel_table': nrm((REL_BUCKETS, N_HEADS), 0.5),
        'mod_w': nrm((DEPTH, 2, D, 3 * D), 0.5 * D ** -0.5),
        'mod_b': nrm((DEPTH, 2, 3 * D), 0.02),
        'ln_g': 1.0 + nrm((DEPTH, 2, D), 0.02),
        'ln_b': nrm((DEPTH, 2, D), 0.02),
        'gla_w_in': nrm((n_a, D, GLA_IN), D ** -0.5),
        'gla_w_a2': nrm((n_a, GLA_GATE_RANK, GLA_HEADS * GLA_DK), GLA_GATE_RANK ** -0.5),
        'gla_b_a': nrm((n_a, GLA_HEADS * GLA_DK), 0.1),
        'gla_norm_g': 1.0 + nrm((n_a, GLA_DV), 0.02),
        'gla_w_o': nrm((n_a, GLA_HEADS * GLA_DV, D), DN_BETA * (GLA_HEADS * GLA_DV) ** -0.5),
        'nsa_w_in': nrm((n_b, D, NSA_IN), D ** -0.5),
        'nsa_cmp_pe': nrm((n_b, 2, CMP_LEN, HEAD_DIM), 0.1),
        'nsa_cmp_w1': nrm((n_b, 2, CMP_LEN * HEAD_DIM, CMP_HIDDEN), (CMP_LEN * HEAD_DIM) ** -0.5),
        'nsa_cmp_b1': nrm((n_b, 2, CMP_HIDDEN), 0.02),
        'nsa_cmp_w2': nrm((n_b, 2, CMP_HIDDEN, HEAD_DIM), CMP_HIDDEN ** -0.5),
        'nsa_w_o': nrm((n_b, hd, D), DN_BETA * hd ** -0.5),
        'dil_w_in': nrm((n_c, D, DIL_IN), D ** -0.5),
        'dil_w_o': nrm((n_c, hd, D), DN_BETA * hd ** -0.5),
        'ffn_w_up': nrm((DEPTH, D, 2 * D_FF), D ** -0.5),
        'ffn_conv_w': nrm((DEPTH, CONV_W, D_FF), CONV_W ** -0.5),
        'ffn_conv_b': nrm((DEPTH, D_FF), 0.02),
        'ffn_w_down': nrm((DEPTH, D_FF, D), DN_BETA * D_FF ** -0.5),
    }


def reference(x, c, rel_table, mod_w, mod_b, ln_g, ln_b,
              gla_w_in, gla_w_a2, gla_b_a, gla_norm_g, gla_w_o,
              nsa_w_in, nsa_cmp_pe, nsa_cmp_w1, nsa_cmp_b1, nsa_cmp_w2, nsa_w_o,
              dil_w_in, dil_w_o,
              ffn_w_up, ffn_conv_w, ffn_conv_b, ffn_w_down):
    cs = jax.nn.silu(c)
    for i in range(DEPTH):
        kind, j = i % N_MIXERS, i // N_MIXERS
        if kind == 0:
            mixer = functools.partial(gla_mixer, w_in=gla_w_in[j], w_a2=gla_w_a2[j], b_a=gla_b_a[j],
                                      norm_g=gla_norm_g[j], w_o=gla_w_o[j])
        elif kind == 1:
            mixer = functools.partial(nsa_mixer, w_in=nsa_w_in[j], cmp_pe=nsa_cmp_pe[j], cmp_w1=nsa_cmp_w1[j],
                                      cmp_b1=nsa_cmp_b1[j], cmp_w2=nsa_cmp_w2[j], rel_table=rel_table,
                                      w_o=nsa_w_o[j])
        else:
            mixer = functools.partial(dilated_mixer, w_in=dil_w_in[j], rel_table=rel_table, w_o=dil_w_o[j])
        x = post_norm_residual(x, cs, mod_w[i, 0], mod_b[i, 0], ln_g[i, 0], ln_b[i, 0], mixer)
        ffn = functools.partial(conv_ffn, w_up=ffn_w_up[i], conv_w=ffn_conv_w[i], conv_b=ffn_conv_b[i],
                                w_down=ffn_w_down[i])
        x = post_norm_residual(x, cs, mod_w[i, 1], mod_b[i, 1], ln_g[i, 1], ln_b[i, 1], ffn)
    return x
```

```python
import math
from contextlib import ExitStack
import numpy as np
import ml_dtypes
import concourse.bass as bass
import concourse.mybir as mybir
from concourse.bass_utils import run_bass_kernel_spmd

F32 = mybir.dt.float32
BF16 = mybir.dt.bfloat16
AF = mybir.ActivationFunctionType
ALU = mybir.AluOpType
AX = mybir.AxisListType

D = 1024
SEQ = 16384
NB = 2
DEPTH = 4
D_FF = 2816
NFC = D_FF // 128
DN_ALPHA = (2 * DEPTH) ** 0.25
LN_EPS = 1e-5
NEG = -30000.0
NCORES = 8


class Buf:
    __slots__ = ("w", "r", "t", "parts")

    def __init__(self, t=None, nparts=0):
        self.w = None
        self.r = {}
        self.t = t
        self.parts = [Buf(t) for _ in range(nparts)]

    def __getitem__(self, k):
        return self.t[k]


class Eng:
    def __init__(self, name, h, sem):
        self.name, self.h, self.sem = name, h, sem
        self.count = 0
        self.waited = {}


class KB:
    def __init__(self, nc, es, ndma=40):
        self.nc, self.es = nc, es
        self.eng = {}
        for name, h in (("pe", nc.tensor), ("act", nc.scalar), ("dve", nc.vector), ("pool", nc.gpsimd), ("sp", nc.sync)):
            self.eng[name] = Eng(name, h, es.enter_context(nc.semaphore("sem_" + name)))
        self.dsl = [[es.enter_context(nc.semaphore("dsem%d" % i)), 0] for i in range(ndma)]
        self.di = {"sp": 0, "pool": 0, "act": 0}
        self.dq = {"sp": list(range(0, ndma // 2)), "pool": list(range(ndma // 2, ndma - 4)), "act": list(range(ndma - 4, ndma))}
        self.nt = 0

    def sb(self, shape, dt=F32, name=None, nparts=0, es=None):
        self.nt += 1
        return Buf((es or self.es).enter_context(self.nc.sbuf_tensor(name or "t%d" % self.nt, list(shape), dt)), nparts)

    def ps(self, shape=(128, 512), dt=F32, name=None):
        self.nt += 1
        return Buf(self.es.enter_context(self.nc.psum_tensor(name or "p%d" % self.nt, list(shape), dt)))

    def dram(self, name, shape, dt, kind="Internal"):
        return Buf(self.nc.dram_tensor(name, list(shape), dt, kind=kind).ap())

    def _semof(self, key):
        if isinstance(key, str):
            return self.eng[key].sem
        return self.dsl[key][0]

    def _wait(self, E, deps):
        need = {}
        for key, val in deps:
            if key == E.name and key in ("pe", "sp"):
                continue
            if val > need.get(key, 0):
                need[key] = val
        for key, val in need.items():
            if E.waited.get(key, 0) >= val:
                continue
            E.h.wait_ge(self._semof(key), val)
            E.waited[key] = val

    @staticmethod
    def _deps(reads, writes):
        deps = []
        for b in reads:
            if b.w is not None:
                deps.append(b.w)
        for b in writes:
            if b.w is not None:
                deps.append(b.w)
            deps.extend(b.r.items())
        return deps

    @staticmethod
    def _record(ev, reads, writes):
        key, val = ev
        for b in reads:
            if b.r.get(key, 0) < val:
                b.r[key] = val
        for b in writes:
            b.w = ev
            b.r = {}

    def op(self, en, fn, reads=(), writes=()):
        E = self.eng[en]
        self._wait(E, self._deps(reads, writes))
        ins = fn(E.h)
        E.count += 1
        ins.then_inc(E.sem, 1)
        self._record((en, E.count), reads, writes)

    def dma(self, qn, out, in_, reads=(), writes=(), **kw):
        Q = self.eng[qn]
        k = self.dq[qn][self.di[qn] % len(self.dq[qn])]
        self.di[qn] += 1
        slot = self.dsl[k]
        deps = self._deps(reads, writes)
        if slot[1] > 0:
            deps.append((k, slot[1]))
        self._wait(Q, deps)
        Q.h.dma_start(out=out, in_=in_, **kw).then_inc(slot[0], 16)
        slot[1] += 16
        self._record((k, slot[1]), reads, writes)

    def barrier(self):
        deps = [(n, E.count) for n, E in self.eng.items() if E.count > 0]
        deps += [(k, sl[1]) for k, sl in enumerate(self.dsl) if sl[1] > 0]
        for E in self.eng.values():
            self._wait(E, [d for d in deps if d[0] != E.name])

    def finish(self, outs):
        E = self.eng["sp"]
        self._wait(E, [b.w for b in outs if b.w is not None])

    def mm(self, out, lhsT, rhs, start, stop, reads, writes):
        self.op("pe", lambda e: e.matmul(out, lhsT=lhsT, rhs=rhs, start=start, stop=stop), reads, writes)


def new_nc():
    return bass.Bass("TRN2", target_bir_lowering=False)


def dr_in(nc, name, shape, dt=F32):
    return Buf(nc.dram_tensor(name, list(shape), dt, kind="ExternalInput").ap())


def dr_out(nc, name, shape, dt=F32):
    return Buf(nc.dram_tensor(name, list(shape), dt, kind="ExternalOutput").ap())


def load_bcast(kb, q, dst, src_row_ap, n=128):
    kb.dma(q, dst.t[0:n, :], src_row_ap.partition_broadcast(n), writes=[dst])


def make_ident(kb, dt=BF16):
    idf = kb.sb([128, 128], F32)
    kb.op("pool", lambda e: e.memset(idf[:], 0.0), writes=[idf])
    kb.op("pool", lambda e: e.affine_select(out=idf[:], in_=idf[:], pattern=[[-1, 128]], compare_op=ALU.not_equal,
                                             fill=1.0, base=0, channel_multiplier=1), reads=[idf], writes=[idf])
    if dt == F32:
        return idf
    idb = kb.sb([128, 128], dt)
    kb.op("dve", lambda e: e.tensor_copy(out=idb[:], in_=idf[:]), reads=[idf], writes=[idb])
    return idb


def load_w_bf16(kb, dst, src_ap_fn, nk, ncols, stage, qs=("sp", "pool"), chunk=512, npart=128):
    i = 0
    for k in range(nk):
        for c0 in range(0, ncols, chunk):
            c1 = min(ncols, c0 + chunk)
            st = stage[i % len(stage)]
            kb.dma(qs[i % len(qs)], st.t[0:npart, 0:c1 - c0], src_ap_fn(k, c0, c1), writes=[st])
            en = "act" if i % 2 else "dve"
            if en == "act":
                kb.op("act", lambda e, st=st, k=k, c0=c0, c1=c1: e.copy(out=dst.t[0:npart, k, c0:c1], in_=st.t[0:npart, 0:c1 - c0]),
                      reads=[st], writes=[dst.parts[k]])
            else:
                kb.op("dve", lambda e, st=st, k=k, c0=c0, c1=c1: e.tensor_copy(out=dst.t[0:npart, k, c0:c1], in_=st.t[0:npart, 0:c1 - c0]),
                      reads=[st], writes=[dst.parts[k]])
            i += 1


def emit_ln(kb, z, st, mv, lng, lnb):
    for c in range(2):
        kb.op("dve", lambda e, c=c: e.bn_stats(out=st.t[:, c, :], in_=z.t[:, c * 512:(c + 1) * 512]), reads=[z], writes=[st])
    kb.op("dve", lambda e: e.bn_aggr(out=mv.t[:, 0:2], in_=st.t[:].rearrange("p a b -> p (a b)")), reads=[st], writes=[mv])
    kb.op("dve", lambda e: e.tensor_scalar_add(out=mv.t[:, 2:3], in0=mv.t[:, 1:2], scalar1=LN_EPS), reads=[mv], writes=[mv])
    kb.op("act", lambda e: e.sqrt(out=mv.t[:, 2:3], in_=mv.t[:, 2:3]), reads=[mv], writes=[mv])
    kb.op("dve", lambda e: e.reciprocal(out=mv.t[:, 2:3], in_=mv.t[:, 2:3]), reads=[mv], writes=[mv])
    kb.op("dve", lambda e: e.tensor_scalar(out=z.t[:], in0=z.t[:], scalar1=mv.t[:, 0:1], scalar2=mv.t[:, 2:3],
                                           op0=ALU.subtract, op1=ALU.mult), reads=[z, mv], writes=[z])
    kb.op("pool", lambda e: e.tensor_tensor(out=z.t[:], in0=z.t[:], in1=lng.t[:], op=ALU.mult), reads=[z, lng], writes=[z])
    kb.op("pool", lambda e: e.tensor_tensor(out=z.t[:], in0=z.t[:], in1=lnb.t[:], op=ALU.add), reads=[z, lnb], writes=[z])


def emit_modT(kb, x, pst, identf, scp, sh, ht, ncol=128, c0=0):
    for k in range(8):
        kb.op("pe", lambda e, k=k: e.transpose(out=pst.t[:, k * 128:(k + 1) * 128], in_=x.t[:, k * 128:(k + 1) * 128],
                                               identity=identf.t[:]), reads=[x, identf], writes=[pst])
    for k in range(8):
        kb.op("act", lambda e, k=k: e.activation(out=ht.t[:, k, c0:c0 + 128], in_=pst.t[:, k * 128:(k + 1) * 128],
                                                 func=AF.Identity, scale=scp.t[:, k:k + 1], bias=sh.t[:, k:k + 1]),
              reads=[pst, scp, sh], writes=[ht])


def load_pp(kb, q, dst, row_ap, nk=8):
    kb.dma(q, dst.t[:, 0:nk], row_ap.rearrange("(k p) -> p k", p=128), writes=[dst], allow_slow_non_contiguous=True)


def build_prep(T):
    nc = new_nc()
    x = dr_in(nc, "x", [T, D])
    vec = dr_in(nc, "vec", [2, D])
    hTo = dr_out(nc, "hT", [D, T], BF16)
    with ExitStack() as es:
        kb = KB(nc, es)
        identf = make_ident(kb, F32)
        sh = kb.sb([128, 8]); scp = kb.sb([128, 8])
        load_pp(kb, "sp", sh, vec.t[0, :]); load_pp(kb, "sp", scp, vec.t[1, :])
        kb.op("dve", lambda e: e.tensor_scalar_add(out=scp.t[:], in0=scp.t[:], scalar1=1.0), reads=[scp], writes=[scp])
        xs = [kb.sb([128, D]) for _ in range(3)]
        hts = [kb.sb([128, 8, 128], BF16) for _ in range(2)]
        psts = [kb.ps([128, 1024]) for _ in range(2)]
        hv = hTo.t.rearrange("(k p) t -> p k t", p=128)
        for i in range(T // 128):
            xt = xs[i % 3]; ht = hts[i % 2]; pst = psts[i % 2]
            kb.dma("sp", xt.t[:], x.t[i * 128:(i + 1) * 128, :], writes=[xt])
            emit_modT(kb, xt, pst, identf, scp, sh, ht)
            kb.dma("pool", hv[:, :, i * 128:(i + 1) * 128], ht.t[:], reads=[ht], writes=[hTo])
        kb.finish([hTo])
    return nc


def build_post(T):
    TT = 256
    nc = new_nc()
    x = dr_in(nc, "x", [T + 128, D])
    oT = dr_in(nc, "oT", [D, T + 128], BF16)
    wo = dr_in(nc, "wo", [D, D])
    wup = dr_in(nc, "wup", [D, 2 * D_FF])
    wdn = dr_in(nc, "wdn", [D_FF, D])
    convw = dr_in(nc, "convw", [3, D_FF])
    convb = dr_in(nc, "convb", [D_FF])
    vec = dr_in(nc, "vec", [10, D])
    flag = dr_in(nc, "flag", [128, 1])
    xo = dr_out(nc, "xo", [T, D])
    hTo = dr_out(nc, "hT", [D, T], BF16)
    x1s = Buf(nc.dram_tensor("x1s", [T, D], F32, kind="Internal").ap())
    h1s = Buf(nc.dram_tensor("h1s", [D, T], BF16, kind="Internal").ap())
    ntile = T // 128
    with ExitStack() as es:
        kb = KB(nc, es)
        identf = make_ident(kb, F32)
        banks = [kb.ps([128, 512]) for _ in range(4)]
        pst2 = [kb.ps([128, 1024]) for _ in range(2)]
        h1halo = kb.sb([128, 8, 128], BF16)
        pp = kb.sb([128, 6, 8])
        ppb = [Buf(pp.t) for _ in range(4)]
        for j, r in enumerate((1, 2, 4, 5)):
            kb.dma("sp", pp.t[:, j, :], vec.t[r, :].rearrange("(k p) -> p k", p=128), writes=[ppb[j]], allow_slow_non_contiguous=True)
        for j in (1, 3):
            kb.op("dve", lambda e, j=j: e.tensor_scalar_add(out=pp.t[:, j, :], in0=pp.t[:, j, :], scalar1=1.0), reads=[ppb[j]], writes=[ppb[j]])

        class PV:
            pass
        sh2 = Buf(pp.t[:, 0, :]); sc2 = Buf(pp.t[:, 1, :]); shn = Buf(pp.t[:, 2, :]); scn = Buf(pp.t[:, 3, :])
        for v, b in ((sh2, ppb[0]), (sc2, ppb[1]), (shn, ppb[2]), (scn, ppb[3])):
            v.w = b.w
        flg = kb.sb([128, 1])
        kb.dma("sp", flg.t[:], flag.t[:, :], writes=[flg])
        st = kb.sb([128, 2, 6]); mv = kb.sb([128, 4])
        zs = [kb.sb([128, D]) for _ in range(2)]
        xs = [kb.sb([128, D]) for _ in range(2)]
        hts = [kb.sb([128, 8, 128], BF16) for _ in range(2)]
        g1p = kb.sb([128, D]); lng = kb.sb([128, D]); lnb = kb.sb([128, D])

        def load_gl(gr, lgr, lbr):
            load_bcast(kb, "sp", g1p, vec.t[gr:gr + 1, :])
            load_bcast(kb, "sp", lng, vec.t[lgr:lgr + 1, :])
            load_bcast(kb, "sp", lnb, vec.t[lbr:lbr + 1, :])
            kb.op("pool", lambda e: e.tensor_scalar_add(out=g1p.t[:], in0=g1p.t[:], scalar1=1.0), reads=[g1p], writes=[g1p])

        load_gl(0, 6, 7)
        h1v = h1s.t.rearrange("(k p) t -> p k t", p=128)
        hov = hTo.t.rearrange("(k p) t -> p k t", p=128)
        oTv = oT.t.rearrange("(k p) t -> p k t", p=128)

        def resid_ln(psy, xt, z):
            for half in range(2):
                kb.op("dve", lambda e, half=half: e.tensor_tensor(out=z.t[:, half * 512:(half + 1) * 512], in0=psy[half].t[:, :],
                                                                   in1=g1p.t[:, half * 512:(half + 1) * 512], op=ALU.mult),
                      reads=[psy[half], g1p], writes=[z])
            kb.op("dve", lambda e: e.scalar_tensor_tensor(out=z.t[:], in0=xt.t[:], scalar=DN_ALPHA, in1=z.t[:],
                                                           op0=ALU.mult, op1=ALU.add), reads=[xt, z], writes=[z])
            emit_ln(kb, z, st, mv, lng, lnb)

        with ExitStack() as esA:
            wob = kb.sb([128, 8, D], BF16, nparts=8, es=esA)
            stage = [kb.sb([128, 512], es=esA) for _ in range(2)]
            ots = [kb.sb([128, 8, 128], BF16, es=esA) for _ in range(2)]
            load_w_bf16(kb, wob, lambda k, c0, c1: wo.t[k * 128:(k + 1) * 128, c0:c1], 8, D, stage)
            for i in range(ntile + 1):
                xt = xs[i % 2]; z = zs[i % 2]; ot = ots[i % 2]; ht = hts[i % 2]
                psy = banks[2 * (i % 2):2 * (i % 2) + 2]
                kb.dma("sp", xt.t[:], x.t[i * 128:(i + 1) * 128, :], writes=[xt])
                kb.dma("pool", ot.t[:], oTv[:, :, i * 128:(i + 1) * 128], writes=[ot])
                for half in range(2):
                    for k in range(8):
                        kb.mm(psy[half].t[:, :], ot.t[:, k, :], wob.t[:, k, half * 512:(half + 1) * 512], k == 0, k == 7,
                              [ot, wob.parts[k]], [psy[half]])
                resid_ln(psy, xt, z)
                if i > 0:
                    kb.dma("sp", x1s.t[(i - 1) * 128:i * 128, :], z.t[:], reads=[z], writes=[x1s])
                emit_modT(kb, z, pst2[i % 2], identf, sc2, sh2, h1halo if i == 0 else ht)
                if i > 0:
                    kb.dma("pool", h1v[:, :, (i - 1) * 128:i * 128], ht.t[:], reads=[ht], writes=[h1s])

        kb.barrier()
        load_gl(3, 8, 9)
        with ExitStack() as esB:
            wub = kb.sb([128, 8, 2 * D_FF], BF16, nparts=8, es=esB)
            wdb = kb.sb([128, NFC, D], BF16, nparts=NFC, es=esB)
            stage = [kb.sb([128, 512], es=esB) for _ in range(2)]
            load_w_bf16(kb, wub, lambda k, c0, c1: wup.t[k * 128:(k + 1) * 128, c0:c1], 8, 2 * D_FF, stage)
            load_w_bf16(kb, wdb, lambda k, c0, c1: wdn.t[k * 128:(k + 1) * 128, c0:c1], NFC, D, stage)
            cw = kb.sb([128, 3, NFC], es=esB); cb = kb.sb([128, NFC], es=esB)
            for j in range(3):
                kb.dma("sp", cw.t[:, j, :], convw.t[j, :].rearrange("(k p) -> p k", p=128), writes=[cw], allow_slow_non_contiguous=True)
            kb.dma("sp", cb.t[:, :], convb.t[:].rearrange("(k p) -> p k", p=128), writes=[cb], allow_slow_non_contiguous=True)
            uprev = kb.sb([128, NFC, 2], nparts=NFC, es=esB)
            h1t = [kb.sb([128, 8, TT], BF16, es=esB) for _ in range(2)]
            aT = kb.sb([128, NFC, TT], BF16, nparts=NFC, es=esB)
            ubs = [kb.sb([128, TT + 2], es=esB) for _ in range(2)]
            cbs = [kb.sb([128, TT], es=esB) for _ in range(2)]
            sbs = [kb.sb([128, TT], es=esB) for _ in range(2)]
            pu = banks[0]
            for fc in range(NFC):
                for k in range(8):
                    kb.mm(pu.t[:, fc * 2:fc * 2 + 2], wub.t[:, k, fc * 128:(fc + 1) * 128], h1halo.t[:, k, 126:128], k == 0, k == 7,
                          [wub.parts[k], h1halo], [pu])
            kb.op("dve", lambda e: e.tensor_scalar_mul(out=uprev.t[:].rearrange("p a b -> p (a b)"), in0=pu.t[:, 0:2 * NFC],
                                                       scalar1=flg.t[:, 0:1]), reads=[pu, flg], writes=uprev.parts)
            nug = 0
            for it in range(T // TT):
                hh = h1t[it % 2]
                kb.dma("sp", hh.t[:], h1v[:, :, it * TT:(it + 1) * TT], reads=[h1s], writes=[hh])
                for fc in range(NFC):
                    pug = banks[nug % 2]; ub = ubs[nug % 2]; cbuf = cbs[nug % 2]; sbuf = sbs[nug % 2]; nug += 1
                    for k in range(8):
                        kb.mm(pug.t[:, 0:TT], wub.t[:, k, fc * 128:(fc + 1) * 128], hh.t[:, k, :], k == 0, k == 7, [wub.parts[k], hh], [pug])
                    for k in range(8):
                        kb.mm(pug.t[:, TT:2 * TT], wub.t[:, k, D_FF + fc * 128:D_FF + (fc + 1) * 128], hh.t[:, k, :], k == 0, k == 7,
                              [wub.parts[k], hh], [pug])
                    kb.op("pool", lambda e, ub=ub, fc=fc: e.tensor_copy(out=ub.t[:, 0:2], in_=uprev.t[:, fc, :]), reads=[uprev.parts[fc]], writes=[ub])
                    kb.op("act", lambda e, ub=ub, pug=pug: e.copy(out=ub.t[:, 2:TT + 2], in_=pug.t[:, 0:TT]), reads=[pug], writes=[ub])
                    kb.op("pool", lambda e, ub=ub, fc=fc: e.tensor_copy(out=uprev.t[:, fc, :], in_=ub.t[:, TT:TT + 2]), reads=[ub], writes=[uprev.parts[fc]])
                    kb.op("act", lambda e, ub=ub, cbuf=cbuf, fc=fc: e.activation(out=cbuf.t[:], in_=ub.t[:, 2:TT + 2], func=AF.Identity,
                                                                                 scale=cw.t[:, 2, fc:fc + 1], bias=cb.t[:, fc:fc + 1]),
                          reads=[ub, cw, cb], writes=[cbuf])
                    kb.op("dve", lambda e, ub=ub, cbuf=cbuf, fc=fc: e.scalar_tensor_tensor(out=cbuf.t[:], in0=ub.t[:, 1:TT + 1], scalar=cw.t[:, 1, fc:fc + 1],
                                                                                           in1=cbuf.t[:], op0=ALU.mult, op1=ALU.add),
                          reads=[ub, cw, cbuf], writes=[cbuf])
                    kb.op("dve", lambda e, ub=ub, cbuf=cbuf, fc=fc: e.scalar_tensor_tensor(out=cbuf.t[:], in0=ub.t[:, 0:TT], scalar=cw.t[:, 0, fc:fc + 1],
                                                                                           in1=cbuf.t[:], op0=ALU.mult, op1=ALU.add),
                          reads=[ub, cw, cbuf], writes=[cbuf])
                    kb.op("act", lambda e, cbuf=cbuf, sbuf=sbuf: e.activation(out=sbuf.t[:], in_=cbuf.t[:], func=AF.Silu), reads=[cbuf], writes=[sbuf])
                    kb.op("dve", lambda e, sbuf=sbuf, pug=pug, fc=fc: e.tensor_tensor(out=aT.t[:, fc, :], in0=pug.t[:, TT:2 * TT], in1=sbuf.t[:], op=ALU.mult),
                          reads=[pug, sbuf], writes=[aT.parts[fc]])
                for sub in range(TT // 128):
                    i = it * (TT // 128) + sub
                    psy = banks[2:4]
                    xt = xs[i % 2]; z = zs[i % 2]; ht = hts[i % 2]
                    kb.dma("sp", xt.t[:], x1s.t[i * 128:(i + 1) * 128, :], reads=[x1s], writes=[xt])
                    for half in range(2):
                        for fc in range(NFC):
                            kb.mm(psy[half].t[:, :], aT.t[:, fc, sub * 128:(sub + 1) * 128], wdb.t[:, fc, half * 512:(half + 1) * 512],
                                  fc == 0, fc == NFC - 1, [aT.parts[fc], wdb.parts[fc]], [psy[half]])
                    resid_ln(psy, xt, z)
                    kb.dma("sp", xo.t[i * 128:(i + 1) * 128, :], z.t[:], reads=[z], writes=[xo])
                    emit_modT(kb, z, pst2[i % 2], identf, scn, shn, ht)
                    kb.dma("pool", hov[:, :, i * 128:(i + 1) * 128], ht.t[:], reads=[ht], writes=[hTo])
        kb.finish([xo, hTo])
    return nc


def build_mod():
    nc = new_nc()
    c = dr_in(nc, "c", [NB, D])
    w = dr_in(nc, "w", [D, 3 * D])
    b = dr_in(nc, "b", [1, 3 * D])
    out = dr_out(nc, "out", [NB, 3 * D])
    with ExitStack() as es:
        kb = KB(nc, es)
        cs = kb.sb([128, 8, NB])
        for bb in range(NB):
            kb.dma("sp", cs.t[:, :, bb], c.t[bb, :].rearrange("(k p) -> p k", p=128), writes=[cs], allow_slow_non_contiguous=True)
        kb.op("act", lambda e: e.activation(out=cs.t[:], in_=cs.t[:], func=AF.Silu), reads=[cs], writes=[cs])
        wt = kb.sb([128, 8, 3 * D], nparts=8)
        for k in range(8):
            kb.dma("sp" if k % 2 else "pool", wt.t[:, k, :], w.t[k * 128:(k + 1) * 128, :], writes=[wt.parts[k]])
        bt = kb.sb([NB, 3 * D])
        load_bcast(kb, "sp", bt, b.t[0:1, :], n=NB)
        ot = kb.sb([NB, 3 * D])
        banks = [kb.ps([128, 512]) for _ in range(6)]
        for n in range(6):
            for k in range(8):
                kb.mm(banks[n].t[0:NB, :], cs.t[:, k, :], wt.t[:, k, n * 512:(n + 1) * 512], k == 0, k == 7, [cs, wt.parts[k]], [banks[n]])
            kb.op("dve", lambda e, n=n: e.tensor_tensor(out=ot.t[:, n * 512:(n + 1) * 512], in0=banks[n].t[0:NB, :],
                                                         in1=bt.t[:, n * 512:(n + 1) * 512], op=ALU.add), reads=[banks[n], bt], writes=[ot])
        kb.dma("sp", out.t[:, :], ot.t[:], reads=[ot], writes=[out])
        kb.finish([out])
    return nc


GLA_DK, GLA_DV = 128, 256


def tri_const(kb, val, upper_strict):
    m = kb.sb([128, 128])
    kb.op("pool", lambda e: e.memset(m.t[:], val), writes=[m])
    if not upper_strict:
        kb.op("pool", lambda e: e.affine_select(out=m.t[:], in_=m.t[:], pattern=[[1, 128]], compare_op=ALU.is_ge, fill=0.0,
                                                 base=0, channel_multiplier=-1), reads=[m], writes=[m])
        kb.op("pool", lambda e: e.memset(m.t[0:64, 64:128], 0.0), reads=[m], writes=[m])
    else:
        kb.op("pool", lambda e: e.affine_select(out=m.t[:], in_=m.t[:], pattern=[[-1, 128]], compare_op=ALU.is_ge, fill=0.0,
                                                 base=-1, channel_multiplier=1), reads=[m], writes=[m])
        kb.op("pool", lambda e: e.memset(m.t[64:128, 0:64], 0.0), reads=[m], writes=[m])
    return m


GLA_STAGE = 7
GLA_SUB = 99


def build_gla(S):
    nc = new_nc()
    hT = dr_in(nc, "hT", [D, S], BF16)
    wq = dr_in(nc, "wq", [D, 128]); wk = dr_in(nc, "wk", [D, 128])
    wv = dr_in(nc, "wv", [D, 256]); wr = dr_in(nc, "wr", [D, 256]); wa = dr_in(nc, "wa", [D, 16])
    wa2 = dr_in(nc, "wa2", [33, 128])
    ng = dr_in(nc, "ng", [1, 256])
    oT = dr_out(nc, "oT", [256, S], BF16)
    with ExitStack() as es:
        kb = KB(nc, es)
        identb = make_ident(kb, BF16)
        Lneg = tri_const(kb, -1.0 / 16.0, False)
        Uneg = tri_const(kb, -1.0 / 16.0, True)
        M1 = tri_const(kb, 1.0, False)
        stage = [kb.sb([128, 512]) for _ in range(2)]
        wqb = kb.sb([128, 8, 128], BF16, nparts=8); wkb = kb.sb([128, 8, 128], BF16, nparts=8)
        wvb = kb.sb([128, 8, 256], BF16, nparts=8); wrb = kb.sb([128, 8, 256], BF16, nparts=8)
        wab = kb.sb([128, 8, 16], BF16, nparts=8)
        for dst, src, n in ((wqb, wq, 128), (wkb, wk, 128), (wvb, wv, 256), (wrb, wr, 256), (wab, wa, 16)):
            load_w_bf16(kb, dst, lambda k, c0, c1, src=src: src.t[k * 128:(k + 1) * 128, c0:c1], 8, n, stage)
        wa2b = kb.sb([33, 1, 128], BF16, nparts=1)
        load_w_bf16(kb, wa2b, lambda k, c0, c1: wa2.t[0:33, c0:c1], 1, 128, stage, npart=33)
        ngb = kb.sb([128, 256])
        load_bcast(kb, "sp", ngb, ng.t[0:1, :])
        alT = kb.sb([33, 512], BF16)
        kb.op("pool", lambda e: e.memset(alT.t[:], 0.0), writes=[alT])
        kb.op("pool", lambda e: e.memset(alT.t[32:33, :], 1.0), reads=[alT], writes=[alT])
        S32 = kb.sb([128, 256]); SbA = kb.sb([128, 256], BF16); SbB = kb.sb([128, 256], BF16)
        kb.op("pool", lambda e: e.memset(S32.t[:], 0.0), writes=[S32])
        kb.op("pool", lambda e: e.memset(SbA.t[:], 0.0), writes=[SbA])
        pq = kb.ps(); pk = kb.ps(); pal = kb.ps(); pD = kb.ps(); pE = kb.ps(); pF = kb.ps(); pG = kb.ps(); pH = kb.ps()
        pDk = pDl = pDt = pD
        pEv = pEr = pE
        pFb = pFr = pFa = pF
        pGa = pGb = pG
        pH0 = pH1 = pH
        pT = pD.t[:, 256:512].bitcast(BF16)
        hhs = [kb.sb([128, 8, 512], BF16) for _ in range(2)]
        R = 2
        ex = [kb.sb([128, 128]) for _ in range(R)]; ltm = [kb.sb([128, 128]) for _ in range(R)]
        ebT = [kb.sb([128, 128]) for _ in range(R)]; einvT = [kb.sb([128, 128]) for _ in range(R)]
        erem = [kb.sb([128, 128]) for _ in range(R)]
        qdT = [kb.sb([128, 128], BF16) for _ in range(R)]; kiT = [kb.sb([128, 128], BF16) for _ in range(R)]
        kdec = [kb.sb([128, 128], BF16) for _ in range(R)]; vsb = [kb.sb([128, 256], BF16) for _ in range(R)]
        kdec1 = [kb.sb([128, 128], BF16) for _ in range(R)]
        hm01 = kb.sb([128, 2])
        kb.op("pool", lambda e: e.memset(hm01.t[:], 0.0), writes=[hm01])
        kb.op("pool", lambda e: e.memset(hm01.t[0:64, 0:1], 1.0), reads=[hm01], writes=[hm01])
        kb.op("pool", lambda e: e.memset(hm01.t[64:128, 1:2], 1.0), reads=[hm01], writes=[hm01])
        sr = [kb.sb([128, 256]) for _ in range(R)]; gr = [kb.sb([128, 256]) for _ in range(R)]
        attm = [kb.sb([128, 128], BF16) for _ in range(R)]
        oint = [kb.sb([128, 256]) for _ in range(R)]; osb = [kb.sb([128, 256]) for _ in range(R)]
        junk = kb.sb([128, 256]); ss = [kb.sb([128, 2]) for _ in range(R)]
        og = [kb.sb([128, 256], BF16) for _ in range(R)]; oTt = [kb.sb([128, 2, 128], BF16) for _ in range(R)]
        hv = hT.t.rearrange("(k p) t -> p k t", p=128)
        ov = oT.t.rearrange("(c p) t -> p c t", p=128)
        scale = GLA_DK ** -0.5
        n = 0
        for st in range(S // 512):
            hh = hhs[st % 2]
            kb.dma("sp", hh.t[:], hv[:, :, st * 512:(st + 1) * 512], writes=[hh])
            for k in range(8):
                kb.mm(pq.t[:, :], wqb.t[:, k, :], hh.t[:, k, :], k == 0, k == 7, [wqb.parts[k], hh], [pq])
            for k in range(8):
                kb.mm(pk.t[:, :], wkb.t[:, k, :], hh.t[:, k, :], k == 0, k == 7, [wkb.parts[k], hh], [pk])
            for k in range(8):
                kb.mm(pal.t[0:16, :], wab.t[:, k, :], hh.t[:, k, :], k == 0, k == 7, [wab.parts[k], hh], [pal])
            kb.op("act", lambda e: e.copy(out=alT.t[0:16, :], in_=pal.t[0:16, :]), reads=[pal], writes=[alT])
            for j in range(4):
                r = n % R; n += 1
                sub = slice(j * 128, (j + 1) * 128)
                for k in range(8):
                    kb.mm(pD.t[:, 0:128], hh.t[:, k, sub], wkb.t[:, k, :], k == 0, k == 7, [wkb.parts[k], hh], [pDk])
                kb.mm(pD.t[:, 128:256], alT.t[0:33, sub], wa2b.t[0:33, 0, :], True, True, [alT, wa2b.parts[0]], [pDl])
                for k in range(8):
                    kb.mm(pE.t[:, 0:256], hh.t[:, k, sub], wvb.t[:, k, :], k == 0, k == 7, [wvb.parts[k], hh], [pEv])
                for k in range(8):
                    kb.mm(pE.t[:, 256:512], hh.t[:, k, sub], wrb.t[:, k, :], k == 0, k == 7, [wrb.parts[k], hh], [pEr])
                if GLA_STAGE >= 2:
                    kb.op("act", lambda e, r=r: e.activation(out=ex[r].t[:], in_=pD.t[:, 128:256], func=AF.Exp, scale=-1.0), reads=[pDl], writes=[ex[r]])
                    kb.op("act", lambda e, r=r: e.activation(out=ltm[r].t[:], in_=ex[r].t[:], func=AF.Ln, bias=1.0), reads=[ex[r]], writes=[ltm[r]])
                if GLA_STAGE >= 3:
                    kb.mm(pF.t[:, 0:128], ltm[r].t[:], Lneg.t[:], True, True, [ltm[r], Lneg], [pFb])
                    kb.mm(pF.t[:, 128:256], Uneg.t[:], ltm[r].t[:], True, True, [ltm[r], Uneg], [pFr])
                    kb.op("act", lambda e, r=r: e.activation(out=ebT[r].t[:], in_=pF.t[:, 0:128], func=AF.Exp), reads=[pFb], writes=[ebT[r]])
                    kb.op("act", lambda e, r=r: e.activation(out=einvT[r].t[:], in_=pF.t[:, 0:128], func=AF.Exp, scale=-1.0), reads=[pFb], writes=[einvT[r]])
                    kb.op("act", lambda e, r=r: e.activation(out=erem[r].t[:], in_=pF.t[:, 128:256], func=AF.Exp), reads=[pFr], writes=[erem[r]])
                    kb.op("dve", lambda e, r=r, sub=sub: e.scalar_tensor_tensor(out=qdT[r].t[:], in0=pq.t[:, sub], scalar=scale, in1=ebT[r].t[:],
                                                                                op0=ALU.mult, op1=ALU.mult), reads=[pq, ebT[r]], writes=[qdT[r]])
                    kb.op("dve", lambda e, r=r, sub=sub: e.tensor_tensor(out=kiT[r].t[:], in0=pk.t[:, sub], in1=einvT[r].t[:], op=ALU.mult),
                          reads=[pk, einvT[r]], writes=[kiT[r]])
                    kb.op("dve", lambda e, r=r: e.scalar_tensor_tensor(out=kdec[r].t[:], in0=pD.t[:, 0:128], scalar=hm01.t[:, 0:1], in1=erem[r].t[:], op0=ALU.mult, op1=ALU.mult),
                          reads=[pDk, erem[r], hm01], writes=[kdec[r]])
                    kb.op("dve", lambda e, r=r: e.scalar_tensor_tensor(out=kdec1[r].t[:], in0=pD.t[:, 0:128], scalar=hm01.t[:, 1:2], in1=erem[r].t[:], op0=ALU.mult, op1=ALU.mult),
                          reads=[pDk, erem[r], hm01], writes=[kdec1[r]])
                    kb.op("act", lambda e, r=r: e.copy(out=vsb[r].t[:], in_=pE.t[:, 0:256]), reads=[pEv], writes=[vsb[r]])
                    kb.op("act", lambda e, r=r: e.activation(out=sr[r].t[:], in_=pE.t[:, 256:512], func=AF.Silu), reads=[pEr], writes=[sr[r]])
                    kb.op("pool", lambda e, r=r: e.tensor_tensor(out=gr[r].t[:], in0=sr[r].t[:], in1=ngb.t[:], op=ALU.mult), reads=[sr[r], ngb], writes=[gr[r]])
                if GLA_STAGE >= 4:
                    kb.mm(pF.t[:, 256:384], kiT[r].t[:], qdT[r].t[:], True, True, [kiT[r], qdT[r]], [pFa])
                    kb.op("dve", lambda e, r=r: e.tensor_tensor(out=attm[r].t[:], in0=pF.t[:, 256:384], in1=M1.t[:], op=ALU.mult),
                          reads=[pFa, M1], writes=[attm[r]])
                    kb.mm(pG.t[:, 0:256], attm[r].t[:], vsb[r].t[:], True, True, [attm[r], vsb[r]], [pGa])
                if GLA_STAGE >= 5:
                    if GLA_SUB >= 1:
                        kb.mm(pH.t[:, 0:256], kdec[r].t[:, :], vsb[r].t[:, :], True, True, [kdec[r], vsb[r]], [pH0])
                    if GLA_SUB >= 2:
                        kb.mm(pH.t[:, 256:512], kdec1[r].t[:, :], vsb[r].t[:, :], True, True, [kdec1[r], vsb[r]], [pH1])
                    if GLA_SUB >= 3:
                        kb.mm(pG.t[:, 256:512], qdT[r].t[:, :], SbA.t[:], True, True, [qdT[r], SbA], [pGb])
                    if GLA_SUB >= 4:
                        kb.op("dve", lambda e, r=r: e.scalar_tensor_tensor(out=S32.t[:], in0=S32.t[:], scalar=ebT[r].t[:, 63:64], in1=pH.t[:, 0:256],
                                                                           op0=ALU.mult, op1=ALU.add), reads=[S32, ebT[r], pH0], writes=[S32])
                    if GLA_SUB >= 5:
                        kb.op("pool", lambda e: e.tensor_copy(out=SbB.t[:], in_=S32.t[:]), reads=[S32], writes=[SbB])
                    if GLA_SUB >= 6:
                        kb.mm(pal.t[:, 0:256], qdT[r].t[:, :], SbB.t[:], True, True, [qdT[r], SbB], [pal])
                    if GLA_SUB >= 7:
                        kb.op("dve", lambda e, r=r: e.scalar_tensor_tensor(out=S32.t[:], in0=S32.t[:], scalar=ebT[r].t[:, 127:128], in1=pH.t[:, 256:512],
                                                                           op0=ALU.mult, op1=ALU.add), reads=[S32, ebT[r], pH1], writes=[S32])
                    if GLA_SUB >= 8:
                        kb.op("pool", lambda e: e.tensor_copy(out=SbA.t[:], in_=S32.t[:]), reads=[S32], writes=[SbA])
                if GLA_STAGE >= 6:
                    kb.op("act", lambda e, r=r: e.copy(out=oint[r].t[0:64, :], in_=pG.t[0:64, 256:512]), reads=[pGb], writes=[oint[r]])
                    kb.op("act", lambda e, r=r: e.copy(out=oint[r].t[64:128, :], in_=pal.t[64:128, 0:256]), reads=[pal, oint[r]], writes=[oint[r]])
                    kb.op("dve", lambda e, r=r: e.tensor_tensor(out=osb[r].t[:], in0=pG.t[:, 0:256], in1=oint[r].t[:], op=ALU.add),
                          reads=[pGa, oint[r]], writes=[osb[r]])
                    kb.op("act", lambda e, r=r: e.activation(out=junk.t[:], in_=osb[r].t[:], func=AF.Square, accum_out=ss[r].t[:, 0:1]),
                          reads=[osb[r]], writes=[junk, ss[r]])
                    kb.op("dve", lambda e, r=r: e.tensor_scalar(out=ss[r].t[:, 1:2], in0=ss[r].t[:, 0:1], scalar1=1.0 / GLA_DV, scalar2=LN_EPS,
                                                                op0=ALU.mult, op1=ALU.add), reads=[ss[r]], writes=[ss[r]])
                    kb.op("act", lambda e, r=r: e.sqrt(out=ss[r].t[:, 1:2], in_=ss[r].t[:, 1:2]), reads=[ss[r]], writes=[ss[r]])
                    kb.op("dve", lambda e, r=r: e.reciprocal(out=ss[r].t[:, 1:2], in_=ss[r].t[:, 1:2]), reads=[ss[r]], writes=[ss[r]])
                    kb.op("dve", lambda e, r=r: e.scalar_tensor_tensor(out=og[r].t[:], in0=osb[r].t[:], scalar=ss[r].t[:, 1:2], in1=gr[r].t[:],
                                                                       op0=ALU.mult, op1=ALU.mult), reads=[osb[r], ss[r], gr[r]], writes=[og[r]])
                if GLA_STAGE >= 7:
                    for c in range(2):
                        kb.op("pe", lambda e, r=r, c=c: e.transpose(out=pT[:, c * 128:(c + 1) * 128], in_=og[r].t[:, c * 128:(c + 1) * 128],
                                                                    identity=identb.t[:]), reads=[og[r], identb], writes=[pDt])
                    kb.op("act", lambda e, r=r: e.copy(out=oTt[r].t[:].rearrange("p a b -> p (a b)"), in_=pT[:, 0:256]), reads=[pDt], writes=[oTt[r]])
                t0 = st * 512 + j * 128
                kb.dma("pool", ov[:, :, t0:t0 + 128], oTt[r].t[:], reads=[oTt[r]], writes=[oT])
        kb.finish([oT])
    return nc


REL_BUCKETS, REL_MAX_DIST = 32, 2048


def rel_bucket_np(n):
    n = np.maximum(n, 0)
    exact = REL_BUCKETS // 2
    logn = np.log(np.maximum(n, 1).astype(np.float32) / exact)
    large = exact + (logn / np.float32(math.log(REL_MAX_DIST / exact)) * (REL_BUCKETS - exact)).astype(np.int32)
    large = np.minimum(large, REL_BUCKETS - 1)
    return np.where(n < exact, n, large)


def onehot_struct(dists, valid):
    L = len(dists)
    oh = np.zeros((33, L), np.float32)
    b = rel_bucket_np(np.asarray(dists))
    for i in range(L):
        if valid[i]:
            oh[b[i], i] = 1.0
        else:
            oh[32, i] = 1.0
    return oh


def make_flipJ(kb):
    jm = kb.sb([128, 128])
    kb.op("pool", lambda e: e.memset(jm.t[:], 0.0), writes=[jm])
    kb.op("pool", lambda e: e.affine_select(out=jm.t[:], in_=jm.t[:], pattern=[[1, 128]], compare_op=ALU.not_equal, fill=1.0,
                                             base=-127, channel_multiplier=1), reads=[jm], writes=[jm])
    return jm


def build_wide_bias(kb, tab, oh_ap, nh, L, W, Fd, fd_row0, wides, wbuf, pbank, jm, es):
    oht = kb.sb([33, L], es=es)
    kb.dma("sp", oht.t[:], oh_ap, writes=[oht])
    fs = kb.sb([nh, L], es=es)
    for c0 in range(0, L, 512):
        c1 = min(L, c0 + 512)
        kb.mm(pbank.t[0:nh, 0:c1 - c0], tab.t[0:33, 0:nh], oht.t[0:33, c0:c1], True, True, [tab, oht], [pbank])
        kb.op("dve", lambda e, c0=c0, c1=c1: e.tensor_copy(out=fs.t[0:nh, c0:c1], in_=pbank.t[0:nh, 0:c1 - c0]), reads=[pbank], writes=[fs])
    kb.dma("sp", Fd.t[fd_row0:fd_row0 + nh, 0:L], fs.t[0:nh, :], reads=[fs], writes=[Fd])
    tp = kb.sb([128, W], es=es)
    for h in range(nh):
        src = bass.AP(Fd.t.tensor, (fd_row0 + h) * Fd.t.shape[1], [[1, 128], [1, W]])
        kb.dma("sp", tp.t[:, :], src, reads=[Fd], writes=[tp])
        for c0 in range(0, W, 512):
            c1 = min(W, c0 + 512)
            kb.mm(pbank.t[:, 0:c1 - c0], jm.t[:], tp.t[:, c0:c1], True, True, [jm, tp], [pbank])
            kb.op("dve", lambda e, h=h, c0=c0, c1=c1: e.tensor_copy(out=wides[h][:, c0:c1], in_=pbank.t[:, 0:c1 - c0]), reads=[pbank], writes=[wbuf])


DIL_GROUPS = ((128, 1), (512, 4), (2048, 16))
CH = 2048


def dil_onehots():
    ohs = []
    for (window, dil) in DIL_GROUPS:
        m = np.arange(383) - 127
        ohs.append(onehot_struct(m * dil, (m >= 0) & (m <= window // dil)))
    return np.stack(ohs)


def build_dil(S):
    nc = new_nc()
    hT = dr_in(nc, "hT", [D, S], BF16)
    wqkv = dr_in(nc, "wqkv", [3, D, 768])
    tabi = dr_in(nc, "tab", [33, 4])
    ohi = dr_in(nc, "oh", [3, 33, 383])
    oT = dr_out(nc, "oT", [256, S], BF16)
    Fd = Buf(nc.dram_tensor("Fd", [12, 383], F32, kind="Internal").ap())
    wbf = Buf(nc.dram_tensor("wbf", [3, 128, 8 * 768], BF16, kind="Internal").ap())
    nchunk = S // CH
    scale = 64 ** -0.5
    with ExitStack() as es:
        kb = KB(nc, es)
        banks = [kb.ps() for _ in range(8)]
        jm = make_flipJ(kb)
        tab = kb.sb([33, 4])
        kb.dma("sp", tab.t[:], tabi.t[:, :], writes=[tab])
        bias = [kb.sb([128, 4, 256]) for _ in range(3)]
        sel65 = kb.sb([128, 64])
        kb.op("pool", lambda e: e.memset(sel65.t[:], 0.0), writes=[sel65])
        kb.op("pool", lambda e: e.memset(sel65.t[64:65, :], 1.0), reads=[sel65], writes=[sel65])
        with ExitStack() as es0:
            for g in range(3):
                build_wide_bias(kb, tab, ohi.t[g], 4, 383, 256, Fd, 4 * g, [bias[g].t[:, h, :] for h in range(4)], bias[g], banks[0], jm, es0)
        kb.barrier()
        with ExitStack() as es1:
            stage = [kb.sb([128, 768], es=es1) for _ in range(2)]
            wtmp = [kb.sb([128, 8, 768], BF16, nparts=8, es=es1) for _ in range(1)]
            for g in range(3):
                load_w_bf16(kb, wtmp[0], lambda k, c0, c1, g=g: wqkv.t[g, k * 128:(k + 1) * 128, c0:c1], 8, 768, stage, chunk=768)
                kb.dma("sp", wbf.t[g].rearrange("p (k c) -> p k c", k=8), wtmp[0].t[:], reads=wtmp[0].parts, writes=[wbf])
                for p_ in wtmp[0].parts:
                    p_.r[kb.dq["sp"][(kb.di["sp"] - 1) % len(kb.dq["sp"])]] = kb.dsl[kb.dq["sp"][(kb.di["sp"] - 1) % len(kb.dq["sp"])]][1]
        kb.barrier()
        wg = [kb.sb([128, 8, 768], BF16) for _ in range(2)]
        hh = [kb.sb([128, 8, CH], BF16) for _ in range(2)]
        qTm = [[kb.sb([128, CH], BF16) for _ in range(2)] for _ in range(2)]
        for hp_ in range(2):
            for hd_ in range(2):
                kb.op("pool", lambda e, hp_=hp_, hd_=hd_: e.memset(qTm[hp_][hd_].t[:], 0.0), writes=[qTm[hp_][hd_]])
        kT = [kb.sb([128, 2 * CH], BF16) for _ in range(2)]
        V = kb.sb([128, 32, 4, 65], BF16)
        kb.op("pool", lambda e: e.memset(V.t[:].rearrange("p a b c -> p (a b c)"), 1.0), writes=[V])
        acc = kb.sb([65, 4, CH])
        tmp = [kb.sb([128, 512]) for _ in range(2)]
        pT = [kb.sb([128, 512], BF16) for _ in range(2)]
        oTt = [kb.sb([64, CH], BF16) for _ in range(2)]
        rdb = kb.sb([64, 512])
        hv = hT.t.rearrange("(k p) t -> p k t", p=128)
        nw = 0
        nit = 0
        for c in range(nchunk):
            cur = hh[c % 2]
            kb.dma("sp", cur.t[:], hv[:, :, c * CH:(c + 1) * CH], writes=[cur])
            slots = ([(hh[(c - 1) % 2], 0)] if c > 0 else []) + [(cur, 1)]
            for g, (window, d) in enumerate(DIL_GROUPS):
                w = wg[nw % 2]; nw += 1
                kb.dma("pool", w.t[:], wbf.t[g].rearrange("p (k c) -> p k c", k=8), reads=[wbf], writes=[w])
                nbk = 16 // d
                for hp in range(2):
                    for n4 in range(4):
                        pb = banks[(n4 + hp) % 2]
                        for k in range(8):
                            kb.mm(pb.t[:, :], w.t[:, k, hp * 128:(hp + 1) * 128], cur.t[:, k, n4 * 512:(n4 + 1) * 512], k == 0, k == 7, [w, cur], [pb])
                        kb.op("act", lambda e, pb=pb, hp=hp, n4=n4: e.copy(out=qTm[hp][0].t[0:64, n4 * 512:(n4 + 1) * 512], in_=pb.t[0:64, :]), reads=[pb], writes=[qTm[hp][0]])
                        kb.op("act", lambda e, pb=pb, hp=hp, n4=n4: e.copy(out=qTm[hp][1].t[64:128, n4 * 512:(n4 + 1) * 512], in_=pb.t[64:128, :]), reads=[pb], writes=[qTm[hp][1]])
                    for (hb, si) in slots:
                        for n4 in range(4):
                            pb = banks[(n4 + hp) % 2]
                            for k in range(8):
                                kb.mm(pb.t[:, :], w.t[:, k, 256 + hp * 128:256 + (hp + 1) * 128], hb.t[:, k, n4 * 512:(n4 + 1) * 512], k == 0, k == 7, [w, hb], [pb])
                            kb.op("dve", lambda e, pb=pb, hp=hp, n4=n4, si=si: e.tensor_copy(out=kT[hp].t[:, si * CH + n4 * 512:si * CH + (n4 + 1) * 512], in_=pb.t[:, :]),
                                  reads=[pb], writes=[kT[hp]])
                for (hb, si) in slots:
                    for r in range(d):
                        for bl in range(nbk):
                            ti = si * 16 + r * nbk + bl
                            pb = banks[2 + ti % 2]
                            st0 = r + d * 128 * bl
                            for k in range(8):
                                kb.mm(pb.t[:, 0:256], hb.t[:, k, st0:st0 + 127 * d + 1:d], w.t[:, k, 512:768], k == 0, k == 7, [w, hb], [pb])
                            kb.op("act" if ti % 2 else "dve",
                                  (lambda e, pb=pb, ti=ti: e.copy(out=V.t[:, ti, :, 0:64], in_=pb.t[:, 0:256].rearrange("p (h c) -> p h c", h=4))) if ti % 2 else
                                  (lambda e, pb=pb, ti=ti: e.tensor_copy(out=V.t[:, ti, :, 0:64], in_=pb.t[:, 0:256].rearrange("p (h c) -> p h c", h=4))),
                                  reads=[pb], writes=[V])
                for r in range(d):
                    for bl in range(nbk):
                        q0 = r + d * 128 * bl
                        keys = [(1, r, bl)]
                        if bl > 0:
                            keys.append((1, r, bl - 1))
                        elif c > 0:
                            keys.append((0, r, nbk - 1))
                        nd = len(keys)
                        for hp in range(2):
                            it = nit % 2; nit += 1
                            ps = banks[4 + it]; po = banks[6 + it]
                            for hd in range(2):
                                for dl, (si, kr, kbl) in enumerate(keys):
                                    k0 = si * CH + kr + d * 128 * kbl
                                    kb.mm(ps.t[:, (hd * 2 + dl) * 128:(hd * 2 + dl + 1) * 128], kT[hp].t[:, k0:k0 + 127 * d + 1:d],
                                          qTm[hp][hd].t[:, q0:q0 + 127 * d + 1:d], True, True, [kT[hp], qTm[hp][hd]], [ps])
                            psv = ps.t[:, :].rearrange("p (h e i) -> p h e i", h=2, e=2)
                            tv = tmp[it].t[:, :].rearrange("p (h e i) -> p h e i", h=2, e=2)
                            pv = pT[it].t[:, :].rearrange("p (h e i) -> p h e i", h=2, e=2)
                            bv = bias[g].t[:, 2 * hp:2 * hp + 2, :].rearrange("p h (e i) -> p h e i", e=2)
                            kb.op("dve", lambda e, psv=psv, tv=tv, bv=bv, nd=nd: e.scalar_tensor_tensor(out=tv[:, :, 0:nd, :], in0=psv[:, :, 0:nd, :], scalar=scale,
                                                                                                in1=bv[:, :, 0:nd, :], op0=ALU.mult, op1=ALU.add),
                                  reads=[ps, bias[g]], writes=[tmp[it]])
                            kb.op("act", lambda e, tv=tv, pv=pv, nd=nd: e.activation(out=pv[:, :, 0:nd, :], in_=tv[:, :, 0:nd, :], func=AF.Exp),
                                  reads=[tmp[it]], writes=[pT[it]])
                            for hd in range(2):
                                for dl, (si, kr, kbl) in enumerate(keys):
                                    ti = si * 16 + kr * nbk + kbl
                                    kb.mm(po.t[0:65, hd * 128:(hd + 1) * 128], V.t[:, ti, 2 * hp + hd, :], pv[:, hd, dl, :], dl == 0, dl == nd - 1, [V, pT[it]], [po])
                            av = acc.t[:, 2 * hp:2 * hp + 2, q0:q0 + 127 * d + 1:d]
                            pov = po.t[0:65, 0:256].rearrange("p (h i) -> p h i", h=2)
                            if g == 0:
                                kb.op("dve", lambda e, av=av, pov=pov: e.tensor_copy(out=av, in_=pov), reads=[po], writes=[acc])
                            else:
                                kb.op("dve", lambda e, av=av, pov=pov: e.tensor_tensor(out=av, in0=av, in1=pov, op=ALU.add), reads=[po, acc], writes=[acc])
            for h in range(4):
                ot = oTt[h % 2]
                for n4 in range(4):
                    pb = banks[n4 % 2]
                    kb.mm(pb.t[0:64, :], sel65.t[0:65, 0:64], acc.t[0:65, h, n4 * 512:(n4 + 1) * 512], True, True, [sel65, acc], [pb])
                    kb.op("dve", lambda e, pb=pb: e.reciprocal(out=rdb.t[:, :], in_=pb.t[0:64, :]), reads=[pb], writes=[rdb])
                    kb.op("dve", lambda e, h=h, n4=n4, ot=ot: e.tensor_tensor(out=ot.t[:, n4 * 512:(n4 + 1) * 512], in0=acc.t[0:64, h, n4 * 512:(n4 + 1) * 512],
                                                                            in1=rdb.t[:, :], op=ALU.mult), reads=[acc, rdb], writes=[ot])
                kb.dma("sp", oT.t[h * 64:(h + 1) * 64, c * CH:(c + 1) * CH], ot.t[:, :], reads=[ot], writes=[oT])
        kb.finish([oT])
    return nc


SELW, WINW = 2560, 1408
SEL_FARD = 13
BIGF, BIGI = 1.0e4, -1.0e6
EB = 30000.0


def nsa_onehots():
    m = np.arange(SELW + 127) - 127 - 384
    oh_sel = onehot_struct(m, m >= 0)
    m = np.arange(WINW + 127) - 127 - 384
    oh_win = onehot_struct(m, (m >= 0) & (m <= 511))
    m = np.arange(1776 + 16) - 127
    oh_cmp = onehot_struct(m, m >= 0)
    return oh_sel, oh_win, oh_cmp


NSA_STAGE = 99
NSA_SUB = 99


def build_nsa(S):
    nc = new_nc()
    NT = S // 128
    NCK = S // 16 - 1
    NCP = ((NCK + 127) // 128) * 128
    hT = dr_in(nc, "hT", [D, S], BF16)
    wqd = dr_in(nc, "wqd", [D, 512]); wkv = dr_in(nc, "wkv", [D, 384]); wgi = dr_in(nc, "wg", [D, 12])
    pei = dr_in(nc, "pe", [2, 32, 64]); w1i = dr_in(nc, "w1", [2, 2048, 256]); b1i = dr_in(nc, "b1", [2, 256]); w2i = dr_in(nc, "w2", [2, 256, 64])
    tabi = dr_in(nc, "tab", [33, 4])
    ohs = dr_in(nc, "ohs", [33, SELW + 127]); ohw = dr_in(nc, "ohw", [33, WINW + 127]); ohc = dr_in(nc, "ohc", [33, 1792])
    oT = dr_out(nc, "oT", [256, S], BF16)
    Fd = Buf(nc.dram_tensor("Fd", [12, SELW + 127], F32, kind="Internal").ap())
    scale = 64 ** -0.5
    with ExitStack() as es:
        kb = KB(nc, es)
        banks = [kb.ps() for _ in range(8)]
        identb = make_ident(kb, BF16)
        jm = make_flipJ(kb)
        tab = kb.sb([33, 4])
        kb.dma("sp", tab.t[:], tabi.t[:, :], writes=[tab])
        kswT = kb.sb([128, S], BF16)
        Vs = kb.sb([128, NT, 65], BF16); Vw = kb.sb([128, NT, 65], BF16)
        kb.op("pool", lambda e: e.memset(Vs.t[:].rearrange("p a b -> p (a b)"), 1.0), writes=[Vs])
        kb.op("pool", lambda e: e.memset(Vw.t[:].rearrange("p a b -> p (a b)"), 1.0), writes=[Vw])
        kcmpT = kb.sb([64, NCP], BF16); vcmp = kb.sb([128, NCP // 128, 64], BF16)
        if NSA_STAGE == 1.1:
            kb.finish([oT]); return nc
        stage = [kb.sb([128, 512]) for _ in range(2)]
        wqb = kb.sb([128, 8, 512], BF16, nparts=8); wkvb = kb.sb([128, 8, 384], BF16, nparts=8); wgb = kb.sb([128, 8, 12], BF16, nparts=8)
        for dst, src, n in ((wqb, wqd, 512), (wkvb, wkv, 384), (wgb, wgi, 12)):
            load_w_bf16(kb, dst, lambda k, c0, c1, src=src: src.t[k * 128:(k + 1) * 128, c0:c1], 8, n, stage)
        if NSA_STAGE == 1.2:
            kb.finish([oT]); return nc
        hhs = [kb.sb([128, 8, 512], BF16) for _ in range(2)]
        hv = hT.t.rearrange("(k p) t -> p k t", p=128)
        with ExitStack() as es1:
            kcvcT = kb.sb([128, S], BF16, es=es1)
            w1m = [kb.sb([128, 32, 256], BF16, es=es1) for _ in range(2)]
            for x_ in range(2):
                kb.op("pool", lambda e, x_=x_: e.memset(w1m[x_].t[:].rearrange("p a b -> p (a b)"), 0.0), writes=[w1m[x_]])
            w2b = kb.sb([128, 2, 2, 64], BF16, es=es1)
            st1 = [kb.sb([128, 8, 256], es=es1) for _ in range(1)]
            nst1 = 0
            for x_ in range(2):
                rws = slice(x_ * 64, (x_ + 1) * 64)
                for pc in range(4):
                    stg = st1[0]; nst1 += 1
                    kb.dma("sp", stg.t[rws, :, :], w1i.t[x_].rearrange("(p d) h -> d p h", d=64)[:, pc * 8:(pc + 1) * 8, :], writes=[stg])
                    kb.op("dve", lambda e, x_=x_, pc=pc, stg=stg, rws=rws: e.tensor_copy(out=w1m[x_].t[rws, pc * 8:(pc + 1) * 8, :], in_=stg.t[rws, :, :]),
                          reads=[stg], writes=[w1m[x_]])
            st2 = kb.sb([128, 2, 2, 64], es=es1)
            for x_ in range(2):
                kb.dma("sp", st2.t[:, x_, :, :], w2i.t[x_].rearrange("(a p) c -> p a c", p=128), writes=[st2])
            kb.op("dve", lambda e: e.tensor_copy(out=w2b.t[:].rearrange("p a b c -> p (a b c)"), in_=st2.t[:].rearrange("p a b c -> p (a b c)")),
                  reads=[st2], writes=[w2b])
            peT = kb.sb([128, 32], es=es1); peTb = kb.sb([128, 32], BF16, es=es1)
            for x_ in range(2):
                for p4 in range(4):
                    kb.dma("sp", peT.t[x_ * 64:(x_ + 1) * 64, p4 * 8:(p4 + 1) * 8], pei.t[x_, p4 * 8:(p4 + 1) * 8, :].rearrange("p d -> d p"), writes=[peT],
                           allow_slow_non_contiguous=True)
            kb.op("dve", lambda e: e.tensor_copy(out=peTb.t[:], in_=peT.t[:]), reads=[peT], writes=[peTb])
            b1t = kb.sb([128, 2, 2], es=es1)
            for x_ in range(2):
                kb.dma("sp", b1t.t[:, x_, :], b1i.t[x_, :].rearrange("(a p) -> p a", p=128), writes=[b1t], allow_slow_non_contiguous=True)
            if NSA_STAGE == 1.3:
                kb.finish([oT]); return nc
            for st in range(S // 512):
                hh = hhs[st % 2]
                kb.dma("sp", hh.t[:], hv[:, :, st * 512:(st + 1) * 512], writes=[hh])
                for (c0, dstT, eng) in ((0, kcvcT, "act"), (128, kswT, "dve")):
                    if NSA_STAGE == 1.5:
                        break
                    pb = banks[1 + (c0 // 128)]
                    for k in range(8):
                        kb.mm(pb.t[:, :], wkvb.t[:, k, c0:c0 + 128], hh.t[:, k, :], k == 0, k == 7, [wkvb.parts[k], hh], [pb])
                    if eng == "act":
                        kb.op("act", lambda e, pb=pb, dstT=dstT, st=st: e.copy(out=dstT.t[:, st * 512:(st + 1) * 512], in_=pb.t[:, :]), reads=[pb], writes=[dstT])
                    else:
                        kb.op("dve", lambda e, pb=pb, dstT=dstT, st=st: e.tensor_copy(out=dstT.t[:, st * 512:(st + 1) * 512], in_=pb.t[:, :]), reads=[pb], writes=[dstT])
                if NSA_STAGE == 1.4:
                    continue
                pb = banks[3 + st % 2]
                for j in range(4):
                    for k in range(8):
                        kb.mm(pb.t[:, j * 128:(j + 1) * 128], hh.t[:, k, j * 128:(j + 1) * 128], wkvb.t[:, k, 256:384], k == 0, k == 7, [wkvb.parts[k], hh], [pb])
                pv4 = pb.t[:, :].rearrange("p (j c) -> p j c", j=4)
                for j in range(4):
                    if NSA_SUB >= 1:
                        kb.op("act", lambda e, pb=pb, st=st, j=j: e.copy(out=Vs.t[:, st * 4 + j, 0:64], in_=pb.t[:, j * 128:j * 128 + 64]), reads=[pb], writes=[Vs])
                    if NSA_SUB >= 2:
                        kb.op("act", lambda e, pb=pb, st=st, j=j: e.copy(out=Vw.t[:, st * 4 + j, 0:64], in_=pb.t[:, j * 128 + 64:j * 128 + 128]), reads=[pb], writes=[Vw])
            if NSA_STAGE in (2, 1.4, 1.5):
                kb.finish([oT]); return nc
            hidT = kb.sb([128, 2, 2, NCP], BF16, es=es1)
            kb.op("pool", lambda e: e.memset(hidT.t[:].rearrange("p a b c -> p (a b c)"), 0.0), writes=[hidT])
            cbias = kb.sb([128, 2, 2], es=es1)
            for x_ in range(2):
                rows = slice(x_ * 64, (x_ + 1) * 64)
                for half in range(2):
                    pb = banks[1]
                    for p in range(32):
                        kb.mm(pb.t[:, 0:1], w1m[x_].t[:, p, half * 128:(half + 1) * 128], peTb.t[:, p:p + 1], p == 0, p == 31, [w1m[x_], peTb], [pb])
                    kb.op("dve", lambda e, x_=x_, half=half, pb=pb: e.tensor_tensor(out=cbias.t[:, x_, half:half + 1], in0=pb.t[:, 0:1], in1=b1t.t[:, x_, half:half + 1], op=ALU.add),
                          reads=[pb, b1t], writes=[cbias])
                    for n0 in range(0, NCK, 512):
                        n1 = min(NCK, n0 + 512)
                        pb2 = banks[2 + (n0 // 512) % 2]
                        for p in range(32):
                            kb.mm(pb2.t[:, 0:n1 - n0], w1m[x_].t[:, p, half * 128:(half + 1) * 128], kcvcT.t[:, 16 * n0 + p:16 * (n1 - 1) + p + 1:16],
                                  p == 0, p == 31, [w1m[x_], kcvcT], [pb2])
                        kb.op("act", lambda e, x_=x_, half=half, n0=n0, n1=n1, pb2=pb2: e.activation(out=hidT.t[:, x_, half, n0:n1], in_=pb2.t[:, 0:n1 - n0], func=AF.Silu,
                                                                                                     bias=cbias.t[:, x_, half:half + 1]), reads=[pb2, cbias], writes=[hidT])
            for n0 in range(0, NCP, 512):
                n1 = min(NCP, n0 + 512)
                pb = banks[1]
                for half in range(2):
                    kb.mm(pb.t[0:64, 0:n1 - n0], w2b.t[:, 0, half, :], hidT.t[:, 0, half, n0:n1], half == 0, half == 1, [w2b, hidT], [pb])
                kb.op("dve", lambda e, n0=n0, n1=n1, pb=pb: e.tensor_copy(out=kcmpT.t[:, n0:n1], in_=pb.t[0:64, 0:n1 - n0]), reads=[pb], writes=[kcmpT])
            for ct in range(NCP // 128):
                pb = banks[2 + ct % 2]
                for half in range(2):
                    kb.mm(pb.t[:, 0:64], hidT.t[:, 1, half, ct * 128:(ct + 1) * 128], w2b.t[:, 1, half, :], half == 0, half == 1, [w2b, hidT], [pb])
                kb.op("act", lambda e, ct=ct, pb=pb: e.copy(out=vcmp.t[:, ct, :], in_=pb.t[:, 0:64]), reads=[pb], writes=[vcmp])
        if NSA_STAGE == 3:
            kb.finish([oT]); return nc
        kb.barrier()
        wsel = kb.sb([128, 4, SELW], BF16); wwin = kb.sb([128, 4, WINW], BF16)
        nearb = kb.sb([128, 4, 104]); farb = kb.sb([128, 4])
        with ExitStack() as es0:
            wtmp = kb.sb([128, SELW], es=es0)
            for (oh, L, W, dst) in ((ohs, SELW + 127, SELW, wsel), (ohw, WINW + 127, WINW, wwin)):
                for h in range(4):
                    tabh = Buf(tab.t[:, h:h + 1]); tabh.w = tab.w
                    with ExitStack() as esx:
                        build_wide_bias(kb, tabh, oh.t[:, :], 1, L, W, Fd, h, [wtmp.t[:, 0:W]], wtmp, banks[0], jm, esx)
                        kb.op("act", lambda e, dst=dst, h=h, W=W: e.copy(out=dst.t[:, h, :], in_=wtmp.t[:, 0:W]), reads=[wtmp], writes=[dst])
                    kb.barrier()
            oht = kb.sb([33, 1792], es=es0)
            kb.dma("sp", oht.t[:], ohc.t[:, :], writes=[oht])
            fs = kb.sb([4, 1792], es=es0)
            for c0 in range(0, 1792, 512):
                c1 = min(1792, c0 + 512)
                kb.mm(banks[0].t[0:4, 0:c1 - c0], tab.t[0:33, 0:4], oht.t[0:33, c0:c1], True, True, [tab, oht], [banks[0]])
                kb.op("dve", lambda e, c0=c0, c1=c1: e.tensor_copy(out=fs.t[0:4, c0:c1], in_=banks[0].t[0:4, 0:c1 - c0]), reads=[banks[0]], writes=[fs])
            kb.dma("sp", Fd.t[8:12, 0:1792], fs.t[0:4, :], reads=[fs], writes=[Fd])
            jc = kb.sb([128, 104], es=es0)
            kb.op("pool", lambda e: e.memset(jc.t[:], 0.0), writes=[jc])
            kb.op("pool", lambda e: e.affine_select(out=jc.t[:], in_=jc.t[:], pattern=[[1, 104]], compare_op=ALU.not_equal, fill=1.0,
                                                     base=-103, channel_multiplier=1), reads=[jc], writes=[jc])
            xt = kb.sb([104, 128], es=es0)
            for h in range(4):
                src = bass.AP(Fd.t.tensor, (8 + h) * Fd.t.shape[1], [[16, 104], [1, 128]])
                kb.dma("sp", xt.t[:, :], src, reads=[Fd], writes=[xt])
                kb.mm(banks[0].t[:, 0:104], xt.t[0:104, :], jc.t[0:104, :], True, True, [xt, jc], [banks[0]])
                kb.op("dve", lambda e, h=h: e.tensor_copy(out=nearb.t[:, h, :], in_=banks[0].t[:, 0:104]), reads=[banks[0]], writes=[nearb])
            onesr = kb.sb([33, 128], es=es0)
            kb.op("pool", lambda e: e.memset(onesr.t[:], 0.0), writes=[onesr])
            kb.op("pool", lambda e: e.memset(onesr.t[0:1, :], 1.0), reads=[onesr], writes=[onesr])
            t31 = kb.sb([1, 4], es=es0)
            kb.dma("sp", t31.t[:, :], tabi.t[31:32, :], writes=[t31])
            kb.mm(banks[0].t[:, 0:4], onesr.t[0:1, :], t31.t[0:1, :], True, True, [onesr, t31], [banks[0]])
            kb.op("dve", lambda e: e.tensor_copy(out=farb.t[:, :], in_=banks[0].t[:, 0:4]), reads=[banks[0]], writes=[farb])
        kb.barrier()
        Aw = kb.sb([128, 512])
        kb.op("pool", lambda e: e.memset(Aw.t[:], 0.0), writes=[Aw])
        for (rows, c0) in ((slice(0, 64), 255), (slice(64, 128), 256)):
            kb.op("pool", lambda e, rows=rows, c0=c0: e.memset(Aw.t[rows, c0:c0 + 2], BIGF), reads=[Aw], writes=[Aw])
            kb.op("pool", lambda e, rows=rows, c0=c0: e.memset(Aw.t[rows, c0 + 2:512], BIGI), reads=[Aw], writes=[Aw])
        e2f = kb.sb([128, 64, 2])
        kb.op("pool", lambda e: e.memset(e2f.t[:].rearrange("p a b -> p (a b)"), 0.0), writes=[e2f])
        kb.op("pool", lambda e: e.affine_select(out=e2f.t[:], in_=e2f.t[:], pattern=[[-2, 64], [-1, 2]], compare_op=ALU.not_equal, fill=EB,
                                                 base=0, channel_multiplier=1), reads=[e2f], writes=[e2f])
        Exp_ = kb.sb([128, 64, 128], BF16)
        for half in range(2):
            kb.op("dve", lambda e, half=half: e.tensor_copy(out=Exp_.t[:, :, half * 64:(half + 1) * 64], in_=e2f.t[:, :, half:half + 1].to_broadcast([128, 64, 64])),
                  reads=[e2f], writes=[Exp_])
        qsel = [kb.sb([128, 512], BF16) for _ in range(4)]
        qwin = [kb.sb([128, 512], BF16) for _ in range(4)]
        for h_ in range(4):
            kb.op("pool", lambda e, h_=h_: e.memset(qsel[h_].t[:], 0.0), writes=[qsel[h_]])
            kb.op("pool", lambda e, h_=h_: e.memset(qwin[h_].t[:], 0.0), writes=[qwin[h_]])
        gates = kb.sb([128, 4, 12])
        negselT = kb.sb([128, 2, 512], BF16)
        kb.op("pool", lambda e: e.memset(negselT.t[:].rearrange("p a b -> p (a b)"), -1.0), writes=[negselT])
        tmpc = [kb.sb([128, 1024]) for _ in range(2)]; ebuf = tmpc; pg = kb.sb([128, 1024])
        pbf = [kb.sb([128, 1024], BF16) for _ in range(2)]
        kb.op("pool", lambda e: e.memset(pg.t[:], 0.0), writes=[pg])
        pTc = [kb.sb([128, NCP // 128, 128], BF16) for _ in range(2)]
        den = [kb.sb([128, 2]) for _ in range(2)]; imp = kb.sb([128, 256]); sc2 = kb.sb([128, 256]); m8 = kb.sb([128, 16]); nsel = kb.sb([128, 256], BF16)
        ofin = [kb.sb([128, 4, 64]) for _ in range(4)]
        tmp = [kb.sb([128, 512]) for _ in range(3)]; pT = [kb.sb([128, 512], BF16) for _ in range(3)]
        fcol = kb.sb([128, 8]); ogb = kb.sb([128, 256], BF16); oTt = kb.sb([128, 2, 128], BF16)
        poS = [kb.sb([65, 512]) for _ in range(2)]; identf = make_ident(kb, F32)
        ov = oT.t.rearrange("(c p) t -> p c t", p=128)
        pgv = pg.t[:, :].rearrange("p (b m) -> p b m", m=4)
        if NSA_STAGE == 4:
            kb.finish([oT]); return nc
        for qs in range(S // 512):
            hh = hhs[qs % 2]
            kb.dma("sp", hh.t[:], hv[:, :, qs * 512:(qs + 1) * 512], writes=[hh])
            for h in range(4):
                pb = banks[0]
                for k in range(8):
                    kb.mm(pb.t[:, :], wqb.t[:, k, h * 128:(h + 1) * 128], hh.t[:, k, :], k == 0, k == 7, [wqb.parts[k], hh], [pb])
                kb.op("act", lambda e, h=h, pb=pb: e.copy(out=qsel[h].t[0:64, :], in_=pb.t[0:64, :]), reads=[pb], writes=[qsel[h]])
                kb.op("dve", lambda e, h=h, pb=pb: e.tensor_copy(out=qwin[h].t[64:128, :], in_=pb.t[64:128, :]), reads=[pb], writes=[qwin[h]])
            pb = banks[0]
            for j in range(4):
                for k in range(8):
                    kb.mm(pb.t[:, j * 12:(j + 1) * 12], hh.t[:, k, j * 128:(j + 1) * 128], wgb.t[:, k, :], k == 0, k == 7, [wgb.parts[k], hh], [pb])
            kb.op("act", lambda e, pb=pb: e.activation(out=gates.t[:].rearrange("p a b -> p (a b)"), in_=pb.t[:, 0:48], func=AF.Sigmoid), reads=[pb], writes=[gates])
            def cmpA(j, h, pp):
                qb = qs * 4 + j
                sub = slice(j * 128, (j + 1) * 128)
                ncv = min(8 * qb + 7, NCK)
                nlo = max(0, 8 * qb - 97); u0 = nlo - (8 * qb - 97)
                sbk = (banks[1], banks[2]) if pp == 0 else (banks[5], banks[6])
                for c0 in range(0, ncv, 512):
                    c1 = min(ncv, c0 + 512)
                    kb.mm(sbk[c0 // 512].t[:, 0:c1 - c0], qsel[h].t[0:64, sub], kcmpT.t[0:64, c0:c1], True, True, [qsel[h], kcmpT], [sbk[c0 // 512]])
                for c0 in range(0, ncv, 512):
                    c1 = min(ncv, c0 + 512)
                    pbk = sbk[c0 // 512]
                    kb.op("dve", lambda e, c0=c0, c1=c1, pbk=pbk: e.tensor_scalar(out=tmpc[pp].t[:, c0:c1], in0=pbk.t[:, 0:c1 - c0], scalar1=scale, scalar2=farb.t[:, h:h + 1],
                                                                             op0=ALU.mult, op1=ALU.add), reads=[pbk, farb], writes=[tmpc[pp]])
                    a0 = max(c0, nlo)
                    if a0 < c1:
                        kb.op("dve", lambda e, c0=c0, c1=c1, a0=a0, pbk=pbk: e.scalar_tensor_tensor(
                            out=tmpc[pp].t[:, a0:c1], in0=pbk.t[:, a0 - c0:c1 - c0], scalar=scale, in1=nearb.t[:, h, u0 + a0 - nlo:u0 + c1 - nlo],
                            op0=ALU.mult, op1=ALU.add), reads=[pbk, nearb, tmpc[pp]], writes=[tmpc[pp]])
                kb.op("act", lambda e: e.activation(out=ebuf[pp].t[:, 0:ncv], in_=tmpc[pp].t[:, 0:ncv], func=AF.Exp, accum_out=den[pp].t[:, 0:1]),
                      reads=[tmpc[pp]], writes=[tmpc[pp], den[pp]])

            def cmpB(j, h, pp):
                qb = qs * 4 + j
                ncv = min(8 * qb + 7, NCK)
                nct = (ncv + 127) // 128
                dn = den[pp]
                kb.op("dve", lambda e: e.tensor_scalar(out=dn.t[:, 1:2], in0=dn.t[:, 0:1], scalar1=1e-30, scalar2=None, op0=ALU.max), reads=[dn], writes=[dn])
                kb.op("dve", lambda e: e.reciprocal(out=dn.t[:, 1:2], in_=dn.t[:, 1:2]), reads=[dn], writes=[dn])
                if h == 0:
                    kb.op("dve", lambda e: e.tensor_scalar(out=pg.t[:, 0:ncv], in0=ebuf[pp].t[:, 0:ncv], scalar1=dn.t[:, 1:2], scalar2=None, op0=ALU.mult),
                          reads=[ebuf[pp], dn], writes=[pg])
                else:
                    kb.op("dve", lambda e: e.scalar_tensor_tensor(out=pg.t[:, 0:ncv], in0=ebuf[pp].t[:, 0:ncv], scalar=dn.t[:, 1:2], in1=pg.t[:, 0:ncv],
                                                                  op0=ALU.mult, op1=ALU.add), reads=[ebuf[pp], dn, pg], writes=[pg])
                kb.op("pool", lambda e: e.tensor_scalar(out=pbf[pp].t[:, 0:ncv], in0=ebuf[pp].t[:, 0:ncv], scalar1=dn.t[:, 1:2], scalar2=None, op0=ALU.mult),
                      reads=[ebuf[pp], dn], writes=[pbf[pp]])
                ptk = banks[3] if pp == 0 else banks[0]
                ptb = ptk.t[:, :].bitcast(BF16)
                for ct in range(nct):
                    w_ = min(128, ncv - ct * 128)
                    kb.op("pe", lambda e, ct=ct, w_=w_: e.transpose(out=ptb[0:w_, ct * 128:(ct + 1) * 128], in_=pbf[pp].t[:, ct * 128:ct * 128 + w_],
                                                                    identity=identb.t[:]), reads=[pbf[pp], identb], writes=[ptk])
                hlf = (nct + 1) // 2
                for (ca, cb, en) in ((0, hlf, "act"), (hlf, nct, "dve")):
                    if cb <= ca:
                        continue
                    wl = min(128, ncv - (cb - 1) * 128)
                    if wl == 128 or cb - ca == 1:
                        w_ = wl if cb - ca == 1 else 128
                        src_ = ptb[0:w_, ca * 128:cb * 128].rearrange("p (c i) -> p c i", i=128)
                        dst_ = pTc[pp].t[0:w_, ca:cb, :]
                        if en == "act":
                            kb.op("act", lambda e, src_=src_, dst_=dst_: e.copy(out=dst_, in_=src_), reads=[ptk], writes=[pTc[pp]])
                        else:
                            kb.op("dve", lambda e, src_=src_, dst_=dst_: e.tensor_copy(out=dst_, in_=src_), reads=[ptk], writes=[pTc[pp]])
                    else:
                        for (a_, b_, w_) in ((ca, cb - 1, 128), (cb - 1, cb, wl)):
                            src_ = ptb[0:w_, a_ * 128:b_ * 128].rearrange("p (c i) -> p c i", i=128)
                            dst_ = pTc[pp].t[0:w_, a_:b_, :]
                            kb.op("act", lambda e, src_=src_, dst_=dst_: e.copy(out=dst_, in_=src_), reads=[ptk], writes=[pTc[pp]])
                for ct in range(nct):
                    w_ = min(128, ncv - ct * 128)
                    kb.mm(banks[4].t[:, 0:64], pTc[pp].t[0:w_, ct, :], vcmp.t[0:w_, ct, :], ct == 0, ct == nct - 1, [pTc[pp], vcmp], [banks[4]])
                kb.op("dve", lambda e: e.tensor_scalar(out=ofin[j].t[:, h, :], in0=banks[4].t[:, 0:64], scalar1=gates.t[:, j, h:h + 1], scalar2=None, op0=ALU.mult),
                      reads=[banks[4], gates], writes=[ofin[j]])

            def select(j):
                qb = qs * 4 + j
                sub = slice(j * 128, (j + 1) * 128)
                kb.op("dve", lambda e: e.tensor_tensor(out=imp.t[:, :], in0=pgv[:, :, 0], in1=pgv[:, :, 1], op=ALU.add), reads=[pg], writes=[imp])
                kb.op("dve", lambda e: e.tensor_tensor(out=imp.t[:, :], in0=imp.t[:, :], in1=pgv[:, :, 2], op=ALU.add), reads=[pg, imp], writes=[imp])
                kb.op("dve", lambda e: e.scalar_tensor_tensor(out=imp.t[:, :], in0=imp.t[:, :], scalar=2.0, in1=pgv[:, :, 3], op0=ALU.mult, op1=ALU.add),
                      reads=[pg, imp], writes=[imp])
                kb.op("dve", lambda e: e.tensor_tensor(out=imp.t[:, 1:256], in0=imp.t[:, 1:256], in1=pgv[:, 0:255, 3], op=ALU.add), reads=[pg, imp], writes=[imp])
                kb.op("dve", lambda e: e.tensor_tensor(out=imp.t[:, :], in0=imp.t[:, :], in1=Aw.t[:, 256 - 2 * qb:512 - 2 * qb], op=ALU.add), reads=[Aw, imp], writes=[imp])
                kb.op("dve", lambda e: e.memset(imp.t[:, 0:1], BIGF), reads=[imp], writes=[imp])
                kb.op("dve", lambda e: e.max(out=m8.t[:, 0:8], in_=imp.t[:, :]), reads=[imp], writes=[m8])
                kb.op("dve", lambda e: e.match_replace(out=sc2.t[:, :], in_to_replace=m8.t[:, 0:8], in_values=imp.t[:, :], imm_value=2 * BIGI), reads=[imp, m8], writes=[sc2])
                kb.op("dve", lambda e: e.max(out=m8.t[:, 8:16], in_=sc2.t[:, :]), reads=[sc2], writes=[m8])
                kb.op("dve", lambda e: e.tensor_scalar(out=m8.t[:, 15:16], in0=m8.t[:, 15:16], scalar1=0.1 * BIGI, scalar2=None, op0=ALU.max), reads=[m8], writes=[m8])
                kb.op("dve", lambda e: e.tensor_scalar(out=nsel.t[:, :], in0=imp.t[:, :], scalar1=m8.t[:, 15:16], scalar2=-1.0, op0=ALU.is_ge, op1=ALU.add),
                      reads=[imp, m8], writes=[nsel])
                ptb = banks[3].t[:, :].bitcast(BF16)
                for ch in range(2):
                    kb.op("pe", lambda e, ch=ch: e.transpose(out=ptb[:, ch * 128:(ch + 1) * 128], in_=nsel.t[:, ch * 128:(ch + 1) * 128], identity=identb.t[:]),
                          reads=[nsel, identb], writes=[banks[3]])
                kb.op("act", lambda e: e.copy(out=negselT.t[:, :, sub], in_=ptb[:, 0:256].rearrange("p (c i) -> p c i", c=2)), reads=[banks[3]], writes=[negselT])

            items = [(j, h) for j in range(4) for h in range(4)]
            cmpA(items[0][0], items[0][1], 0)
            for n_, (j, h) in enumerate(items):
                if n_ + 1 < len(items):
                    cmpA(items[n_ + 1][0], items[n_ + 1][1], (n_ + 1) % 2)
                cmpB(j, h, n_ % 2)
                if h == 3:
                    select(j)
            its = []
            for br in range(2):
                kt_lo = 0 if br == 0 else max(0, 4 * qs - 4)
                kt_hi = 4 * qs + 3
                for h in range(4):
                    for kt in range(kt_lo, kt_hi + 1):
                        its.append((br, h, kt, kt == kt_lo, kt == kt_hi))
            sbanks = (banks[5], banks[6], banks[1])
            pobanks = (banks[7], banks[2])

            def selA(n_):
                br, h, kt, first, last = its[n_]
                ps = sbanks[n_ % 3]
                qq = qsel[h] if br == 0 else qwin[h]
                wide = wsel if br == 0 else wwin
                dl = 4 * qs - kt
                kb.mm(ps.t[:, :], kswT.t[:, kt * 128:(kt + 1) * 128], qq.t[:, :], True, br == 1, [kswT, qq], [ps])
                if br == 0:
                    kb.mm(ps.t[:, :], Exp_.t[:, kt % 64, :], negselT.t[:, kt // 64, :], False, True, [Exp_, negselT], [ps])
                off = 384 + 128 * (min(dl, SEL_FARD) if br == 0 else dl)
                it = n_ % 3
                kb.op("dve", lambda e: e.scalar_tensor_tensor(out=tmp[it].t[:, :], in0=ps.t[:, :], scalar=scale, in1=wide.t[:, h, off:off + 512],
                                                              op0=ALU.mult, op1=ALU.add), reads=[ps, wide], writes=[tmp[it]])
                kb.op("act", lambda e: e.activation(out=pT[it].t[:, :], in_=tmp[it].t[:, :], func=AF.Exp), reads=[tmp[it]], writes=[pT[it]])

            def selB(n_, grp):
                br, h, kt, first, last = its[n_]
                it = n_ % 3
                Vt = Vs if br == 0 else Vw
                po = pobanks[grp % 2]
                kb.mm(po.t[0:65, :], Vt.t[:, kt, 0:65], pT[it].t[:, :], first, last, [pT[it], Vt], [po])
                if not last:
                    return
                pS = poS[grp % 2]
                kb.op("act", lambda e: e.copy(out=pS.t[:, :], in_=po.t[0:65, :]), reads=[po], writes=[pS])
                pt = banks[4]
                for j in range(4):
                    kb.op("pe", lambda e, j=j: e.transpose(out=pt.t[:, j * 65:(j + 1) * 65], in_=pS.t[0:65, j * 128:(j + 1) * 128], identity=identf.t[0:65, 0:65]),
                          reads=[pS, identf], writes=[pt])
                for j in range(4):
                    gcol = (1 + br) * 4 + h
                    kb.op("dve", lambda e, j=j: e.reciprocal(out=fcol.t[:, j:j + 1], in_=pt.t[:, j * 65 + 64:j * 65 + 65]), reads=[pt], writes=[fcol])
                    kb.op("dve", lambda e, j=j, gcol=gcol: e.tensor_tensor(out=fcol.t[:, 4 + j:5 + j], in0=fcol.t[:, j:j + 1], in1=gates.t[:, j, gcol:gcol + 1], op=ALU.mult),
                          reads=[fcol, gates], writes=[fcol])
                    kb.op("dve", lambda e, j=j: e.scalar_tensor_tensor(out=ofin[j].t[:, h, :], in0=pt.t[:, j * 65:j * 65 + 64], scalar=fcol.t[:, 4 + j:5 + j], in1=ofin[j].t[:, h, :],
                                                                       op0=ALU.mult, op1=ALU.add), reads=[pt, fcol, ofin[j]], writes=[ofin[j]])

            grp = 0
            selA(0)
            if len(its) > 1:
                selA(1)
            for n_ in range(len(its)):
                if n_ + 2 < len(its):
                    selA(n_ + 2)
                selB(n_, grp)
                if its[n_][4]:
                    grp += 1
            for j in range(4):
                kb.op("act", lambda e, j=j: e.copy(out=ogb.t[:, :], in_=ofin[j].t[:].rearrange("p a b -> p (a b)")), reads=[ofin[j]], writes=[ogb])
                ptb = banks[3].t[:, :].bitcast(BF16)
                for c in range(2):
                    kb.op("pe", lambda e, c=c, ptb=ptb: e.transpose(out=ptb[:, c * 128:(c + 1) * 128], in_=ogb.t[:, c * 128:(c + 1) * 128], identity=identb.t[:]),
                          reads=[ogb, identb], writes=[banks[3]])
                kb.op("act", lambda e, ptb=ptb: e.copy(out=oTt.t[:].rearrange("p a b -> p (a b)"), in_=ptb[:, 0:256]), reads=[banks[3]], writes=[oTt])
                t0 = qs * 512 + j * 128
                kb.dma("pool", ov[:, :, t0:t0 + 128], oTt.t[:], reads=[oTt], writes=[oT])
        kb.finish([oT])
    return nc


def nsa_inputs(hT_b, w_in, pe, w1, b1, w2, tab, g):
    q = w_in[:, 0:1024].reshape(D, 4, 4, 64)[:, g]
    wqd = np.concatenate([q, q], axis=2).reshape(D, 512)
    blk = lambda i: w_in[:, 1024 + i * 256 + g * 64:1024 + i * 256 + (g + 1) * 64]
    wkv = np.concatenate([blk(0), blk(1), blk(2), blk(4), blk(3), blk(5)], axis=1)
    gates = w_in[:, 2560:2608].reshape(D, 3, 4, 4)[:, :, g, :].reshape(D, 12)
    tb = np.full((33, 4), NEG, np.float32)
    tb[:32] = tab[:, g * 4:(g + 1) * 4]
    ohs, ohw, ohc = nsa_onehots()
    return {"hT": hT_b, "wqd": np.ascontiguousarray(wqd), "wkv": np.ascontiguousarray(wkv), "wg": np.ascontiguousarray(gates),
            "pe": np.ascontiguousarray(pe), "w1": np.ascontiguousarray(w1), "b1": np.ascontiguousarray(b1), "w2": np.ascontiguousarray(w2),
            "tab": tb, "ohs": ohs, "ohw": ohw, "ohc": ohc}


_NC_CACHE = {}


def _get(name, fn, *a):
    key = (name,) + a
    if key not in _NC_CACHE:
        _NC_CACHE[key] = fn(*a)
    return _NC_CACHE[key]


def _run(nc, in_maps):
    res = run_bass_kernel_spmd(nc, in_maps, core_ids=list(range(NCORES)))
    return res.results


def kernel(x, c, rel_table, mod_w, mod_b, ln_g, ln_b,
           gla_w_in, gla_w_a2, gla_b_a, gla_norm_g, gla_w_o,
           nsa_w_in, nsa_cmp_pe, nsa_cmp_w1, nsa_cmp_b1, nsa_cmp_w2, nsa_w_o,
           dil_w_in, dil_w_o,
           ffn_w_up, ffn_conv_w, ffn_conv_b, ffn_w_down):
    f32 = lambda a: np.ascontiguousarray(np.asarray(a, dtype=np.float32))
    x = f32(x); c = f32(c); rel_table = f32(rel_table); mod_w = f32(mod_w); mod_b = f32(mod_b)
    ln_g = f32(ln_g); ln_b = f32(ln_b)
    S = x.shape[1]
    T = S // 4
    dbg = globals().get("_DBG")
    res = _run(_get("mod", build_mod), [{"c": c, "w": f32(mod_w[s // 2, s % 2]), "b": f32(mod_b[s // 2, s % 2][None])} for s in range(8)])
    mod = [r["out"] for r in res]

    def shards():
        for k in range(NCORES):
            yield k, k // 4, (k % 4) * T

    res = _run(_get("prep", build_prep, T), [{"x": f32(x[b, t0:t0 + T]), "vec": f32(np.stack([mod[0][b, 0:D], mod[0][b, D:2 * D]]))} for k, b, t0 in shards()])
    hT = np.zeros((NB, D, S), ml_dtypes.bfloat16)
    for (k, b, t0), r in zip(shards(), res):
        hT[b, :, t0:t0 + T] = r["hT"]
    xcur = x
    for i in range(DEPTH):
        kind, j = i % 3, i // 3
        ins = []
        for k in range(NCORES):
            b, g = k // 4, k % 4
            hb = np.ascontiguousarray(hT[b])
            if kind == 0:
                w_in = f32(gla_w_in[j])
                wa2 = np.zeros((33, 128), np.float32)
                wa2[:16] = f32(gla_w_a2[j])[:, g * 128:(g + 1) * 128]
                wa2[32] = f32(gla_b_a[j])[g * 128:(g + 1) * 128]
                ins.append({"hT": hb, "wq": f32(w_in[:, g * 128:(g + 1) * 128]), "wk": f32(w_in[:, 512 + g * 128:512 + (g + 1) * 128]),
                            "wv": f32(w_in[:, 1024 + g * 256:1024 + (g + 1) * 256]), "wr": f32(w_in[:, 2048 + g * 256:2048 + (g + 1) * 256]),
                            "wa": f32(w_in[:, 3072:3088]), "wa2": wa2, "ng": f32(gla_norm_g[j])[None]})
            elif kind == 1:
                ins.append(nsa_inputs(hb, f32(nsa_w_in[j]), f32(nsa_cmp_pe[j]), f32(nsa_cmp_w1[j]), f32(nsa_cmp_b1[j]), f32(nsa_cmp_w2[j]), rel_table, g))
            else:
                wgd = f32(dil_w_in[j]).reshape(D, 3, 3, 1024)
                wqkv = np.stack([np.concatenate([wgd[:, gg, cc, g * 256:(g + 1) * 256] for cc in range(3)], axis=1) for gg in range(3)])
                tb = np.full((33, 4), NEG, np.float32)
                tb[:32] = rel_table[:, g * 4:(g + 1) * 4]
                ins.append({"hT": hb, "wqkv": f32(wqkv), "tab": tb, "oh": dil_onehots()})
        ncm = _get(("gla", "nsa", "dil")[kind], (build_gla, build_nsa, build_dil)[kind], S)
        res = _run(ncm, ins)
        oT = np.zeros((NB, D, S), ml_dtypes.bfloat16)
        for k in range(NCORES):
            oT[k // 4, (k % 4) * 256:(k % 4 + 1) * 256, :] = res[k]["oT"]
        w_o = f32((gla_w_o, nsa_w_o, dil_w_o)[kind][j])
        ins = []
        for k, b, t0 in shards():
            xh = np.zeros((T + 128, D), np.float32)
            oh_ = np.zeros((D, T + 128), ml_dtypes.bfloat16)
            xh[128:] = xcur[b, t0:t0 + T]
            oh_[:, 128:] = oT[b, :, t0:t0 + T]
            if t0 > 0:
                xh[:128] = xcur[b, t0 - 128:t0]
                oh_[:, :128] = oT[b, :, t0 - 128:t0]
            vec = np.zeros((10, D), np.float32)
            vec[0] = mod[2 * i][b, 2 * D:3 * D]
            vec[1] = mod[2 * i + 1][b, 0:D]; vec[2] = mod[2 * i + 1][b, D:2 * D]; vec[3] = mod[2 * i + 1][b, 2 * D:3 * D]
            if i + 1 < DEPTH:
                vec[4] = mod[2 * i + 2][b, 0:D]; vec[5] = mod[2 * i + 2][b, D:2 * D]
            vec[6] = ln_g[i, 0]; vec[7] = ln_b[i, 0]; vec[8] = ln_g[i, 1]; vec[9] = ln_b[i, 1]
            ins.append({"x": xh, "oT": oh_, "wo": w_o, "wup": f32(ffn_w_up[i]), "wdn": f32(ffn_w_down[i]), "convw": f32(ffn_conv_w[i]),
                        "convb": f32(ffn_conv_b[i]), "vec": vec, "flag": np.full((128, 1), 0.0 if t0 == 0 else 1.0, np.float32)})
        res = _run(_get("post", build_post, T), ins)
        xn = np.zeros((NB, S, D), np.float32)
        for (k, b, t0), r in zip(shards(), res):
            xn[b, t0:t0 + T] = r["xo"]
            hT[b, :, t0:t0 + T] = r["hT"]
        xcur = xn
        if dbg is not None:
            dbg.append(xn.copy())
    return xcur
```

```python
import math
from contextlib import ExitStack
import numpy as np
import ml_dtypes
import concourse.bass as bass
import concourse.mybir as mybir
from concourse.bass_utils import run_bass_kernel_spmd

F32 = mybir.dt.float32
BF16 = mybir.dt.bfloat16
AF = mybir.ActivationFunctionType
ALU = mybir.AluOpType
AX = mybir.AxisListType

D = 1024
SEQ = 16384
NB = 2
DEPTH = 4
D_FF = 2816
NFC = D_FF // 128
DN_ALPHA = (2 * DEPTH) ** 0.25
LN_EPS = 1e-5
NEG = -30000.0
NCORES = 8


class Buf:
    __slots__ = ("w", "r", "t", "parts")

    def __init__(self, t=None, nparts=0):
        self.w = None
        self.r = {}
        self.t = t
        self.parts = [Buf(t) for _ in range(nparts)]

    def __getitem__(self, k):
        return self.t[k]


class Eng:
    def __init__(self, name, h, sem):
        self.name, self.h, self.sem = name, h, sem
        self.count = 0
        self.waited = {}


class KB:
    def __init__(self, nc, es, ndma=40):
        self.nc, self.es = nc, es
        self.eng = {}
        for name, h in (("pe", nc.tensor), ("act", nc.scalar), ("dve", nc.vector), ("pool", nc.gpsimd), ("sp", nc.sync)):
            self.eng[name] = Eng(name, h, es.enter_context(nc.semaphore("sem_" + name)))
        self.dsl = [[es.enter_context(nc.semaphore("dsem%d" % i)), 0] for i in range(ndma)]
        self.di = {"sp": 0, "pool": 0, "act": 0}
        self.dq = {"sp": list(range(0, ndma // 2)), "pool": list(range(ndma // 2, ndma - 4)), "act": list(range(ndma - 4, ndma))}
        self.nt = 0

    def sb(self, shape, dt=F32, name=None, nparts=0, es=None):
        self.nt += 1
        return Buf((es or self.es).enter_context(self.nc.sbuf_tensor(name or "t%d" % self.nt, list(shape), dt)), nparts)

    def ps(self, shape=(128, 512), dt=F32, name=None):
        self.nt += 1
        return Buf(self.es.enter_context(self.nc.psum_tensor(name or "p%d" % self.nt, list(shape), dt)))

    def dram(self, name, shape, dt, kind="Internal"):
        return Buf(self.nc.dram_tensor(name, list(shape), dt, kind=kind).ap())

    def _semof(self, key):
        if isinstance(key, str):
            return self.eng[key].sem
        return self.dsl[key][0]

    def _wait(self, E, deps):
        need = {}
        for key, val in deps:
            if key == E.name and key in ("pe", "sp"):
                continue
            if val > need.get(key, 0):
                need[key] = val
        for key, val in need.items():
            if E.waited.get(key, 0) >= val:
                continue
            E.h.wait_ge(self._semof(key), val)
            E.waited[key] = val

    @staticmethod
    def _deps(reads, writes):
        deps = []
        for b in reads:
            if b.w is not None:
                deps.append(b.w)
        for b in writes:
            if b.w is not None:
                deps.append(b.w)
            deps.extend(b.r.items())
        return deps

    @staticmethod
    def _record(ev, reads, writes):
        key, val = ev
        for b in reads:
            if b.r.get(key, 0) < val:
                b.r[key] = val
        for b in writes:
            b.w = ev
            b.r = {}

    def op(self, en, fn, reads=(), writes=()):
        E = self.eng[en]
        self._wait(E, self._deps(reads, writes))
        ins = fn(E.h)
        E.count += 1
        ins.then_inc(E.sem, 1)
        self._record((en, E.count), reads, writes)

    def dma(self, qn, out, in_, reads=(), writes=(), **kw):
        Q = self.eng[qn]
        k = self.dq[qn][self.di[qn] % len(self.dq[qn])]
        self.di[qn] += 1
        slot = self.dsl[k]
        deps = self._deps(reads, writes)
        if slot[1] > 0:
            deps.append((k, slot[1]))
        self._wait(Q, deps)
        Q.h.dma_start(out=out, in_=in_, **kw).then_inc(slot[0], 16)
        slot[1] += 16
        self._record((k, slot[1]), reads, writes)

    def barrier(self):
        deps = [(n, E.count) for n, E in self.eng.items() if E.count > 0]
        deps += [(k, sl[1]) for k, sl in enumerate(self.dsl) if sl[1] > 0]
        for E in self.eng.values():
            self._wait(E, [d for d in deps if d[0] != E.name])

    def finish(self, outs):
        E = self.eng["sp"]
        self._wait(E, [b.w for b in outs if b.w is not None])

    def mm(self, out, lhsT, rhs, start, stop, reads, writes):
        self.op("pe", lambda e: e.matmul(out, lhsT=lhsT, rhs=rhs, start=start, stop=stop), reads, writes)


def new_nc():
    return bass.Bass("TRN2", target_bir_lowering=False)


def dr_in(nc, name, shape, dt=F32):
    return Buf(nc.dram_tensor(name, list(shape), dt, kind="ExternalInput").ap())


def dr_out(nc, name, shape, dt=F32):
    return Buf(nc.dram_tensor(name, list(shape), dt, kind="ExternalOutput").ap())


def load_bcast(kb, q, dst, src_row_ap, n=128):
    kb.dma(q, dst.t[0:n, :], src_row_ap.partition_broadcast(n), writes=[dst])


def make_ident(kb, dt=BF16):
    idf = kb.sb([128, 128], F32)
    kb.op("pool", lambda e: e.memset(idf[:], 0.0), writes=[idf])
    kb.op("pool", lambda e: e.affine_select(out=idf[:], in_=idf[:], pattern=[[-1, 128]], compare_op=ALU.not_equal,
                                             fill=1.0, base=0, channel_multiplier=1), reads=[idf], writes=[idf])
    if dt == F32:
        return idf
    idb = kb.sb([128, 128], dt)
    kb.op("dve", lambda e: e.tensor_copy(out=idb[:], in_=idf[:]), reads=[idf], writes=[idb])
    return idb


def gen_load_w_bf16(kb, dst, src_ap_fn, nk, ncols, stage, qs=("sp", "pool"), chunk=512, npart=128):
    i = 0
    for k in range(nk):
        for c0 in range(0, ncols, chunk):
            c1 = min(ncols, c0 + chunk)
            st = stage[i % len(stage)]
            kb.dma(qs[i % len(qs)], st.t[0:npart, 0:c1 - c0], src_ap_fn(k, c0, c1), writes=[st])
            if i % 2:
                kb.op("act", lambda e, st=st, k=k, c0=c0, c1=c1: e.copy(out=dst.t[0:npart, k, c0:c1], in_=st.t[0:npart, 0:c1 - c0]),
                      reads=[st], writes=[dst.parts[k]])
            else:
                kb.op("dve", lambda e, st=st, k=k, c0=c0, c1=c1: e.tensor_copy(out=dst.t[0:npart, k, c0:c1], in_=st.t[0:npart, 0:c1 - c0]),
                      reads=[st], writes=[dst.parts[k]])
            i += 1
            yield


def load_w_bf16(kb, dst, src_ap_fn, nk, ncols, stage, qs=("sp", "pool"), chunk=512, npart=128):
    i = 0
    for k in range(nk):
        for c0 in range(0, ncols, chunk):
            c1 = min(ncols, c0 + chunk)
            st = stage[i % len(stage)]
            kb.dma(qs[i % len(qs)], st.t[0:npart, 0:c1 - c0], src_ap_fn(k, c0, c1), writes=[st])
            en = "act" if i % 2 else "dve"
            if en == "act":
                kb.op("act", lambda e, st=st, k=k, c0=c0, c1=c1: e.copy(out=dst.t[0:npart, k, c0:c1], in_=st.t[0:npart, 0:c1 - c0]),
                      reads=[st], writes=[dst.parts[k]])
            else:
                kb.op("dve", lambda e, st=st, k=k, c0=c0, c1=c1: e.tensor_copy(out=dst.t[0:npart, k, c0:c1], in_=st.t[0:npart, 0:c1 - c0]),
                      reads=[st], writes=[dst.parts[k]])
            i += 1


def emit_ln(kb, z, st, mv, lng, lnb):
    for c in range(2):
        kb.op("dve", lambda e, c=c: e.bn_stats(out=st.t[:, c, :], in_=z.t[:, c * 512:(c + 1) * 512]), reads=[z], writes=[st])
    kb.op("dve", lambda e: e.bn_aggr(out=mv.t[:, 0:2], in_=st.t[:].rearrange("p a b -> p (a b)")), reads=[st], writes=[mv])
    kb.op("dve", lambda e: e.tensor_scalar_add(out=mv.t[:, 2:3], in0=mv.t[:, 1:2], scalar1=LN_EPS), reads=[mv], writes=[mv])
    kb.op("act", lambda e: e.sqrt(out=mv.t[:, 2:3], in_=mv.t[:, 2:3]), reads=[mv], writes=[mv])
    kb.op("dve", lambda e: e.reciprocal(out=mv.t[:, 2:3], in_=mv.t[:, 2:3]), reads=[mv], writes=[mv])
    kb.op("dve", lambda e: e.tensor_scalar(out=z.t[:], in0=z.t[:], scalar1=mv.t[:, 0:1], scalar2=mv.t[:, 2:3],
                                           op0=ALU.subtract, op1=ALU.mult), reads=[z, mv], writes=[z])
    kb.op("pool", lambda e: e.tensor_tensor(out=z.t[:], in0=z.t[:], in1=lng.t[:], op=ALU.mult), reads=[z, lng], writes=[z])
    kb.op("pool", lambda e: e.tensor_tensor(out=z.t[:], in0=z.t[:], in1=lnb.t[:], op=ALU.add), reads=[z, lnb], writes=[z])


def emit_modT(kb, x, pst, identf, scp, sh, ht, ncol=128, c0=0):
    for k in range(8):
        kb.op("pe", lambda e, k=k: e.transpose(out=pst.t[:, k * 128:(k + 1) * 128], in_=x.t[:, k * 128:(k + 1) * 128],
                                               identity=identf.t[:]), reads=[x, identf], writes=[pst])
    for k in range(8):
        kb.op("act", lambda e, k=k: e.activation(out=ht.t[:, k, c0:c0 + 128], in_=pst.t[:, k * 128:(k + 1) * 128],
                                                 func=AF.Identity, scale=scp.t[:, k:k + 1], bias=sh.t[:, k:k + 1]),
              reads=[pst, scp, sh], writes=[ht])


def load_pp(kb, q, dst, row_ap, nk=8):
    kb.dma(q, dst.t[:, 0:nk], row_ap.rearrange("(k p) -> p k", p=128), writes=[dst], allow_slow_non_contiguous=True)


def build_prep(T):
    nc = new_nc()
    x = dr_in(nc, "x", [T, D])
    vec = dr_in(nc, "vec", [2, D])
    hTo = dr_out(nc, "hT", [D, T], BF16)
    with ExitStack() as es:
        kb = KB(nc, es)
        identf = make_ident(kb, F32)
        sh = kb.sb([128, 8]); scp = kb.sb([128, 8])
        load_pp(kb, "sp", sh, vec.t[0, :]); load_pp(kb, "sp", scp, vec.t[1, :])
        kb.op("dve", lambda e: e.tensor_scalar_add(out=scp.t[:], in0=scp.t[:], scalar1=1.0), reads=[scp], writes=[scp])
        xs = [kb.sb([128, D]) for _ in range(3)]
        hts = [kb.sb([128, 8, 128], BF16) for _ in range(2)]
        psts = [kb.ps([128, 1024]) for _ in range(2)]
        hv = hTo.t.rearrange("(k p) t -> p k t", p=128)
        for i in range(T // 128):
            xt = xs[i % 3]; ht = hts[i % 2]; pst = psts[i % 2]
            kb.dma("sp", xt.t[:], x.t[i * 128:(i + 1) * 128, :], writes=[xt])
            emit_modT(kb, xt, pst, identf, scp, sh, ht)
            kb.dma("pool", hv[:, :, i * 128:(i + 1) * 128], ht.t[:], reads=[ht], writes=[hTo])
        kb.finish([hTo])
    return nc


def build_post(T):
    TT = 256
    nc = new_nc()
    x = dr_in(nc, "x", [T + 128, D])
    oT = dr_in(nc, "oT", [D, T + 128], BF16)
    wo = dr_in(nc, "wo", [D, D])
    wup = dr_in(nc, "wup", [D, 2 * D_FF])
    wdn = dr_in(nc, "wdn", [D_FF, D])
    convw = dr_in(nc, "convw", [3, D_FF])
    convb = dr_in(nc, "convb", [D_FF])
    vec = dr_in(nc, "vec", [10, D])
    flag = dr_in(nc, "flag", [128, 1])
    xo = dr_out(nc, "xo", [T, D])
    hTo = dr_out(nc, "hT", [D, T], BF16)
    x1s = Buf(nc.dram_tensor("x1s", [T, D], F32, kind="Internal").ap())
    h1s = Buf(nc.dram_tensor("h1s", [D, T], BF16, kind="Internal").ap())
    ntile = T // 128
    with ExitStack() as es:
        kb = KB(nc, es)
        identf = make_ident(kb, F32)
        banks = [kb.ps([128, 512]) for _ in range(4)]
        pst2 = [kb.ps([128, 1024]) for _ in range(2)]
        h1halo = kb.sb([128, 8, 128], BF16)
        pp = kb.sb([128, 6, 8])
        ppb = [Buf(pp.t) for _ in range(4)]
        for j, r in enumerate((1, 2, 4, 5)):
            kb.dma("sp", pp.t[:, j, :], vec.t[r, :].rearrange("(k p) -> p k", p=128), writes=[ppb[j]], allow_slow_non_contiguous=True)
        for j in (1, 3):
            kb.op("dve", lambda e, j=j: e.tensor_scalar_add(out=pp.t[:, j, :], in0=pp.t[:, j, :], scalar1=1.0), reads=[ppb[j]], writes=[ppb[j]])

        class PV:
            pass
        sh2 = Buf(pp.t[:, 0, :]); sc2 = Buf(pp.t[:, 1, :]); shn = Buf(pp.t[:, 2, :]); scn = Buf(pp.t[:, 3, :])
        for v, b in ((sh2, ppb[0]), (sc2, ppb[1]), (shn, ppb[2]), (scn, ppb[3])):
            v.w = b.w
        flg = kb.sb([128, 1])
        kb.dma("sp", flg.t[:], flag.t[:, :], writes=[flg])
        st = kb.sb([128, 2, 6]); mv = kb.sb([128, 4])
        zs = [kb.sb([128, D]) for _ in range(2)]
        xs = [kb.sb([128, D]) for _ in range(2)]
        hts = [kb.sb([128, 8, 128], BF16) for _ in range(2)]
        g1p = kb.sb([128, D]); lng = kb.sb([128, D]); lnb = kb.sb([128, D])

        def load_gl(gr, lgr, lbr):
            load_bcast(kb, "sp", g1p, vec.t[gr:gr + 1, :])
            load_bcast(kb, "sp", lng, vec.t[lgr:lgr + 1, :])
            load_bcast(kb, "sp", lnb, vec.t[lbr:lbr + 1, :])
            kb.op("pool", lambda e: e.tensor_scalar_add(out=g1p.t[:], in0=g1p.t[:], scalar1=1.0), reads=[g1p], writes=[g1p])

        load_gl(0, 6, 7)
        h1v = h1s.t.rearrange("(k p) t -> p k t", p=128)
        hov = hTo.t.rearrange("(k p) t -> p k t", p=128)
        oTv = oT.t.rearrange("(k p) t -> p k t", p=128)

        def resid_ln(psy, xt, z):
            for half in range(2):
                kb.op("dve", lambda e, half=half: e.tensor_tensor(out=z.t[:, half * 512:(half + 1) * 512], in0=psy[half].t[:, :],
                                                                   in1=g1p.t[:, half * 512:(half + 1) * 512], op=ALU.mult),
                      reads=[psy[half], g1p], writes=[z])
            kb.op("dve", lambda e: e.scalar_tensor_tensor(out=z.t[:], in0=xt.t[:], scalar=DN_ALPHA, in1=z.t[:],
                                                           op0=ALU.mult, op1=ALU.add), reads=[xt, z], writes=[z])
            emit_ln(kb, z, st, mv, lng, lnb)

        wub = kb.sb([128, 8, 2 * D_FF], BF16, nparts=8)
        wdb = kb.sb([128, NFC, D], BF16, nparts=NFC)
        stageB = [kb.sb([128, 512]) for _ in range(2)]

        def _wgen():
            yield from gen_load_w_bf16(kb, wub, lambda k, c0, c1: wup.t[k * 128:(k + 1) * 128, c0:c1], 8, 2 * D_FF, stageB)
            yield from gen_load_w_bf16(kb, wdb, lambda k, c0, c1: wdn.t[k * 128:(k + 1) * 128, c0:c1], NFC, D, stageB)
        wgen = _wgen()
        nchunks_w = 8 * 11 + NFC * 2
        per_tile = -(-nchunks_w // (ntile + 1))
        with ExitStack() as esA:
            wob = kb.sb([128, 8, D], BF16, nparts=8, es=esA)
            stage = [kb.sb([128, 512], es=esA) for _ in range(2)]
            ots = [kb.sb([128, 8, 128], BF16, es=esA) for _ in range(2)]
            load_w_bf16(kb, wob, lambda k, c0, c1: wo.t[k * 128:(k + 1) * 128, c0:c1], 8, D, stage)
            for i in range(ntile + 1):
                xt = xs[i % 2]; z = zs[i % 2]; ot = ots[i % 2]; ht = hts[i % 2]
                psy = banks[2 * (i % 2):2 * (i % 2) + 2]
                kb.dma("sp", xt.t[:], x.t[i * 128:(i + 1) * 128, :], writes=[xt])
                kb.dma("pool", ot.t[:], oTv[:, :, i * 128:(i + 1) * 128], writes=[ot])
                for half in range(2):
                    for k in range(8):
                        kb.mm(psy[half].t[:, :], ot.t[:, k, :], wob.t[:, k, half * 512:(half + 1) * 512], k == 0, k == 7,
                              [ot, wob.parts[k]], [psy[half]])
                resid_ln(psy, xt, z)
                if i > 0:
                    kb.dma("sp", x1s.t[(i - 1) * 128:i * 128, :], z.t[:], reads=[z], writes=[x1s])
                emit_modT(kb, z, pst2[i % 2], identf, sc2, sh2, h1halo if i == 0 else ht)
                if i > 0:
                    kb.dma("pool", h1v[:, :, (i - 1) * 128:i * 128], ht.t[:], reads=[ht], writes=[h1s])
                for _ in range(per_tile):
                    next(wgen, None)
            for _ in wgen:
                pass

        kb.barrier()
        load_gl(3, 8, 9)
        with ExitStack() as esB:
            cw = kb.sb([128, 3, NFC], es=esB); cb = kb.sb([128, NFC], es=esB)
            for j in range(3):
                kb.dma("sp", cw.t[:, j, :], convw.t[j, :].rearrange("(k p) -> p k", p=128), writes=[cw], allow_slow_non_contiguous=True)
            kb.dma("sp", cb.t[:, :], convb.t[:].rearrange("(k p) -> p k", p=128), writes=[cb], allow_slow_non_contiguous=True)
            uprev = kb.sb([128, NFC, 2], nparts=NFC, es=esB)
            h1t = [kb.sb([128, 8, TT], BF16, es=esB) for _ in range(2)]
            aT = kb.sb([128, NFC, TT], BF16, nparts=NFC, es=esB)
            ubs = [kb.sb([128, TT + 2], es=esB) for _ in range(2)]
            cbs = [kb.sb([128, TT], es=esB) for _ in range(2)]
            sbs = [kb.sb([128, TT], es=esB) for _ in range(2)]
            pu = banks[0]
            for fc in range(NFC):
                for k in range(8):
                    kb.mm(pu.t[:, fc * 2:fc * 2 + 2], wub.t[:, k, fc * 128:(fc + 1) * 128], h1halo.t[:, k, 126:128], k == 0, k == 7,
                          [wub.parts[k], h1halo], [pu])
            kb.op("dve", lambda e: e.tensor_scalar_mul(out=uprev.t[:].rearrange("p a b -> p (a b)"), in0=pu.t[:, 0:2 * NFC],
                                                       scalar1=flg.t[:, 0:1]), reads=[pu, flg], writes=uprev.parts)
            nug = 0
            for it in range(T // TT):
                hh = h1t[it % 2]
                kb.dma("sp", hh.t[:], h1v[:, :, it * TT:(it + 1) * TT], reads=[h1s], writes=[hh])
                for fc in range(NFC):
                    pug = banks[nug % 2]; ub = ubs[nug % 2]; cbuf = cbs[nug % 2]; sbuf = sbs[nug % 2]; nug += 1
                    for k in range(8):
                        kb.mm(pug.t[:, 0:TT], wub.t[:, k, fc * 128:(fc + 1) * 128], hh.t[:, k, :], k == 0, k == 7, [wub.parts[k], hh], [pug])
                    for k in range(8):
                        kb.mm(pug.t[:, TT:2 * TT], wub.t[:, k, D_FF + fc * 128:D_FF + (fc + 1) * 128], hh.t[:, k, :], k == 0, k == 7,
                              [wub.parts[k], hh], [pug])
                    kb.op("pool", lambda e, ub=ub, fc=fc: e.tensor_copy(out=ub.t[:, 0:2], in_=uprev.t[:, fc, :]), reads=[uprev.parts[fc]], writes=[ub])
                    kb.op("act", lambda e, ub=ub, pug=pug: e.copy(out=ub.t[:, 2:TT + 2], in_=pug.t[:, 0:TT]), reads=[pug], writes=[ub])
                    kb.op("pool", lambda e, ub=ub, fc=fc: e.tensor_copy(out=uprev.t[:, fc, :], in_=ub.t[:, TT:TT + 2]), reads=[ub], writes=[uprev.parts[fc]])
                    kb.op("act", lambda e, ub=ub, cbuf=cbuf, fc=fc: e.activation(out=cbuf.t[:], in_=ub.t[:, 2:TT + 2], func=AF.Identity,
                                                                                 scale=cw.t[:, 2, fc:fc + 1], bias=cb.t[:, fc:fc + 1]),
                          reads=[ub, cw, cb], writes=[cbuf])
                    kb.op("dve", lambda e, ub=ub, cbuf=cbuf, fc=fc: e.scalar_tensor_tensor(out=cbuf.t[:], in0=ub.t[:, 1:TT + 1], scalar=cw.t[:, 1, fc:fc + 1],
                                                                                           in1=cbuf.t[:], op0=ALU.mult, op1=ALU.add),
                          reads=[ub, cw, cbuf], writes=[cbuf])
                    kb.op("dve", lambda e, ub=ub, cbuf=cbuf, fc=fc: e.scalar_tensor_tensor(out=cbuf.t[:], in0=ub.t[:, 0:TT], scalar=cw.t[:, 0, fc:fc + 1],
                                                                                           in1=cbuf.t[:], op0=ALU.mult, op1=ALU.add),
                          reads=[ub, cw, cbuf], writes=[cbuf])
                    kb.op("act", lambda e, cbuf=cbuf, sbuf=sbuf: e.activation(out=sbuf.t[:], in_=cbuf.t[:], func=AF.Silu), reads=[cbuf], writes=[sbuf])
                    kb.op("dve", lambda e, sbuf=sbuf, pug=pug, fc=fc: e.tensor_tensor(out=aT.t[:, fc, :], in0=pug.t[:, TT:2 * TT], in1=sbuf.t[:], op=ALU.mult),
                          reads=[pug, sbuf], writes=[aT.parts[fc]])
                for sub in range(TT // 128):
                    i = it * (TT // 128) + sub
                    psy = banks[2:4]
                    xt = xs[i % 2]; z = zs[i % 2]; ht = hts[i % 2]
                    kb.dma("sp", xt.t[:], x1s.t[i * 128:(i + 1) * 128, :], reads=[x1s], writes=[xt])
                    for half in range(2):
                        for fc in range(NFC):
                            kb.mm(psy[half].t[:, :], aT.t[:, fc, sub * 128:(sub + 1) * 128], wdb.t[:, fc, half * 512:(half + 1) * 512],
                                  fc == 0, fc == NFC - 1, [aT.parts[fc], wdb.parts[fc]], [psy[half]])
                    resid_ln(psy, xt, z)
                    kb.dma("sp", xo.t[i * 128:(i + 1) * 128, :], z.t[:], reads=[z], writes=[xo])
                    emit_modT(kb, z, pst2[i % 2], identf, scn, shn, ht)
                    kb.dma("pool", hov[:, :, i * 128:(i + 1) * 128], ht.t[:], reads=[ht], writes=[hTo])
        kb.finish([xo, hTo])
    return nc


def build_mod():
    nc = new_nc()
    c = dr_in(nc, "c", [NB, D])
    w = dr_in(nc, "w", [D, 3 * D])
    b = dr_in(nc, "b", [1, 3 * D])
    out = dr_out(nc, "out", [NB, 3 * D])
    with ExitStack() as es:
        kb = KB(nc, es)
        cs = kb.sb([128, 8, NB])
        for bb in range(NB):
            kb.dma("sp", cs.t[:, :, bb], c.t[bb, :].rearrange("(k p) -> p k", p=128), writes=[cs], allow_slow_non_contiguous=True)
        kb.op("act", lambda e: e.activation(out=cs.t[:], in_=cs.t[:], func=AF.Silu), reads=[cs], writes=[cs])
        wt = kb.sb([128, 8, 3 * D], nparts=8)
        for k in range(8):
            kb.dma("sp" if k % 2 else "pool", wt.t[:, k, :], w.t[k * 128:(k + 1) * 128, :], writes=[wt.parts[k]])
        bt = kb.sb([NB, 3 * D])
        load_bcast(kb, "sp", bt, b.t[0:1, :], n=NB)
        ot = kb.sb([NB, 3 * D])
        banks = [kb.ps([128, 512]) for _ in range(6)]
        for n in range(6):
            for k in range(8):
                kb.mm(banks[n].t[0:NB, :], cs.t[:, k, :], wt.t[:, k, n * 512:(n + 1) * 512], k == 0, k == 7, [cs, wt.parts[k]], [banks[n]])
            kb.op("dve", lambda e, n=n: e.tensor_tensor(out=ot.t[:, n * 512:(n + 1) * 512], in0=banks[n].t[0:NB, :],
                                                         in1=bt.t[:, n * 512:(n + 1) * 512], op=ALU.add), reads=[banks[n], bt], writes=[ot])
        kb.dma("sp", out.t[:, :], ot.t[:], reads=[ot], writes=[out])
        kb.finish([out])
    return nc


GLA_DK, GLA_DV = 128, 256


def tri_const(kb, val, upper_strict):
    m = kb.sb([128, 128])
    kb.op("pool", lambda e: e.memset(m.t[:], val), writes=[m])
    if not upper_strict:
        kb.op("pool", lambda e: e.affine_select(out=m.t[:], in_=m.t[:], pattern=[[1, 128]], compare_op=ALU.is_ge, fill=0.0,
                                                 base=0, channel_multiplier=-1), reads=[m], writes=[m])
        kb.op("pool", lambda e: e.memset(m.t[0:64, 64:128], 0.0), reads=[m], writes=[m])
    else:
        kb.op("pool", lambda e: e.affine_select(out=m.t[:], in_=m.t[:], pattern=[[-1, 128]], compare_op=ALU.is_ge, fill=0.0,
                                                 base=-1, channel_multiplier=1), reads=[m], writes=[m])
        kb.op("pool", lambda e: e.memset(m.t[64:128, 0:64], 0.0), reads=[m], writes=[m])
    return m


GLA_STAGE = 7
GLA_SUB = 99


def build_gla(S):
    nc = new_nc()
    hT = dr_in(nc, "hT", [D, S], BF16)
    wq = dr_in(nc, "wq", [D, 128]); wk = dr_in(nc, "wk", [D, 128])
    wv = dr_in(nc, "wv", [D, 256]); wr = dr_in(nc, "wr", [D, 256]); wa = dr_in(nc, "wa", [D, 16])
    wa2 = dr_in(nc, "wa2", [33, 128])
    ng = dr_in(nc, "ng", [1, 256])
    oT = dr_out(nc, "oT", [256, S], BF16)
    with ExitStack() as es:
        kb = KB(nc, es)
        identb = make_ident(kb, BF16)
        Lneg = tri_const(kb, -1.0 / 16.0, False)
        Uneg = tri_const(kb, -1.0 / 16.0, True)
        M1 = tri_const(kb, 1.0, False)
        stage = [kb.sb([128, 512]) for _ in range(2)]
        wqb = kb.sb([128, 8, 128], BF16, nparts=8); wkb = kb.sb([128, 8, 128], BF16, nparts=8)
        wvb = kb.sb([128, 8, 256], BF16, nparts=8); wrb = kb.sb([128, 8, 256], BF16, nparts=8)
        wab = kb.sb([128, 8, 16], BF16, nparts=8)
        for dst, src, n in ((wqb, wq, 128), (wkb, wk, 128), (wvb, wv, 256), (wrb, wr, 256), (wab, wa, 16)):
            load_w_bf16(kb, dst, lambda k, c0, c1, src=src: src.t[k * 128:(k + 1) * 128, c0:c1], 8, n, stage)
        wa2b = kb.sb([33, 1, 128], BF16, nparts=1)
        load_w_bf16(kb, wa2b, lambda k, c0, c1: wa2.t[0:33, c0:c1], 1, 128, stage, npart=33)
        ngb = kb.sb([128, 256])
        load_bcast(kb, "sp", ngb, ng.t[0:1, :])
        alT = kb.sb([33, 512], BF16)
        kb.op("pool", lambda e: e.memset(alT.t[:], 0.0), writes=[alT])
        kb.op("pool", lambda e: e.memset(alT.t[32:33, :], 1.0), reads=[alT], writes=[alT])
        S32 = kb.sb([128, 256]); SbA = kb.sb([128, 256], BF16); SbB = kb.sb([128, 256], BF16)
        kb.op("pool", lambda e: e.memset(S32.t[:], 0.0), writes=[S32])
        kb.op("pool", lambda e: e.memset(SbA.t[:], 0.0), writes=[SbA])
        pq = kb.ps(); pk = kb.ps(); pal = kb.ps(); pD = kb.ps(); pE = kb.ps(); pF = kb.ps(); pG = kb.ps(); pH = kb.ps()
        pDk = pDl = pDt = pD
        pEv = pEr = pE
        pFb = pFr = pFa = pF
        pGa = pGb = pG
        pH0 = pH1 = pH
        pT = pD.t[:, 256:512].bitcast(BF16)
        hhs = [kb.sb([128, 8, 512], BF16) for _ in range(2)]
        R = 2
        ex = [kb.sb([128, 128]) for _ in range(R)]; ltm = [kb.sb([128, 128]) for _ in range(R)]
        ebT = [kb.sb([128, 128]) for _ in range(R)]; einvT = [kb.sb([128, 128]) for _ in range(R)]
        erem = [kb.sb([128, 128]) for _ in range(R)]
        qdT = [kb.sb([128, 128], BF16) for _ in range(R)]; kiT = [kb.sb([128, 128], BF16) for _ in range(R)]
        kdec = [kb.sb([128, 128], BF16) for _ in range(R)]; vsb = [kb.sb([128, 256], BF16) for _ in range(R)]
        kdec1 = [kb.sb([128, 128], BF16) for _ in range(R)]
        hm01 = kb.sb([128, 2])
        kb.op("pool", lambda e: e.memset(hm01.t[:], 0.0), writes=[hm01])
        kb.op("pool", lambda e: e.memset(hm01.t[0:64, 0:1], 1.0), reads=[hm01], writes=[hm01])
        kb.op("pool", lambda e: e.memset(hm01.t[64:128, 1:2], 1.0), reads=[hm01], writes=[hm01])
        sr = [kb.sb([128, 256]) for _ in range(R)]; gr = [kb.sb([128, 256]) for _ in range(R)]
        attm = [kb.sb([128, 128], BF16) for _ in range(R)]
        oint = [kb.sb([128, 256]) for _ in range(R)]; osb = [kb.sb([128, 256]) for _ in range(R)]
        junk = kb.sb([128, 256]); ss = [kb.sb([128, 2]) for _ in range(R)]
        og = [kb.sb([128, 256], BF16) for _ in range(R)]; oTt = [kb.sb([128, 2, 128], BF16) for _ in range(R)]
        hv = hT.t.rearrange("(k p) t -> p k t", p=128)
        ov = oT.t.rearrange("(c p) t -> p c t", p=128)
        scale = GLA_DK ** -0.5
        n = 0
        def superA(st):
            hh = hhs[st % 2]
            kb.dma("sp", hh.t[:], hv[:, :, st * 512:(st + 1) * 512], writes=[hh])
            for k in range(8):
                kb.mm(pq.t[:, :], wqb.t[:, k, :], hh.t[:, k, :], k == 0, k == 7, [wqb.parts[k], hh], [pq])
            for k in range(8):
                kb.mm(pk.t[:, :], wkb.t[:, k, :], hh.t[:, k, :], k == 0, k == 7, [wkb.parts[k], hh], [pk])
            for k in range(8):
                kb.mm(pal.t[0:16, :], wab.t[:, k, :], hh.t[:, k, :], k == 0, k == 7, [wab.parts[k], hh], [pal])
            kb.op("act", lambda e: e.copy(out=alT.t[0:16, :], in_=pal.t[0:16, :]), reads=[pal], writes=[alT])

        def tileA(st, j, r):
            hh = hhs[st % 2]
            sub = slice(j * 128, (j + 1) * 128)
            for k in range(8):
                kb.mm(pD.t[:, 0:128], hh.t[:, k, sub], wkb.t[:, k, :], k == 0, k == 7, [wkb.parts[k], hh], [pDk])
            kb.mm(pD.t[:, 128:256], alT.t[0:33, sub], wa2b.t[0:33, 0, :], True, True, [alT, wa2b.parts[0]], [pDl])
            for k in range(8):
                kb.mm(pE.t[:, 0:256], hh.t[:, k, sub], wvb.t[:, k, :], k == 0, k == 7, [wvb.parts[k], hh], [pEv])
            for k in range(8):
                kb.mm(pE.t[:, 256:512], hh.t[:, k, sub], wrb.t[:, k, :], k == 0, k == 7, [wrb.parts[k], hh], [pEr])
            if GLA_STAGE >= 2:
                kb.op("act", lambda e, r=r: e.activation(out=ex[r].t[:], in_=pD.t[:, 128:256], func=AF.Exp, scale=-1.0), reads=[pDl], writes=[ex[r]])
                kb.op("act", lambda e, r=r: e.activation(out=ltm[r].t[:], in_=ex[r].t[:], func=AF.Ln, bias=1.0), reads=[ex[r]], writes=[ltm[r]])
            if GLA_STAGE >= 3:
                kb.mm(pF.t[:, 0:128], ltm[r].t[:], Lneg.t[:], True, True, [ltm[r], Lneg], [pFb])
                kb.mm(pF.t[:, 128:256], Uneg.t[:], ltm[r].t[:], True, True, [ltm[r], Uneg], [pFr])
                kb.op("act", lambda e, r=r: e.activation(out=ebT[r].t[:], in_=pF.t[:, 0:128], func=AF.Exp), reads=[pFb], writes=[ebT[r]])
                kb.op("act", lambda e, r=r: e.activation(out=einvT[r].t[:], in_=pF.t[:, 0:128], func=AF.Exp, scale=-1.0), reads=[pFb], writes=[einvT[r]])
                kb.op("act", lambda e, r=r: e.activation(out=erem[r].t[:], in_=pF.t[:, 128:256], func=AF.Exp), reads=[pFr], writes=[erem[r]])
                kb.op("dve", lambda e, r=r, sub=sub: e.scalar_tensor_tensor(out=qdT[r].t[:], in0=pq.t[:, sub], scalar=scale, in1=ebT[r].t[:],
                                                                            op0=ALU.mult, op1=ALU.mult), reads=[pq, ebT[r]], writes=[qdT[r]])
                kb.op("dve", lambda e, r=r, sub=sub: e.tensor_tensor(out=kiT[r].t[:], in0=pk.t[:, sub], in1=einvT[r].t[:], op=ALU.mult),
                      reads=[pk, einvT[r]], writes=[kiT[r]])
                kb.op("dve", lambda e, r=r: e.scalar_tensor_tensor(out=kdec[r].t[:], in0=pD.t[:, 0:128], scalar=hm01.t[:, 0:1], in1=erem[r].t[:], op0=ALU.mult, op1=ALU.mult),
                      reads=[pDk, erem[r], hm01], writes=[kdec[r]])
                kb.op("dve", lambda e, r=r: e.scalar_tensor_tensor(out=kdec1[r].t[:], in0=pD.t[:, 0:128], scalar=hm01.t[:, 1:2], in1=erem[r].t[:], op0=ALU.mult, op1=ALU.mult),
                      reads=[pDk, erem[r], hm01], writes=[kdec1[r]])
                kb.op("act", lambda e, r=r: e.copy(out=vsb[r].t[:], in_=pE.t[:, 0:256]), reads=[pEv], writes=[vsb[r]])
                kb.op("act", lambda e, r=r: e.activation(out=sr[r].t[:], in_=pE.t[:, 256:512], func=AF.Silu), reads=[pEr], writes=[sr[r]])
                kb.op("pool", lambda e, r=r: e.tensor_tensor(out=gr[r].t[:], in0=sr[r].t[:], in1=ngb.t[:], op=ALU.mult), reads=[sr[r], ngb], writes=[gr[r]])

        def tileB(st, j, r):
            sub = slice(j * 128, (j + 1) * 128)
            if GLA_STAGE >= 4:
                kb.mm(pF.t[:, 256:384], kiT[r].t[:], qdT[r].t[:], True, True, [kiT[r], qdT[r]], [pFa])
                kb.op("dve", lambda e, r=r: e.tensor_tensor(out=attm[r].t[:], in0=pF.t[:, 256:384], in1=M1.t[:], op=ALU.mult),
                      reads=[pFa, M1], writes=[attm[r]])
                kb.mm(pG.t[:, 0:256], attm[r].t[:], vsb[r].t[:], True, True, [attm[r], vsb[r]], [pGa])
            if GLA_STAGE >= 5:
                if GLA_SUB >= 1:
                    kb.mm(pH.t[:, 0:256], kdec[r].t[:, :], vsb[r].t[:, :], True, True, [kdec[r], vsb[r]], [pH0])
                if GLA_SUB >= 2:
                    kb.mm(pH.t[:, 256:512], kdec1[r].t[:, :], vsb[r].t[:, :], True, True, [kdec1[r], vsb[r]], [pH1])
                if GLA_SUB >= 3:
                    kb.mm(pG.t[:, 256:512], qdT[r].t[:, :], SbA.t[:], True, True, [qdT[r], SbA], [pGb])
                if GLA_SUB >= 4:
                    kb.op("dve", lambda e, r=r: e.scalar_tensor_tensor(out=S32.t[:], in0=S32.t[:], scalar=ebT[r].t[:, 63:64], in1=pH.t[:, 0:256],
                                                                       op0=ALU.mult, op1=ALU.add), reads=[S32, ebT[r], pH0], writes=[S32])
                if GLA_SUB >= 5:
                    kb.op("pool", lambda e: e.tensor_copy(out=SbB.t[:], in_=S32.t[:]), reads=[S32], writes=[SbB])
                if GLA_SUB >= 6:
                    kb.mm(pal.t[:, 0:256], qdT[r].t[:, :], SbB.t[:], True, True, [qdT[r], SbB], [pal])
                if GLA_SUB >= 7:
                    kb.op("dve", lambda e, r=r: e.scalar_tensor_tensor(out=S32.t[:], in0=S32.t[:], scalar=ebT[r].t[:, 127:128], in1=pH.t[:, 256:512],
                                                                       op0=ALU.mult, op1=ALU.add), reads=[S32, ebT[r], pH1], writes=[S32])
                if GLA_SUB >= 8:
                    kb.op("pool", lambda e: e.tensor_copy(out=SbA.t[:], in_=S32.t[:]), reads=[S32], writes=[SbA])
            if GLA_STAGE >= 6:
                kb.op("act", lambda e, r=r: e.copy(out=oint[r].t[0:64, :], in_=pG.t[0:64, 256:512]), reads=[pGb], writes=[oint[r]])
                kb.op("act", lambda e, r=r: e.copy(out=oint[r].t[64:128, :], in_=pal.t[64:128, 0:256]), reads=[pal, oint[r]], writes=[oint[r]])
                kb.op("dve", lambda e, r=r: e.tensor_tensor(out=osb[r].t[:], in0=pG.t[:, 0:256], in1=oint[r].t[:], op=ALU.add),
                      reads=[pGa, oint[r]], writes=[osb[r]])
                kb.op("act", lambda e, r=r: e.activation(out=junk.t[:], in_=osb[r].t[:], func=AF.Square, accum_out=ss[r].t[:, 0:1]),
                      reads=[osb[r]], writes=[junk, ss[r]])
                kb.op("dve", lambda e, r=r: e.tensor_scalar(out=ss[r].t[:, 1:2], in0=ss[r].t[:, 0:1], scalar1=1.0 / GLA_DV, scalar2=LN_EPS,
                                                            op0=ALU.mult, op1=ALU.add), reads=[ss[r]], writes=[ss[r]])
                kb.op("act", lambda e, r=r: e.sqrt(out=ss[r].t[:, 1:2], in_=ss[r].t[:, 1:2]), reads=[ss[r]], writes=[ss[r]])
                kb.op("dve", lambda e, r=r: e.reciprocal(out=ss[r].t[:, 1:2], in_=ss[r].t[:, 1:2]), reads=[ss[r]], writes=[ss[r]])
                kb.op("dve", lambda e, r=r: e.scalar_tensor_tensor(out=og[r].t[:], in0=osb[r].t[:], scalar=ss[r].t[:, 1:2], in1=gr[r].t[:],
                                                                   op0=ALU.mult, op1=ALU.mult), reads=[osb[r], ss[r], gr[r]], writes=[og[r]])
            if GLA_STAGE >= 7:
                for c in range(2):
                    kb.op("pe", lambda e, r=r, c=c: e.transpose(out=pT[:, c * 128:(c + 1) * 128], in_=og[r].t[:, c * 128:(c + 1) * 128],
                                                                identity=identb.t[:]), reads=[og[r], identb], writes=[pDt])
                kb.op("act", lambda e, r=r: e.copy(out=oTt[r].t[:].rearrange("p a b -> p (a b)"), in_=pT[:, 0:256]), reads=[pDt], writes=[oTt[r]])
            t0 = st * 512 + j * 128
            kb.dma("pool", ov[:, :, t0:t0 + 128], oTt[r].t[:], reads=[oTt[r]], writes=[oT])

        gits = [(st, j) for st in range(S // 512) for j in range(4)]
        superA(0)
        tileA(0, 0, 0)
        for n_, (st, j) in enumerate(gits):
            if n_ + 1 < len(gits):
                st2, j2 = gits[n_ + 1]
                if j2 == 0:
                    superA(st2)
                tileA(st2, j2, (n_ + 1) % R)
            tileB(st, j, n_ % R)
        kb.finish([oT])
    return nc


REL_BUCKETS, REL_MAX_DIST = 32, 2048


def rel_bucket_np(n):
    n = np.maximum(n, 0)
    exact = REL_BUCKETS // 2
    logn = np.log(np.maximum(n, 1).astype(np.float32) / exact)
    large = exact + (logn / np.float32(math.log(REL_MAX_DIST / exact)) * (REL_BUCKETS - exact)).astype(np.int32)
    large = np.minimum(large, REL_BUCKETS - 1)
    return np.where(n < exact, n, large)


def onehot_struct(dists, valid):
    L = len(dists)
    oh = np.zeros((33, L), np.float32)
    b = rel_bucket_np(np.asarray(dists))
    for i in range(L):
        if valid[i]:
            oh[b[i], i] = 1.0
        else:
            oh[32, i] = 1.0
    return oh


def make_flipJ(kb):
    jm = kb.sb([128, 128])
    kb.op("pool", lambda e: e.memset(jm.t[:], 0.0), writes=[jm])
    kb.op("pool", lambda e: e.affine_select(out=jm.t[:], in_=jm.t[:], pattern=[[1, 128]], compare_op=ALU.not_equal, fill=1.0,
                                             base=-127, channel_multiplier=1), reads=[jm], writes=[jm])
    return jm


def build_wide_bias(kb, tab, oh_ap, nh, L, W, Fd, fd_row0, wides, wbuf, pbank, jm, es):
    oht = kb.sb([33, L], es=es)
    kb.dma("sp", oht.t[:], oh_ap, writes=[oht])
    fs = kb.sb([nh, L], es=es)
    for c0 in range(0, L, 512):
        c1 = min(L, c0 + 512)
        kb.mm(pbank.t[0:nh, 0:c1 - c0], tab.t[0:33, 0:nh], oht.t[0:33, c0:c1], True, True, [tab, oht], [pbank])
        kb.op("dve", lambda e, c0=c0, c1=c1: e.tensor_copy(out=fs.t[0:nh, c0:c1], in_=pbank.t[0:nh, 0:c1 - c0]), reads=[pbank], writes=[fs])
    kb.dma("sp", Fd.t[fd_row0:fd_row0 + nh, 0:L], fs.t[0:nh, :], reads=[fs], writes=[Fd])
    tp = kb.sb([128, W], es=es)
    for h in range(nh):
        src = bass.AP(Fd.t.tensor, (fd_row0 + h) * Fd.t.shape[1], [[1, 128], [1, W]])
        kb.dma("sp", tp.t[:, :], src, reads=[Fd], writes=[tp])
        for c0 in range(0, W, 512):
            c1 = min(W, c0 + 512)
            kb.mm(pbank.t[:, 0:c1 - c0], jm.t[:], tp.t[:, c0:c1], True, True, [jm, tp], [pbank])
            kb.op("dve", lambda e, h=h, c0=c0, c1=c1: e.tensor_copy(out=wides[h][:, c0:c1], in_=pbank.t[:, 0:c1 - c0]), reads=[pbank], writes=[wbuf])


DIL_GROUPS = ((128, 1), (512, 4), (2048, 16))
CH = 2048


def dil_onehots():
    ohs = []
    for (window, dil) in DIL_GROUPS:
        m = np.arange(383) - 127
        ohs.append(onehot_struct(m * dil, (m >= 0) & (m <= window // dil)))
    return np.stack(ohs)


def build_dil(S):
    nc = new_nc()
    hT = dr_in(nc, "hT", [D, S], BF16)
    wqkv = dr_in(nc, "wqkv", [3, D, 768])
    tabi = dr_in(nc, "tab", [33, 4])
    ohi = dr_in(nc, "oh", [3, 33, 383])
    oT = dr_out(nc, "oT", [256, S], BF16)
    Fd = Buf(nc.dram_tensor("Fd", [12, 383], F32, kind="Internal").ap())
    wbf = Buf(nc.dram_tensor("wbf", [3, 128, 8 * 768], BF16, kind="Internal").ap())
    nchunk = S // CH
    scale = 64 ** -0.5
    with ExitStack() as es:
        kb = KB(nc, es)
        banks = [kb.ps() for _ in range(8)]
        jm = make_flipJ(kb)
        tab = kb.sb([33, 4])
        kb.dma("sp", tab.t[:], tabi.t[:, :], writes=[tab])
        bias = [kb.sb([128, 4, 256]) for _ in range(3)]
        sel65 = kb.sb([128, 64])
        kb.op("pool", lambda e: e.memset(sel65.t[:], 0.0), writes=[sel65])
        kb.op("pool", lambda e: e.memset(sel65.t[64:65, :], 1.0), reads=[sel65], writes=[sel65])
        with ExitStack() as es0:
            for g in range(3):
                build_wide_bias(kb, tab, ohi.t[g], 4, 383, 256, Fd, 4 * g, [bias[g].t[:, h, :] for h in range(4)], bias[g], banks[0], jm, es0)
        kb.barrier()
        with ExitStack() as es1:
            stage = [kb.sb([128, 768], es=es1) for _ in range(2)]
            wtmp = [kb.sb([128, 8, 768], BF16, nparts=8, es=es1) for _ in range(1)]
            for g in range(3):
                load_w_bf16(kb, wtmp[0], lambda k, c0, c1, g=g: wqkv.t[g, k * 128:(k + 1) * 128, c0:c1], 8, 768, stage, chunk=768)
                kb.dma("sp", wbf.t[g].rearrange("p (k c) -> p k c", k=8), wtmp[0].t[:], reads=wtmp[0].parts, writes=[wbf])
                for p_ in wtmp[0].parts:
                    p_.r[kb.dq["sp"][(kb.di["sp"] - 1) % len(kb.dq["sp"])]] = kb.dsl[kb.dq["sp"][(kb.di["sp"] - 1) % len(kb.dq["sp"])]][1]
        kb.barrier()
        wg = [kb.sb([128, 8, 768], BF16) for _ in range(2)]
        hh = [kb.sb([128, 8, CH], BF16) for _ in range(2)]
        qTm = [[kb.sb([128, CH], BF16) for _ in range(2)] for _ in range(2)]
        for hp_ in range(2):
            for hd_ in range(2):
                kb.op("pool", lambda e, hp_=hp_, hd_=hd_: e.memset(qTm[hp_][hd_].t[:], 0.0), writes=[qTm[hp_][hd_]])
        kT = [kb.sb([128, 2 * CH], BF16) for _ in range(2)]
        V = kb.sb([128, 32, 4, 65], BF16)
        kb.op("pool", lambda e: e.memset(V.t[:].rearrange("p a b c -> p (a b c)"), 1.0), writes=[V])
        acc = kb.sb([65, 4, CH])
        tmp = [kb.sb([128, 512]) for _ in range(2)]
        pT = [kb.sb([128, 512], BF16) for _ in range(2)]
        oTt = [kb.sb([64, CH], BF16) for _ in range(2)]
        rdb = kb.sb([64, 512])
        hv = hT.t.rearrange("(k p) t -> p k t", p=128)
        nw = 0
        nit = 0
        for c in range(nchunk):
            cur = hh[c % 2]
            kb.dma("sp", cur.t[:], hv[:, :, c * CH:(c + 1) * CH], writes=[cur])
            slots = ([(hh[(c - 1) % 2], 0)] if c > 0 else []) + [(cur, 1)]
            for g, (window, d) in enumerate(DIL_GROUPS):
                w = wg[nw % 2]; nw += 1
                kb.dma("pool", w.t[:], wbf.t[g].rearrange("p (k c) -> p k c", k=8), reads=[wbf], writes=[w])
                nbk = 16 // d
                for hp in range(2):
                    for n4 in range(4):
                        pb = banks[(n4 + hp) % 2]
                        for k in range(8):
                            kb.mm(pb.t[:, :], w.t[:, k, hp * 128:(hp + 1) * 128], cur.t[:, k, n4 * 512:(n4 + 1) * 512], k == 0, k == 7, [w, cur], [pb])
                        kb.op("act", lambda e, pb=pb, hp=hp, n4=n4: e.copy(out=qTm[hp][0].t[0:64, n4 * 512:(n4 + 1) * 512], in_=pb.t[0:64, :]), reads=[pb], writes=[qTm[hp][0]])
                        kb.op("act", lambda e, pb=pb, hp=hp, n4=n4: e.copy(out=qTm[hp][1].t[64:128, n4 * 512:(n4 + 1) * 512], in_=pb.t[64:128, :]), reads=[pb], writes=[qTm[hp][1]])
                    for (hb, si) in slots:
                        for n4 in range(4):
                            pb = banks[(n4 + hp) % 2]
                            for k in range(8):
                                kb.mm(pb.t[:, :], w.t[:, k, 256 + hp * 128:256 + (hp + 1) * 128], hb.t[:, k, n4 * 512:(n4 + 1) * 512], k == 0, k == 7, [w, hb], [pb])
                            kb.op("dve", lambda e, pb=pb, hp=hp, n4=n4, si=si: e.tensor_copy(out=kT[hp].t[:, si * CH + n4 * 512:si * CH + (n4 + 1) * 512], in_=pb.t[:, :]),
                                  reads=[pb], writes=[kT[hp]])
                for (hb, si) in slots:
                    for r in range(d):
                        for bl in range(nbk):
                            ti = si * 16 + r * nbk + bl
                            pb = banks[2 + ti % 2]
                            st0 = r + d * 128 * bl
                            for k in range(8):
                                kb.mm(pb.t[:, 0:256], hb.t[:, k, st0:st0 + 127 * d + 1:d], w.t[:, k, 512:768], k == 0, k == 7, [w, hb], [pb])
                            kb.op("act" if ti % 2 else "dve",
                                  (lambda e, pb=pb, ti=ti: e.copy(out=V.t[:, ti, :, 0:64], in_=pb.t[:, 0:256].rearrange("p (h c) -> p h c", h=4))) if ti % 2 else
                                  (lambda e, pb=pb, ti=ti: e.tensor_copy(out=V.t[:, ti, :, 0:64], in_=pb.t[:, 0:256].rearrange("p (h c) -> p h c", h=4))),
                                  reads=[pb], writes=[V])
                aits = []
                for r in range(d):
                    for bl in range(nbk):
                        q0 = r + d * 128 * bl
                        keys = [(1, r, bl)]
                        if bl > 0:
                            keys.append((1, r, bl - 1))
                        elif c > 0:
                            keys.append((0, r, nbk - 1))
                        for hp in range(2):
                            aits.append((q0, keys, hp))

                def dilA(n_):
                    q0, keys, hp = aits[n_]
                    nd = len(keys)
                    it = (nit + n_) % 2
                    ps = banks[4 + it]
                    for hd in range(2):
                        for dl, (si, kr, kbl) in enumerate(keys):
                            k0 = si * CH + kr + d * 128 * kbl
                            kb.mm(ps.t[:, (hd * 2 + dl) * 128:(hd * 2 + dl + 1) * 128], kT[hp].t[:, k0:k0 + 127 * d + 1:d],
                                  qTm[hp][hd].t[:, q0:q0 + 127 * d + 1:d], True, True, [kT[hp], qTm[hp][hd]], [ps])
                    psv = ps.t[:, :].rearrange("p (h e i) -> p h e i", h=2, e=2)
                    tv = tmp[it].t[:, :].rearrange("p (h e i) -> p h e i", h=2, e=2)
                    pv = pT[it].t[:, :].rearrange("p (h e i) -> p h e i", h=2, e=2)
                    bv = bias[g].t[:, 2 * hp:2 * hp + 2, :].rearrange("p h (e i) -> p h e i", e=2)
                    kb.op("dve", lambda e: e.scalar_tensor_tensor(out=tv[:, :, 0:nd, :], in0=psv[:, :, 0:nd, :], scalar=scale,
                                                                  in1=bv[:, :, 0:nd, :], op0=ALU.mult, op1=ALU.add),
                          reads=[ps, bias[g]], writes=[tmp[it]])
                    kb.op("act", lambda e: e.activation(out=pv[:, :, 0:nd, :], in_=tv[:, :, 0:nd, :], func=AF.Exp),
                          reads=[tmp[it]], writes=[pT[it]])

                def dilB(n_):
                    q0, keys, hp = aits[n_]
                    nd = len(keys)
                    it = (nit + n_) % 2
                    po = banks[6 + it]
                    pv = pT[it].t[:, :].rearrange("p (h e i) -> p h e i", h=2, e=2)
                    for hd in range(2):
                        for dl, (si, kr, kbl) in enumerate(keys):
                            ti = si * 16 + kr * nbk + kbl
                            kb.mm(po.t[0:65, hd * 128:(hd + 1) * 128], V.t[:, ti, 2 * hp + hd, :], pv[:, hd, dl, :], dl == 0, dl == nd - 1, [V, pT[it]], [po])
                    av = acc.t[:, 2 * hp:2 * hp + 2, q0:q0 + 127 * d + 1:d]
                    pov = po.t[0:65, 0:256].rearrange("p (h i) -> p h i", h=2)
                    if g == 0:
                        kb.op("dve", lambda e: e.tensor_copy(out=av, in_=pov), reads=[po], writes=[acc])
                    else:
                        kb.op("dve", lambda e: e.tensor_tensor(out=av, in0=av, in1=pov, op=ALU.add), reads=[po, acc], writes=[acc])

                dilA(0)
                for n_ in range(len(aits)):
                    if n_ + 1 < len(aits):
                        dilA(n_ + 1)
                    dilB(n_)
                nit += len(aits)
            for h in range(4):
                ot = oTt[h % 2]
                for n4 in range(4):
                    pb = banks[n4 % 2]
                    kb.mm(pb.t[0:64, :], sel65.t[0:65, 0:64], acc.t[0:65, h, n4 * 512:(n4 + 1) * 512], True, True, [sel65, acc], [pb])
                    kb.op("dve", lambda e, pb=pb: e.reciprocal(out=rdb.t[:, :], in_=pb.t[0:64, :]), reads=[pb], writes=[rdb])
                    kb.op("dve", lambda e, h=h, n4=n4, ot=ot: e.tensor_tensor(out=ot.t[:, n4 * 512:(n4 + 1) * 512], in0=acc.t[0:64, h, n4 * 512:(n4 + 1) * 512],
                                                                            in1=rdb.t[:, :], op=ALU.mult), reads=[acc, rdb], writes=[ot])
                kb.dma("sp", oT.t[h * 64:(h + 1) * 64, c * CH:(c + 1) * CH], ot.t[:, :], reads=[ot], writes=[oT])
        kb.finish([oT])
    return nc


SELW, WINW = 2560, 1408
SEL_FARD = 13
BIGF, BIGI = 1.0e4, -1.0e6
EB = 30000.0


def nsa_onehots():
    m = np.arange(SELW + 127) - 127 - 384
    oh_sel = onehot_struct(m, m >= 0)
    m = np.arange(WINW + 127) - 127 - 384
    oh_win = onehot_struct(m, (m >= 0) & (m <= 511))
    m = np.arange(1776 + 16) - 127
    oh_cmp = onehot_struct(m, m >= 0)
    return oh_sel, oh_win, oh_cmp


NSA_STAGE = 99
NSA_SUB = 99


def build_nsa(S):
    nc = new_nc()
    NT = S // 128
    NCK = S // 16 - 1
    NCP = ((NCK + 127) // 128) * 128
    hT = dr_in(nc, "hT", [D, S], BF16)
    wqd = dr_in(nc, "wqd", [D, 512]); wkv = dr_in(nc, "wkv", [D, 384]); wgi = dr_in(nc, "wg", [D, 12])
    pei = dr_in(nc, "pe", [2, 32, 64]); w1i = dr_in(nc, "w1", [2, 2048, 256]); b1i = dr_in(nc, "b1", [2, 256]); w2i = dr_in(nc, "w2", [2, 256, 64])
    tabi = dr_in(nc, "tab", [33, 4])
    ohs = dr_in(nc, "ohs", [33, SELW + 127]); ohw = dr_in(nc, "ohw", [33, WINW + 127]); ohc = dr_in(nc, "ohc", [33, 1792])
    oT = dr_out(nc, "oT", [256, S], BF16)
    Fd = Buf(nc.dram_tensor("Fd", [12, SELW + 127], F32, kind="Internal").ap())
    scale = 64 ** -0.5
    with ExitStack() as es:
        kb = KB(nc, es)
        banks = [kb.ps() for _ in range(8)]
        identb = make_ident(kb, BF16)
        jm = make_flipJ(kb)
        tab = kb.sb([33, 4])
        kb.dma("sp", tab.t[:], tabi.t[:, :], writes=[tab])
        kswT = kb.sb([128, S], BF16)
        Vs = kb.sb([128, NT, 65], BF16); Vw = kb.sb([128, NT, 65], BF16)
        kb.op("pool", lambda e: e.memset(Vs.t[:].rearrange("p a b -> p (a b)"), 1.0), writes=[Vs])
        kb.op("pool", lambda e: e.memset(Vw.t[:].rearrange("p a b -> p (a b)"), 1.0), writes=[Vw])
        kcmpT = kb.sb([64, NCP], BF16); vcmp = kb.sb([128, NCP // 128, 64], BF16)
        if NSA_STAGE == 1.1:
            kb.finish([oT]); return nc
        stage = [kb.sb([128, 512]) for _ in range(2)]
        wqb = kb.sb([128, 8, 512], BF16, nparts=8); wkvb = kb.sb([128, 8, 384], BF16, nparts=8); wgb = kb.sb([128, 8, 12], BF16, nparts=8)
        for dst, src, n in ((wqb, wqd, 512), (wkvb, wkv, 384), (wgb, wgi, 12)):
            load_w_bf16(kb, dst, lambda k, c0, c1, src=src: src.t[k * 128:(k + 1) * 128, c0:c1], 8, n, stage)
        if NSA_STAGE == 1.2:
            kb.finish([oT]); return nc
        hhs = [kb.sb([128, 8, 512], BF16) for _ in range(2)]
        hv = hT.t.rearrange("(k p) t -> p k t", p=128)
        with ExitStack() as es1:
            kcvcT = kb.sb([128, S], BF16, es=es1)
            w1m = [kb.sb([128, 32, 256], BF16, es=es1) for _ in range(2)]
            for x_ in range(2):
                kb.op("pool", lambda e, x_=x_: e.memset(w1m[x_].t[:].rearrange("p a b -> p (a b)"), 0.0), writes=[w1m[x_]])
            w2b = kb.sb([128, 2, 2, 64], BF16, es=es1)
            st1 = [kb.sb([128, 8, 256], es=es1) for _ in range(1)]
            nst1 = 0
            for x_ in range(2):
                rws = slice(x_ * 64, (x_ + 1) * 64)
                for pc in range(4):
                    stg = st1[0]; nst1 += 1
                    kb.dma("sp", stg.t[rws, :, :], w1i.t[x_].rearrange("(p d) h -> d p h", d=64)[:, pc * 8:(pc + 1) * 8, :], writes=[stg])
                    kb.op("dve", lambda e, x_=x_, pc=pc, stg=stg, rws=rws: e.tensor_copy(out=w1m[x_].t[rws, pc * 8:(pc + 1) * 8, :], in_=stg.t[rws, :, :]),
                          reads=[stg], writes=[w1m[x_]])
            st2 = kb.sb([128, 2, 2, 64], es=es1)
            for x_ in range(2):
                kb.dma("sp", st2.t[:, x_, :, :], w2i.t[x_].rearrange("(a p) c -> p a c", p=128), writes=[st2])
            kb.op("dve", lambda e: e.tensor_copy(out=w2b.t[:].rearrange("p a b c -> p (a b c)"), in_=st2.t[:].rearrange("p a b c -> p (a b c)")),
                  reads=[st2], writes=[w2b])
            peT = kb.sb([128, 32], es=es1); peTb = kb.sb([128, 32], BF16, es=es1)
            for x_ in range(2):
                for p4 in range(4):
                    kb.dma("sp", peT.t[x_ * 64:(x_ + 1) * 64, p4 * 8:(p4 + 1) * 8], pei.t[x_, p4 * 8:(p4 + 1) * 8, :].rearrange("p d -> d p"), writes=[peT],
                           allow_slow_non_contiguous=True)
            kb.op("dve", lambda e: e.tensor_copy(out=peTb.t[:], in_=peT.t[:]), reads=[peT], writes=[peTb])
            b1t = kb.sb([128, 2, 2], es=es1)
            for x_ in range(2):
                kb.dma("sp", b1t.t[:, x_, :], b1i.t[x_, :].rearrange("(a p) -> p a", p=128), writes=[b1t], allow_slow_non_contiguous=True)
            if NSA_STAGE == 1.3:
                kb.finish([oT]); return nc
            for st in range(S // 512):
                hh = hhs[st % 2]
                kb.dma("sp", hh.t[:], hv[:, :, st * 512:(st + 1) * 512], writes=[hh])
                for (c0, dstT, eng) in ((0, kcvcT, "act"), (128, kswT, "dve")):
                    if NSA_STAGE == 1.5:
                        break
                    pb = banks[1 + (c0 // 128)]
                    for k in range(8):
                        kb.mm(pb.t[:, :], wkvb.t[:, k, c0:c0 + 128], hh.t[:, k, :], k == 0, k == 7, [wkvb.parts[k], hh], [pb])
                    if eng == "act":
                        kb.op("act", lambda e, pb=pb, dstT=dstT, st=st: e.copy(out=dstT.t[:, st * 512:(st + 1) * 512], in_=pb.t[:, :]), reads=[pb], writes=[dstT])
                    else:
                        kb.op("dve", lambda e, pb=pb, dstT=dstT, st=st: e.tensor_copy(out=dstT.t[:, st * 512:(st + 1) * 512], in_=pb.t[:, :]), reads=[pb], writes=[dstT])
                if NSA_STAGE == 1.4:
                    continue
                pb = banks[3 + st % 2]
                for j in range(4):
                    for k in range(8):
                        kb.mm(pb.t[:, j * 128:(j + 1) * 128], hh.t[:, k, j * 128:(j + 1) * 128], wkvb.t[:, k, 256:384], k == 0, k == 7, [wkvb.parts[k], hh], [pb])
                pv4 = pb.t[:, :].rearrange("p (j c) -> p j c", j=4)
                for j in range(4):
                    if NSA_SUB >= 1:
                        kb.op("act", lambda e, pb=pb, st=st, j=j: e.copy(out=Vs.t[:, st * 4 + j, 0:64], in_=pb.t[:, j * 128:j * 128 + 64]), reads=[pb], writes=[Vs])
                    if NSA_SUB >= 2:
                        kb.op("act", lambda e, pb=pb, st=st, j=j: e.copy(out=Vw.t[:, st * 4 + j, 0:64], in_=pb.t[:, j * 128 + 64:j * 128 + 128]), reads=[pb], writes=[Vw])
            if NSA_STAGE in (2, 1.4, 1.5):
                kb.finish([oT]); return nc
            hidT = kb.sb([128, 2, 2, NCP], BF16, es=es1)
            kb.op("pool", lambda e: e.memset(hidT.t[:].rearrange("p a b c -> p (a b c)"), 0.0), writes=[hidT])
            cbias = kb.sb([128, 2, 2], es=es1)
            for x_ in range(2):
                rows = slice(x_ * 64, (x_ + 1) * 64)
                for half in range(2):
                    pb = banks[1]
                    for p in range(32):
                        kb.mm(pb.t[:, 0:1], w1m[x_].t[:, p, half * 128:(half + 1) * 128], peTb.t[:, p:p + 1], p == 0, p == 31, [w1m[x_], peTb], [pb])
                    kb.op("dve", lambda e, x_=x_, half=half, pb=pb: e.tensor_tensor(out=cbias.t[:, x_, half:half + 1], in0=pb.t[:, 0:1], in1=b1t.t[:, x_, half:half + 1], op=ALU.add),
                          reads=[pb, b1t], writes=[cbias])
                    for n0 in range(0, NCK, 512):
                        n1 = min(NCK, n0 + 512)
                        pb2 = banks[2 + (n0 // 512) % 2]
                        for p in range(32):
                            kb.mm(pb2.t[:, 0:n1 - n0], w1m[x_].t[:, p, half * 128:(half + 1) * 128], kcvcT.t[:, 16 * n0 + p:16 * (n1 - 1) + p + 1:16],
                                  p == 0, p == 31, [w1m[x_], kcvcT], [pb2])
                        kb.op("act", lambda e, x_=x_, half=half, n0=n0, n1=n1, pb2=pb2: e.activation(out=hidT.t[:, x_, half, n0:n1], in_=pb2.t[:, 0:n1 - n0], func=AF.Silu,
                                                                                                     bias=cbias.t[:, x_, half:half + 1]), reads=[pb2, cbias], writes=[hidT])
            for n0 in range(0, NCP, 512):
                n1 = min(NCP, n0 + 512)
                pb = banks[1]
                for half in range(2):
                    kb.mm(pb.t[0:64, 0:n1 - n0], w2b.t[:, 0, half, :], hidT.t[:, 0, half, n0:n1], half == 0, half == 1, [w2b, hidT], [pb])
                kb.op("dve", lambda e, n0=n0, n1=n1, pb=pb: e.tensor_copy(out=kcmpT.t[:, n0:n1], in_=pb.t[0:64, 0:n1 - n0]), reads=[pb], writes=[kcmpT])
            for ct in range(NCP // 128):
                pb = banks[2 + ct % 2]
                for half in range(2):
                    kb.mm(pb.t[:, 0:64], hidT.t[:, 1, half, ct * 128:(ct + 1) * 128], w2b.t[:, 1, half, :], half == 0, half == 1, [w2b, hidT], [pb])
                kb.op("act", lambda e, ct=ct, pb=pb: e.copy(out=vcmp.t[:, ct, :], in_=pb.t[:, 0:64]), reads=[pb], writes=[vcmp])
        if NSA_STAGE == 3:
            kb.finish([oT]); return nc
        kb.barrier()
        wsel = kb.sb([128, 4, SELW], BF16); wwin = kb.sb([128, 4, WINW], BF16)
        nearb = kb.sb([128, 4, 104]); farb = kb.sb([128, 4])
        with ExitStack() as es0:
            wtmp = kb.sb([128, SELW], es=es0)
            for (oh, L, W, dst) in ((ohs, SELW + 127, SELW, wsel), (ohw, WINW + 127, WINW, wwin)):
                for h in range(4):
                    tabh = Buf(tab.t[:, h:h + 1]); tabh.w = tab.w
                    with ExitStack() as esx:
                        build_wide_bias(kb, tabh, oh.t[:, :], 1, L, W, Fd, h, [wtmp.t[:, 0:W]], wtmp, banks[0], jm, esx)
                        kb.op("act", lambda e, dst=dst, h=h, W=W: e.copy(out=dst.t[:, h, :], in_=wtmp.t[:, 0:W]), reads=[wtmp], writes=[dst])
                    kb.barrier()
            oht = kb.sb([33, 1792], es=es0)
            kb.dma("sp", oht.t[:], ohc.t[:, :], writes=[oht])
            fs = kb.sb([4, 1792], es=es0)
            for c0 in range(0, 1792, 512):
                c1 = min(1792, c0 + 512)
                kb.mm(banks[0].t[0:4, 0:c1 - c0], tab.t[0:33, 0:4], oht.t[0:33, c0:c1], True, True, [tab, oht], [banks[0]])
                kb.op("dve", lambda e, c0=c0, c1=c1: e.tensor_copy(out=fs.t[0:4, c0:c1], in_=banks[0].t[0:4, 0:c1 - c0]), reads=[banks[0]], writes=[fs])
            kb.dma("sp", Fd.t[8:12, 0:1792], fs.t[0:4, :], reads=[fs], writes=[Fd])
            jc = kb.sb([128, 104], es=es0)
            kb.op("pool", lambda e: e.memset(jc.t[:], 0.0), writes=[jc])
            kb.op("pool", lambda e: e.affine_select(out=jc.t[:], in_=jc.t[:], pattern=[[1, 104]], compare_op=ALU.not_equal, fill=1.0,
                                                     base=-103, channel_multiplier=1), reads=[jc], writes=[jc])
            xt = kb.sb([104, 128], es=es0)
            for h in range(4):
                src = bass.AP(Fd.t.tensor, (8 + h) * Fd.t.shape[1], [[16, 104], [1, 128]])
                kb.dma("sp", xt.t[:, :], src, reads=[Fd], writes=[xt])
                kb.mm(banks[0].t[:, 0:104], xt.t[0:104, :], jc.t[0:104, :], True, True, [xt, jc], [banks[0]])
                kb.op("dve", lambda e, h=h: e.tensor_copy(out=nearb.t[:, h, :], in_=banks[0].t[:, 0:104]), reads=[banks[0]], writes=[nearb])
            onesr = kb.sb([33, 128], es=es0)
            kb.op("pool", lambda e: e.memset(onesr.t[:], 0.0), writes=[onesr])
            kb.op("pool", lambda e: e.memset(onesr.t[0:1, :], 1.0), reads=[onesr], writes=[onesr])
            t31 = kb.sb([1, 4], es=es0)
            kb.dma("sp", t31.t[:, :], tabi.t[31:32, :], writes=[t31])
            kb.mm(banks[0].t[:, 0:4], onesr.t[0:1, :], t31.t[0:1, :], True, True, [onesr, t31], [banks[0]])
            kb.op("dve", lambda e: e.tensor_copy(out=farb.t[:, :], in_=banks[0].t[:, 0:4]), reads=[banks[0]], writes=[farb])
        kb.barrier()
        Aw = kb.sb([128, 512])
        kb.op("pool", lambda e: e.memset(Aw.t[:], 0.0), writes=[Aw])
        for (rows, c0) in ((slice(0, 64), 255), (slice(64, 128), 256)):
            kb.op("pool", lambda e, rows=rows, c0=c0: e.memset(Aw.t[rows, c0:c0 + 2], BIGF), reads=[Aw], writes=[Aw])
            kb.op("pool", lambda e, rows=rows, c0=c0: e.memset(Aw.t[rows, c0 + 2:512], BIGI), reads=[Aw], writes=[Aw])
        e2f = kb.sb([128, 64, 2])
        kb.op("pool", lambda e: e.memset(e2f.t[:].rearrange("p a b -> p (a b)"), 0.0), writes=[e2f])
        kb.op("pool", lambda e: e.affine_select(out=e2f.t[:], in_=e2f.t[:], pattern=[[-2, 64], [-1, 2]], compare_op=ALU.not_equal, fill=EB,
                                                 base=0, channel_multiplier=1), reads=[e2f], writes=[e2f])
        Exp_ = kb.sb([128, 64, 128], BF16)
        for half in range(2):
            kb.op("dve", lambda e, half=half: e.tensor_copy(out=Exp_.t[:, :, half * 64:(half + 1) * 64], in_=e2f.t[:, :, half:half + 1].to_broadcast([128, 64, 64])),
                  reads=[e2f], writes=[Exp_])
        qsel = [kb.sb([128, 512], BF16) for _ in range(4)]
        qwin = [kb.sb([128, 512], BF16) for _ in range(4)]
        for h_ in range(4):
            kb.op("pool", lambda e, h_=h_: e.memset(qsel[h_].t[:], 0.0), writes=[qsel[h_]])
            kb.op("pool", lambda e, h_=h_: e.memset(qwin[h_].t[:], 0.0), writes=[qwin[h_]])
        gates = kb.sb([128, 4, 12])
        negselT = kb.sb([128, 2, 512], BF16)
        kb.op("pool", lambda e: e.memset(negselT.t[:].rearrange("p a b -> p (a b)"), -1.0), writes=[negselT])
        tmpc = [kb.sb([128, 1024]) for _ in range(2)]; ebuf = tmpc; pg = kb.sb([128, 1024])
        pbf = [kb.sb([128, 1024], BF16) for _ in range(2)]
        kb.op("pool", lambda e: e.memset(pg.t[:], 0.0), writes=[pg])
        pTc = [kb.sb([128, NCP // 128, 128], BF16) for _ in range(2)]
        den = [kb.sb([128, 2]) for _ in range(2)]; imp = kb.sb([128, 256]); sc2 = kb.sb([128, 256]); m8 = kb.sb([128, 16]); nsel = kb.sb([128, 256], BF16)
        ofin = [kb.sb([128, 4, 64]) for _ in range(4)]
        tmp = [kb.sb([128, 512]) for _ in range(3)]; pT = [kb.sb([128, 512], BF16) for _ in range(3)]
        fcol = kb.sb([128, 8]); ogb = kb.sb([128, 256], BF16); oTt = kb.sb([128, 2, 128], BF16)
        poS = [kb.sb([65, 512]) for _ in range(2)]; identf = make_ident(kb, F32)
        ov = oT.t.rearrange("(c p) t -> p c t", p=128)
        pgv = pg.t[:, :].rearrange("p (b m) -> p b m", m=4)
        if NSA_STAGE == 4:
            kb.finish([oT]); return nc
        for qs in range(S // 512):
            hh = hhs[qs % 2]
            kb.dma("sp", hh.t[:], hv[:, :, qs * 512:(qs + 1) * 512], writes=[hh])
            for h in range(4):
                pb = banks[0]
                for k in range(8):
                    kb.mm(pb.t[:, :], wqb.t[:, k, h * 128:(h + 1) * 128], hh.t[:, k, :], k == 0, k == 7, [wqb.parts[k], hh], [pb])
                kb.op("act", lambda e, h=h, pb=pb: e.copy(out=qsel[h].t[0:64, :], in_=pb.t[0:64, :]), reads=[pb], writes=[qsel[h]])
                kb.op("dve", lambda e, h=h, pb=pb: e.tensor_copy(out=qwin[h].t[64:128, :], in_=pb.t[64:128, :]), reads=[pb], writes=[qwin[h]])
            pb = banks[0]
            for j in range(4):
                for k in range(8):
                    kb.mm(pb.t[:, j * 12:(j + 1) * 12], hh.t[:, k, j * 128:(j + 1) * 128], wgb.t[:, k, :], k == 0, k == 7, [wgb.parts[k], hh], [pb])
            kb.op("act", lambda e, pb=pb: e.activation(out=gates.t[:].rearrange("p a b -> p (a b)"), in_=pb.t[:, 0:48], func=AF.Sigmoid), reads=[pb], writes=[gates])
            def cmpA(j, h, pp):
                qb = qs * 4 + j
                sub = slice(j * 128, (j + 1) * 128)
                ncv = min(8 * qb + 7, NCK)
                nlo = max(0, 8 * qb - 97); u0 = nlo - (8 * qb - 97)
                sbk = (banks[1], banks[2]) if pp == 0 else (banks[5], banks[6])
                for c0 in range(0, ncv, 512):
                    c1 = min(ncv, c0 + 512)
                    kb.mm(sbk[c0 // 512].t[:, 0:c1 - c0], qsel[h].t[0:64, sub], kcmpT.t[0:64, c0:c1], True, True, [qsel[h], kcmpT], [sbk[c0 // 512]])
                for c0 in range(0, ncv, 512):
                    c1 = min(ncv, c0 + 512)
                    pbk = sbk[c0 // 512]
                    kb.op("dve", lambda e, c0=c0, c1=c1, pbk=pbk: e.tensor_scalar(out=tmpc[pp].t[:, c0:c1], in0=pbk.t[:, 0:c1 - c0], scalar1=scale, scalar2=farb.t[:, h:h + 1],
                                                                             op0=ALU.mult, op1=ALU.add), reads=[pbk, farb], writes=[tmpc[pp]])
                    a0 = max(c0, nlo)
                    if a0 < c1:
                        kb.op("dve", lambda e, c0=c0, c1=c1, a0=a0, pbk=pbk: e.scalar_tensor_tensor(
                            out=tmpc[pp].t[:, a0:c1], in0=pbk.t[:, a0 - c0:c1 - c0], scalar=scale, in1=nearb.t[:, h, u0 + a0 - nlo:u0 + c1 - nlo],
                            op0=ALU.mult, op1=ALU.add), reads=[pbk, nearb, tmpc[pp]], writes=[tmpc[pp]])
                kb.op("act", lambda e: e.activation(out=ebuf[pp].t[:, 0:ncv], in_=tmpc[pp].t[:, 0:ncv], func=AF.Exp, accum_out=den[pp].t[:, 0:1]),
                      reads=[tmpc[pp]], writes=[tmpc[pp], den[pp]])

            def cmpB(j, h, pp):
                qb = qs * 4 + j
                ncv = min(8 * qb + 7, NCK)
                nct = (ncv + 127) // 128
                dn = den[pp]
                kb.op("dve", lambda e: e.tensor_scalar(out=dn.t[:, 1:2], in0=dn.t[:, 0:1], scalar1=1e-30, scalar2=None, op0=ALU.max), reads=[dn], writes=[dn])
                kb.op("dve", lambda e: e.reciprocal(out=dn.t[:, 1:2], in_=dn.t[:, 1:2]), reads=[dn], writes=[dn])
                if h == 0:
                    kb.op("dve", lambda e: e.tensor_scalar(out=pg.t[:, 0:ncv], in0=ebuf[pp].t[:, 0:ncv], scalar1=dn.t[:, 1:2], scalar2=None, op0=ALU.mult),
                          reads=[ebuf[pp], dn], writes=[pg])
                else:
                    kb.op("dve", lambda e: e.scalar_tensor_tensor(out=pg.t[:, 0:ncv], in0=ebuf[pp].t[:, 0:ncv], scalar=dn.t[:, 1:2], in1=pg.t[:, 0:ncv],
                                                                  op0=ALU.mult, op1=ALU.add), reads=[ebuf[pp], dn, pg], writes=[pg])
                kb.op("act", lambda e: e.activation(out=pbf[pp].t[:, 0:ncv], in_=ebuf[pp].t[:, 0:ncv], func=AF.Copy, scale=dn.t[:, 1:2]),
                      reads=[ebuf[pp], dn], writes=[pbf[pp]])
                ptk = banks[3] if pp == 0 else banks[0]
                ptb = ptk.t[:, :].bitcast(BF16)
                for ct in range(nct):
                    w_ = min(128, ncv - ct * 128)
                    kb.op("pe", lambda e, ct=ct, w_=w_: e.transpose(out=ptb[0:w_, ct * 128:(ct + 1) * 128], in_=pbf[pp].t[:, ct * 128:ct * 128 + w_],
                                                                    identity=identb.t[:]), reads=[pbf[pp], identb], writes=[ptk])
                hlf = (nct + 1) // 2
                for (ca, cb, en) in ((0, hlf, "act"), (hlf, nct, "dve")):
                    if cb <= ca:
                        continue
                    wl = min(128, ncv - (cb - 1) * 128)
                    if wl == 128 or cb - ca == 1:
                        w_ = wl if cb - ca == 1 else 128
                        src_ = ptb[0:w_, ca * 128:cb * 128].rearrange("p (c i) -> p c i", i=128)
                        dst_ = pTc[pp].t[0:w_, ca:cb, :]
                        if en == "act":
                            kb.op("act", lambda e, src_=src_, dst_=dst_: e.copy(out=dst_, in_=src_), reads=[ptk], writes=[pTc[pp]])
                        else:
                            kb.op("dve", lambda e, src_=src_, dst_=dst_: e.tensor_copy(out=dst_, in_=src_), reads=[ptk], writes=[pTc[pp]])
                    else:
                        for (a_, b_, w_) in ((ca, cb - 1, 128), (cb - 1, cb, wl)):
                            src_ = ptb[0:w_, a_ * 128:b_ * 128].rearrange("p (c i) -> p c i", i=128)
                            dst_ = pTc[pp].t[0:w_, a_:b_, :]
                            kb.op("act", lambda e, src_=src_, dst_=dst_: e.copy(out=dst_, in_=src_), reads=[ptk], writes=[pTc[pp]])
                for ct in range(nct):
                    w_ = min(128, ncv - ct * 128)
                    kb.mm(banks[4].t[:, 0:64], pTc[pp].t[0:w_, ct, :], vcmp.t[0:w_, ct, :], ct == 0, ct == nct - 1, [pTc[pp], vcmp], [banks[4]])
                kb.op("dve", lambda e: e.tensor_scalar(out=ofin[j].t[:, h, :], in0=banks[4].t[:, 0:64], scalar1=gates.t[:, j, h:h + 1], scalar2=None, op0=ALU.mult),
                      reads=[banks[4], gates], writes=[ofin[j]])

            def select(j):
                qb = qs * 4 + j
                sub = slice(j * 128, (j + 1) * 128)
                kb.op("dve", lambda e: e.tensor_tensor(out=imp.t[:, :], in0=pgv[:, :, 0], in1=pgv[:, :, 1], op=ALU.add), reads=[pg], writes=[imp])
                kb.op("dve", lambda e: e.tensor_tensor(out=imp.t[:, :], in0=imp.t[:, :], in1=pgv[:, :, 2], op=ALU.add), reads=[pg, imp], writes=[imp])
                kb.op("dve", lambda e: e.scalar_tensor_tensor(out=imp.t[:, :], in0=imp.t[:, :], scalar=2.0, in1=pgv[:, :, 3], op0=ALU.mult, op1=ALU.add),
                      reads=[pg, imp], writes=[imp])
                kb.op("dve", lambda e: e.tensor_tensor(out=imp.t[:, 1:256], in0=imp.t[:, 1:256], in1=pgv[:, 0:255, 3], op=ALU.add), reads=[pg, imp], writes=[imp])
                kb.op("dve", lambda e: e.tensor_tensor(out=imp.t[:, :], in0=imp.t[:, :], in1=Aw.t[:, 256 - 2 * qb:512 - 2 * qb], op=ALU.add), reads=[Aw, imp], writes=[imp])
                kb.op("dve", lambda e: e.memset(imp.t[:, 0:1], BIGF), reads=[imp], writes=[imp])
                kb.op("dve", lambda e: e.max(out=m8.t[:, 0:8], in_=imp.t[:, :]), reads=[imp], writes=[m8])
                kb.op("dve", lambda e: e.match_replace(out=sc2.t[:, :], in_to_replace=m8.t[:, 0:8], in_values=imp.t[:, :], imm_value=2 * BIGI), reads=[imp, m8], writes=[sc2])
                kb.op("dve", lambda e: e.max(out=m8.t[:, 8:16], in_=sc2.t[:, :]), reads=[sc2], writes=[m8])
                kb.op("dve", lambda e: e.tensor_scalar(out=m8.t[:, 15:16], in0=m8.t[:, 15:16], scalar1=0.1 * BIGI, scalar2=None, op0=ALU.max), reads=[m8], writes=[m8])
                kb.op("dve", lambda e: e.tensor_scalar(out=nsel.t[:, :], in0=imp.t[:, :], scalar1=m8.t[:, 15:16], scalar2=-1.0, op0=ALU.is_ge, op1=ALU.add),
                      reads=[imp, m8], writes=[nsel])
                ptb = banks[3].t[:, :].bitcast(BF16)
                for ch in range(2):
                    kb.op("pe", lambda e, ch=ch: e.transpose(out=ptb[:, ch * 128:(ch + 1) * 128], in_=nsel.t[:, ch * 128:(ch + 1) * 128], identity=identb.t[:]),
                          reads=[nsel, identb], writes=[banks[3]])
                kb.op("act", lambda e: e.copy(out=negselT.t[:, :, sub], in_=ptb[:, 0:256].rearrange("p (c i) -> p c i", c=2)), reads=[banks[3]], writes=[negselT])

            items = [(j, h) for j in range(4) for h in range(4)]
            cmpA(items[0][0], items[0][1], 0)
            for n_, (j, h) in enumerate(items):
                if n_ + 1 < len(items):
                    cmpA(items[n_ + 1][0], items[n_ + 1][1], (n_ + 1) % 2)
                cmpB(j, h, n_ % 2)
                if h == 3:
                    select(j)
            its = []
            for br in range(2):
                kt_lo = 0 if br == 0 else max(0, 4 * qs - 4)
                kt_hi = 4 * qs + 3
                for h in range(4):
                    for kt in range(kt_lo, kt_hi + 1):
                        its.append((br, h, kt, kt == kt_lo, kt == kt_hi))
            sbanks = (banks[5], banks[6], banks[1])
            pobanks = (banks[7], banks[2])

            def selA(n_):
                br, h, kt, first, last = its[n_]
                ps = sbanks[n_ % 3]
                qq = qsel[h] if br == 0 else qwin[h]
                wide = wsel if br == 0 else wwin
                dl = 4 * qs - kt
                kb.mm(ps.t[:, :], kswT.t[:, kt * 128:(kt + 1) * 128], qq.t[:, :], True, br == 1, [kswT, qq], [ps])
                if br == 0:
                    kb.mm(ps.t[:, :], Exp_.t[:, kt % 64, :], negselT.t[:, kt // 64, :], False, True, [Exp_, negselT], [ps])
                off = 384 + 128 * (min(dl, SEL_FARD) if br == 0 else dl)
                it = n_ % 3
                kb.op("dve", lambda e: e.scalar_tensor_tensor(out=tmp[it].t[:, :], in0=ps.t[:, :], scalar=scale, in1=wide.t[:, h, off:off + 512],
                                                              op0=ALU.mult, op1=ALU.add), reads=[ps, wide], writes=[tmp[it]])
                kb.op("act", lambda e: e.activation(out=pT[it].t[:, :], in_=tmp[it].t[:, :], func=AF.Exp), reads=[tmp[it]], writes=[pT[it]])

            def selB(n_, grp):
                br, h, kt, first, last = its[n_]
                it = n_ % 3
                Vt = Vs if br == 0 else Vw
                po = pobanks[grp % 2]
                kb.mm(po.t[0:65, :], Vt.t[:, kt, 0:65], pT[it].t[:, :], first, last, [pT[it], Vt], [po])
                if not last:
                    return
                pS = poS[grp % 2]
                kb.op("act", lambda e: e.copy(out=pS.t[:, :], in_=po.t[0:65, :]), reads=[po], writes=[pS])
                pt = banks[4]
                for j in range(4):
                    kb.op("pe", lambda e, j=j: e.transpose(out=pt.t[:, j * 65:(j + 1) * 65], in_=pS.t[0:65, j * 128:(j + 1) * 128], identity=identf.t[0:65, 0:65]),
                          reads=[pS, identf], writes=[pt])
                for j in range(4):
                    gcol = (1 + br) * 4 + h
                    kb.op("dve", lambda e, j=j: e.reciprocal(out=fcol.t[:, j:j + 1], in_=pt.t[:, j * 65 + 64:j * 65 + 65]), reads=[pt], writes=[fcol])
                    kb.op("dve", lambda e, j=j, gcol=gcol: e.tensor_tensor(out=fcol.t[:, 4 + j:5 + j], in0=fcol.t[:, j:j + 1], in1=gates.t[:, j, gcol:gcol + 1], op=ALU.mult),
                          reads=[fcol, gates], writes=[fcol])
                    kb.op("dve", lambda e, j=j: e.scalar_tensor_tensor(out=ofin[j].t[:, h, :], in0=pt.t[:, j * 65:j * 65 + 64], scalar=fcol.t[:, 4 + j:5 + j], in1=ofin[j].t[:, h, :],
                                                                       op0=ALU.mult, op1=ALU.add), reads=[pt, fcol, ofin[j]], writes=[ofin[j]])

            grp = 0
            if NSA_STAGE == 7:
                its = []
                continue
            selA(0)
            if len(its) > 1:
                selA(1)
            for n_ in range(len(its)):
                if n_ + 2 < len(its):
                    selA(n_ + 2)
                selB(n_, grp)
                if its[n_][4]:
                    grp += 1
            for j in range(4):
                kb.op("act", lambda e, j=j: e.copy(out=ogb.t[:, :], in_=ofin[j].t[:].rearrange("p a b -> p (a b)")), reads=[ofin[j]], writes=[ogb])
                ptb = banks[3].t[:, :].bitcast(BF16)
                for c in range(2):
                    kb.op("pe", lambda e, c=c, ptb=ptb: e.transpose(out=ptb[:, c * 128:(c + 1) * 128], in_=ogb.t[:, c * 128:(c + 1) * 128], identity=identb.t[:]),
                          reads=[ogb, identb], writes=[banks[3]])
                kb.op("act", lambda e, ptb=ptb: e.copy(out=oTt.t[:].rearrange("p a b -> p (a b)"), in_=ptb[:, 0:256]), reads=[banks[3]], writes=[oTt])
                t0 = qs * 512 + j * 128
                kb.dma("pool", ov[:, :, t0:t0 + 128], oTt.t[:], reads=[oTt], writes=[oT])
        kb.finish([oT])
    return nc


def nsa_inputs(hT_b, w_in, pe, w1, b1, w2, tab, g):
    q = w_in[:, 0:1024].reshape(D, 4, 4, 64)[:, g]
    wqd = np.concatenate([q, q], axis=2).reshape(D, 512)
    blk = lambda i: w_in[:, 1024 + i * 256 + g * 64:1024 + i * 256 + (g + 1) * 64]
    wkv = np.concatenate([blk(0), blk(1), blk(2), blk(4), blk(3), blk(5)], axis=1)
    gates = w_in[:, 2560:2608].reshape(D, 3, 4, 4)[:, :, g, :].reshape(D, 12)
    tb = np.full((33, 4), NEG, np.float32)
    tb[:32] = tab[:, g * 4:(g + 1) * 4]
    ohs, ohw, ohc = nsa_onehots()
    return {"hT": hT_b, "wqd": np.ascontiguousarray(wqd), "wkv": np.ascontiguousarray(wkv), "wg": np.ascontiguousarray(gates),
            "pe": np.ascontiguousarray(pe), "w1": np.ascontiguousarray(w1), "b1": np.ascontiguousarray(b1), "w2": np.ascontiguousarray(w2),
            "tab": tb, "ohs": ohs, "ohw": ohw, "ohc": ohc}


_NC_CACHE = {}


def _get(name, fn, *a):
    key = (name,) + a
    if key not in _NC_CACHE:
        _NC_CACHE[key] = fn(*a)
    return _NC_CACHE[key]


def _run(nc, in_maps):
    res = run_bass_kernel_spmd(nc, in_maps, core_ids=list(range(NCORES)))
    return res.results


def kernel(x, c, rel_table, mod_w, mod_b, ln_g, ln_b,
           gla_w_in, gla_w_a2, gla_b_a, gla_norm_g, gla_w_o,
           nsa_w_in, nsa_cmp_pe, nsa_cmp_w1, nsa_cmp_b1, nsa_cmp_w2, nsa_w_o,
           dil_w_in, dil_w_o,
           ffn_w_up, ffn_conv_w, ffn_conv_b, ffn_w_down):
    f32 = lambda a: np.ascontiguousarray(np.asarray(a, dtype=np.float32))
    x = f32(x); c = f32(c); rel_table = f32(rel_table); mod_w = f32(mod_w); mod_b = f32(mod_b)
    ln_g = f32(ln_g); ln_b = f32(ln_b)
    S = x.shape[1]
    T = S // 4
    dbg = globals().get("_DBG")
    res = _run(_get("mod", build_mod), [{"c": c, "w": f32(mod_w[s // 2, s % 2]), "b": f32(mod_b[s // 2, s % 2][None])} for s in range(8)])
    mod = [r["out"] for r in res]

    def shards():
        for k in range(NCORES):
            yield k, k // 4, (k % 4) * T

    res = _run(_get("prep", build_prep, T), [{"x": f32(x[b, t0:t0 + T]), "vec": f32(np.stack([mod[0][b, 0:D], mod[0][b, D:2 * D]]))} for k, b, t0 in shards()])
    hT = np.zeros((NB, D, S), ml_dtypes.bfloat16)
    for (k, b, t0), r in zip(shards(), res):
        hT[b, :, t0:t0 + T] = r["hT"]
    xcur = x
    for i in range(DEPTH):
        kind, j = i % 3, i // 3
        ins = []
        for k in range(NCORES):
            b, g = k // 4, k % 4
            hb = np.ascontiguousarray(hT[b])
            if kind == 0:
                w_in = f32(gla_w_in[j])
                wa2 = np.zeros((33, 128), np.float32)
                wa2[:16] = f32(gla_w_a2[j])[:, g * 128:(g + 1) * 128]
                wa2[32] = f32(gla_b_a[j])[g * 128:(g + 1) * 128]
                ins.append({"hT": hb, "wq": f32(w_in[:, g * 128:(g + 1) * 128]), "wk": f32(w_in[:, 512 + g * 128:512 + (g + 1) * 128]),
                            "wv": f32(w_in[:, 1024 + g * 256:1024 + (g + 1) * 256]), "wr": f32(w_in[:, 2048 + g * 256:2048 + (g + 1) * 256]),
                            "wa": f32(w_in[:, 3072:3088]), "wa2": wa2, "ng": f32(gla_norm_g[j])[None]})
            elif kind == 1:
                ins.append(nsa_inputs(hb, f32(nsa_w_in[j]), f32(nsa_cmp_pe[j]), f32(nsa_cmp_w1[j]), f32(nsa_cmp_b1[j]), f32(nsa_cmp_w2[j]), rel_table, g))
            else:
                wgd = f32(dil_w_in[j]).reshape(D, 3, 3, 1024)
                wqkv = np.stack([np.concatenate([wgd[:, gg, cc, g * 256:(g + 1) * 256] for cc in range(3)], axis=1) for gg in range(3)])
                tb = np.full((33, 4), NEG, np.float32)
                tb[:32] = rel_table[:, g * 4:(g + 1) * 4]
                ins.append({"hT": hb, "wqkv": f32(wqkv), "tab": tb, "oh": dil_onehots()})
        ncm = _get(("gla", "nsa", "dil")[kind], (build_gla, build_nsa, build_dil)[kind], S)
        res = _run(ncm, ins)
        oT = np.zeros((NB, D, S), ml_dtypes.bfloat16)
        for k in range(NCORES):
            oT[k // 4, (k % 4) * 256:(k % 4 + 1) * 256, :] = res[k]["oT"]
        w_o = f32((gla_w_o, nsa_w_o, dil_w_o)[kind][j])
        ins = []
        for k, b, t0 in shards():
            xh = np.zeros((T + 128, D), np.float32)
            oh_ = np.zeros((D, T + 128), ml_dtypes.bfloat16)
            xh[128:] = xcur[b, t0:t0 + T]
            oh_[:, 128:] = oT[b, :, t0:t0 + T]
            if t0 > 0:
                xh[:128] = xcur[b, t0 - 128:t0]
                oh_[:, :128] = oT[b, :, t0 - 128:t0]
            vec = np.zeros((10, D), np.float32)
            vec[0] = mod[2 * i][b, 2 * D:3 * D]
            vec[1] = mod[2 * i + 1][b, 0:D]; vec[2] = mod[2 * i + 1][b, D:2 * D]; vec[3] = mod[2 * i + 1][b, 2 * D:3 * D]
            if i + 1 < DEPTH:
                vec[4] = mod[2 * i + 2][b, 0:D]; vec[5] = mod[2 * i + 2][b, D:2 * D]
            vec[6] = ln_g[i, 0]; vec[7] = ln_b[i, 0]; vec[8] = ln_g[i, 1]; vec[9] = ln_b[i, 1]
            ins.append({"x": xh, "oT": oh_, "wo": w_o, "wup": f32(ffn_w_up[i]), "wdn": f32(ffn_w_down[i]), "convw": f32(ffn_conv_w[i]),
                        "convb": f32(ffn_conv_b[i]), "vec": vec, "flag": np.full((128, 1), 0.0 if t0 == 0 else 1.0, np.float32)})
        res = _run(_get("post", build_post, T), ins)
        xn = np.zeros((NB, S, D), np.float32)
        for (k, b, t0), r in zip(shards(), res):
            xn[b, t0:t0 + T] = r["xo"]
            hT[b, :, t0:t0 + T] = r["hT"]
        xcur = xn
        if dbg is not None:
            dbg.append(xn.copy())
    return xcur
```

```python
import math
from contextlib import ExitStack
import numpy as np
import ml_dtypes
import concourse.bass as bass
import concourse.mybir as mybir
from concourse.bass_utils import run_bass_kernel_spmd

F32 = mybir.dt.float32
BF16 = mybir.dt.bfloat16
AF = mybir.ActivationFunctionType
ALU = mybir.AluOpType
AX = mybir.AxisListType

D = 1024
SEQ = 16384
NB = 2
DEPTH = 4
D_FF = 2816
NFC = D_FF // 128
DN_ALPHA = (2 * DEPTH) ** 0.25
LN_EPS = 1e-5
NEG = -30000.0
NCORES = 8


class Buf:
    __slots__ = ("w", "r", "t", "parts")

    def __init__(self, t=None, nparts=0):
        self.w = None
        self.r = {}
        self.t = t
        self.parts = [Buf(t) for _ in range(nparts)]

    def __getitem__(self, k):
        return self.t[k]


class Eng:
    def __init__(self, name, h, sem):
        self.name, self.h, self.sem = name, h, sem
        self.count = 0
        self.waited = {}


class KB:
    def __init__(self, nc, es, ndma=40):
        self.nc, self.es = nc, es
        self.eng = {}
        for name, h in (("pe", nc.tensor), ("act", nc.scalar), ("dve", nc.vector), ("pool", nc.gpsimd), ("sp", nc.sync)):
            self.eng[name] = Eng(name, h, es.enter_context(nc.semaphore("sem_" + name)))
        self.dsl = [[es.enter_context(nc.semaphore("dsem%d" % i)), 0] for i in range(ndma)]
        self.di = {"sp": 0, "pool": 0, "act": 0}
        self.dq = {"sp": list(range(0, ndma // 2)), "pool": list(range(ndma // 2, ndma - 4)), "act": list(range(ndma - 4, ndma))}
        self.nt = 0

    def sb(self, shape, dt=F32, name=None, nparts=0, es=None):
        self.nt += 1
        return Buf((es or self.es).enter_context(self.nc.sbuf_tensor(name or "t%d" % self.nt, list(shape), dt)), nparts)

    def ps(self, shape=(128, 512), dt=F32, name=None):
        self.nt += 1
        return Buf(self.es.enter_context(self.nc.psum_tensor(name or "p%d" % self.nt, list(shape), dt)))

    def dram(self, name, shape, dt, kind="Internal"):
        return Buf(self.nc.dram_tensor(name, list(shape), dt, kind=kind).ap())

    def _semof(self, key):
        if isinstance(key, str):
            return self.eng[key].sem
        return self.dsl[key][0]

    def _wait(self, E, deps):
        need = {}
        for key, val in deps:
            if key == E.name and key in ("pe", "sp"):
                continue
            if val > need.get(key, 0):
                need[key] = val
        for key, val in need.items():
            if E.waited.get(key, 0) >= val:
                continue
            E.h.wait_ge(self._semof(key), val)
            E.waited[key] = val

    @staticmethod
    def _deps(reads, writes):
        deps = []
        for b in reads:
            if b.w is not None:
                deps.append(b.w)
        for b in writes:
            if b.w is not None:
                deps.append(b.w)
            deps.extend(b.r.items())
        return deps

    @staticmethod
    def _record(ev, reads, writes):
        key, val = ev
        for b in reads:
            if b.r.get(key, 0) < val:
                b.r[key] = val
        for b in writes:
            b.w = ev
            b.r = {}

    def op(self, en, fn, reads=(), writes=()):
        E = self.eng[en]
        self._wait(E, self._deps(reads, writes))
        ins = fn(E.h)
        E.count += 1
        ins.then_inc(E.sem, 1)
        self._record((en, E.count), reads, writes)

    def dma(self, qn, out, in_, reads=(), writes=(), **kw):
        Q = self.eng[qn]
        k = self.dq[qn][self.di[qn] % len(self.dq[qn])]
        self.di[qn] += 1
        slot = self.dsl[k]
        deps = self._deps(reads, writes)
        if slot[1] > 0:
            deps.append((k, slot[1]))
        self._wait(Q, deps)
        Q.h.dma_start(out=out, in_=in_, **kw).then_inc(slot[0], 16)
        slot[1] += 16
        self._record((k, slot[1]), reads, writes)

    def barrier(self):
        deps = [(n, E.count) for n, E in self.eng.items() if E.count > 0]
        deps += [(k, sl[1]) for k, sl in enumerate(self.dsl) if sl[1] > 0]
        for E in self.eng.values():
            self._wait(E, [d for d in deps if d[0] != E.name])

    def finish(self, outs):
        E = self.eng["sp"]
        self._wait(E, [b.w for b in outs if b.w is not None])

    def mm(self, out, lhsT, rhs, start, stop, reads, writes):
        self.op("pe", lambda e: e.matmul(out, lhsT=lhsT, rhs=rhs, start=start, stop=stop), reads, writes)


def new_nc():
    return bass.Bass("TRN2", target_bir_lowering=False)


def dr_in(nc, name, shape, dt=F32):
    return Buf(nc.dram_tensor(name, list(shape), dt, kind="ExternalInput").ap())


def dr_out(nc, name, shape, dt=F32):
    return Buf(nc.dram_tensor(name, list(shape), dt, kind="ExternalOutput").ap())


def load_bcast(kb, q, dst, src_row_ap, n=128):
    kb.dma(q, dst.t[0:n, :], src_row_ap.partition_broadcast(n), writes=[dst])


def make_ident(kb, dt=BF16):
    idf = kb.sb([128, 128], F32)
    kb.op("pool", lambda e: e.memset(idf[:], 0.0), writes=[idf])
    kb.op("pool", lambda e: e.affine_select(out=idf[:], in_=idf[:], pattern=[[-1, 128]], compare_op=ALU.not_equal,
                                             fill=1.0, base=0, channel_multiplier=1), reads=[idf], writes=[idf])
    if dt == F32:
        return idf
    idb = kb.sb([128, 128], dt)
    kb.op("dve", lambda e: e.tensor_copy(out=idb[:], in_=idf[:]), reads=[idf], writes=[idb])
    return idb


def gen_load_w_bf16(kb, dst, src_ap_fn, nk, ncols, stage, qs=("sp", "pool"), chunk=512, npart=128):
    i = 0
    for k in range(nk):
        for c0 in range(0, ncols, chunk):
            c1 = min(ncols, c0 + chunk)
            st = stage[i % len(stage)]
            kb.dma(qs[i % len(qs)], st.t[0:npart, 0:c1 - c0], src_ap_fn(k, c0, c1), writes=[st])
            if i % 2:
                kb.op("act", lambda e, st=st, k=k, c0=c0, c1=c1: e.copy(out=dst.t[0:npart, k, c0:c1], in_=st.t[0:npart, 0:c1 - c0]),
                      reads=[st], writes=[dst.parts[k]])
            else:
                kb.op("dve", lambda e, st=st, k=k, c0=c0, c1=c1: e.tensor_copy(out=dst.t[0:npart, k, c0:c1], in_=st.t[0:npart, 0:c1 - c0]),
                      reads=[st], writes=[dst.parts[k]])
            i += 1
            yield


def load_w_bf16(kb, dst, src_ap_fn, nk, ncols, stage, qs=("sp", "pool"), chunk=512, npart=128):
    i = 0
    for k in range(nk):
        for c0 in range(0, ncols, chunk):
            c1 = min(ncols, c0 + chunk)
            st = stage[i % len(stage)]
            kb.dma(qs[i % len(qs)], st.t[0:npart, 0:c1 - c0], src_ap_fn(k, c0, c1), writes=[st])
            en = "act" if i % 2 else "dve"
            if en == "act":
                kb.op("act", lambda e, st=st, k=k, c0=c0, c1=c1: e.copy(out=dst.t[0:npart, k, c0:c1], in_=st.t[0:npart, 0:c1 - c0]),
                      reads=[st], writes=[dst.parts[k]])
            else:
                kb.op("dve", lambda e, st=st, k=k, c0=c0, c1=c1: e.tensor_copy(out=dst.t[0:npart, k, c0:c1], in_=st.t[0:npart, 0:c1 - c0]),
                      reads=[st], writes=[dst.parts[k]])
            i += 1


def emit_ln(kb, z, st, mv, lng, lnb):
    for c in range(2):
        kb.op("dve", lambda e, c=c: e.bn_stats(out=st.t[:, c, :], in_=z.t[:, c * 512:(c + 1) * 512]), reads=[z], writes=[st])
    kb.op("dve", lambda e: e.bn_aggr(out=mv.t[:, 0:2], in_=st.t[:].rearrange("p a b -> p (a b)")), reads=[st], writes=[mv])
    kb.op("dve", lambda e: e.tensor_scalar_add(out=mv.t[:, 2:3], in0=mv.t[:, 1:2], scalar1=LN_EPS), reads=[mv], writes=[mv])
    kb.op("act", lambda e: e.sqrt(out=mv.t[:, 2:3], in_=mv.t[:, 2:3]), reads=[mv], writes=[mv])
    kb.op("dve", lambda e: e.reciprocal(out=mv.t[:, 2:3], in_=mv.t[:, 2:3]), reads=[mv], writes=[mv])
    kb.op("dve", lambda e: e.tensor_scalar(out=z.t[:], in0=z.t[:], scalar1=mv.t[:, 0:1], scalar2=mv.t[:, 2:3],
                                           op0=ALU.subtract, op1=ALU.mult), reads=[z, mv], writes=[z])
    kb.op("pool", lambda e: e.tensor_tensor(out=z.t[:], in0=z.t[:], in1=lng.t[:], op=ALU.mult), reads=[z, lng], writes=[z])
    kb.op("pool", lambda e: e.tensor_tensor(out=z.t[:], in0=z.t[:], in1=lnb.t[:], op=ALU.add), reads=[z, lnb], writes=[z])


def emit_modT(kb, x, pst, identf, scp, sh, ht, ncol=128, c0=0):
    for k in range(8):
        kb.op("pe", lambda e, k=k: e.transpose(out=pst.t[:, k * 128:(k + 1) * 128], in_=x.t[:, k * 128:(k + 1) * 128],
                                               identity=identf.t[:]), reads=[x, identf], writes=[pst])
    for k in range(8):
        kb.op("act", lambda e, k=k: e.activation(out=ht.t[:, k, c0:c0 + 128], in_=pst.t[:, k * 128:(k + 1) * 128],
                                                 func=AF.Identity, scale=scp.t[:, k:k + 1], bias=sh.t[:, k:k + 1]),
              reads=[pst, scp, sh], writes=[ht])


def load_pp(kb, q, dst, row_ap, nk=8):
    kb.dma(q, dst.t[:, 0:nk], row_ap.rearrange("(k p) -> p k", p=128), writes=[dst], allow_slow_non_contiguous=True)


def build_prep(T):
    nc = new_nc()
    x = dr_in(nc, "x", [T, D])
    vec = dr_in(nc, "vec", [2, D])
    hTo = dr_out(nc, "hT", [D, T], BF16)
    with ExitStack() as es:
        kb = KB(nc, es)
        identf = make_ident(kb, F32)
        sh = kb.sb([128, 8]); scp = kb.sb([128, 8])
        load_pp(kb, "sp", sh, vec.t[0, :]); load_pp(kb, "sp", scp, vec.t[1, :])
        kb.op("dve", lambda e: e.tensor_scalar_add(out=scp.t[:], in0=scp.t[:], scalar1=1.0), reads=[scp], writes=[scp])
        xs = [kb.sb([128, D]) for _ in range(3)]
        hts = [kb.sb([128, 8, 128], BF16) for _ in range(2)]
        psts = [kb.ps([128, 1024]) for _ in range(2)]
        hv = hTo.t.rearrange("(k p) t -> p k t", p=128)
        for i in range(T // 128):
            xt = xs[i % 3]; ht = hts[i % 2]; pst = psts[i % 2]
            kb.dma("sp", xt.t[:], x.t[i * 128:(i + 1) * 128, :], writes=[xt])
            emit_modT(kb, xt, pst, identf, scp, sh, ht)
            kb.dma("pool", hv[:, :, i * 128:(i + 1) * 128], ht.t[:], reads=[ht], writes=[hTo])
        kb.finish([hTo])
    return nc


def build_post(T):
    TT = 256
    nc = new_nc()
    x = dr_in(nc, "x", [T + 128, D])
    oT = dr_in(nc, "oT", [D, T + 128], BF16)
    wo = dr_in(nc, "wo", [D, D])
    wup = dr_in(nc, "wup", [D, 2 * D_FF])
    wdn = dr_in(nc, "wdn", [D_FF, D])
    convw = dr_in(nc, "convw", [3, D_FF])
    convb = dr_in(nc, "convb", [D_FF])
    vec = dr_in(nc, "vec", [10, D])
    flag = dr_in(nc, "flag", [128, 1])
    xo = dr_out(nc, "xo", [T, D])
    hTo = dr_out(nc, "hT", [D, T], BF16)
    x1s = Buf(nc.dram_tensor("x1s", [T, D], F32, kind="Internal").ap())
    h1s = Buf(nc.dram_tensor("h1s", [D, T], BF16, kind="Internal").ap())
    ntile = T // 128
    with ExitStack() as es:
        kb = KB(nc, es)
        identf = make_ident(kb, F32)
        banks = [kb.ps([128, 512]) for _ in range(4)]
        pst2 = [kb.ps([128, 1024]) for _ in range(2)]
        h1halo = kb.sb([128, 8, 128], BF16)
        pp = kb.sb([128, 6, 8])
        ppb = [Buf(pp.t) for _ in range(4)]
        for j, r in enumerate((1, 2, 4, 5)):
            kb.dma("sp", pp.t[:, j, :], vec.t[r, :].rearrange("(k p) -> p k", p=128), writes=[ppb[j]], allow_slow_non_contiguous=True)
        for j in (1, 3):
            kb.op("dve", lambda e, j=j: e.tensor_scalar_add(out=pp.t[:, j, :], in0=pp.t[:, j, :], scalar1=1.0), reads=[ppb[j]], writes=[ppb[j]])

        class PV:
            pass
        sh2 = Buf(pp.t[:, 0, :]); sc2 = Buf(pp.t[:, 1, :]); shn = Buf(pp.t[:, 2, :]); scn = Buf(pp.t[:, 3, :])
        for v, b in ((sh2, ppb[0]), (sc2, ppb[1]), (shn, ppb[2]), (scn, ppb[3])):
            v.w = b.w
        flg = kb.sb([128, 1])
        kb.dma("sp", flg.t[:], flag.t[:, :], writes=[flg])
        st = kb.sb([128, 2, 6]); mv = kb.sb([128, 4])
        zs = [kb.sb([128, D]) for _ in range(2)]
        xs = [kb.sb([128, D]) for _ in range(2)]
        hts = [kb.sb([128, 8, 128], BF16) for _ in range(2)]
        g1p = kb.sb([128, D]); lng = kb.sb([128, D]); lnb = kb.sb([128, D])

        def load_gl(gr, lgr, lbr):
            load_bcast(kb, "sp", g1p, vec.t[gr:gr + 1, :])
            load_bcast(kb, "sp", lng, vec.t[lgr:lgr + 1, :])
            load_bcast(kb, "sp", lnb, vec.t[lbr:lbr + 1, :])
            kb.op("pool", lambda e: e.tensor_scalar_add(out=g1p.t[:], in0=g1p.t[:], scalar1=1.0), reads=[g1p], writes=[g1p])

        load_gl(0, 6, 7)
        h1v = h1s.t.rearrange("(k p) t -> p k t", p=128)
        hov = hTo.t.rearrange("(k p) t -> p k t", p=128)
        oTv = oT.t.rearrange("(k p) t -> p k t", p=128)

        def resid_ln(psy, xt, z):
            for half in range(2):
                kb.op("dve", lambda e, half=half: e.tensor_tensor(out=z.t[:, half * 512:(half + 1) * 512], in0=psy[half].t[:, :],
                                                                   in1=g1p.t[:, half * 512:(half + 1) * 512], op=ALU.mult),
                      reads=[psy[half], g1p], writes=[z])
            kb.op("dve", lambda e: e.scalar_tensor_tensor(out=z.t[:], in0=xt.t[:], scalar=DN_ALPHA, in1=z.t[:],
                                                           op0=ALU.mult, op1=ALU.add), reads=[xt, z], writes=[z])
            emit_ln(kb, z, st, mv, lng, lnb)

        wub = kb.sb([128, 8, 2 * D_FF], BF16, nparts=8)
        wdb = kb.sb([128, NFC, D], BF16, nparts=NFC)
        stageB = [kb.sb([128, 512]) for _ in range(6)]

        def _wgen():
            yield from gen_load_w_bf16(kb, wub, lambda k, c0, c1: wup.t[k * 128:(k + 1) * 128, c0:c1], 8, 2 * D_FF, stageB)
            yield from gen_load_w_bf16(kb, wdb, lambda k, c0, c1: wdn.t[k * 128:(k + 1) * 128, c0:c1], NFC, D, stageB)
        wgen = _wgen()
        nchunks_w = 8 * 11 + NFC * 2
        per_tile = -(-nchunks_w // (ntile + 1))
        with ExitStack() as esA:
            wob = kb.sb([128, 8, D], BF16, nparts=8, es=esA)
            stage = [kb.sb([128, 512], es=esA) for _ in range(2)]
            ots = [kb.sb([128, 8, 128], BF16, es=esA) for _ in range(2)]
            load_w_bf16(kb, wob, lambda k, c0, c1: wo.t[k * 128:(k + 1) * 128, c0:c1], 8, D, stage)
            for i in range(ntile + 1):
                xt = xs[i % 2]; z = zs[i % 2]; ot = ots[i % 2]; ht = hts[i % 2]
                psy = banks[2 * (i % 2):2 * (i % 2) + 2]
                kb.dma("sp", xt.t[:], x.t[i * 128:(i + 1) * 128, :], writes=[xt])
                kb.dma("pool", ot.t[:], oTv[:, :, i * 128:(i + 1) * 128], writes=[ot])
                for half in range(2):
                    for k in range(8):
                        kb.mm(psy[half].t[:, :], ot.t[:, k, :], wob.t[:, k, half * 512:(half + 1) * 512], k == 0, k == 7,
                              [ot, wob.parts[k]], [psy[half]])
                resid_ln(psy, xt, z)
                if i > 0:
                    kb.dma("sp", x1s.t[(i - 1) * 128:i * 128, :], z.t[:], reads=[z], writes=[x1s])
                emit_modT(kb, z, pst2[i % 2], identf, sc2, sh2, h1halo if i == 0 else ht)
                if i > 0:
                    kb.dma("pool", h1v[:, :, (i - 1) * 128:i * 128], ht.t[:], reads=[ht], writes=[h1s])
                for _ in range(per_tile):
                    next(wgen, None)
            for _ in wgen:
                pass

        kb.barrier()
        load_gl(3, 8, 9)
        with ExitStack() as esB:
            cw = kb.sb([128, 3, NFC], es=esB); cb = kb.sb([128, NFC], es=esB)
            for j in range(3):
                kb.dma("sp", cw.t[:, j, :], convw.t[j, :].rearrange("(k p) -> p k", p=128), writes=[cw], allow_slow_non_contiguous=True)
            kb.dma("sp", cb.t[:, :], convb.t[:].rearrange("(k p) -> p k", p=128), writes=[cb], allow_slow_non_contiguous=True)
            uprev = kb.sb([128, NFC, 2], nparts=NFC, es=esB)
            h1t = [kb.sb([128, 8, TT], BF16, es=esB) for _ in range(2)]
            aT = kb.sb([128, NFC, TT], BF16, nparts=NFC, es=esB)
            ubs = [kb.sb([128, TT + 2], es=esB) for _ in range(2)]
            cbs = [kb.sb([128, TT], es=esB) for _ in range(2)]
            sbs = [kb.sb([128, TT], es=esB) for _ in range(2)]
            pu = banks[0]
            for fc in range(NFC):
                for k in range(8):
                    kb.mm(pu.t[:, fc * 2:fc * 2 + 2], wub.t[:, k, fc * 128:(fc + 1) * 128], h1halo.t[:, k, 126:128], k == 0, k == 7,
                          [wub.parts[k], h1halo], [pu])
            kb.op("dve", lambda e: e.tensor_scalar_mul(out=uprev.t[:].rearrange("p a b -> p (a b)"), in0=pu.t[:, 0:2 * NFC],
                                                       scalar1=flg.t[:, 0:1]), reads=[pu, flg], writes=uprev.parts)
            nug = 0
            for it in range(T // TT):
                hh = h1t[it % 2]
                kb.dma("sp", hh.t[:], h1v[:, :, it * TT:(it + 1) * TT], reads=[h1s], writes=[hh])
                for fc in range(NFC):
                    pug = banks[nug % 2]; ub = ubs[nug % 2]; cbuf = cbs[nug % 2]; sbuf = sbs[nug % 2]; nug += 1
                    for k in range(8):
                        kb.mm(pug.t[:, 0:TT], wub.t[:, k, fc * 128:(fc + 1) * 128], hh.t[:, k, :], k == 0, k == 7, [wub.parts[k], hh], [pug])
                    for k in range(8):
                        kb.mm(pug.t[:, TT:2 * TT], wub.t[:, k, D_FF + fc * 128:D_FF + (fc + 1) * 128], hh.t[:, k, :], k == 0, k == 7,
                              [wub.parts[k], hh], [pug])
                    kb.op("pool", lambda e, ub=ub, fc=fc: e.tensor_copy(out=ub.t[:, 0:2], in_=uprev.t[:, fc, :]), reads=[uprev.parts[fc]], writes=[ub])
                    kb.op("act", lambda e, ub=ub, pug=pug: e.copy(out=ub.t[:, 2:TT + 2], in_=pug.t[:, 0:TT]), reads=[pug], writes=[ub])
                    kb.op("pool", lambda e, ub=ub, fc=fc: e.tensor_copy(out=uprev.t[:, fc, :], in_=ub.t[:, TT:TT + 2]), reads=[ub], writes=[uprev.parts[fc]])
                    kb.op("act", lambda e, ub=ub, cbuf=cbuf, fc=fc: e.activation(out=cbuf.t[:], in_=ub.t[:, 2:TT + 2], func=AF.Identity,
                                                                                 scale=cw.t[:, 2, fc:fc + 1], bias=cb.t[:, fc:fc + 1]),
                          reads=[ub, cw, cb], writes=[cbuf])
                    kb.op("dve", lambda e, ub=ub, cbuf=cbuf, fc=fc: e.scalar_tensor_tensor(out=cbuf.t[:], in0=ub.t[:, 1:TT + 1], scalar=cw.t[:, 1, fc:fc + 1],
                                                                                           in1=cbuf.t[:], op0=ALU.mult, op1=ALU.add),
                          reads=[ub, cw, cbuf], writes=[cbuf])
                    kb.op("dve", lambda e, ub=ub, cbuf=cbuf, fc=fc: e.scalar_tensor_tensor(out=cbuf.t[:], in0=ub.t[:, 0:TT], scalar=cw.t[:, 0, fc:fc + 1],
                                                                                           in1=cbuf.t[:], op0=ALU.mult, op1=ALU.add),
                          reads=[ub, cw, cbuf], writes=[cbuf])
                    kb.op("act", lambda e, cbuf=cbuf, sbuf=sbuf: e.activation(out=sbuf.t[:], in_=cbuf.t[:], func=AF.Silu), reads=[cbuf], writes=[sbuf])
                    kb.op("dve", lambda e, sbuf=sbuf, pug=pug, fc=fc: e.tensor_tensor(out=aT.t[:, fc, :], in0=pug.t[:, TT:2 * TT], in1=sbuf.t[:], op=ALU.mult),
                          reads=[pug, sbuf], writes=[aT.parts[fc]])
                for sub in range(TT // 128):
                    i = it * (TT // 128) + sub
                    psy = banks[2:4]
                    xt = xs[i % 2]; z = zs[i % 2]; ht = hts[i % 2]
                    kb.dma("sp", xt.t[:], x1s.t[i * 128:(i + 1) * 128, :], reads=[x1s], writes=[xt])
                    for half in range(2):
                        for fc in range(NFC):
                            kb.mm(psy[half].t[:, :], aT.t[:, fc, sub * 128:(sub + 1) * 128], wdb.t[:, fc, half * 512:(half + 1) * 512],
                                  fc == 0, fc == NFC - 1, [aT.parts[fc], wdb.parts[fc]], [psy[half]])
                    resid_ln(psy, xt, z)
                    kb.dma("sp", xo.t[i * 128:(i + 1) * 128, :], z.t[:], reads=[z], writes=[xo])
                    emit_modT(kb, z, pst2[i % 2], identf, scn, shn, ht)
                    kb.dma("pool", hov[:, :, i * 128:(i + 1) * 128], ht.t[:], reads=[ht], writes=[hTo])
        kb.finish([xo, hTo])
    return nc


def build_mod():
    nc = new_nc()
    c = dr_in(nc, "c", [NB, D])
    w = dr_in(nc, "w", [D, 3 * D])
    b = dr_in(nc, "b", [1, 3 * D])
    out = dr_out(nc, "out", [NB, 3 * D])
    with ExitStack() as es:
        kb = KB(nc, es)
        cs = kb.sb([128, 8, NB])
        for bb in range(NB):
            kb.dma("sp", cs.t[:, :, bb], c.t[bb, :].rearrange("(k p) -> p k", p=128), writes=[cs], allow_slow_non_contiguous=True)
        kb.op("act", lambda e: e.activation(out=cs.t[:], in_=cs.t[:], func=AF.Silu), reads=[cs], writes=[cs])
        wt = kb.sb([128, 8, 3 * D], nparts=8)
        for k in range(8):
            kb.dma("sp" if k % 2 else "pool", wt.t[:, k, :], w.t[k * 128:(k + 1) * 128, :], writes=[wt.parts[k]])
        bt = kb.sb([NB, 3 * D])
        load_bcast(kb, "sp", bt, b.t[0:1, :], n=NB)
        ot = kb.sb([NB, 3 * D])
        banks = [kb.ps([128, 512]) for _ in range(6)]
        for n in range(6):
            for k in range(8):
                kb.mm(banks[n].t[0:NB, :], cs.t[:, k, :], wt.t[:, k, n * 512:(n + 1) * 512], k == 0, k == 7, [cs, wt.parts[k]], [banks[n]])
            kb.op("dve", lambda e, n=n: e.tensor_tensor(out=ot.t[:, n * 512:(n + 1) * 512], in0=banks[n].t[0:NB, :],
                                                         in1=bt.t[:, n * 512:(n + 1) * 512], op=ALU.add), reads=[banks[n], bt], writes=[ot])
        kb.dma("sp", out.t[:, :], ot.t[:], reads=[ot], writes=[out])
        kb.finish([out])
    return nc


GLA_DK, GLA_DV = 128, 256


def tri_const(kb, val, upper_strict):
    m = kb.sb([128, 128])
    kb.op("pool", lambda e: e.memset(m.t[:], val), writes=[m])
    if not upper_strict:
        kb.op("pool", lambda e: e.affine_select(out=m.t[:], in_=m.t[:], pattern=[[1, 128]], compare_op=ALU.is_ge, fill=0.0,
                                                 base=0, channel_multiplier=-1), reads=[m], writes=[m])
        kb.op("pool", lambda e: e.memset(m.t[0:64, 64:128], 0.0), reads=[m], writes=[m])
    else:
        kb.op("pool", lambda e: e.affine_select(out=m.t[:], in_=m.t[:], pattern=[[-1, 128]], compare_op=ALU.is_ge, fill=0.0,
                                                 base=-1, channel_multiplier=1), reads=[m], writes=[m])
        kb.op("pool", lambda e: e.memset(m.t[64:128, 0:64], 0.0), reads=[m], writes=[m])
    return m


GLA_STAGE = 7
GLA_SUB = 99


def build_gla(S):
    nc = new_nc()
    hT = dr_in(nc, "hT", [D, S], BF16)
    wq = dr_in(nc, "wq", [D, 128]); wk = dr_in(nc, "wk", [D, 128])
    wv = dr_in(nc, "wv", [D, 256]); wr = dr_in(nc, "wr", [D, 256]); wa = dr_in(nc, "wa", [D, 16])
    wa2 = dr_in(nc, "wa2", [33, 128])
    ng = dr_in(nc, "ng", [1, 256])
    oT = dr_out(nc, "oT", [256, S], BF16)
    with ExitStack() as es:
        kb = KB(nc, es)
        identb = make_ident(kb, BF16)
        Lneg = tri_const(kb, -1.0 / 16.0, False)
        Uneg = tri_const(kb, -1.0 / 16.0, True)
        M1 = tri_const(kb, 1.0, False)
        stage = [kb.sb([128, 512]) for _ in range(2)]
        wqb = kb.sb([128, 8, 128], BF16, nparts=8); wkb = kb.sb([128, 8, 128], BF16, nparts=8)
        wvb = kb.sb([128, 8, 256], BF16, nparts=8); wrb = kb.sb([128, 8, 256], BF16, nparts=8)
        wab = kb.sb([128, 8, 16], BF16, nparts=8)
        for dst, src, n in ((wqb, wq, 128), (wkb, wk, 128), (wvb, wv, 256), (wrb, wr, 256), (wab, wa, 16)):
            load_w_bf16(kb, dst, lambda k, c0, c1, src=src: src.t[k * 128:(k + 1) * 128, c0:c1], 8, n, stage)
        wa2b = kb.sb([33, 1, 128], BF16, nparts=1)
        load_w_bf16(kb, wa2b, lambda k, c0, c1: wa2.t[0:33, c0:c1], 1, 128, stage, npart=33)
        ngb = kb.sb([128, 256])
        load_bcast(kb, "sp", ngb, ng.t[0:1, :])
        alT = kb.sb([33, 512], BF16)
        kb.op("pool", lambda e: e.memset(alT.t[:], 0.0), writes=[alT])
        kb.op("pool", lambda e: e.memset(alT.t[32:33, :], 1.0), reads=[alT], writes=[alT])
        S32 = kb.sb([128, 256]); SbA = kb.sb([128, 256], BF16); SbB = kb.sb([128, 256], BF16)
        kb.op("pool", lambda e: e.memset(S32.t[:], 0.0), writes=[S32])
        kb.op("pool", lambda e: e.memset(SbA.t[:], 0.0), writes=[SbA])
        pq = kb.ps(); pk = kb.ps(); pal = kb.ps(); pD = kb.ps(); pE = kb.ps(); pF = kb.ps(); pG = kb.ps(); pH = kb.ps()
        pDk = pDl = pDt = pD
        pEv = pEr = pE
        pFb = pFr = pFa = pF
        pGa = pGb = pG
        pH0 = pH1 = pH
        pT = pD.t[:, 256:512].bitcast(BF16)
        hhs = [kb.sb([128, 8, 512], BF16) for _ in range(2)]
        R = 2
        ex = [kb.sb([128, 128]) for _ in range(R)]; ltm = [kb.sb([128, 128]) for _ in range(R)]
        ebT = [kb.sb([128, 128]) for _ in range(R)]; einvT = [kb.sb([128, 128]) for _ in range(R)]
        erem = [kb.sb([128, 128]) for _ in range(R)]
        qdT = [kb.sb([128, 128], BF16) for _ in range(R)]; kiT = [kb.sb([128, 128], BF16) for _ in range(R)]
        kdec = [kb.sb([128, 128], BF16) for _ in range(R)]; vsb = [kb.sb([128, 256], BF16) for _ in range(R)]
        kdec1 = [kb.sb([128, 128], BF16) for _ in range(R)]
        hm01 = kb.sb([128, 2])
        kb.op("pool", lambda e: e.memset(hm01.t[:], 0.0), writes=[hm01])
        kb.op("pool", lambda e: e.memset(hm01.t[0:64, 0:1], 1.0), reads=[hm01], writes=[hm01])
        kb.op("pool", lambda e: e.memset(hm01.t[64:128, 1:2], 1.0), reads=[hm01], writes=[hm01])
        sr = [kb.sb([128, 256]) for _ in range(R)]; gr = [kb.sb([128, 256]) for _ in range(R)]
        attm = [kb.sb([128, 128], BF16) for _ in range(R)]
        oint = [kb.sb([128, 256]) for _ in range(R)]; osb = [kb.sb([128, 256]) for _ in range(R)]
        junk = kb.sb([128, 256]); ss = [kb.sb([128, 2]) for _ in range(R)]
        og = [kb.sb([128, 256], BF16) for _ in range(R)]; oTt = [kb.sb([128, 2, 128], BF16) for _ in range(R)]
        hv = hT.t.rearrange("(k p) t -> p k t", p=128)
        ov = oT.t.rearrange("(c p) t -> p c t", p=128)
        scale = GLA_DK ** -0.5
        n = 0
        def superA(st):
            hh = hhs[st % 2]
            kb.dma("sp", hh.t[:], hv[:, :, st * 512:(st + 1) * 512], writes=[hh])
            for k in range(8):
                kb.mm(pq.t[:, :], wqb.t[:, k, :], hh.t[:, k, :], k == 0, k == 7, [wqb.parts[k], hh], [pq])
            for k in range(8):
                kb.mm(pk.t[:, :], wkb.t[:, k, :], hh.t[:, k, :], k == 0, k == 7, [wkb.parts[k], hh], [pk])
            for k in range(8):
                kb.mm(pal.t[0:16, :], wab.t[:, k, :], hh.t[:, k, :], k == 0, k == 7, [wab.parts[k], hh], [pal])
            kb.op("act", lambda e: e.copy(out=alT.t[0:16, :], in_=pal.t[0:16, :]), reads=[pal], writes=[alT])

        def tileA(st, j, r):
            hh = hhs[st % 2]
            sub = slice(j * 128, (j + 1) * 128)
            for k in range(8):
                kb.mm(pD.t[:, 0:128], hh.t[:, k, sub], wkb.t[:, k, :], k == 0, k == 7, [wkb.parts[k], hh], [pDk])
            kb.mm(pD.t[:, 128:256], alT.t[0:33, sub], wa2b.t[0:33, 0, :], True, True, [alT, wa2b.parts[0]], [pDl])
            for k in range(8):
                kb.mm(pE.t[:, 0:256], hh.t[:, k, sub], wvb.t[:, k, :], k == 0, k == 7, [wvb.parts[k], hh], [pEv])
            for k in range(8):
                kb.mm(pE.t[:, 256:512], hh.t[:, k, sub], wrb.t[:, k, :], k == 0, k == 7, [wrb.parts[k], hh], [pEr])
            if GLA_STAGE >= 2:
                kb.op("act", lambda e, r=r: e.activation(out=ex[r].t[:], in_=pD.t[:, 128:256], func=AF.Exp, scale=-1.0), reads=[pDl], writes=[ex[r]])
                kb.op("act", lambda e, r=r: e.activation(out=ltm[r].t[:], in_=ex[r].t[:], func=AF.Ln, bias=1.0), reads=[ex[r]], writes=[ltm[r]])
            if GLA_STAGE >= 3:
                kb.mm(pF.t[:, 0:128], ltm[r].t[:], Lneg.t[:], True, True, [ltm[r], Lneg], [pFb])
                kb.mm(pF.t[:, 128:256], Uneg.t[:], ltm[r].t[:], True, True, [ltm[r], Uneg], [pFr])
                kb.op("act", lambda e, r=r: e.activation(out=ebT[r].t[:], in_=pF.t[:, 0:128], func=AF.Exp), reads=[pFb], writes=[ebT[r]])
                kb.op("act", lambda e, r=r: e.activation(out=einvT[r].t[:], in_=pF.t[:, 0:128], func=AF.Exp, scale=-1.0), reads=[pFb], writes=[einvT[r]])
                kb.op("act", lambda e, r=r: e.activation(out=erem[r].t[:], in_=pF.t[:, 128:256], func=AF.Exp), reads=[pFr], writes=[erem[r]])
                kb.op("dve", lambda e, r=r, sub=sub: e.scalar_tensor_tensor(out=qdT[r].t[:], in0=pq.t[:, sub], scalar=scale, in1=ebT[r].t[:],
                                                                            op0=ALU.mult, op1=ALU.mult), reads=[pq, ebT[r]], writes=[qdT[r]])
                kb.op("dve", lambda e, r=r, sub=sub: e.tensor_tensor(out=kiT[r].t[:], in0=pk.t[:, sub], in1=einvT[r].t[:], op=ALU.mult),
                      reads=[pk, einvT[r]], writes=[kiT[r]])
                kb.op("dve", lambda e, r=r: e.scalar_tensor_tensor(out=kdec[r].t[:], in0=pD.t[:, 0:128], scalar=hm01.t[:, 0:1], in1=erem[r].t[:], op0=ALU.mult, op1=ALU.mult),
                      reads=[pDk, erem[r], hm01], writes=[kdec[r]])
                kb.op("dve", lambda e, r=r: e.scalar_tensor_tensor(out=kdec1[r].t[:], in0=pD.t[:, 0:128], scalar=hm01.t[:, 1:2], in1=erem[r].t[:], op0=ALU.mult, op1=ALU.mult),
                      reads=[pDk, erem[r], hm01], writes=[kdec1[r]])
                kb.op("act", lambda e, r=r: e.copy(out=vsb[r].t[:], in_=pE.t[:, 0:256]), reads=[pEv], writes=[vsb[r]])
                kb.op("act", lambda e, r=r: e.activation(out=sr[r].t[:], in_=pE.t[:, 256:512], func=AF.Silu), reads=[pEr], writes=[sr[r]])
                kb.op("pool", lambda e, r=r: e.tensor_tensor(out=gr[r].t[:], in0=sr[r].t[:], in1=ngb.t[:], op=ALU.mult), reads=[sr[r], ngb], writes=[gr[r]])

        def tileB(st, j, r):
            sub = slice(j * 128, (j + 1) * 128)
            if GLA_STAGE >= 4:
                kb.mm(pF.t[:, 256:384], kiT[r].t[:], qdT[r].t[:], True, True, [kiT[r], qdT[r]], [pFa])
                kb.op("dve", lambda e, r=r: e.tensor_tensor(out=attm[r].t[:], in0=pF.t[:, 256:384], in1=M1.t[:], op=ALU.mult),
                      reads=[pFa, M1], writes=[attm[r]])
                kb.mm(pG.t[:, 0:256], attm[r].t[:], vsb[r].t[:], True, True, [attm[r], vsb[r]], [pGa])
            if GLA_STAGE >= 5:
                if GLA_SUB >= 1:
                    kb.mm(pH.t[:, 0:256], kdec[r].t[:, :], vsb[r].t[:, :], True, True, [kdec[r], vsb[r]], [pH0])
                if GLA_SUB >= 2:
                    kb.mm(pH.t[:, 256:512], kdec1[r].t[:, :], vsb[r].t[:, :], True, True, [kdec1[r], vsb[r]], [pH1])
                if GLA_SUB >= 3:
                    kb.mm(pG.t[:, 256:512], qdT[r].t[:, :], SbA.t[:], True, True, [qdT[r], SbA], [pGb])
                if GLA_SUB >= 4:
                    kb.op("dve", lambda e, r=r: e.scalar_tensor_tensor(out=S32.t[:], in0=S32.t[:], scalar=ebT[r].t[:, 63:64], in1=pH.t[:, 0:256],
                                                                       op0=ALU.mult, op1=ALU.add), reads=[S32, ebT[r], pH0], writes=[S32])
                if GLA_SUB >= 5:
                    kb.op("act", lambda e: e.copy(out=SbB.t[:], in_=S32.t[:]), reads=[S32], writes=[SbB])
                if GLA_SUB >= 6:
                    kb.mm(pal.t[:, 0:256], qdT[r].t[:, :], SbB.t[:], True, True, [qdT[r], SbB], [pal])
                if GLA_SUB >= 7:
                    kb.op("dve", lambda e, r=r: e.scalar_tensor_tensor(out=S32.t[:], in0=S32.t[:], scalar=ebT[r].t[:, 127:128], in1=pH.t[:, 256:512],
                                                                       op0=ALU.mult, op1=ALU.add), reads=[S32, ebT[r], pH1], writes=[S32])
                if GLA_SUB >= 8:
                    kb.op("act", lambda e: e.copy(out=SbA.t[:], in_=S32.t[:]), reads=[S32], writes=[SbA])
            if GLA_STAGE >= 6:
                kb.op("act", lambda e, r=r: e.copy(out=oint[r].t[0:64, :], in_=pG.t[0:64, 256:512]), reads=[pGb], writes=[oint[r]])
                kb.op("act", lambda e, r=r: e.copy(out=oint[r].t[64:128, :], in_=pal.t[64:128, 0:256]), reads=[pal, oint[r]], writes=[oint[r]])
                kb.op("dve", lambda e, r=r: e.tensor_tensor(out=osb[r].t[:], in0=pG.t[:, 0:256], in1=oint[r].t[:], op=ALU.add),
                      reads=[pGa, oint[r]], writes=[osb[r]])
                kb.op("act", lambda e, r=r: e.activation(out=junk.t[:], in_=osb[r].t[:], func=AF.Square, accum_out=ss[r].t[:, 0:1]),
                      reads=[osb[r]], writes=[junk, ss[r]])
                kb.op("dve", lambda e, r=r: e.tensor_scalar(out=ss[r].t[:, 1:2], in0=ss[r].t[:, 0:1], scalar1=1.0 / GLA_DV, scalar2=LN_EPS,
                                                            op0=ALU.mult, op1=ALU.add), reads=[ss[r]], writes=[ss[r]])
                kb.op("act", lambda e, r=r: e.sqrt(out=ss[r].t[:, 1:2], in_=ss[r].t[:, 1:2]), reads=[ss[r]], writes=[ss[r]])
                kb.op("dve", lambda e, r=r: e.reciprocal(out=ss[r].t[:, 1:2], in_=ss[r].t[:, 1:2]), reads=[ss[r]], writes=[ss[r]])
                kb.op("dve", lambda e, r=r: e.scalar_tensor_tensor(out=og[r].t[:], in0=osb[r].t[:], scalar=ss[r].t[:, 1:2], in1=gr[r].t[:],
                                                                   op0=ALU.mult, op1=ALU.mult), reads=[osb[r], ss[r], gr[r]], writes=[og[r]])
            if GLA_STAGE >= 7:
                for c in range(2):
                    kb.op("pe", lambda e, r=r, c=c: e.transpose(out=pT[:, c * 128:(c + 1) * 128], in_=og[r].t[:, c * 128:(c + 1) * 128],
                                                                identity=identb.t[:]), reads=[og[r], identb], writes=[pDt])
                kb.op("act", lambda e, r=r: e.copy(out=oTt[r].t[:].rearrange("p a b -> p (a b)"), in_=pT[:, 0:256]), reads=[pDt], writes=[oTt[r]])
            t0 = st * 512 + j * 128
            kb.dma("pool", ov[:, :, t0:t0 + 128], oTt[r].t[:], reads=[oTt[r]], writes=[oT])

        gits = [(st, j) for st in range(S // 512) for j in range(4)]
        superA(0)
        tileA(0, 0, 0)
        for n_, (st, j) in enumerate(gits):
            if n_ + 1 < len(gits):
                st2, j2 = gits[n_ + 1]
                if j2 == 0:
                    superA(st2)
                tileA(st2, j2, (n_ + 1) % R)
            tileB(st, j, n_ % R)
        kb.finish([oT])
    return nc


REL_BUCKETS, REL_MAX_DIST = 32, 2048


def rel_bucket_np(n):
    n = np.maximum(n, 0)
    exact = REL_BUCKETS // 2
    logn = np.log(np.maximum(n, 1).astype(np.float32) / exact)
    large = exact + (logn / np.float32(math.log(REL_MAX_DIST / exact)) * (REL_BUCKETS - exact)).astype(np.int32)
    large = np.minimum(large, REL_BUCKETS - 1)
    return np.where(n < exact, n, large)


def onehot_struct(dists, valid):
    L = len(dists)
    oh = np.zeros((33, L), np.float32)
    b = rel_bucket_np(np.asarray(dists))
    for i in range(L):
        if valid[i]:
            oh[b[i], i] = 1.0
        else:
            oh[32, i] = 1.0
    return oh


def make_flipJ(kb):
    jm = kb.sb([128, 128])
    kb.op("pool", lambda e: e.memset(jm.t[:], 0.0), writes=[jm])
    kb.op("pool", lambda e: e.affine_select(out=jm.t[:], in_=jm.t[:], pattern=[[1, 128]], compare_op=ALU.not_equal, fill=1.0,
                                             base=-127, channel_multiplier=1), reads=[jm], writes=[jm])
    return jm


def build_wide_bias(kb, tab, oh_ap, nh, L, W, Fd, fd_row0, wides, wbuf, pbank, jm, es):
    oht = kb.sb([33, L], es=es)
    kb.dma("sp", oht.t[:], oh_ap, writes=[oht])
    fs = kb.sb([nh, L], es=es)
    for c0 in range(0, L, 512):
        c1 = min(L, c0 + 512)
        kb.mm(pbank.t[0:nh, 0:c1 - c0], tab.t[0:33, 0:nh], oht.t[0:33, c0:c1], True, True, [tab, oht], [pbank])
        kb.op("dve", lambda e, c0=c0, c1=c1: e.tensor_copy(out=fs.t[0:nh, c0:c1], in_=pbank.t[0:nh, 0:c1 - c0]), reads=[pbank], writes=[fs])
    kb.dma("sp", Fd.t[fd_row0:fd_row0 + nh, 0:L], fs.t[0:nh, :], reads=[fs], writes=[Fd])
    tp = kb.sb([128, W], es=es)
    for h in range(nh):
        src = bass.AP(Fd.t.tensor, (fd_row0 + h) * Fd.t.shape[1], [[1, 128], [1, W]])
        kb.dma("sp", tp.t[:, :], src, reads=[Fd], writes=[tp])
        for c0 in range(0, W, 512):
            c1 = min(W, c0 + 512)
            kb.mm(pbank.t[:, 0:c1 - c0], jm.t[:], tp.t[:, c0:c1], True, True, [jm, tp], [pbank])
            kb.op("dve", lambda e, h=h, c0=c0, c1=c1: e.tensor_copy(out=wides[h][:, c0:c1], in_=pbank.t[:, 0:c1 - c0]), reads=[pbank], writes=[wbuf])


DIL_GROUPS = ((128, 1), (512, 4), (2048, 16))
CH = 2048


def dil_onehots():
    ohs = []
    for (window, dil) in DIL_GROUPS:
        m = np.arange(383) - 127
        ohs.append(onehot_struct(m * dil, (m >= 0) & (m <= window // dil)))
    return np.stack(ohs)


def build_dil(S):
    nc = new_nc()
    hT = dr_in(nc, "hT", [D, S], BF16)
    wqkv = dr_in(nc, "wqkv", [3, D, 768])
    tabi = dr_in(nc, "tab", [33, 4])
    ohi = dr_in(nc, "oh", [3, 33, 383])
    oT = dr_out(nc, "oT", [256, S], BF16)
    Fd = Buf(nc.dram_tensor("Fd", [12, 383], F32, kind="Internal").ap())
    wbf = Buf(nc.dram_tensor("wbf", [3, 128, 8 * 768], BF16, kind="Internal").ap())
    nchunk = S // CH
    scale = 64 ** -0.5
    with ExitStack() as es:
        kb = KB(nc, es)
        banks = [kb.ps() for _ in range(8)]
        jm = make_flipJ(kb)
        tab = kb.sb([33, 4])
        kb.dma("sp", tab.t[:], tabi.t[:, :], writes=[tab])
        bias = [kb.sb([128, 4, 256]) for _ in range(3)]
        sel65 = kb.sb([128, 64])
        kb.op("pool", lambda e: e.memset(sel65.t[:], 0.0), writes=[sel65])
        kb.op("pool", lambda e: e.memset(sel65.t[64:65, :], 1.0), reads=[sel65], writes=[sel65])
        with ExitStack() as es0:
            for g in range(3):
                build_wide_bias(kb, tab, ohi.t[g], 4, 383, 256, Fd, 4 * g, [bias[g].t[:, h, :] for h in range(4)], bias[g], banks[0], jm, es0)
        kb.barrier()
        with ExitStack() as es1:
            stage = [kb.sb([128, 768], es=es1) for _ in range(2)]
            wtmp = [kb.sb([128, 8, 768], BF16, nparts=8, es=es1) for _ in range(1)]
            for g in range(3):
                load_w_bf16(kb, wtmp[0], lambda k, c0, c1, g=g: wqkv.t[g, k * 128:(k + 1) * 128, c0:c1], 8, 768, stage, chunk=768)
                kb.dma("sp", wbf.t[g].rearrange("p (k c) -> p k c", k=8), wtmp[0].t[:], reads=wtmp[0].parts, writes=[wbf])
                for p_ in wtmp[0].parts:
                    p_.r[kb.dq["sp"][(kb.di["sp"] - 1) % len(kb.dq["sp"])]] = kb.dsl[kb.dq["sp"][(kb.di["sp"] - 1) % len(kb.dq["sp"])]][1]
        kb.barrier()
        wg = [kb.sb([128, 8, 768], BF16) for _ in range(2)]
        hh = [kb.sb([128, 8, CH], BF16) for _ in range(2)]
        qTm = [[kb.sb([128, CH], BF16) for _ in range(2)] for _ in range(2)]
        for hp_ in range(2):
            for hd_ in range(2):
                kb.op("pool", lambda e, hp_=hp_, hd_=hd_: e.memset(qTm[hp_][hd_].t[:], 0.0), writes=[qTm[hp_][hd_]])
        kT = [kb.sb([128, 2 * CH], BF16) for _ in range(2)]
        V = kb.sb([128, 32, 4, 65], BF16)
        kb.op("pool", lambda e: e.memset(V.t[:].rearrange("p a b c -> p (a b c)"), 1.0), writes=[V])
        acc = kb.sb([65, 4, CH])
        tmp = [kb.sb([128, 512]) for _ in range(2)]
        pT = [kb.sb([128, 512], BF16) for _ in range(2)]
        oTt = [kb.sb([64, CH], BF16) for _ in range(2)]
        rdb = kb.sb([64, 512])
        hv = hT.t.rearrange("(k p) t -> p k t", p=128)
        nw = 0
        nit = 0
        for c in range(nchunk):
            cur = hh[c % 2]
            kb.dma("sp", cur.t[:], hv[:, :, c * CH:(c + 1) * CH], writes=[cur])
            slots = ([(hh[(c - 1) % 2], 0)] if c > 0 else []) + [(cur, 1)]
            for g, (window, d) in enumerate(DIL_GROUPS):
                w = wg[nw % 2]; nw += 1
                kb.dma("pool", w.t[:], wbf.t[g].rearrange("p (k c) -> p k c", k=8), reads=[wbf], writes=[w])
                nbk = 16 // d
                for hp in range(2):
                    for n4 in range(4):
                        pb = banks[(n4 + hp) % 2]
                        for k in range(8):
                            kb.mm(pb.t[:, :], w.t[:, k, hp * 128:(hp + 1) * 128], cur.t[:, k, n4 * 512:(n4 + 1) * 512], k == 0, k == 7, [w, cur], [pb])
                        kb.op("act", lambda e, pb=pb, hp=hp, n4=n4: e.copy(out=qTm[hp][0].t[0:64, n4 * 512:(n4 + 1) * 512], in_=pb.t[0:64, :]), reads=[pb], writes=[qTm[hp][0]])
                        kb.op("act", lambda e, pb=pb, hp=hp, n4=n4: e.copy(out=qTm[hp][1].t[64:128, n4 * 512:(n4 + 1) * 512], in_=pb.t[64:128, :]), reads=[pb], writes=[qTm[hp][1]])
                    for (hb, si) in slots:
                        for n4 in range(4):
                            pb = banks[(n4 + hp) % 2]
                            for k in range(8):
                                kb.mm(pb.t[:, :], w.t[:, k, 256 + hp * 128:256 + (hp + 1) * 128], hb.t[:, k, n4 * 512:(n4 + 1) * 512], k == 0, k == 7, [w, hb], [pb])
                            kb.op("dve", lambda e, pb=pb, hp=hp, n4=n4, si=si: e.tensor_copy(out=kT[hp].t[:, si * CH + n4 * 512:si * CH + (n4 + 1) * 512], in_=pb.t[:, :]),
                                  reads=[pb], writes=[kT[hp]])
                for (hb, si) in slots:
                    for r in range(d):
                        for bl in range(nbk):
                            ti = si * 16 + r * nbk + bl
                            pb = banks[2 + ti % 2]
                            st0 = r + d * 128 * bl
                            for k in range(8):
                                kb.mm(pb.t[:, 0:256], hb.t[:, k, st0:st0 + 127 * d + 1:d], w.t[:, k, 512:768], k == 0, k == 7, [w, hb], [pb])
                            kb.op("act" if ti % 2 else "dve",
                                  (lambda e, pb=pb, ti=ti: e.copy(out=V.t[:, ti, :, 0:64], in_=pb.t[:, 0:256].rearrange("p (h c) -> p h c", h=4))) if ti % 2 else
                                  (lambda e, pb=pb, ti=ti: e.tensor_copy(out=V.t[:, ti, :, 0:64], in_=pb.t[:, 0:256].rearrange("p (h c) -> p h c", h=4))),
                                  reads=[pb], writes=[V])
                aits = []
                for r in range(d):
                    for bl in range(nbk):
                        q0 = r + d * 128 * bl
                        keys = [(1, r, bl)]
                        if bl > 0:
                            keys.append((1, r, bl - 1))
                        elif c > 0:
                            keys.append((0, r, nbk - 1))
                        for hp in range(2):
                            aits.append((q0, keys, hp))

                def dilA(n_):
                    q0, keys, hp = aits[n_]
                    nd = len(keys)
                    it = (nit + n_) % 2
                    ps = banks[4 + it]
                    for hd in range(2):
                        for dl, (si, kr, kbl) in enumerate(keys):
                            k0 = si * CH + kr + d * 128 * kbl
                            kb.mm(ps.t[:, (hd * 2 + dl) * 128:(hd * 2 + dl + 1) * 128], kT[hp].t[:, k0:k0 + 127 * d + 1:d],
                                  qTm[hp][hd].t[:, q0:q0 + 127 * d + 1:d], True, True, [kT[hp], qTm[hp][hd]], [ps])
                    psv = ps.t[:, :].rearrange("p (h e i) -> p h e i", h=2, e=2)
                    tv = tmp[it].t[:, :].rearrange("p (h e i) -> p h e i", h=2, e=2)
                    pv = pT[it].t[:, :].rearrange("p (h e i) -> p h e i", h=2, e=2)
                    bv = bias[g].t[:, 2 * hp:2 * hp + 2, :].rearrange("p h (e i) -> p h e i", e=2)
                    kb.op("dve", lambda e: e.scalar_tensor_tensor(out=tv[:, :, 0:nd, :], in0=psv[:, :, 0:nd, :], scalar=scale,
                                                                  in1=bv[:, :, 0:nd, :], op0=ALU.mult, op1=ALU.add),
                          reads=[ps, bias[g]], writes=[tmp[it]])
                    kb.op("act", lambda e: e.activation(out=pv[:, :, 0:nd, :], in_=tv[:, :, 0:nd, :], func=AF.Exp),
                          reads=[tmp[it]], writes=[pT[it]])

                def dilB(n_):
                    q0, keys, hp = aits[n_]
                    nd = len(keys)
                    it = (nit + n_) % 2
                    po = banks[6 + it]
                    pv = pT[it].t[:, :].rearrange("p (h e i) -> p h e i", h=2, e=2)
                    for hd in range(2):
                        for dl, (si, kr, kbl) in enumerate(keys):
                            ti = si * 16 + kr * nbk + kbl
                            kb.mm(po.t[0:65, hd * 128:(hd + 1) * 128], V.t[:, ti, 2 * hp + hd, :], pv[:, hd, dl, :], dl == 0, dl == nd - 1, [V, pT[it]], [po])
                    av = acc.t[:, 2 * hp:2 * hp + 2, q0:q0 + 127 * d + 1:d]
                    pov = po.t[0:65, 0:256].rearrange("p (h i) -> p h i", h=2)
                    if g == 0:
                        kb.op("dve", lambda e: e.tensor_copy(out=av, in_=pov), reads=[po], writes=[acc])
                    else:
                        kb.op("dve", lambda e: e.tensor_tensor(out=av, in0=av, in1=pov, op=ALU.add), reads=[po, acc], writes=[acc])

                dilA(0)
                for n_ in range(len(aits)):
                    if n_ + 1 < len(aits):
                        dilA(n_ + 1)
                    dilB(n_)
                nit += len(aits)
            for h in range(4):
                ot = oTt[h % 2]
                for n4 in range(4):
                    pb = banks[n4 % 2]
                    kb.mm(pb.t[0:64, :], sel65.t[0:65, 0:64], acc.t[0:65, h, n4 * 512:(n4 + 1) * 512], True, True, [sel65, acc], [pb])
                    kb.op("dve", lambda e, pb=pb: e.reciprocal(out=rdb.t[:, :], in_=pb.t[0:64, :]), reads=[pb], writes=[rdb])
                    kb.op("dve", lambda e, h=h, n4=n4, ot=ot: e.tensor_tensor(out=ot.t[:, n4 * 512:(n4 + 1) * 512], in0=acc.t[0:64, h, n4 * 512:(n4 + 1) * 512],
                                                                            in1=rdb.t[:, :], op=ALU.mult), reads=[acc, rdb], writes=[ot])
                kb.dma("sp", oT.t[h * 64:(h + 1) * 64, c * CH:(c + 1) * CH], ot.t[:, :], reads=[ot], writes=[oT])
        kb.finish([oT])
    return nc


SELW, WINW = 2560, 1408
SEL_FARD = 13
BIGF, BIGI = 1.0e4, -1.0e6
EB = 30000.0


def nsa_onehots():
    m = np.arange(SELW + 127) - 127 - 384
    oh_sel = onehot_struct(m, m >= 0)
    m = np.arange(WINW + 127) - 127 - 384
    oh_win = onehot_struct(m, (m >= 0) & (m <= 511))
    m = np.arange(1776 + 16) - 127
    oh_cmp = onehot_struct(m, m >= 0)
    return oh_sel, oh_win, oh_cmp


NSA_STAGE = 99
NSA_SUB = 99


def build_nsa(S):
    nc = new_nc()
    NT = S // 128
    NCK = S // 16 - 1
    NCP = ((NCK + 127) // 128) * 128
    hT = dr_in(nc, "hT", [D, S], BF16)
    wqd = dr_in(nc, "wqd", [D, 512]); wkv = dr_in(nc, "wkv", [D, 384]); wgi = dr_in(nc, "wg", [D, 12])
    pei = dr_in(nc, "pe", [2, 32, 64]); w1i = dr_in(nc, "w1", [2, 2048, 256]); b1i = dr_in(nc, "b1", [2, 256]); w2i = dr_in(nc, "w2", [2, 256, 64])
    tabi = dr_in(nc, "tab", [33, 4])
    ohs = dr_in(nc, "ohs", [33, SELW + 127]); ohw = dr_in(nc, "ohw", [33, WINW + 127]); ohc = dr_in(nc, "ohc", [33, 1792])
    oT = dr_out(nc, "oT", [256, S], BF16)
    Fd = Buf(nc.dram_tensor("Fd", [12, SELW + 127], F32, kind="Internal").ap())
    scale = 64 ** -0.5
    with ExitStack() as es:
        kb = KB(nc, es)
        banks = [kb.ps() for _ in range(8)]
        identb = make_ident(kb, BF16)
        jm = make_flipJ(kb)
        tab = kb.sb([33, 4])
        kb.dma("sp", tab.t[:], tabi.t[:, :], writes=[tab])
        kswT = kb.sb([128, S], BF16)
        Vs = kb.sb([128, NT, 65], BF16); Vw = kb.sb([128, NT, 65], BF16)
        kb.op("pool", lambda e: e.memset(Vs.t[:].rearrange("p a b -> p (a b)"), 1.0), writes=[Vs])
        kb.op("pool", lambda e: e.memset(Vw.t[:].rearrange("p a b -> p (a b)"), 1.0), writes=[Vw])
        kcmpT = kb.sb([64, NCP], BF16); vcmp = kb.sb([128, NCP // 128, 64], BF16)
        if NSA_STAGE == 1.1:
            kb.finish([oT]); return nc
        stage = [kb.sb([128, 512]) for _ in range(2)]
        wqb = kb.sb([128, 8, 512], BF16, nparts=8); wkvb = kb.sb([128, 8, 384], BF16, nparts=8); wgb = kb.sb([128, 8, 12], BF16, nparts=8)
        for dst, src, n in ((wqb, wqd, 512), (wkvb, wkv, 384), (wgb, wgi, 12)):
            load_w_bf16(kb, dst, lambda k, c0, c1, src=src: src.t[k * 128:(k + 1) * 128, c0:c1], 8, n, stage)
        if NSA_STAGE == 1.2:
            kb.finish([oT]); return nc
        hhs = [kb.sb([128, 8, 512], BF16) for _ in range(2)]
        hv = hT.t.rearrange("(k p) t -> p k t", p=128)
        with ExitStack() as es1:
            kcvcT = kb.sb([128, S], BF16, es=es1)
            w1m = [kb.sb([128, 32, 256], BF16, es=es1) for _ in range(2)]
            for x_ in range(2):
                kb.op("pool", lambda e, x_=x_: e.memset(w1m[x_].t[:].rearrange("p a b -> p (a b)"), 0.0), writes=[w1m[x_]])
            w2b = kb.sb([128, 2, 2, 64], BF16, es=es1)
            st1 = [kb.sb([128, 8, 256], es=es1) for _ in range(1)]
            nst1 = 0
            for x_ in range(2):
                rws = slice(x_ * 64, (x_ + 1) * 64)
                for pc in range(4):
                    stg = st1[0]; nst1 += 1
                    kb.dma("sp", stg.t[rws, :, :], w1i.t[x_].rearrange("(p d) h -> d p h", d=64)[:, pc * 8:(pc + 1) * 8, :], writes=[stg])
                    kb.op("dve", lambda e, x_=x_, pc=pc, stg=stg, rws=rws: e.tensor_copy(out=w1m[x_].t[rws, pc * 8:(pc + 1) * 8, :], in_=stg.t[rws, :, :]),
                          reads=[stg], writes=[w1m[x_]])
            st2 = kb.sb([128, 2, 2, 64], es=es1)
            for x_ in range(2):
                kb.dma("sp", st2.t[:, x_, :, :], w2i.t[x_].rearrange("(a p) c -> p a c", p=128), writes=[st2])
            kb.op("dve", lambda e: e.tensor_copy(out=w2b.t[:].rearrange("p a b c -> p (a b c)"), in_=st2.t[:].rearrange("p a b c -> p (a b c)")),
                  reads=[st2], writes=[w2b])
            peT = kb.sb([128, 32], es=es1); peTb = kb.sb([128, 32], BF16, es=es1)
            for x_ in range(2):
                for p4 in range(4):
                    kb.dma("sp", peT.t[x_ * 64:(x_ + 1) * 64, p4 * 8:(p4 + 1) * 8], pei.t[x_, p4 * 8:(p4 + 1) * 8, :].rearrange("p d -> d p"), writes=[peT],
                           allow_slow_non_contiguous=True)
            kb.op("dve", lambda e: e.tensor_copy(out=peTb.t[:], in_=peT.t[:]), reads=[peT], writes=[peTb])
            b1t = kb.sb([128, 2, 2], es=es1)
            for x_ in range(2):
                kb.dma("sp", b1t.t[:, x_, :], b1i.t[x_, :].rearrange("(a p) -> p a", p=128), writes=[b1t], allow_slow_non_contiguous=True)
            if NSA_STAGE == 1.3:
                kb.finish([oT]); return nc
            for st in range(S // 512):
                hh = hhs[st % 2]
                kb.dma("sp", hh.t[:], hv[:, :, st * 512:(st + 1) * 512], writes=[hh])
                for (c0, dstT, eng) in ((0, kcvcT, "act"), (128, kswT, "dve")):
                    if NSA_STAGE == 1.5:
                        break
                    pb = banks[1 + (c0 // 128)]
                    for k in range(8):
                        kb.mm(pb.t[:, :], wkvb.t[:, k, c0:c0 + 128], hh.t[:, k, :], k == 0, k == 7, [wkvb.parts[k], hh], [pb])
                    if eng == "act":
                        kb.op("act", lambda e, pb=pb, dstT=dstT, st=st: e.copy(out=dstT.t[:, st * 512:(st + 1) * 512], in_=pb.t[:, :]), reads=[pb], writes=[dstT])
                    else:
                        kb.op("dve", lambda e, pb=pb, dstT=dstT, st=st: e.tensor_copy(out=dstT.t[:, st * 512:(st + 1) * 512], in_=pb.t[:, :]), reads=[pb], writes=[dstT])
                if NSA_STAGE == 1.4:
                    continue
                pb = banks[3 + st % 2]
                for j in range(4):
                    for k in range(8):
                        kb.mm(pb.t[:, j * 128:(j + 1) * 128], hh.t[:, k, j * 128:(j + 1) * 128], wkvb.t[:, k, 256:384], k == 0, k == 7, [wkvb.parts[k], hh], [pb])
                pv4 = pb.t[:, :].rearrange("p (j c) -> p j c", j=4)
                for j in range(4):
                    if NSA_SUB >= 1:
                        kb.op("act", lambda e, pb=pb, st=st, j=j: e.copy(out=Vs.t[:, st * 4 + j, 0:64], in_=pb.t[:, j * 128:j * 128 + 64]), reads=[pb], writes=[Vs])
                    if NSA_SUB >= 2:
                        kb.op("act", lambda e, pb=pb, st=st, j=j: e.copy(out=Vw.t[:, st * 4 + j, 0:64], in_=pb.t[:, j * 128 + 64:j * 128 + 128]), reads=[pb], writes=[Vw])
            if NSA_STAGE in (2, 1.4, 1.5):
                kb.finish([oT]); return nc
            hidT = kb.sb([128, 2, 2, NCP], BF16, es=es1)
            kb.op("pool", lambda e: e.memset(hidT.t[:].rearrange("p a b c -> p (a b c)"), 0.0), writes=[hidT])
            cbias = kb.sb([128, 2, 2], es=es1)
            for x_ in range(2):
                rows = slice(x_ * 64, (x_ + 1) * 64)
                for half in range(2):
                    pb = banks[1]
                    for p in range(32):
                        kb.mm(pb.t[:, 0:1], w1m[x_].t[:, p, half * 128:(half + 1) * 128], peTb.t[:, p:p + 1], p == 0, p == 31, [w1m[x_], peTb], [pb])
                    kb.op("dve", lambda e, x_=x_, half=half, pb=pb: e.tensor_tensor(out=cbias.t[:, x_, half:half + 1], in0=pb.t[:, 0:1], in1=b1t.t[:, x_, half:half + 1], op=ALU.add),
                          reads=[pb, b1t], writes=[cbias])
                    for n0 in range(0, NCK, 512):
                        n1 = min(NCK, n0 + 512)
                        pb2 = banks[2 + (n0 // 512) % 2]
                        for p in range(32):
                            kb.mm(pb2.t[:, 0:n1 - n0], w1m[x_].t[:, p, half * 128:(half + 1) * 128], kcvcT.t[:, 16 * n0 + p:16 * (n1 - 1) + p + 1:16],
                                  p == 0, p == 31, [w1m[x_], kcvcT], [pb2])
                        kb.op("act", lambda e, x_=x_, half=half, n0=n0, n1=n1, pb2=pb2: e.activation(out=hidT.t[:, x_, half, n0:n1], in_=pb2.t[:, 0:n1 - n0], func=AF.Silu,
                                                                                                     bias=cbias.t[:, x_, half:half + 1]), reads=[pb2, cbias], writes=[hidT])
            for n0 in range(0, NCP, 512):
                n1 = min(NCP, n0 + 512)
                pb = banks[1]
                for half in range(2):
                    kb.mm(pb.t[0:64, 0:n1 - n0], w2b.t[:, 0, half, :], hidT.t[:, 0, half, n0:n1], half == 0, half == 1, [w2b, hidT], [pb])
                kb.op("dve", lambda e, n0=n0, n1=n1, pb=pb: e.tensor_copy(out=kcmpT.t[:, n0:n1], in_=pb.t[0:64, 0:n1 - n0]), reads=[pb], writes=[kcmpT])
            for ct in range(NCP // 128):
                pb = banks[2 + ct % 2]
                for half in range(2):
                    kb.mm(pb.t[:, 0:64], hidT.t[:, 1, half, ct * 128:(ct + 1) * 128], w2b.t[:, 1, half, :], half == 0, half == 1, [w2b, hidT], [pb])
                kb.op("act", lambda e, ct=ct, pb=pb: e.copy(out=vcmp.t[:, ct, :], in_=pb.t[:, 0:64]), reads=[pb], writes=[vcmp])
        if NSA_STAGE == 3:
            kb.finish([oT]); return nc
        kb.barrier()
        wsel = kb.sb([128, 4, SELW], BF16); wwin = kb.sb([128, 4, WINW], BF16)
        nearb = kb.sb([128, 4, 104]); farb = kb.sb([128, 4])
        with ExitStack() as es0:
            wtmp = kb.sb([128, SELW], es=es0)
            for (oh, L, W, dst) in ((ohs, SELW + 127, SELW, wsel), (ohw, WINW + 127, WINW, wwin)):
                for h in range(4):
                    tabh = Buf(tab.t[:, h:h + 1]); tabh.w = tab.w
                    with ExitStack() as esx:
                        build_wide_bias(kb, tabh, oh.t[:, :], 1, L, W, Fd, h, [wtmp.t[:, 0:W]], wtmp, banks[0], jm, esx)
                        kb.op("act", lambda e, dst=dst, h=h, W=W: e.copy(out=dst.t[:, h, :], in_=wtmp.t[:, 0:W]), reads=[wtmp], writes=[dst])
                    kb.barrier()
            oht = kb.sb([33, 1792], es=es0)
            kb.dma("sp", oht.t[:], ohc.t[:, :], writes=[oht])
            fs = kb.sb([4, 1792], es=es0)
            for c0 in range(0, 1792, 512):
                c1 = min(1792, c0 + 512)
                kb.mm(banks[0].t[0:4, 0:c1 - c0], tab.t[0:33, 0:4], oht.t[0:33, c0:c1], True, True, [tab, oht], [banks[0]])
                kb.op("dve", lambda e, c0=c0, c1=c1: e.tensor_copy(out=fs.t[0:4, c0:c1], in_=banks[0].t[0:4, 0:c1 - c0]), reads=[banks[0]], writes=[fs])
            kb.dma("sp", Fd.t[8:12, 0:1792], fs.t[0:4, :], reads=[fs], writes=[Fd])
            jc = kb.sb([128, 104], es=es0)
            kb.op("pool", lambda e: e.memset(jc.t[:], 0.0), writes=[jc])
            kb.op("pool", lambda e: e.affine_select(out=jc.t[:], in_=jc.t[:], pattern=[[1, 104]], compare_op=ALU.not_equal, fill=1.0,
                                                     base=-103, channel_multiplier=1), reads=[jc], writes=[jc])
            xt = kb.sb([104, 128], es=es0)
            for h in range(4):
                src = bass.AP(Fd.t.tensor, (8 + h) * Fd.t.shape[1], [[16, 104], [1, 128]])
                kb.dma("sp", xt.t[:, :], src, reads=[Fd], writes=[xt])
                kb.mm(banks[0].t[:, 0:104], xt.t[0:104, :], jc.t[0:104, :], True, True, [xt, jc], [banks[0]])
                kb.op("dve", lambda e, h=h: e.tensor_copy(out=nearb.t[:, h, :], in_=banks[0].t[:, 0:104]), reads=[banks[0]], writes=[nearb])
            onesr = kb.sb([33, 128], es=es0)
            kb.op("pool", lambda e: e.memset(onesr.t[:], 0.0), writes=[onesr])
            kb.op("pool", lambda e: e.memset(onesr.t[0:1, :], 1.0), reads=[onesr], writes=[onesr])
            t31 = kb.sb([1, 4], es=es0)
            kb.dma("sp", t31.t[:, :], tabi.t[31:32, :], writes=[t31])
            kb.mm(banks[0].t[:, 0:4], onesr.t[0:1, :], t31.t[0:1, :], True, True, [onesr, t31], [banks[0]])
            kb.op("dve", lambda e: e.tensor_copy(out=farb.t[:, :], in_=banks[0].t[:, 0:4]), reads=[banks[0]], writes=[farb])
        kb.barrier()
        Aw = kb.sb([128, 512])
        kb.op("pool", lambda e: e.memset(Aw.t[:], 0.0), writes=[Aw])
        for (rows, c0) in ((slice(0, 64), 255), (slice(64, 128), 256)):
            kb.op("pool", lambda e, rows=rows, c0=c0: e.memset(Aw.t[rows, c0:c0 + 2], BIGF), reads=[Aw], writes=[Aw])
            kb.op("pool", lambda e, rows=rows, c0=c0: e.memset(Aw.t[rows, c0 + 2:512], BIGI), reads=[Aw], writes=[Aw])
        e2f = kb.sb([128, 64, 2])
        kb.op("pool", lambda e: e.memset(e2f.t[:].rearrange("p a b -> p (a b)"), 0.0), writes=[e2f])
        kb.op("pool", lambda e: e.affine_select(out=e2f.t[:], in_=e2f.t[:], pattern=[[-2, 64], [-1, 2]], compare_op=ALU.not_equal, fill=EB,
                                                 base=0, channel_multiplier=1), reads=[e2f], writes=[e2f])
        Exp_ = kb.sb([128, 64, 128], BF16)
        for half in range(2):
            kb.op("dve", lambda e, half=half: e.tensor_copy(out=Exp_.t[:, :, half * 64:(half + 1) * 64], in_=e2f.t[:, :, half:half + 1].to_broadcast([128, 64, 64])),
                  reads=[e2f], writes=[Exp_])
        qsel = [kb.sb([128, 512], BF16) for _ in range(4)]
        qwin = [kb.sb([128, 512], BF16) for _ in range(4)]
        for h_ in range(4):
            kb.op("pool", lambda e, h_=h_: e.memset(qsel[h_].t[:], 0.0), writes=[qsel[h_]])
            kb.op("pool", lambda e, h_=h_: e.memset(qwin[h_].t[:], 0.0), writes=[qwin[h_]])
        gates = kb.sb([128, 4, 12])
        negselT = kb.sb([128, 2, 512], BF16)
        kb.op("pool", lambda e: e.memset(negselT.t[:].rearrange("p a b -> p (a b)"), -1.0), writes=[negselT])
        tmpc = [kb.sb([128, 1024]) for _ in range(2)]; ebuf = tmpc; pg = kb.sb([128, 1024])
        pbf = [kb.sb([128, 1024], BF16) for _ in range(2)]
        kb.op("pool", lambda e: e.memset(pg.t[:], 0.0), writes=[pg])
        pTc = [kb.sb([128, NCP // 128, 128], BF16) for _ in range(2)]
        den = [kb.sb([128, 2]) for _ in range(2)]; imp = kb.sb([128, 256]); sc2 = kb.sb([128, 256]); m8 = kb.sb([128, 16]); nsel = kb.sb([128, 256], BF16)
        ofin = [kb.sb([128, 4, 64]) for _ in range(4)]
        tmp = [kb.sb([128, 512]) for _ in range(3)]; pT = [kb.sb([128, 512], BF16) for _ in range(3)]
        fcol = kb.sb([128, 8]); ogb = kb.sb([128, 256], BF16); oTt = kb.sb([128, 2, 128], BF16)
        poS = [kb.sb([65, 512]) for _ in range(2)]; identf = make_ident(kb, F32)
        ov = oT.t.rearrange("(c p) t -> p c t", p=128)
        pgv = pg.t[:, :].rearrange("p (b m) -> p b m", m=4)
        if NSA_STAGE == 4:
            kb.finish([oT]); return nc
        for qs in range(S // 512):
            hh = hhs[qs % 2]
            kb.dma("sp", hh.t[:], hv[:, :, qs * 512:(qs + 1) * 512], writes=[hh])
            for h in range(4):
                pb = banks[0]
                for k in range(8):
                    kb.mm(pb.t[:, :], wqb.t[:, k, h * 128:(h + 1) * 128], hh.t[:, k, :], k == 0, k == 7, [wqb.parts[k], hh], [pb])
                kb.op("act", lambda e, h=h, pb=pb: e.copy(out=qsel[h].t[0:64, :], in_=pb.t[0:64, :]), reads=[pb], writes=[qsel[h]])
                kb.op("dve", lambda e, h=h, pb=pb: e.tensor_copy(out=qwin[h].t[64:128, :], in_=pb.t[64:128, :]), reads=[pb], writes=[qwin[h]])
            pb = banks[0]
            for j in range(4):
                for k in range(8):
                    kb.mm(pb.t[:, j * 12:(j + 1) * 12], hh.t[:, k, j * 128:(j + 1) * 128], wgb.t[:, k, :], k == 0, k == 7, [wgb.parts[k], hh], [pb])
            kb.op("act", lambda e, pb=pb: e.activation(out=gates.t[:].rearrange("p a b -> p (a b)"), in_=pb.t[:, 0:48], func=AF.Sigmoid), reads=[pb], writes=[gates])
            def cmpA(j, h, pp):
                qb = qs * 4 + j
                sub = slice(j * 128, (j + 1) * 128)
                ncv = min(8 * qb + 7, NCK)
                nlo = max(0, 8 * qb - 97); u0 = nlo - (8 * qb - 97)
                sbk = (banks[1], banks[2]) if pp == 0 else (banks[5], banks[6])
                for c0 in range(0, ncv, 512):
                    c1 = min(ncv, c0 + 512)
                    kb.mm(sbk[c0 // 512].t[:, 0:c1 - c0], qsel[h].t[0:64, sub], kcmpT.t[0:64, c0:c1], True, True, [qsel[h], kcmpT], [sbk[c0 // 512]])
                for c0 in range(0, ncv, 512):
                    c1 = min(ncv, c0 + 512)
                    pbk = sbk[c0 // 512]
                    kb.op("dve", lambda e, c0=c0, c1=c1, pbk=pbk: e.tensor_scalar(out=tmpc[pp].t[:, c0:c1], in0=pbk.t[:, 0:c1 - c0], scalar1=scale, scalar2=farb.t[:, h:h + 1],
                                                                             op0=ALU.mult, op1=ALU.add), reads=[pbk, farb], writes=[tmpc[pp]])
                    a0 = max(c0, nlo)
                    if a0 < c1:
                        kb.op("dve", lambda e, c0=c0, c1=c1, a0=a0, pbk=pbk: e.scalar_tensor_tensor(
                            out=tmpc[pp].t[:, a0:c1], in0=pbk.t[:, a0 - c0:c1 - c0], scalar=scale, in1=nearb.t[:, h, u0 + a0 - nlo:u0 + c1 - nlo],
                            op0=ALU.mult, op1=ALU.add), reads=[pbk, nearb, tmpc[pp]], writes=[tmpc[pp]])
                kb.op("act", lambda e: e.activation(out=ebuf[pp].t[:, 0:ncv], in_=tmpc[pp].t[:, 0:ncv], func=AF.Exp, accum_out=den[pp].t[:, 0:1]),
                      reads=[tmpc[pp]], writes=[tmpc[pp], den[pp]])

            def cmpB(j, h, pp):
                qb = qs * 4 + j
                ncv = min(8 * qb + 7, NCK)
                nct = (ncv + 127) // 128
                dn = den[pp]
                kb.op("dve", lambda e: e.tensor_scalar(out=dn.t[:, 1:2], in0=dn.t[:, 0:1], scalar1=1e-30, scalar2=None, op0=ALU.max), reads=[dn], writes=[dn])
                kb.op("dve", lambda e: e.reciprocal(out=dn.t[:, 1:2], in_=dn.t[:, 1:2]), reads=[dn], writes=[dn])
                if h == 0:
                    kb.op("dve", lambda e: e.tensor_scalar(out=pg.t[:, 0:ncv], in0=ebuf[pp].t[:, 0:ncv], scalar1=dn.t[:, 1:2], scalar2=None, op0=ALU.mult),
                          reads=[ebuf[pp], dn], writes=[pg])
                else:
                    kb.op("dve", lambda e: e.scalar_tensor_tensor(out=pg.t[:, 0:ncv], in0=ebuf[pp].t[:, 0:ncv], scalar=dn.t[:, 1:2], in1=pg.t[:, 0:ncv],
                                                                  op0=ALU.mult, op1=ALU.add), reads=[ebuf[pp], dn, pg], writes=[pg])
                kb.op("act", lambda e: e.activation(out=pbf[pp].t[:, 0:ncv], in_=ebuf[pp].t[:, 0:ncv], func=AF.Copy, scale=dn.t[:, 1:2]),
                      reads=[ebuf[pp], dn], writes=[pbf[pp]])
                ptk = banks[3] if pp == 0 else banks[0]
                ptb = ptk.t[:, :].bitcast(BF16)
                for ct in range(nct):
                    w_ = min(128, ncv - ct * 128)
                    kb.op("pe", lambda e, ct=ct, w_=w_: e.transpose(out=ptb[0:w_, ct * 128:(ct + 1) * 128], in_=pbf[pp].t[:, ct * 128:ct * 128 + w_],
                                                                    identity=identb.t[:]), reads=[pbf[pp], identb], writes=[ptk])
                hlf = (nct + 1) // 2
                for (ca, cb, en) in ((0, hlf, "act"), (hlf, nct, "dve")):
                    if cb <= ca:
                        continue
                    wl = min(128, ncv - (cb - 1) * 128)
                    if wl == 128 or cb - ca == 1:
                        w_ = wl if cb - ca == 1 else 128
                        src_ = ptb[0:w_, ca * 128:cb * 128].rearrange("p (c i) -> p c i", i=128)
                        dst_ = pTc[pp].t[0:w_, ca:cb, :]
                        if en == "act":
                            kb.op("act", lambda e, src_=src_, dst_=dst_: e.copy(out=dst_, in_=src_), reads=[ptk], writes=[pTc[pp]])
                        else:
                            kb.op("dve", lambda e, src_=src_, dst_=dst_: e.tensor_copy(out=dst_, in_=src_), reads=[ptk], writes=[pTc[pp]])
                    else:
                        for (a_, b_, w_) in ((ca, cb - 1, 128), (cb - 1, cb, wl)):
                            src_ = ptb[0:w_, a_ * 128:b_ * 128].rearrange("p (c i) -> p c i", i=128)
                            dst_ = pTc[pp].t[0:w_, a_:b_, :]
                            kb.op("act", lambda e, src_=src_, dst_=dst_: e.copy(out=dst_, in_=src_), reads=[ptk], writes=[pTc[pp]])
                for ct in range(nct):
                    w_ = min(128, ncv - ct * 128)
                    kb.mm(banks[4].t[:, 0:64], pTc[pp].t[0:w_, ct, :], vcmp.t[0:w_, ct, :], ct == 0, ct == nct - 1, [pTc[pp], vcmp], [banks[4]])
                kb.op("dve", lambda e: e.tensor_scalar(out=ofin[j].t[:, h, :], in0=banks[4].t[:, 0:64], scalar1=gates.t[:, j, h:h + 1], scalar2=None, op0=ALU.mult),
                      reads=[banks[4], gates], writes=[ofin[j]])

            def select(j):
                qb = qs * 4 + j
                sub = slice(j * 128, (j + 1) * 128)
                kb.op("dve", lambda e: e.tensor_tensor(out=imp.t[:, :], in0=pgv[:, :, 0], in1=pgv[:, :, 1], op=ALU.add), reads=[pg], writes=[imp])
                kb.op("dve", lambda e: e.tensor_tensor(out=imp.t[:, :], in0=imp.t[:, :], in1=pgv[:, :, 2], op=ALU.add), reads=[pg, imp], writes=[imp])
                kb.op("dve", lambda e: e.scalar_tensor_tensor(out=imp.t[:, :], in0=imp.t[:, :], scalar=2.0, in1=pgv[:, :, 3], op0=ALU.mult, op1=ALU.add),
                      reads=[pg, imp], writes=[imp])
                kb.op("dve", lambda e: e.tensor_tensor(out=imp.t[:, 1:256], in0=imp.t[:, 1:256], in1=pgv[:, 0:255, 3], op=ALU.add), reads=[pg, imp], writes=[imp])
                kb.op("dve", lambda e: e.tensor_tensor(out=imp.t[:, :], in0=imp.t[:, :], in1=Aw.t[:, 256 - 2 * qb:512 - 2 * qb], op=ALU.add), reads=[Aw, imp], writes=[imp])
                kb.op("dve", lambda e: e.memset(imp.t[:, 0:1], BIGF), reads=[imp], writes=[imp])
                kb.op("dve", lambda e: e.max(out=m8.t[:, 0:8], in_=imp.t[:, :]), reads=[imp], writes=[m8])
                kb.op("dve", lambda e: e.match_replace(out=sc2.t[:, :], in_to_replace=m8.t[:, 0:8], in_values=imp.t[:, :], imm_value=2 * BIGI), reads=[imp, m8], writes=[sc2])
                kb.op("dve", lambda e: e.max(out=m8.t[:, 8:16], in_=sc2.t[:, :]), reads=[sc2], writes=[m8])
                kb.op("dve", lambda e: e.tensor_scalar(out=m8.t[:, 15:16], in0=m8.t[:, 15:16], scalar1=0.1 * BIGI, scalar2=None, op0=ALU.max), reads=[m8], writes=[m8])
                kb.op("dve", lambda e: e.tensor_scalar(out=nsel.t[:, :], in0=imp.t[:, :], scalar1=m8.t[:, 15:16], scalar2=-1.0, op0=ALU.is_ge, op1=ALU.add),
                      reads=[imp, m8], writes=[nsel])
                ptb = banks[3].t[:, :].bitcast(BF16)
                for ch in range(2):
                    kb.op("pe", lambda e, ch=ch: e.transpose(out=ptb[:, ch * 128:(ch + 1) * 128], in_=nsel.t[:, ch * 128:(ch + 1) * 128], identity=identb.t[:]),
                          reads=[nsel, identb], writes=[banks[3]])
                kb.op("act", lambda e: e.copy(out=negselT.t[:, :, sub], in_=ptb[:, 0:256].rearrange("p (c i) -> p c i", c=2)), reads=[banks[3]], writes=[negselT])

            items = [(j, h) for j in range(4) for h in range(4)]
            cmpA(items[0][0], items[0][1], 0)
            for n_, (j, h) in enumerate(items):
                if n_ + 1 < len(items):
                    cmpA(items[n_ + 1][0], items[n_ + 1][1], (n_ + 1) % 2)
                cmpB(j, h, n_ % 2)
                if h == 3:
                    select(j)
            its = []
            for br in range(2):
                kt_lo = 0 if br == 0 else max(0, 4 * qs - 4)
                kt_hi = 4 * qs + 3
                for h in range(4):
                    for kt in range(kt_lo, kt_hi + 1):
                        its.append((br, h, kt, kt == kt_lo, kt == kt_hi))
            sbanks = (banks[5], banks[6], banks[1])
            pobanks = (banks[7], banks[2])

            def selA(n_):
                br, h, kt, first, last = its[n_]
                ps = sbanks[n_ % 3]
                qq = qsel[h] if br == 0 else qwin[h]
                wide = wsel if br == 0 else wwin
                dl = 4 * qs - kt
                kb.mm(ps.t[:, :], kswT.t[:, kt * 128:(kt + 1) * 128], qq.t[:, :], True, br == 1, [kswT, qq], [ps])
                if br == 0:
                    kb.mm(ps.t[:, :], Exp_.t[:, kt % 64, :], negselT.t[:, kt // 64, :], False, True, [Exp_, negselT], [ps])
                off = 384 + 128 * (min(dl, SEL_FARD) if br == 0 else dl)
                it = n_ % 3
                kb.op("dve", lambda e: e.scalar_tensor_tensor(out=tmp[it].t[:, :], in0=ps.t[:, :], scalar=scale, in1=wide.t[:, h, off:off + 512],
                                                              op0=ALU.mult, op1=ALU.add), reads=[ps, wide], writes=[tmp[it]])
                kb.op("act", lambda e: e.activation(out=pT[it].t[:, :], in_=tmp[it].t[:, :], func=AF.Exp), reads=[tmp[it]], writes=[pT[it]])

            def selB(n_, grp):
                br, h, kt, first, last = its[n_]
                it = n_ % 3
                Vt = Vs if br == 0 else Vw
                po = pobanks[grp % 2]
                kb.mm(po.t[0:65, :], Vt.t[:, kt, 0:65], pT[it].t[:, :], first, last, [pT[it], Vt], [po])
                if not last:
                    return
                pS = poS[grp % 2]
                kb.op("act", lambda e: e.copy(out=pS.t[:, :], in_=po.t[0:65, :]), reads=[po], writes=[pS])
                pt = banks[4]
                for j in range(4):
                    kb.op("pe", lambda e, j=j: e.transpose(out=pt.t[:, j * 65:(j + 1) * 65], in_=pS.t[0:65, j * 128:(j + 1) * 128], identity=identf.t[0:65, 0:65]),
                          reads=[pS, identf], writes=[pt])
                for j in range(4):
                    gcol = (1 + br) * 4 + h
                    kb.op("dve", lambda e, j=j: e.reciprocal(out=fcol.t[:, j:j + 1], in_=pt.t[:, j * 65 + 64:j * 65 + 65]), reads=[pt], writes=[fcol])
                    kb.op("dve", lambda e, j=j, gcol=gcol: e.tensor_tensor(out=fcol.t[:, 4 + j:5 + j], in0=fcol.t[:, j:j + 1], in1=gates.t[:, j, gcol:gcol + 1], op=ALU.mult),
                          reads=[fcol, gates], writes=[fcol])
                    kb.op("dve", lambda e, j=j: e.scalar_tensor_tensor(out=ofin[j].t[:, h, :], in0=pt.t[:, j * 65:j * 65 + 64], scalar=fcol.t[:, 4 + j:5 + j], in1=ofin[j].t[:, h, :],
                                                                       op0=ALU.mult, op1=ALU.add), reads=[pt, fcol, ofin[j]], writes=[ofin[j]])

            grp = 0
            if NSA_STAGE == 7:
                its = []
                continue
            selA(0)
            if len(its) > 1:
                selA(1)
            for n_ in range(len(its)):
                if n_ + 2 < len(its):
                    selA(n_ + 2)
                selB(n_, grp)
                if its[n_][4]:
                    grp += 1
            for j in range(4):
                kb.op("act", lambda e, j=j: e.copy(out=ogb.t[:, :], in_=ofin[j].t[:].rearrange("p a b -> p (a b)")), reads=[ofin[j]], writes=[ogb])
                ptb = banks[3].t[:, :].bitcast(BF16)
                for c in range(2):
                    kb.op("pe", lambda e, c=c, ptb=ptb: e.transpose(out=ptb[:, c * 128:(c + 1) * 128], in_=ogb.t[:, c * 128:(c + 1) * 128], identity=identb.t[:]),
                          reads=[ogb, identb], writes=[banks[3]])
                kb.op("act", lambda e, ptb=ptb: e.copy(out=oTt.t[:].rearrange("p a b -> p (a b)"), in_=ptb[:, 0:256]), reads=[banks[3]], writes=[oTt])
                t0 = qs * 512 + j * 128
                kb.dma("pool", ov[:, :, t0:t0 + 128], oTt.t[:], reads=[oTt], writes=[oT])
        kb.finish([oT])
    return nc


def nsa_inputs(hT_b, w_in, pe, w1, b1, w2, tab, g):
    q = w_in[:, 0:1024].reshape(D, 4, 4, 64)[:, g]
    wqd = np.concatenate([q, q], axis=2).reshape(D, 512)
    blk = lambda i: w_in[:, 1024 + i * 256 + g * 64:1024 + i * 256 + (g + 1) * 64]
    wkv = np.concatenate([blk(0), blk(1), blk(2), blk(4), blk(3), blk(5)], axis=1)
    gates = w_in[:, 2560:2608].reshape(D, 3, 4, 4)[:, :, g, :].reshape(D, 12)
    tb = np.full((33, 4), NEG, np.float32)
    tb[:32] = tab[:, g * 4:(g + 1) * 4]
    ohs, ohw, ohc = nsa_onehots()
    return {"hT": hT_b, "wqd": np.ascontiguousarray(wqd), "wkv": np.ascontiguousarray(wkv), "wg": np.ascontiguousarray(gates),
            "pe": np.ascontiguousarray(pe), "w1": np.ascontiguousarray(w1), "b1": np.ascontiguousarray(b1), "w2": np.ascontiguousarray(w2),
            "tab": tb, "ohs": ohs, "ohw": ohw, "ohc": ohc}


_NC_CACHE = {}


def _get(name, fn, *a):
    key = (name,) + a
    if key not in _NC_CACHE:
        _NC_CACHE[key] = fn(*a)
    return _NC_CACHE[key]


def _run(nc, in_maps):
    res = run_bass_kernel_spmd(nc, in_maps, core_ids=list(range(NCORES)))
    return res.results


def kernel(x, c, rel_table, mod_w, mod_b, ln_g, ln_b,
           gla_w_in, gla_w_a2, gla_b_a, gla_norm_g, gla_w_o,
           nsa_w_in, nsa_cmp_pe, nsa_cmp_w1, nsa_cmp_b1, nsa_cmp_w2, nsa_w_o,
           dil_w_in, dil_w_o,
           ffn_w_up, ffn_conv_w, ffn_conv_b, ffn_w_down):
    f32 = lambda a: np.ascontiguousarray(np.asarray(a, dtype=np.float32))
    x = f32(x); c = f32(c); rel_table = f32(rel_table); mod_w = f32(mod_w); mod_b = f32(mod_b)
    ln_g = f32(ln_g); ln_b = f32(ln_b)
    S = x.shape[1]
    T = S // 4
    dbg = globals().get("_DBG")
    res = _run(_get("mod", build_mod), [{"c": c, "w": f32(mod_w[s // 2, s % 2]), "b": f32(mod_b[s // 2, s % 2][None])} for s in range(8)])
    mod = [r["out"] for r in res]

    def shards():
        for k in range(NCORES):
            yield k, k // 4, (k % 4) * T

    res = _run(_get("prep", build_prep, T), [{"x": f32(x[b, t0:t0 + T]), "vec": f32(np.stack([mod[0][b, 0:D], mod[0][b, D:2 * D]]))} for k, b, t0 in shards()])
    hT = np.zeros((NB, D, S), ml_dtypes.bfloat16)
    for (k, b, t0), r in zip(shards(), res):
        hT[b, :, t0:t0 + T] = r["hT"]
    xcur = x
    for i in range(DEPTH):
        kind, j = i % 3, i // 3
        ins = []
        for k in range(NCORES):
            b, g = k // 4, k % 4
            hb = np.ascontiguousarray(hT[b])
            if kind == 0:
                w_in = f32(gla_w_in[j])
                wa2 = np.zeros((33, 128), np.float32)
                wa2[:16] = f32(gla_w_a2[j])[:, g * 128:(g + 1) * 128]
                wa2[32] = f32(gla_b_a[j])[g * 128:(g + 1) * 128]
                ins.append({"hT": hb, "wq": f32(w_in[:, g * 128:(g + 1) * 128]), "wk": f32(w_in[:, 512 + g * 128:512 + (g + 1) * 128]),
                            "wv": f32(w_in[:, 1024 + g * 256:1024 + (g + 1) * 256]), "wr": f32(w_in[:, 2048 + g * 256:2048 + (g + 1) * 256]),
                            "wa": f32(w_in[:, 3072:3088]), "wa2": wa2, "ng": f32(gla_norm_g[j])[None]})
            elif kind == 1:
                ins.append(nsa_inputs(hb, f32(nsa_w_in[j]), f32(nsa_cmp_pe[j]), f32(nsa_cmp_w1[j]), f32(nsa_cmp_b1[j]), f32(nsa_cmp_w2[j]), rel_table, g))
            else:
                wgd = f32(dil_w_in[j]).reshape(D, 3, 3, 1024)
                wqkv = np.stack([np.concatenate([wgd[:, gg, cc, g * 256:(g + 1) * 256] for cc in range(3)], axis=1) for gg in range(3)])
                tb = np.full((33, 4), NEG, np.float32)
                tb[:32] = rel_table[:, g * 4:(g + 1) * 4]
                ins.append({"hT": hb, "wqkv": f32(wqkv), "tab": tb, "oh": dil_onehots()})
        ncm = _get(("gla", "nsa", "dil")[kind], (build_gla, build_nsa, build_dil)[kind], S)
        res = _run(ncm, ins)
        oT = np.zeros((NB, D, S), ml_dtypes.bfloat16)
        for k in range(NCORES):
            oT[k // 4, (k % 4) * 256:(k % 4 + 1) * 256, :] = res[k]["oT"]
        w_o = f32((gla_w_o, nsa_w_o, dil_w_o)[kind][j])
        ins = []
        for k, b, t0 in shards():
            xh = np.zeros((T + 128, D), np.float32)
            oh_ = np.zeros((D, T + 128), ml_dtypes.bfloat16)
            xh[128:] = xcur[b, t0:t0 + T]
            oh_[:, 128:] = oT[b, :, t0:t0 + T]
            if t0 > 0:
                xh[:128] = xcur[b, t0 - 128:t0]
                oh_[:, :128] = oT[b, :, t0 - 128:t0]
            vec = np.zeros((10, D), np.float32)
            vec[0] = mod[2 * i][b, 2 * D:3 * D]
            vec[1] = mod[2 * i + 1][b, 0:D]; vec[2] = mod[2 * i + 1][b, D:2 * D]; vec[3] = mod[2 * i + 1][b, 2 * D:3 * D]
            if i + 1 < DEPTH:
                vec[4] = mod[2 * i + 2][b, 0:D]; vec[5] = mod[2 * i + 2][b, D:2 * D]
            vec[6] = ln_g[i, 0]; vec[7] = ln_b[i, 0]; vec[8] = ln_g[i, 1]; vec[9] = ln_b[i, 1]
            ins.append({"x": xh, "oT": oh_, "wo": w_o, "wup": f32(ffn_w_up[i]), "wdn": f32(ffn_w_down[i]), "convw": f32(ffn_conv_w[i]),
                        "convb": f32(ffn_conv_b[i]), "vec": vec, "flag": np.full((128, 1), 0.0 if t0 == 0 else 1.0, np.float32)})
        res = _run(_get("post", build_post, T), ins)
        xn = np.zeros((NB, S, D), np.float32)
        for (k, b, t0), r in zip(shards(), res):
            xn[b, t0:t0 + T] = r["xo"]
            hT[b, :, t0:t0 + T] = r["hT"]
        xcur = xn
        if dbg is not None:
            dbg.append(xn.copy())
    return xcur
```

```python
import math
from contextlib import ExitStack
import numpy as np
import ml_dtypes
import concourse.bass as bass
import concourse.mybir as mybir
from concourse.bass_utils import run_bass_kernel_spmd

F32 = mybir.dt.float32
BF16 = mybir.dt.bfloat16
AF = mybir.ActivationFunctionType
ALU = mybir.AluOpType
AX = mybir.AxisListType

D = 1024
SEQ = 16384
NB = 2
DEPTH = 4
D_FF = 2816
NFC = D_FF // 128
DN_ALPHA = (2 * DEPTH) ** 0.25
LN_EPS = 1e-5
NEG = -30000.0
NCORES = 8


class Buf:
    __slots__ = ("w", "r", "t", "parts")

    def __init__(self, t=None, nparts=0):
        self.w = None
        self.r = {}
        self.t = t
        self.parts = [Buf(t) for _ in range(nparts)]

    def __getitem__(self, k):
        return self.t[k]


class Eng:
    def __init__(self, name, h, sem):
        self.name, self.h, self.sem = name, h, sem
        self.count = 0
        self.waited = {}


class KB:
    def __init__(self, nc, es, ndma=40):
        self.nc, self.es = nc, es
        self.eng = {}
        for name, h in (("pe", nc.tensor), ("act", nc.scalar), ("dve", nc.vector), ("pool", nc.gpsimd), ("sp", nc.sync)):
            self.eng[name] = Eng(name, h, es.enter_context(nc.semaphore("sem_" + name)))
        self.dsl = [[es.enter_context(nc.semaphore("dsem%d" % i)), 0] for i in range(ndma)]
        self.di = {"sp": 0, "pool": 0, "act": 0}
        self.dq = {"sp": list(range(0, ndma // 2)), "pool": list(range(ndma // 2, ndma - 4)), "act": list(range(ndma - 4, ndma))}
        self.nt = 0

    def sb(self, shape, dt=F32, name=None, nparts=0, es=None):
        self.nt += 1
        return Buf((es or self.es).enter_context(self.nc.sbuf_tensor(name or "t%d" % self.nt, list(shape), dt)), nparts)

    def ps(self, shape=(128, 512), dt=F32, name=None):
        self.nt += 1
        return Buf(self.es.enter_context(self.nc.psum_tensor(name or "p%d" % self.nt, list(shape), dt)))

    def dram(self, name, shape, dt, kind="Internal"):
        return Buf(self.nc.dram_tensor(name, list(shape), dt, kind=kind).ap())

    def _semof(self, key):
        if isinstance(key, str):
            return self.eng[key].sem
        return self.dsl[key][0]

    def _wait(self, E, deps):
        need = {}
        for key, val in deps:
            if key == E.name and key in ("pe", "sp"):
                continue
            if val > need.get(key, 0):
                need[key] = val
        for key, val in need.items():
            if E.waited.get(key, 0) >= val:
                continue
            E.h.wait_ge(self._semof(key), val)
            E.waited[key] = val

    @staticmethod
    def _deps(reads, writes):
        deps = []
        for b in reads:
            if b.w is not None:
                deps.append(b.w)
        for b in writes:
            if b.w is not None:
                deps.append(b.w)
            deps.extend(b.r.items())
        return deps

    @staticmethod
    def _record(ev, reads, writes):
        key, val = ev
        for b in reads:
            if b.r.get(key, 0) < val:
                b.r[key] = val
        for b in writes:
            b.w = ev
            b.r = {}

    def op(self, en, fn, reads=(), writes=()):
        E = self.eng[en]
        self._wait(E, self._deps(reads, writes))
        ins = fn(E.h)
        E.count += 1
        ins.then_inc(E.sem, 1)
        self._record((en, E.count), reads, writes)

    def dma(self, qn, out, in_, reads=(), writes=(), **kw):
        Q = self.eng[qn]
        k = self.dq[qn][self.di[qn] % len(self.dq[qn])]
        self.di[qn] += 1
        slot = self.dsl[k]
        deps = self._deps(reads, writes)
        if slot[1] > 0:
            deps.append((k, slot[1]))
        self._wait(Q, deps)
        Q.h.dma_start(out=out, in_=in_, **kw).then_inc(slot[0], 16)
        slot[1] += 16
        self._record((k, slot[1]), reads, writes)

    def barrier(self):
        deps = [(n, E.count) for n, E in self.eng.items() if E.count > 0]
        deps += [(k, sl[1]) for k, sl in enumerate(self.dsl) if sl[1] > 0]
        for E in self.eng.values():
            self._wait(E, [d for d in deps if d[0] != E.name])

    def finish(self, outs):
        E = self.eng["sp"]
        self._wait(E, [b.w for b in outs if b.w is not None])

    def mm(self, out, lhsT, rhs, start, stop, reads, writes):
        self.op("pe", lambda e: e.matmul(out, lhsT=lhsT, rhs=rhs, start=start, stop=stop), reads, writes)


def new_nc():
    return bass.Bass("TRN2", target_bir_lowering=False)


def dr_in(nc, name, shape, dt=F32):
    return Buf(nc.dram_tensor(name, list(shape), dt, kind="ExternalInput").ap())


def dr_out(nc, name, shape, dt=F32):
    return Buf(nc.dram_tensor(name, list(shape), dt, kind="ExternalOutput").ap())


def load_bcast(kb, q, dst, src_row_ap, n=128):
    kb.dma(q, dst.t[0:n, :], src_row_ap.partition_broadcast(n), writes=[dst])


def make_ident(kb, dt=BF16):
    idf = kb.sb([128, 128], F32)
    kb.op("pool", lambda e: e.memset(idf[:], 0.0), writes=[idf])
    kb.op("pool", lambda e: e.affine_select(out=idf[:], in_=idf[:], pattern=[[-1, 128]], compare_op=ALU.not_equal,
                                             fill=1.0, base=0, channel_multiplier=1), reads=[idf], writes=[idf])
    if dt == F32:
        return idf
    idb = kb.sb([128, 128], dt)
    kb.op("dve", lambda e: e.tensor_copy(out=idb[:], in_=idf[:]), reads=[idf], writes=[idb])
    return idb


def gen_load_w_bf16(kb, dst, src_ap_fn, nk, ncols, stage, qs=("sp", "pool"), chunk=512, npart=128):
    i = 0
    for k in range(nk):
        for c0 in range(0, ncols, chunk):
            c1 = min(ncols, c0 + chunk)
            st = stage[i % len(stage)]
            kb.dma(qs[i % len(qs)], st.t[0:npart, 0:c1 - c0], src_ap_fn(k, c0, c1), writes=[st])
            if i % 2:
                kb.op("act", lambda e, st=st, k=k, c0=c0, c1=c1: e.copy(out=dst.t[0:npart, k, c0:c1], in_=st.t[0:npart, 0:c1 - c0]),
                      reads=[st], writes=[dst.parts[k]])
            else:
                kb.op("dve", lambda e, st=st, k=k, c0=c0, c1=c1: e.tensor_copy(out=dst.t[0:npart, k, c0:c1], in_=st.t[0:npart, 0:c1 - c0]),
                      reads=[st], writes=[dst.parts[k]])
            i += 1
            yield


def load_w_bf16(kb, dst, src_ap_fn, nk, ncols, stage, qs=("sp", "pool"), chunk=512, npart=128):
    i = 0
    for k in range(nk):
        for c0 in range(0, ncols, chunk):
            c1 = min(ncols, c0 + chunk)
            st = stage[i % len(stage)]
            kb.dma(qs[i % len(qs)], st.t[0:npart, 0:c1 - c0], src_ap_fn(k, c0, c1), writes=[st])
            en = "act" if i % 2 else "dve"
            if en == "act":
                kb.op("act", lambda e, st=st, k=k, c0=c0, c1=c1: e.copy(out=dst.t[0:npart, k, c0:c1], in_=st.t[0:npart, 0:c1 - c0]),
                      reads=[st], writes=[dst.parts[k]])
            else:
                kb.op("dve", lambda e, st=st, k=k, c0=c0, c1=c1: e.tensor_copy(out=dst.t[0:npart, k, c0:c1], in_=st.t[0:npart, 0:c1 - c0]),
                      reads=[st], writes=[dst.parts[k]])
            i += 1


def emit_ln(kb, z, st, mv, lng, lnb):
    for c in range(2):
        kb.op("dve", lambda e, c=c: e.bn_stats(out=st.t[:, c, :], in_=z.t[:, c * 512:(c + 1) * 512]), reads=[z], writes=[st])
    kb.op("dve", lambda e: e.bn_aggr(out=mv.t[:, 0:2], in_=st.t[:].rearrange("p a b -> p (a b)")), reads=[st], writes=[mv])
    kb.op("dve", lambda e: e.tensor_scalar_add(out=mv.t[:, 2:3], in0=mv.t[:, 1:2], scalar1=LN_EPS), reads=[mv], writes=[mv])
    kb.op("act", lambda e: e.sqrt(out=mv.t[:, 2:3], in_=mv.t[:, 2:3]), reads=[mv], writes=[mv])
    kb.op("dve", lambda e: e.reciprocal(out=mv.t[:, 2:3], in_=mv.t[:, 2:3]), reads=[mv], writes=[mv])
    kb.op("dve", lambda e: e.tensor_scalar(out=z.t[:], in0=z.t[:], scalar1=mv.t[:, 0:1], scalar2=mv.t[:, 2:3],
                                           op0=ALU.subtract, op1=ALU.mult), reads=[z, mv], writes=[z])
    kb.op("pool", lambda e: e.tensor_tensor(out=z.t[:], in0=z.t[:], in1=lng.t[:], op=ALU.mult), reads=[z, lng], writes=[z])
    kb.op("pool", lambda e: e.tensor_tensor(out=z.t[:], in0=z.t[:], in1=lnb.t[:], op=ALU.add), reads=[z, lnb], writes=[z])


def emit_modT(kb, x, pst, identf, scp, sh, ht, ncol=128, c0=0):
    for k in range(8):
        kb.op("pe", lambda e, k=k: e.transpose(out=pst.t[:, k * 128:(k + 1) * 128], in_=x.t[:, k * 128:(k + 1) * 128],
                                               identity=identf.t[:]), reads=[x, identf], writes=[pst])
    for k in range(8):
        kb.op("act", lambda e, k=k: e.activation(out=ht.t[:, k, c0:c0 + 128], in_=pst.t[:, k * 128:(k + 1) * 128],
                                                 func=AF.Identity, scale=scp.t[:, k:k + 1], bias=sh.t[:, k:k + 1]),
              reads=[pst, scp, sh], writes=[ht])


def load_pp(kb, q, dst, row_ap, nk=8):
    kb.dma(q, dst.t[:, 0:nk], row_ap.rearrange("(k p) -> p k", p=128), writes=[dst], allow_slow_non_contiguous=True)


def build_prep(T):
    nc = new_nc()
    x = dr_in(nc, "x", [T, D])
    vec = dr_in(nc, "vec", [2, D])
    hTo = dr_out(nc, "hT", [D, T], BF16)
    with ExitStack() as es:
        kb = KB(nc, es)
        identf = make_ident(kb, F32)
        sh = kb.sb([128, 8]); scp = kb.sb([128, 8])
        load_pp(kb, "sp", sh, vec.t[0, :]); load_pp(kb, "sp", scp, vec.t[1, :])
        kb.op("dve", lambda e: e.tensor_scalar_add(out=scp.t[:], in0=scp.t[:], scalar1=1.0), reads=[scp], writes=[scp])
        xs = [kb.sb([128, D]) for _ in range(3)]
        hts = [kb.sb([128, 8, 128], BF16) for _ in range(2)]
        psts = [kb.ps([128, 1024]) for _ in range(2)]
        hv = hTo.t.rearrange("(k p) t -> p k t", p=128)
        for i in range(T // 128):
            xt = xs[i % 3]; ht = hts[i % 2]; pst = psts[i % 2]
            kb.dma("sp", xt.t[:], x.t[i * 128:(i + 1) * 128, :], writes=[xt])
            emit_modT(kb, xt, pst, identf, scp, sh, ht)
            kb.dma("pool", hv[:, :, i * 128:(i + 1) * 128], ht.t[:], reads=[ht], writes=[hTo])
        kb.finish([hTo])
    return nc


def build_post(T):
    TT = 256
    nc = new_nc()
    x = dr_in(nc, "x", [T + 128, D])
    oT = dr_in(nc, "oT", [D, T + 128], BF16)
    wo = dr_in(nc, "wo", [D, D])
    wup = dr_in(nc, "wup", [D, 2 * D_FF])
    wdn = dr_in(nc, "wdn", [D_FF, D])
    convw = dr_in(nc, "convw", [3, D_FF])
    convb = dr_in(nc, "convb", [D_FF])
    vec = dr_in(nc, "vec", [10, D])
    flag = dr_in(nc, "flag", [128, 1])
    xo = dr_out(nc, "xo", [T, D])
    hTo = dr_out(nc, "hT", [D, T], BF16)
    x1s = Buf(nc.dram_tensor("x1s", [T, D], F32, kind="Internal").ap())
    h1s = Buf(nc.dram_tensor("h1s", [D, T], BF16, kind="Internal").ap())
    ntile = T // 128
    with ExitStack() as es:
        kb = KB(nc, es)
        identf = make_ident(kb, F32)
        banks = [kb.ps([128, 512]) for _ in range(4)]
        pst2 = [kb.ps([128, 1024]) for _ in range(2)]
        h1halo = kb.sb([128, 8, 128], BF16)
        pp = kb.sb([128, 6, 8])
        ppb = [Buf(pp.t) for _ in range(4)]
        for j, r in enumerate((1, 2, 4, 5)):
            kb.dma("sp", pp.t[:, j, :], vec.t[r, :].rearrange("(k p) -> p k", p=128), writes=[ppb[j]], allow_slow_non_contiguous=True)
        for j in (1, 3):
            kb.op("dve", lambda e, j=j: e.tensor_scalar_add(out=pp.t[:, j, :], in0=pp.t[:, j, :], scalar1=1.0), reads=[ppb[j]], writes=[ppb[j]])

        class PV:
            pass
        sh2 = Buf(pp.t[:, 0, :]); sc2 = Buf(pp.t[:, 1, :]); shn = Buf(pp.t[:, 2, :]); scn = Buf(pp.t[:, 3, :])
        for v, b in ((sh2, ppb[0]), (sc2, ppb[1]), (shn, ppb[2]), (scn, ppb[3])):
            v.w = b.w
        flg = kb.sb([128, 1])
        kb.dma("sp", flg.t[:], flag.t[:, :], writes=[flg])
        st = kb.sb([128, 2, 6]); mv = kb.sb([128, 4])
        zs = [kb.sb([128, D]) for _ in range(2)]
        xs = [kb.sb([128, D]) for _ in range(2)]
        hts = [kb.sb([128, 8, 128], BF16) for _ in range(2)]
        g1p = kb.sb([128, D]); lng = kb.sb([128, D]); lnb = kb.sb([128, D])

        def load_gl(gr, lgr, lbr):
            load_bcast(kb, "sp", g1p, vec.t[gr:gr + 1, :])
            load_bcast(kb, "sp", lng, vec.t[lgr:lgr + 1, :])
            load_bcast(kb, "sp", lnb, vec.t[lbr:lbr + 1, :])
            kb.op("pool", lambda e: e.tensor_scalar_add(out=g1p.t[:], in0=g1p.t[:], scalar1=1.0), reads=[g1p], writes=[g1p])

        load_gl(0, 6, 7)
        h1v = h1s.t.rearrange("(k p) t -> p k t", p=128)
        hov = hTo.t.rearrange("(k p) t -> p k t", p=128)
        oTv = oT.t.rearrange("(k p) t -> p k t", p=128)

        def resid_ln(psy, xt, z):
            for half in range(2):
                kb.op("dve", lambda e, half=half: e.tensor_tensor(out=z.t[:, half * 512:(half + 1) * 512], in0=psy[half].t[:, :],
                                                                   in1=g1p.t[:, half * 512:(half + 1) * 512], op=ALU.mult),
                      reads=[psy[half], g1p], writes=[z])
            kb.op("dve", lambda e: e.scalar_tensor_tensor(out=z.t[:], in0=xt.t[:], scalar=DN_ALPHA, in1=z.t[:],
                                                           op0=ALU.mult, op1=ALU.add), reads=[xt, z], writes=[z])
            emit_ln(kb, z, st, mv, lng, lnb)

        wub = kb.sb([128, 8, 2 * D_FF], BF16, nparts=8)
        wdb = kb.sb([128, NFC, D], BF16, nparts=NFC)
        stageB = [kb.sb([128, 512]) for _ in range(6)]

        def _wgen():
            yield from gen_load_w_bf16(kb, wub, lambda k, c0, c1: wup.t[k * 128:(k + 1) * 128, c0:c1], 8, 2 * D_FF, stageB, qs=("pool",))
            yield from gen_load_w_bf16(kb, wdb, lambda k, c0, c1: wdn.t[k * 128:(k + 1) * 128, c0:c1], NFC, D, stageB, qs=("pool",))
        wgen = _wgen()
        nchunks_w = 8 * 11 + NFC * 2
        per_tile = -(-nchunks_w // (ntile + 1))
        with ExitStack() as esA:
            wob = kb.sb([128, 8, D], BF16, nparts=8, es=esA)
            stage = [kb.sb([128, 512], es=esA) for _ in range(2)]
            ots = [kb.sb([128, 8, 128], BF16, es=esA) for _ in range(2)]
            load_w_bf16(kb, wob, lambda k, c0, c1: wo.t[k * 128:(k + 1) * 128, c0:c1], 8, D, stage)
            def loadA(i):
                kb.dma("sp", xs[i % 2].t[:], x.t[i * 128:(i + 1) * 128, :], writes=[xs[i % 2]])
                kb.dma("sp", ots[i % 2].t[:], oTv[:, :, i * 128:(i + 1) * 128], writes=[ots[i % 2]])

            loadA(0)
            for i in range(ntile + 1):
                xt = xs[i % 2]; z = zs[i % 2]; ot = ots[i % 2]; ht = hts[i % 2]
                psy = banks[2 * (i % 2):2 * (i % 2) + 2]
                for half in range(2):
                    for k in range(8):
                        kb.mm(psy[half].t[:, :], ot.t[:, k, :], wob.t[:, k, half * 512:(half + 1) * 512], k == 0, k == 7,
                              [ot, wob.parts[k]], [psy[half]])
                resid_ln(psy, xt, z)
                if i + 1 <= ntile:
                    loadA(i + 1)
                if i > 0:
                    kb.dma("sp", x1s.t[(i - 1) * 128:i * 128, :], z.t[:], reads=[z], writes=[x1s])
                emit_modT(kb, z, pst2[i % 2], identf, sc2, sh2, h1halo if i == 0 else ht)
                if i > 0:
                    kb.dma("sp", h1v[:, :, (i - 1) * 128:i * 128], ht.t[:], reads=[ht], writes=[h1s])
                for _ in range(per_tile):
                    next(wgen, None)
            for _ in wgen:
                pass

        kb.barrier()
        load_gl(3, 8, 9)
        with ExitStack() as esB:
            cw = kb.sb([128, 3, NFC], es=esB); cb = kb.sb([128, NFC], es=esB)
            for j in range(3):
                kb.dma("sp", cw.t[:, j, :], convw.t[j, :].rearrange("(k p) -> p k", p=128), writes=[cw], allow_slow_non_contiguous=True)
            kb.dma("sp", cb.t[:, :], convb.t[:].rearrange("(k p) -> p k", p=128), writes=[cb], allow_slow_non_contiguous=True)
            uprev = kb.sb([128, NFC, 2], nparts=NFC, es=esB)
            h1t = [kb.sb([128, 8, TT], BF16, es=esB) for _ in range(2)]
            aT = kb.sb([128, NFC, TT], BF16, nparts=NFC, es=esB)
            ubs = [kb.sb([128, TT + 2], es=esB) for _ in range(2)]
            cbs = [kb.sb([128, TT], es=esB) for _ in range(2)]
            sbs = [kb.sb([128, TT], es=esB) for _ in range(2)]
            pu = banks[0]
            for fc in range(NFC):
                for k in range(8):
                    kb.mm(pu.t[:, fc * 2:fc * 2 + 2], wub.t[:, k, fc * 128:(fc + 1) * 128], h1halo.t[:, k, 126:128], k == 0, k == 7,
                          [wub.parts[k], h1halo], [pu])
            kb.op("dve", lambda e: e.tensor_scalar_mul(out=uprev.t[:].rearrange("p a b -> p (a b)"), in0=pu.t[:, 0:2 * NFC],
                                                       scalar1=flg.t[:, 0:1]), reads=[pu, flg], writes=uprev.parts)
            nug = 0
            kb.dma("sp", h1t[0].t[:], h1v[:, :, 0:TT], reads=[h1s], writes=[h1t[0]])
            for it in range(T // TT):
                hh = h1t[it % 2]
                for sub in range(TT // 128):
                    i = it * (TT // 128) + sub
                    kb.dma("sp", xs[i % 2].t[:], x1s.t[i * 128:(i + 1) * 128, :], reads=[x1s], writes=[xs[i % 2]])
                if it + 1 < T // TT:
                    kb.dma("sp", h1t[(it + 1) % 2].t[:], h1v[:, :, (it + 1) * TT:(it + 2) * TT], reads=[h1s], writes=[h1t[(it + 1) % 2]])
                for fc in range(NFC):
                    pug = banks[nug % 2]; ub = ubs[nug % 2]; cbuf = cbs[nug % 2]; sbuf = sbs[nug % 2]; nug += 1
                    for k in range(8):
                        kb.mm(pug.t[:, 0:TT], wub.t[:, k, fc * 128:(fc + 1) * 128], hh.t[:, k, :], k == 0, k == 7, [wub.parts[k], hh], [pug])
                    for k in range(8):
                        kb.mm(pug.t[:, TT:2 * TT], wub.t[:, k, D_FF + fc * 128:D_FF + (fc + 1) * 128], hh.t[:, k, :], k == 0, k == 7,
                              [wub.parts[k], hh], [pug])
                    kb.op("pool", lambda e, ub=ub, fc=fc: e.tensor_copy(out=ub.t[:, 0:2], in_=uprev.t[:, fc, :]), reads=[uprev.parts[fc]], writes=[ub])
                    kb.op("act", lambda e, ub=ub, pug=pug: e.copy(out=ub.t[:, 2:TT + 2], in_=pug.t[:, 0:TT]), reads=[pug], writes=[ub])
                    kb.op("pool", lambda e, ub=ub, fc=fc: e.tensor_copy(out=uprev.t[:, fc, :], in_=ub.t[:, TT:TT + 2]), reads=[ub], writes=[uprev.parts[fc]])
                    kb.op("act", lambda e, ub=ub, cbuf=cbuf, fc=fc: e.activation(out=cbuf.t[:], in_=ub.t[:, 2:TT + 2], func=AF.Identity,
                                                                                 scale=cw.t[:, 2, fc:fc + 1], bias=cb.t[:, fc:fc + 1]),
                          reads=[ub, cw, cb], writes=[cbuf])
                    kb.op("dve", lambda e, ub=ub, cbuf=cbuf, fc=fc: e.scalar_tensor_tensor(out=cbuf.t[:], in0=ub.t[:, 1:TT + 1], scalar=cw.t[:, 1, fc:fc + 1],
                                                                                           in1=cbuf.t[:], op0=ALU.mult, op1=ALU.add),
                          reads=[ub, cw, cbuf], writes=[cbuf])
                    kb.op("dve", lambda e, ub=ub, cbuf=cbuf, fc=fc: e.scalar_tensor_tensor(out=cbuf.t[:], in0=ub.t[:, 0:TT], scalar=cw.t[:, 0, fc:fc + 1],
                                                                                           in1=cbuf.t[:], op0=ALU.mult, op1=ALU.add),
                          reads=[ub, cw, cbuf], writes=[cbuf])
                    kb.op("act", lambda e, cbuf=cbuf, sbuf=sbuf: e.activation(out=sbuf.t[:], in_=cbuf.t[:], func=AF.Silu), reads=[cbuf], writes=[sbuf])
                    kb.op("dve", lambda e, sbuf=sbuf, pug=pug, fc=fc: e.tensor_tensor(out=aT.t[:, fc, :], in0=pug.t[:, TT:2 * TT], in1=sbuf.t[:], op=ALU.mult),
                          reads=[pug, sbuf], writes=[aT.parts[fc]])
                for sub in range(TT // 128):
                    i = it * (TT // 128) + sub
                    psy = banks[2:4]
                    xt = xs[i % 2]; z = zs[i % 2]; ht = hts[i % 2]
                    for half in range(2):
                        for fc in range(NFC):
                            kb.mm(psy[half].t[:, :], aT.t[:, fc, sub * 128:(sub + 1) * 128], wdb.t[:, fc, half * 512:(half + 1) * 512],
                                  fc == 0, fc == NFC - 1, [aT.parts[fc], wdb.parts[fc]], [psy[half]])
                    resid_ln(psy, xt, z)
                    kb.dma("sp", xo.t[i * 128:(i + 1) * 128, :], z.t[:], reads=[z], writes=[xo])
                    emit_modT(kb, z, pst2[i % 2], identf, scn, shn, ht)
                    kb.dma("sp", hov[:, :, i * 128:(i + 1) * 128], ht.t[:], reads=[ht], writes=[hTo])
        kb.finish([xo, hTo])
    return nc


def build_mod():
    nc = new_nc()
    c = dr_in(nc, "c", [NB, D])
    w = dr_in(nc, "w", [D, 3 * D])
    b = dr_in(nc, "b", [1, 3 * D])
    out = dr_out(nc, "out", [NB, 3 * D])
    with ExitStack() as es:
        kb = KB(nc, es)
        cs = kb.sb([128, 8, NB])
        for bb in range(NB):
            kb.dma("sp", cs.t[:, :, bb], c.t[bb, :].rearrange("(k p) -> p k", p=128), writes=[cs], allow_slow_non_contiguous=True)
        kb.op("act", lambda e: e.activation(out=cs.t[:], in_=cs.t[:], func=AF.Silu), reads=[cs], writes=[cs])
        wt = kb.sb([128, 8, 3 * D], nparts=8)
        for k in range(8):
            kb.dma("sp" if k % 2 else "pool", wt.t[:, k, :], w.t[k * 128:(k + 1) * 128, :], writes=[wt.parts[k]])
        bt = kb.sb([NB, 3 * D])
        load_bcast(kb, "sp", bt, b.t[0:1, :], n=NB)
        ot = kb.sb([NB, 3 * D])
        banks = [kb.ps([128, 512]) for _ in range(6)]
        for n in range(6):
            for k in range(8):
                kb.mm(banks[n].t[0:NB, :], cs.t[:, k, :], wt.t[:, k, n * 512:(n + 1) * 512], k == 0, k == 7, [cs, wt.parts[k]], [banks[n]])
            kb.op("dve", lambda e, n=n: e.tensor_tensor(out=ot.t[:, n * 512:(n + 1) * 512], in0=banks[n].t[0:NB, :],
                                                         in1=bt.t[:, n * 512:(n + 1) * 512], op=ALU.add), reads=[banks[n], bt], writes=[ot])
        kb.dma("sp", out.t[:, :], ot.t[:], reads=[ot], writes=[out])
        kb.finish([out])
    return nc


GLA_DK, GLA_DV = 128, 256


def tri_const(kb, val, upper_strict):
    m = kb.sb([128, 128])
    kb.op("pool", lambda e: e.memset(m.t[:], val), writes=[m])
    if not upper_strict:
        kb.op("pool", lambda e: e.affine_select(out=m.t[:], in_=m.t[:], pattern=[[1, 128]], compare_op=ALU.is_ge, fill=0.0,
                                                 base=0, channel_multiplier=-1), reads=[m], writes=[m])
        kb.op("pool", lambda e: e.memset(m.t[0:64, 64:128], 0.0), reads=[m], writes=[m])
    else:
        kb.op("pool", lambda e: e.affine_select(out=m.t[:], in_=m.t[:], pattern=[[-1, 128]], compare_op=ALU.is_ge, fill=0.0,
                                                 base=-1, channel_multiplier=1), reads=[m], writes=[m])
        kb.op("pool", lambda e: e.memset(m.t[64:128, 0:64], 0.0), reads=[m], writes=[m])
    return m


GLA_STAGE = 7
GLA_SUB = 99


def build_gla(S):
    nc = new_nc()
    hT = dr_in(nc, "hT", [D, S], BF16)
    wq = dr_in(nc, "wq", [D, 128]); wk = dr_in(nc, "wk", [D, 128])
    wv = dr_in(nc, "wv", [D, 256]); wr = dr_in(nc, "wr", [D, 256]); wa = dr_in(nc, "wa", [D, 16])
    wa2 = dr_in(nc, "wa2", [33, 128])
    ng = dr_in(nc, "ng", [1, 256])
    oT = dr_out(nc, "oT", [256, S], BF16)
    with ExitStack() as es:
        kb = KB(nc, es)
        identb = make_ident(kb, BF16)
        Lneg = tri_const(kb, -1.0 / 16.0, False)
        Uneg = tri_const(kb, -1.0 / 16.0, True)
        M1 = tri_const(kb, 1.0, False)
        stage = [kb.sb([128, 512]) for _ in range(2)]
        wqb = kb.sb([128, 8, 128], BF16, nparts=8); wkb = kb.sb([128, 8, 128], BF16, nparts=8)
        wvb = kb.sb([128, 8, 256], BF16, nparts=8); wrb = kb.sb([128, 8, 256], BF16, nparts=8)
        wab = kb.sb([128, 8, 16], BF16, nparts=8)
        for dst, src, n in ((wqb, wq, 128), (wkb, wk, 128), (wvb, wv, 256), (wrb, wr, 256), (wab, wa, 16)):
            load_w_bf16(kb, dst, lambda k, c0, c1, src=src: src.t[k * 128:(k + 1) * 128, c0:c1], 8, n, stage)
        wa2b = kb.sb([33, 1, 128], BF16, nparts=1)
        load_w_bf16(kb, wa2b, lambda k, c0, c1: wa2.t[0:33, c0:c1], 1, 128, stage, npart=33)
        ngb = kb.sb([128, 256])
        load_bcast(kb, "sp", ngb, ng.t[0:1, :])
        alT = kb.sb([33, 512], BF16)
        kb.op("pool", lambda e: e.memset(alT.t[:], 0.0), writes=[alT])
        kb.op("pool", lambda e: e.memset(alT.t[32:33, :], 1.0), reads=[alT], writes=[alT])
        S32 = kb.sb([128, 256]); SbA = kb.sb([128, 256], BF16); SbB = kb.sb([128, 256], BF16)
        kb.op("pool", lambda e: e.memset(S32.t[:], 0.0), writes=[S32])
        kb.op("pool", lambda e: e.memset(SbA.t[:], 0.0), writes=[SbA])
        pq = kb.ps(); pk = kb.ps(); pal = kb.ps(); pD = kb.ps(); pE = kb.ps(); pF = kb.ps(); pG = kb.ps(); pH = kb.ps()
        pDk = pDl = pDt = pD
        pEv = pEr = pE
        pFb = pFr = pFa = pF
        pGa = pGb = pG
        pH0 = pH1 = pH
        pT = pD.t[:, 256:512].bitcast(BF16)
        hhs = [kb.sb([128, 8, 512], BF16) for _ in range(2)]
        R = 2
        ex = [kb.sb([128, 128]) for _ in range(R)]; ltm = [kb.sb([128, 128]) for _ in range(R)]
        ebT = [kb.sb([128, 128]) for _ in range(R)]; einvT = [kb.sb([128, 128]) for _ in range(R)]
        erem = [kb.sb([128, 128]) for _ in range(R)]
        qdT = [kb.sb([128, 128], BF16) for _ in range(R)]; kiT = [kb.sb([128, 128], BF16) for _ in range(R)]
        kdec = [kb.sb([128, 128], BF16) for _ in range(R)]; vsb = [kb.sb([128, 256], BF16) for _ in range(R)]
        kdec1 = [kb.sb([128, 128], BF16) for _ in range(R)]
        hm01 = kb.sb([128, 2])
        kb.op("pool", lambda e: e.memset(hm01.t[:], 0.0), writes=[hm01])
        kb.op("pool", lambda e: e.memset(hm01.t[0:64, 0:1], 1.0), reads=[hm01], writes=[hm01])
        kb.op("pool", lambda e: e.memset(hm01.t[64:128, 1:2], 1.0), reads=[hm01], writes=[hm01])
        sr = [kb.sb([128, 256]) for _ in range(R)]; gr = [kb.sb([128, 256]) for _ in range(R)]
        attm = [kb.sb([128, 128], BF16) for _ in range(R)]
        oint = [kb.sb([128, 256]) for _ in range(R)]; osb = [kb.sb([128, 256]) for _ in range(R)]
        junk = kb.sb([128, 256]); ss = [kb.sb([128, 2]) for _ in range(R)]
        og = [kb.sb([128, 256], BF16) for _ in range(R)]; oTt = [kb.sb([128, 2, 128], BF16) for _ in range(R)]
        hv = hT.t.rearrange("(k p) t -> p k t", p=128)
        ov = oT.t.rearrange("(c p) t -> p c t", p=128)
        scale = GLA_DK ** -0.5
        n = 0
        def superA(st):
            hh = hhs[st % 2]
            kb.dma("sp", hh.t[:], hv[:, :, st * 512:(st + 1) * 512], writes=[hh])
            for k in range(8):
                kb.mm(pq.t[:, :], wqb.t[:, k, :], hh.t[:, k, :], k == 0, k == 7, [wqb.parts[k], hh], [pq])
            for k in range(8):
                kb.mm(pk.t[:, :], wkb.t[:, k, :], hh.t[:, k, :], k == 0, k == 7, [wkb.parts[k], hh], [pk])
            for k in range(8):
                kb.mm(pal.t[0:16, :], wab.t[:, k, :], hh.t[:, k, :], k == 0, k == 7, [wab.parts[k], hh], [pal])
            kb.op("act", lambda e: e.copy(out=alT.t[0:16, :], in_=pal.t[0:16, :]), reads=[pal], writes=[alT])

        def tileA(st, j, r):
            hh = hhs[st % 2]
            sub = slice(j * 128, (j + 1) * 128)
            for k in range(8):
                kb.mm(pD.t[:, 0:128], hh.t[:, k, sub], wkb.t[:, k, :], k == 0, k == 7, [wkb.parts[k], hh], [pDk])
            kb.mm(pD.t[:, 128:256], alT.t[0:33, sub], wa2b.t[0:33, 0, :], True, True, [alT, wa2b.parts[0]], [pDl])
            for k in range(8):
                kb.mm(pE.t[:, 0:256], hh.t[:, k, sub], wvb.t[:, k, :], k == 0, k == 7, [wvb.parts[k], hh], [pEv])
            for k in range(8):
                kb.mm(pE.t[:, 256:512], hh.t[:, k, sub], wrb.t[:, k, :], k == 0, k == 7, [wrb.parts[k], hh], [pEr])
            if GLA_STAGE >= 2:
                kb.op("act", lambda e, r=r: e.activation(out=ex[r].t[:], in_=pD.t[:, 128:256], func=AF.Exp, scale=-1.0), reads=[pDl], writes=[ex[r]])
                kb.op("act", lambda e, r=r: e.activation(out=ltm[r].t[:], in_=ex[r].t[:], func=AF.Ln, bias=1.0), reads=[ex[r]], writes=[ltm[r]])
            if GLA_STAGE >= 3:
                kb.mm(pF.t[:, 0:128], ltm[r].t[:], Lneg.t[:], True, True, [ltm[r], Lneg], [pFb])
                kb.mm(pF.t[:, 128:256], Uneg.t[:], ltm[r].t[:], True, True, [ltm[r], Uneg], [pFr])
                kb.op("act", lambda e, r=r: e.activation(out=ebT[r].t[:], in_=pF.t[:, 0:128], func=AF.Exp), reads=[pFb], writes=[ebT[r]])
                kb.op("act", lambda e, r=r: e.activation(out=einvT[r].t[:], in_=pF.t[:, 0:128], func=AF.Exp, scale=-1.0), reads=[pFb], writes=[einvT[r]])
                kb.op("act", lambda e, r=r: e.activation(out=erem[r].t[:], in_=pF.t[:, 128:256], func=AF.Exp), reads=[pFr], writes=[erem[r]])
                kb.op("dve", lambda e, r=r, sub=sub: e.scalar_tensor_tensor(out=qdT[r].t[:], in0=pq.t[:, sub], scalar=scale, in1=ebT[r].t[:],
                                                                            op0=ALU.mult, op1=ALU.mult), reads=[pq, ebT[r]], writes=[qdT[r]])
                kb.op("dve", lambda e, r=r, sub=sub: e.tensor_tensor(out=kiT[r].t[:], in0=pk.t[:, sub], in1=einvT[r].t[:], op=ALU.mult),
                      reads=[pk, einvT[r]], writes=[kiT[r]])
                kb.op("dve", lambda e, r=r: e.scalar_tensor_tensor(out=kdec[r].t[:], in0=pD.t[:, 0:128], scalar=hm01.t[:, 0:1], in1=erem[r].t[:], op0=ALU.mult, op1=ALU.mult),
                      reads=[pDk, erem[r], hm01], writes=[kdec[r]])
                kb.op("dve", lambda e, r=r: e.scalar_tensor_tensor(out=kdec1[r].t[:], in0=pD.t[:, 0:128], scalar=hm01.t[:, 1:2], in1=erem[r].t[:], op0=ALU.mult, op1=ALU.mult),
                      reads=[pDk, erem[r], hm01], writes=[kdec1[r]])
                kb.op("act", lambda e, r=r: e.copy(out=vsb[r].t[:], in_=pE.t[:, 0:256]), reads=[pEv], writes=[vsb[r]])
                kb.op("act", lambda e, r=r: e.activation(out=sr[r].t[:], in_=pE.t[:, 256:512], func=AF.Silu), reads=[pEr], writes=[sr[r]])
                kb.op("pool", lambda e, r=r: e.tensor_tensor(out=gr[r].t[:], in0=sr[r].t[:], in1=ngb.t[:], op=ALU.mult), reads=[sr[r], ngb], writes=[gr[r]])

        def tileB(st, j, r):
            sub = slice(j * 128, (j + 1) * 128)
            if GLA_STAGE >= 4:
                kb.mm(pF.t[:, 256:384], kiT[r].t[:], qdT[r].t[:], True, True, [kiT[r], qdT[r]], [pFa])
                kb.op("dve", lambda e, r=r: e.tensor_tensor(out=attm[r].t[:], in0=pF.t[:, 256:384], in1=M1.t[:], op=ALU.mult),
                      reads=[pFa, M1], writes=[attm[r]])
                kb.mm(pG.t[:, 0:256], attm[r].t[:], vsb[r].t[:], True, True, [attm[r], vsb[r]], [pGa])
            if GLA_STAGE >= 5:
                if GLA_SUB >= 1:
                    kb.mm(pH.t[:, 0:256], kdec[r].t[:, :], vsb[r].t[:, :], True, True, [kdec[r], vsb[r]], [pH0])
                if GLA_SUB >= 2:
                    kb.mm(pH.t[:, 256:512], kdec1[r].t[:, :], vsb[r].t[:, :], True, True, [kdec1[r], vsb[r]], [pH1])
                if GLA_SUB >= 3:
                    kb.mm(pG.t[:, 256:512], qdT[r].t[:, :], SbA.t[:], True, True, [qdT[r], SbA], [pGb])
                if GLA_SUB >= 4:
                    kb.op("dve", lambda e, r=r: e.scalar_tensor_tensor(out=S32.t[:], in0=S32.t[:], scalar=ebT[r].t[:, 63:64], in1=pH.t[:, 0:256],
                                                                       op0=ALU.mult, op1=ALU.add), reads=[S32, ebT[r], pH0], writes=[S32])
                if GLA_SUB >= 5:
                    kb.op("act", lambda e: e.copy(out=SbB.t[:], in_=S32.t[:]), reads=[S32], writes=[SbB])
                if GLA_SUB >= 6:
                    kb.mm(pal.t[:, 0:256], qdT[r].t[:, :], SbB.t[:], True, True, [qdT[r], SbB], [pal])
                if GLA_SUB >= 7:
                    kb.op("dve", lambda e, r=r: e.scalar_tensor_tensor(out=S32.t[:], in0=S32.t[:], scalar=ebT[r].t[:, 127:128], in1=pH.t[:, 256:512],
                                                                       op0=ALU.mult, op1=ALU.add), reads=[S32, ebT[r], pH1], writes=[S32])
                if GLA_SUB >= 8:
                    kb.op("act", lambda e: e.copy(out=SbA.t[:], in_=S32.t[:]), reads=[S32], writes=[SbA])
            if GLA_STAGE >= 6:
                kb.op("act", lambda e, r=r: e.copy(out=oint[r].t[0:64, :], in_=pG.t[0:64, 256:512]), reads=[pGb], writes=[oint[r]])
                kb.op("act", lambda e, r=r: e.copy(out=oint[r].t[64:128, :], in_=pal.t[64:128, 0:256]), reads=[pal, oint[r]], writes=[oint[r]])
                kb.op("dve", lambda e, r=r: e.tensor_tensor(out=osb[r].t[:], in0=pG.t[:, 0:256], in1=oint[r].t[:], op=ALU.add),
                      reads=[pGa, oint[r]], writes=[osb[r]])
                kb.op("act", lambda e, r=r: e.activation(out=junk.t[:], in_=osb[r].t[:], func=AF.Square, accum_out=ss[r].t[:, 0:1]),
                      reads=[osb[r]], writes=[junk, ss[r]])
                kb.op("dve", lambda e, r=r: e.tensor_scalar(out=ss[r].t[:, 1:2], in0=ss[r].t[:, 0:1], scalar1=1.0 / GLA_DV, scalar2=LN_EPS,
                                                            op0=ALU.mult, op1=ALU.add), reads=[ss[r]], writes=[ss[r]])
                kb.op("act", lambda e, r=r: e.sqrt(out=ss[r].t[:, 1:2], in_=ss[r].t[:, 1:2]), reads=[ss[r]], writes=[ss[r]])
                kb.op("dve", lambda e, r=r: e.reciprocal(out=ss[r].t[:, 1:2], in_=ss[r].t[:, 1:2]), reads=[ss[r]], writes=[ss[r]])
                kb.op("dve", lambda e, r=r: e.scalar_tensor_tensor(out=og[r].t[:], in0=osb[r].t[:], scalar=ss[r].t[:, 1:2], in1=gr[r].t[:],
                                                                   op0=ALU.mult, op1=ALU.mult), reads=[osb[r], ss[r], gr[r]], writes=[og[r]])
            if GLA_STAGE >= 7:
                for c in range(2):
                    kb.op("pe", lambda e, r=r, c=c: e.transpose(out=pT[:, c * 128:(c + 1) * 128], in_=og[r].t[:, c * 128:(c + 1) * 128],
                                                                identity=identb.t[:]), reads=[og[r], identb], writes=[pDt])
                kb.op("act", lambda e, r=r: e.copy(out=oTt[r].t[:].rearrange("p a b -> p (a b)"), in_=pT[:, 0:256]), reads=[pDt], writes=[oTt[r]])
            t0 = st * 512 + j * 128
            kb.dma("pool", ov[:, :, t0:t0 + 128], oTt[r].t[:], reads=[oTt[r]], writes=[oT])

        gits = [(st, j) for st in range(S // 512) for j in range(4)]
        superA(0)
        tileA(0, 0, 0)
        for n_, (st, j) in enumerate(gits):
            if n_ + 1 < len(gits):
                st2, j2 = gits[n_ + 1]
                if j2 == 0:
                    superA(st2)
                tileA(st2, j2, (n_ + 1) % R)
            tileB(st, j, n_ % R)
        kb.finish([oT])
    return nc


REL_BUCKETS, REL_MAX_DIST = 32, 2048


def rel_bucket_np(n):
    n = np.maximum(n, 0)
    exact = REL_BUCKETS // 2
    logn = np.log(np.maximum(n, 1).astype(np.float32) / exact)
    large = exact + (logn / np.float32(math.log(REL_MAX_DIST / exact)) * (REL_BUCKETS - exact)).astype(np.int32)
    large = np.minimum(large, REL_BUCKETS - 1)
    return np.where(n < exact, n, large)


def onehot_struct(dists, valid):
    L = len(dists)
    oh = np.zeros((33, L), np.float32)
    b = rel_bucket_np(np.asarray(dists))
    for i in range(L):
        if valid[i]:
            oh[b[i], i] = 1.0
        else:
            oh[32, i] = 1.0
    return oh


def make_flipJ(kb):
    jm = kb.sb([128, 128])
    kb.op("pool", lambda e: e.memset(jm.t[:], 0.0), writes=[jm])
    kb.op("pool", lambda e: e.affine_select(out=jm.t[:], in_=jm.t[:], pattern=[[1, 128]], compare_op=ALU.not_equal, fill=1.0,
                                             base=-127, channel_multiplier=1), reads=[jm], writes=[jm])
    return jm


def build_wide_bias(kb, tab, oh_ap, nh, L, W, Fd, fd_row0, wides, wbuf, pbank, jm, es):
    oht = kb.sb([33, L], es=es)
    kb.dma("sp", oht.t[:], oh_ap, writes=[oht])
    fs = kb.sb([nh, L], es=es)
    for c0 in range(0, L, 512):
        c1 = min(L, c0 + 512)
        kb.mm(pbank.t[0:nh, 0:c1 - c0], tab.t[0:33, 0:nh], oht.t[0:33, c0:c1], True, True, [tab, oht], [pbank])
        kb.op("dve", lambda e, c0=c0, c1=c1: e.tensor_copy(out=fs.t[0:nh, c0:c1], in_=pbank.t[0:nh, 0:c1 - c0]), reads=[pbank], writes=[fs])
    kb.dma("sp", Fd.t[fd_row0:fd_row0 + nh, 0:L], fs.t[0:nh, :], reads=[fs], writes=[Fd])
    tp = kb.sb([128, W], es=es)
    for h in range(nh):
        src = bass.AP(Fd.t.tensor, (fd_row0 + h) * Fd.t.shape[1], [[1, 128], [1, W]])
        kb.dma("sp", tp.t[:, :], src, reads=[Fd], writes=[tp])
        for c0 in range(0, W, 512):
            c1 = min(W, c0 + 512)
            kb.mm(pbank.t[:, 0:c1 - c0], jm.t[:], tp.t[:, c0:c1], True, True, [jm, tp], [pbank])
            kb.op("dve", lambda e, h=h, c0=c0, c1=c1: e.tensor_copy(out=wides[h][:, c0:c1], in_=pbank.t[:, 0:c1 - c0]), reads=[pbank], writes=[wbuf])


DIL_GROUPS = ((128, 1), (512, 4), (2048, 16))
CH = 2048


def dil_onehots():
    ohs = []
    for (window, dil) in DIL_GROUPS:
        m = np.arange(383) - 127
        ohs.append(onehot_struct(m * dil, (m >= 0) & (m <= window // dil)))
    return np.stack(ohs)


def build_dil(S):
    nc = new_nc()
    hT = dr_in(nc, "hT", [D, S], BF16)
    wqkv = dr_in(nc, "wqkv", [3, D, 768])
    tabi = dr_in(nc, "tab", [33, 4])
    ohi = dr_in(nc, "oh", [3, 33, 383])
    oT = dr_out(nc, "oT", [256, S], BF16)
    Fd = Buf(nc.dram_tensor("Fd", [12, 383], F32, kind="Internal").ap())
    wbf = Buf(nc.dram_tensor("wbf", [3, 128, 8 * 768], BF16, kind="Internal").ap())
    nchunk = S // CH
    scale = 64 ** -0.5
    with ExitStack() as es:
        kb = KB(nc, es)
        banks = [kb.ps() for _ in range(8)]
        jm = make_flipJ(kb)
        tab = kb.sb([33, 4])
        kb.dma("sp", tab.t[:], tabi.t[:, :], writes=[tab])
        bias = [kb.sb([128, 4, 256]) for _ in range(3)]
        sel65 = kb.sb([128, 64])
        kb.op("pool", lambda e: e.memset(sel65.t[:], 0.0), writes=[sel65])
        kb.op("pool", lambda e: e.memset(sel65.t[64:65, :], 1.0), reads=[sel65], writes=[sel65])
        with ExitStack() as es0:
            for g in range(3):
                build_wide_bias(kb, tab, ohi.t[g], 4, 383, 256, Fd, 4 * g, [bias[g].t[:, h, :] for h in range(4)], bias[g], banks[0], jm, es0)
        kb.barrier()
        with ExitStack() as es1:
            stage = [kb.sb([128, 768], es=es1) for _ in range(2)]
            wtmp = [kb.sb([128, 8, 768], BF16, nparts=8, es=es1) for _ in range(1)]
            for g in range(3):
                load_w_bf16(kb, wtmp[0], lambda k, c0, c1, g=g: wqkv.t[g, k * 128:(k + 1) * 128, c0:c1], 8, 768, stage, chunk=768)
                kb.dma("sp", wbf.t[g].rearrange("p (k c) -> p k c", k=8), wtmp[0].t[:], reads=wtmp[0].parts, writes=[wbf])
                for p_ in wtmp[0].parts:
                    p_.r[kb.dq["sp"][(kb.di["sp"] - 1) % len(kb.dq["sp"])]] = kb.dsl[kb.dq["sp"][(kb.di["sp"] - 1) % len(kb.dq["sp"])]][1]
        kb.barrier()
        wg = [kb.sb([128, 8, 768], BF16) for _ in range(2)]
        hh = [kb.sb([128, 8, CH], BF16) for _ in range(2)]
        qTm = [[kb.sb([128, CH], BF16) for _ in range(2)] for _ in range(2)]
        for hp_ in range(2):
            for hd_ in range(2):
                kb.op("pool", lambda e, hp_=hp_, hd_=hd_: e.memset(qTm[hp_][hd_].t[:], 0.0), writes=[qTm[hp_][hd_]])
        kT = [kb.sb([128, 2 * CH], BF16) for _ in range(2)]
        V = kb.sb([128, 32, 4, 65], BF16)
        kb.op("pool", lambda e: e.memset(V.t[:].rearrange("p a b c -> p (a b c)"), 1.0), writes=[V])
        acc = kb.sb([65, 4, CH])
        tmp = [kb.sb([128, 512]) for _ in range(2)]
        pT = [kb.sb([128, 512], BF16) for _ in range(2)]
        oTt = [kb.sb([64, CH], BF16) for _ in range(2)]
        rdb = kb.sb([64, 512])
        hv = hT.t.rearrange("(k p) t -> p k t", p=128)
        nw = 0
        nit = 0
        for c in range(nchunk):
            cur = hh[c % 2]
            kb.dma("sp", cur.t[:], hv[:, :, c * CH:(c + 1) * CH], writes=[cur])
            slots = ([(hh[(c - 1) % 2], 0)] if c > 0 else []) + [(cur, 1)]
            for g, (window, d) in enumerate(DIL_GROUPS):
                w = wg[nw % 2]; nw += 1
                kb.dma("pool", w.t[:], wbf.t[g].rearrange("p (k c) -> p k c", k=8), reads=[wbf], writes=[w])
                nbk = 16 // d
                for hp in range(2):
                    for n4 in range(4):
                        pb = banks[(n4 + hp) % 2]
                        for k in range(8):
                            kb.mm(pb.t[:, :], w.t[:, k, hp * 128:(hp + 1) * 128], cur.t[:, k, n4 * 512:(n4 + 1) * 512], k == 0, k == 7, [w, cur], [pb])
                        kb.op("act", lambda e, pb=pb, hp=hp, n4=n4: e.copy(out=qTm[hp][0].t[0:64, n4 * 512:(n4 + 1) * 512], in_=pb.t[0:64, :]), reads=[pb], writes=[qTm[hp][0]])
                        kb.op("act", lambda e, pb=pb, hp=hp, n4=n4: e.copy(out=qTm[hp][1].t[64:128, n4 * 512:(n4 + 1) * 512], in_=pb.t[64:128, :]), reads=[pb], writes=[qTm[hp][1]])
                    for (hb, si) in slots:
                        for n4 in range(4):
                            pb = banks[(n4 + hp) % 2]
                            for k in range(8):
                                kb.mm(pb.t[:, :], w.t[:, k, 256 + hp * 128:256 + (hp + 1) * 128], hb.t[:, k, n4 * 512:(n4 + 1) * 512], k == 0, k == 7, [w, hb], [pb])
                            kb.op("dve", lambda e, pb=pb, hp=hp, n4=n4, si=si: e.tensor_copy(out=kT[hp].t[:, si * CH + n4 * 512:si * CH + (n4 + 1) * 512], in_=pb.t[:, :]),
                                  reads=[pb], writes=[kT[hp]])
                for (hb, si) in slots:
                    for r in range(d):
                        for bl in range(nbk):
                            ti = si * 16 + r * nbk + bl
                            pb = banks[2 + ti % 2]
                            st0 = r + d * 128 * bl
                            for k in range(8):
                                kb.mm(pb.t[:, 0:256], hb.t[:, k, st0:st0 + 127 * d + 1:d], w.t[:, k, 512:768], k == 0, k == 7, [w, hb], [pb])
                            kb.op("act" if ti % 2 else "dve",
                                  (lambda e, pb=pb, ti=ti: e.copy(out=V.t[:, ti, :, 0:64], in_=pb.t[:, 0:256].rearrange("p (h c) -> p h c", h=4))) if ti % 2 else
                                  (lambda e, pb=pb, ti=ti: e.tensor_copy(out=V.t[:, ti, :, 0:64], in_=pb.t[:, 0:256].rearrange("p (h c) -> p h c", h=4))),
                                  reads=[pb], writes=[V])
                aits = []
                for r in range(d):
                    for bl in range(nbk):
                        q0 = r + d * 128 * bl
                        keys = [(1, r, bl)]
                        if bl > 0:
                            keys.append((1, r, bl - 1))
                        elif c > 0:
                            keys.append((0, r, nbk - 1))
                        for hp in range(2):
                            aits.append((q0, keys, hp))

                def dilA(n_):
                    q0, keys, hp = aits[n_]
                    nd = len(keys)
                    it = (nit + n_) % 2
                    ps = banks[4 + it]
                    for hd in range(2):
                        for dl, (si, kr, kbl) in enumerate(keys):
                            k0 = si * CH + kr + d * 128 * kbl
                            kb.mm(ps.t[:, (hd * 2 + dl) * 128:(hd * 2 + dl + 1) * 128], kT[hp].t[:, k0:k0 + 127 * d + 1:d],
                                  qTm[hp][hd].t[:, q0:q0 + 127 * d + 1:d], True, True, [kT[hp], qTm[hp][hd]], [ps])
                    psv = ps.t[:, :].rearrange("p (h e i) -> p h e i", h=2, e=2)
                    tv = tmp[it].t[:, :].rearrange("p (h e i) -> p h e i", h=2, e=2)
                    pv = pT[it].t[:, :].rearrange("p (h e i) -> p h e i", h=2, e=2)
                    bv = bias[g].t[:, 2 * hp:2 * hp + 2, :].rearrange("p h (e i) -> p h e i", e=2)
                    kb.op("dve", lambda e: e.scalar_tensor_tensor(out=tv[:, :, 0:nd, :], in0=psv[:, :, 0:nd, :], scalar=scale,
                                                                  in1=bv[:, :, 0:nd, :], op0=ALU.mult, op1=ALU.add),
                          reads=[ps, bias[g]], writes=[tmp[it]])
                    kb.op("act", lambda e: e.activation(out=pv[:, :, 0:nd, :], in_=tv[:, :, 0:nd, :], func=AF.Exp),
                          reads=[tmp[it]], writes=[pT[it]])

                def dilB(n_):
                    q0, keys, hp = aits[n_]
                    nd = len(keys)
                    it = (nit + n_) % 2
                    po = banks[6 + it]
                    pv = pT[it].t[:, :].rearrange("p (h e i) -> p h e i", h=2, e=2)
                    for hd in range(2):
                        for dl, (si, kr, kbl) in enumerate(keys):
                            ti = si * 16 + kr * nbk + kbl
                            kb.mm(po.t[0:65, hd * 128:(hd + 1) * 128], V.t[:, ti, 2 * hp + hd, :], pv[:, hd, dl, :], dl == 0, dl == nd - 1, [V, pT[it]], [po])
                    av = acc.t[:, 2 * hp:2 * hp + 2, q0:q0 + 127 * d + 1:d]
                    pov = po.t[0:65, 0:256].rearrange("p (h i) -> p h i", h=2)
                    if g == 0:
                        kb.op("dve", lambda e: e.tensor_copy(out=av, in_=pov), reads=[po], writes=[acc])
                    else:
                        kb.op("dve", lambda e: e.tensor_tensor(out=av, in0=av, in1=pov, op=ALU.add), reads=[po, acc], writes=[acc])

                dilA(0)
                for n_ in range(len(aits)):
                    if n_ + 1 < len(aits):
                        dilA(n_ + 1)
                    dilB(n_)
                nit += len(aits)
            for h in range(4):
                ot = oTt[h % 2]
                for n4 in range(4):
                    pb = banks[n4 % 2]
                    kb.mm(pb.t[0:64, :], sel65.t[0:65, 0:64], acc.t[0:65, h, n4 * 512:(n4 + 1) * 512], True, True, [sel65, acc], [pb])
                    kb.op("dve", lambda e, pb=pb: e.reciprocal(out=rdb.t[:, :], in_=pb.t[0:64, :]), reads=[pb], writes=[rdb])
                    kb.op("dve", lambda e, h=h, n4=n4, ot=ot: e.tensor_tensor(out=ot.t[:, n4 * 512:(n4 + 1) * 512], in0=acc.t[0:64, h, n4 * 512:(n4 + 1) * 512],
                                                                            in1=rdb.t[:, :], op=ALU.mult), reads=[acc, rdb], writes=[ot])
                kb.dma("pool", oT.t[h * 64:(h + 1) * 64, c * CH:(c + 1) * CH], ot.t[:, :], reads=[ot], writes=[oT])
        kb.finish([oT])
    return nc


SELW, WINW = 2560, 1408
SEL_FARD = 13
BIGF, BIGI = 1.0e4, -1.0e6
EB = 30000.0


def nsa_onehots():
    m = np.arange(SELW + 127) - 127 - 384
    oh_sel = onehot_struct(m, m >= 0)
    m = np.arange(WINW + 127) - 127 - 384
    oh_win = onehot_struct(m, (m >= 0) & (m <= 511))
    m = np.arange(1776 + 16) - 127
    oh_cmp = onehot_struct(m, m >= 0)
    return oh_sel, oh_win, oh_cmp


NSA_STAGE = 99
NSA_SUB = 99


def build_nsa(S):
    nc = new_nc()
    NT = S // 128
    NCK = S // 16 - 1
    NCP = ((NCK + 127) // 128) * 128
    hT = dr_in(nc, "hT", [D, S], BF16)
    wqd = dr_in(nc, "wqd", [D, 512]); wkv = dr_in(nc, "wkv", [D, 384]); wgi = dr_in(nc, "wg", [D, 12])
    pei = dr_in(nc, "pe", [2, 32, 64]); w1i = dr_in(nc, "w1", [2, 2048, 256]); b1i = dr_in(nc, "b1", [2, 256]); w2i = dr_in(nc, "w2", [2, 256, 64])
    tabi = dr_in(nc, "tab", [33, 4])
    ohs = dr_in(nc, "ohs", [33, SELW + 127]); ohw = dr_in(nc, "ohw", [33, WINW + 127]); ohc = dr_in(nc, "ohc", [33, 1792])
    oT = dr_out(nc, "oT", [256, S], BF16)
    Fd = Buf(nc.dram_tensor("Fd", [12, SELW + 127], F32, kind="Internal").ap())
    scale = 64 ** -0.5
    with ExitStack() as es:
        kb = KB(nc, es)
        banks = [kb.ps() for _ in range(8)]
        identb = make_ident(kb, BF16)
        jm = make_flipJ(kb)
        tab = kb.sb([33, 4])
        kb.dma("sp", tab.t[:], tabi.t[:, :], writes=[tab])
        kswT = kb.sb([128, S], BF16)
        Vs = kb.sb([128, NT, 65], BF16); Vw = kb.sb([128, NT, 65], BF16)
        kb.op("pool", lambda e: e.memset(Vs.t[:].rearrange("p a b -> p (a b)"), 1.0), writes=[Vs])
        kb.op("pool", lambda e: e.memset(Vw.t[:].rearrange("p a b -> p (a b)"), 1.0), writes=[Vw])
        kcmpT = kb.sb([64, NCP], BF16); vcmp = kb.sb([128, NCP // 128, 64], BF16)
        if NSA_STAGE == 1.1:
            kb.finish([oT]); return nc
        stage = [kb.sb([128, 512]) for _ in range(2)]
        wqb = kb.sb([128, 8, 512], BF16, nparts=8); wkvb = kb.sb([128, 8, 384], BF16, nparts=8); wgb = kb.sb([128, 8, 12], BF16, nparts=8)
        for dst, src, n in ((wqb, wqd, 512), (wkvb, wkv, 384), (wgb, wgi, 12)):
            load_w_bf16(kb, dst, lambda k, c0, c1, src=src: src.t[k * 128:(k + 1) * 128, c0:c1], 8, n, stage)
        if NSA_STAGE == 1.2:
            kb.finish([oT]); return nc
        hhs = [kb.sb([128, 8, 512], BF16) for _ in range(2)]
        hv = hT.t.rearrange("(k p) t -> p k t", p=128)
        with ExitStack() as es1:
            kcvcT = kb.sb([128, S], BF16, es=es1)
            w1m = [kb.sb([128, 32, 256], BF16, es=es1) for _ in range(2)]
            for x_ in range(2):
                kb.op("pool", lambda e, x_=x_: e.memset(w1m[x_].t[:].rearrange("p a b -> p (a b)"), 0.0), writes=[w1m[x_]])
            w2b = kb.sb([128, 2, 2, 64], BF16, es=es1)
            st1 = [kb.sb([128, 8, 256], es=es1) for _ in range(1)]
            nst1 = 0
            for x_ in range(2):
                rws = slice(x_ * 64, (x_ + 1) * 64)
                for pc in range(4):
                    stg = st1[0]; nst1 += 1
                    kb.dma("sp", stg.t[rws, :, :], w1i.t[x_].rearrange("(p d) h -> d p h", d=64)[:, pc * 8:(pc + 1) * 8, :], writes=[stg])
                    kb.op("dve", lambda e, x_=x_, pc=pc, stg=stg, rws=rws: e.tensor_copy(out=w1m[x_].t[rws, pc * 8:(pc + 1) * 8, :], in_=stg.t[rws, :, :]),
                          reads=[stg], writes=[w1m[x_]])
            st2 = kb.sb([128, 2, 2, 64], es=es1)
            for x_ in range(2):
                kb.dma("sp", st2.t[:, x_, :, :], w2i.t[x_].rearrange("(a p) c -> p a c", p=128), writes=[st2])
            kb.op("dve", lambda e: e.tensor_copy(out=w2b.t[:].rearrange("p a b c -> p (a b c)"), in_=st2.t[:].rearrange("p a b c -> p (a b c)")),
                  reads=[st2], writes=[w2b])
            peT = kb.sb([128, 32], es=es1); peTb = kb.sb([128, 32], BF16, es=es1)
            for x_ in range(2):
                for p4 in range(4):
                    kb.dma("sp", peT.t[x_ * 64:(x_ + 1) * 64, p4 * 8:(p4 + 1) * 8], pei.t[x_, p4 * 8:(p4 + 1) * 8, :].rearrange("p d -> d p"), writes=[peT],
                           allow_slow_non_contiguous=True)
            kb.op("dve", lambda e: e.tensor_copy(out=peTb.t[:], in_=peT.t[:]), reads=[peT], writes=[peTb])
            b1t = kb.sb([128, 2, 2], es=es1)
            for x_ in range(2):
                kb.dma("sp", b1t.t[:, x_, :], b1i.t[x_, :].rearrange("(a p) -> p a", p=128), writes=[b1t], allow_slow_non_contiguous=True)
            if NSA_STAGE == 1.3:
                kb.finish([oT]); return nc
            for st in range(S // 512):
                hh = hhs[st % 2]
                kb.dma("sp", hh.t[:], hv[:, :, st * 512:(st + 1) * 512], writes=[hh])
                for (c0, dstT, eng) in ((0, kcvcT, "act"), (128, kswT, "dve")):
                    if NSA_STAGE == 1.5:
                        break
                    pb = banks[1 + (c0 // 128)]
                    for k in range(8):
                        kb.mm(pb.t[:, :], wkvb.t[:, k, c0:c0 + 128], hh.t[:, k, :], k == 0, k == 7, [wkvb.parts[k], hh], [pb])
                    if eng == "act":
                        kb.op("act", lambda e, pb=pb, dstT=dstT, st=st: e.copy(out=dstT.t[:, st * 512:(st + 1) * 512], in_=pb.t[:, :]), reads=[pb], writes=[dstT])
                    else:
                        kb.op("dve", lambda e, pb=pb, dstT=dstT, st=st: e.tensor_copy(out=dstT.t[:, st * 512:(st + 1) * 512], in_=pb.t[:, :]), reads=[pb], writes=[dstT])
                if NSA_STAGE == 1.4:
                    continue
                pb = banks[3 + st % 2]
                for j in range(4):
                    for k in range(8):
                        kb.mm(pb.t[:, j * 128:(j + 1) * 128], hh.t[:, k, j * 128:(j + 1) * 128], wkvb.t[:, k, 256:384], k == 0, k == 7, [wkvb.parts[k], hh], [pb])
                pv4 = pb.t[:, :].rearrange("p (j c) -> p j c", j=4)
                for j in range(4):
                    if NSA_SUB >= 1:
                        kb.op("act", lambda e, pb=pb, st=st, j=j: e.copy(out=Vs.t[:, st * 4 + j, 0:64], in_=pb.t[:, j * 128:j * 128 + 64]), reads=[pb], writes=[Vs])
                    if NSA_SUB >= 2:
                        kb.op("act", lambda e, pb=pb, st=st, j=j: e.copy(out=Vw.t[:, st * 4 + j, 0:64], in_=pb.t[:, j * 128 + 64:j * 128 + 128]), reads=[pb], writes=[Vw])
            if NSA_STAGE in (2, 1.4, 1.5):
                kb.finish([oT]); return nc
            hidT = kb.sb([128, 2, 2, NCP], BF16, es=es1)
            kb.op("pool", lambda e: e.memset(hidT.t[:].rearrange("p a b c -> p (a b c)"), 0.0), writes=[hidT])
            cbias = kb.sb([128, 2, 2], es=es1)
            for x_ in range(2):
                rows = slice(x_ * 64, (x_ + 1) * 64)
                for half in range(2):
                    pb = banks[1]
                    for p in range(32):
                        kb.mm(pb.t[:, 0:1], w1m[x_].t[:, p, half * 128:(half + 1) * 128], peTb.t[:, p:p + 1], p == 0, p == 31, [w1m[x_], peTb], [pb])
                    kb.op("dve", lambda e, x_=x_, half=half, pb=pb: e.tensor_tensor(out=cbias.t[:, x_, half:half + 1], in0=pb.t[:, 0:1], in1=b1t.t[:, x_, half:half + 1], op=ALU.add),
                          reads=[pb, b1t], writes=[cbias])
                    for n0 in range(0, NCK, 512):
                        n1 = min(NCK, n0 + 512)
                        pb2 = banks[2 + (n0 // 512) % 2]
                        for p in range(32):
                            kb.mm(pb2.t[:, 0:n1 - n0], w1m[x_].t[:, p, half * 128:(half + 1) * 128], kcvcT.t[:, 16 * n0 + p:16 * (n1 - 1) + p + 1:16],
                                  p == 0, p == 31, [w1m[x_], kcvcT], [pb2])
                        kb.op("act", lambda e, x_=x_, half=half, n0=n0, n1=n1, pb2=pb2: e.activation(out=hidT.t[:, x_, half, n0:n1], in_=pb2.t[:, 0:n1 - n0], func=AF.Silu,
                                                                                                     bias=cbias.t[:, x_, half:half + 1]), reads=[pb2, cbias], writes=[hidT])
            for n0 in range(0, NCP, 512):
                n1 = min(NCP, n0 + 512)
                pb = banks[1]
                for half in range(2):
                    kb.mm(pb.t[0:64, 0:n1 - n0], w2b.t[:, 0, half, :], hidT.t[:, 0, half, n0:n1], half == 0, half == 1, [w2b, hidT], [pb])
                kb.op("dve", lambda e, n0=n0, n1=n1, pb=pb: e.tensor_copy(out=kcmpT.t[:, n0:n1], in_=pb.t[0:64, 0:n1 - n0]), reads=[pb], writes=[kcmpT])
            for ct in range(NCP // 128):
                pb = banks[2 + ct % 2]
                for half in range(2):
                    kb.mm(pb.t[:, 0:64], hidT.t[:, 1, half, ct * 128:(ct + 1) * 128], w2b.t[:, 1, half, :], half == 0, half == 1, [w2b, hidT], [pb])
                kb.op("act", lambda e, ct=ct, pb=pb: e.copy(out=vcmp.t[:, ct, :], in_=pb.t[:, 0:64]), reads=[pb], writes=[vcmp])
        if NSA_STAGE == 3:
            kb.finish([oT]); return nc
        kb.barrier()
        wsel = kb.sb([128, 4, SELW], BF16); wwin = kb.sb([128, 4, WINW], BF16)
        nearb = kb.sb([128, 4, 104]); farb = kb.sb([128, 4])
        with ExitStack() as es0:
            wtmp = kb.sb([128, SELW], es=es0)
            for (oh, L, W, dst) in ((ohs, SELW + 127, SELW, wsel), (ohw, WINW + 127, WINW, wwin)):
                for h in range(4):
                    tabh = Buf(tab.t[:, h:h + 1]); tabh.w = tab.w
                    with ExitStack() as esx:
                        build_wide_bias(kb, tabh, oh.t[:, :], 1, L, W, Fd, h, [wtmp.t[:, 0:W]], wtmp, banks[0], jm, esx)
                        kb.op("act", lambda e, dst=dst, h=h, W=W: e.copy(out=dst.t[:, h, :], in_=wtmp.t[:, 0:W]), reads=[wtmp], writes=[dst])
                    kb.barrier()
            oht = kb.sb([33, 1792], es=es0)
            kb.dma("sp", oht.t[:], ohc.t[:, :], writes=[oht])
            fs = kb.sb([4, 1792], es=es0)
            for c0 in range(0, 1792, 512):
                c1 = min(1792, c0 + 512)
                kb.mm(banks[0].t[0:4, 0:c1 - c0], tab.t[0:33, 0:4], oht.t[0:33, c0:c1], True, True, [tab, oht], [banks[0]])
                kb.op("dve", lambda e, c0=c0, c1=c1: e.tensor_copy(out=fs.t[0:4, c0:c1], in_=banks[0].t[0:4, 0:c1 - c0]), reads=[banks[0]], writes=[fs])
            kb.dma("sp", Fd.t[8:12, 0:1792], fs.t[0:4, :], reads=[fs], writes=[Fd])
            jc = kb.sb([128, 104], es=es0)
            kb.op("pool", lambda e: e.memset(jc.t[:], 0.0), writes=[jc])
            kb.op("pool", lambda e: e.affine_select(out=jc.t[:], in_=jc.t[:], pattern=[[1, 104]], compare_op=ALU.not_equal, fill=1.0,
                                                     base=-103, channel_multiplier=1), reads=[jc], writes=[jc])
            xt = kb.sb([104, 128], es=es0)
            for h in range(4):
                src = bass.AP(Fd.t.tensor, (8 + h) * Fd.t.shape[1], [[16, 104], [1, 128]])
                kb.dma("sp", xt.t[:, :], src, reads=[Fd], writes=[xt])
                kb.mm(banks[0].t[:, 0:104], xt.t[0:104, :], jc.t[0:104, :], True, True, [xt, jc], [banks[0]])
                kb.op("dve", lambda e, h=h: e.tensor_copy(out=nearb.t[:, h, :], in_=banks[0].t[:, 0:104]), reads=[banks[0]], writes=[nearb])
            onesr = kb.sb([33, 128], es=es0)
            kb.op("pool", lambda e: e.memset(onesr.t[:], 0.0), writes=[onesr])
            kb.op("pool", lambda e: e.memset(onesr.t[0:1, :], 1.0), reads=[onesr], writes=[onesr])
            t31 = kb.sb([1, 4], es=es0)
            kb.dma("sp", t31.t[:, :], tabi.t[31:32, :], writes=[t31])
            kb.mm(banks[0].t[:, 0:4], onesr.t[0:1, :], t31.t[0:1, :], True, True, [onesr, t31], [banks[0]])
            kb.op("dve", lambda e: e.tensor_copy(out=farb.t[:, :], in_=banks[0].t[:, 0:4]), reads=[banks[0]], writes=[farb])
        kb.barrier()
        Aw = kb.sb([128, 512])
        kb.op("pool", lambda e: e.memset(Aw.t[:], 0.0), writes=[Aw])
        for (rows, c0) in ((slice(0, 64), 255), (slice(64, 128), 256)):
            kb.op("pool", lambda e, rows=rows, c0=c0: e.memset(Aw.t[rows, c0:c0 + 2], BIGF), reads=[Aw], writes=[Aw])
            kb.op("pool", lambda e, rows=rows, c0=c0: e.memset(Aw.t[rows, c0 + 2:512], BIGI), reads=[Aw], writes=[Aw])
        e2f = kb.sb([128, 64, 2])
        kb.op("pool", lambda e: e.memset(e2f.t[:].rearrange("p a b -> p (a b)"), 0.0), writes=[e2f])
        kb.op("pool", lambda e: e.affine_select(out=e2f.t[:], in_=e2f.t[:], pattern=[[-2, 64], [-1, 2]], compare_op=ALU.not_equal, fill=EB,
                                                 base=0, channel_multiplier=1), reads=[e2f], writes=[e2f])
        Exp_ = kb.sb([128, 64, 128], BF16)
        for half in range(2):
            kb.op("dve", lambda e, half=half: e.tensor_copy(out=Exp_.t[:, :, half * 64:(half + 1) * 64], in_=e2f.t[:, :, half:half + 1].to_broadcast([128, 64, 64])),
                  reads=[e2f], writes=[Exp_])
        qsel = [kb.sb([128, 512], BF16) for _ in range(4)]
        qwin = [kb.sb([128, 512], BF16) for _ in range(4)]
        for h_ in range(4):
            kb.op("pool", lambda e, h_=h_: e.memset(qsel[h_].t[:], 0.0), writes=[qsel[h_]])
            kb.op("pool", lambda e, h_=h_: e.memset(qwin[h_].t[:], 0.0), writes=[qwin[h_]])
        gates = kb.sb([128, 4, 12])
        negselT = kb.sb([128, 2, 512], BF16)
        kb.op("pool", lambda e: e.memset(negselT.t[:].rearrange("p a b -> p (a b)"), -1.0), writes=[negselT])
        tmpc = [kb.sb([128, 1024]) for _ in range(2)]; ebuf = tmpc; pg = kb.sb([128, 1024])
        pbf = [kb.sb([128, 1024], BF16) for _ in range(2)]
        kb.op("pool", lambda e: e.memset(pg.t[:], 0.0), writes=[pg])
        pTc = [kb.sb([128, NCP // 128, 128], BF16) for _ in range(2)]
        den = [kb.sb([128, 2]) for _ in range(2)]; imp = kb.sb([128, 256]); sc2 = kb.sb([128, 256]); m8 = kb.sb([128, 16]); nsel = kb.sb([128, 256], BF16)
        ofin = [kb.sb([128, 4, 64]) for _ in range(4)]
        tmp = [kb.sb([128, 512]) for _ in range(3)]; pT = [kb.sb([128, 512], BF16) for _ in range(3)]
        fcol = kb.sb([128, 8]); ogb = kb.sb([128, 256], BF16); oTt = kb.sb([128, 2, 128], BF16)
        poS = [kb.sb([65, 512]) for _ in range(2)]; identf = make_ident(kb, F32)
        ov = oT.t.rearrange("(c p) t -> p c t", p=128)
        pgv = pg.t[:, :].rearrange("p (b m) -> p b m", m=4)
        if NSA_STAGE == 4:
            kb.finish([oT]); return nc
        for qs in range(S // 512):
            hh = hhs[qs % 2]
            kb.dma("sp", hh.t[:], hv[:, :, qs * 512:(qs + 1) * 512], writes=[hh])
            for h in range(4):
                pb = banks[0]
                for k in range(8):
                    kb.mm(pb.t[:, :], wqb.t[:, k, h * 128:(h + 1) * 128], hh.t[:, k, :], k == 0, k == 7, [wqb.parts[k], hh], [pb])
                kb.op("act", lambda e, h=h, pb=pb: e.copy(out=qsel[h].t[0:64, :], in_=pb.t[0:64, :]), reads=[pb], writes=[qsel[h]])
                kb.op("dve", lambda e, h=h, pb=pb: e.tensor_copy(out=qwin[h].t[64:128, :], in_=pb.t[64:128, :]), reads=[pb], writes=[qwin[h]])
            pb = banks[0]
            for j in range(4):
                for k in range(8):
                    kb.mm(pb.t[:, j * 12:(j + 1) * 12], hh.t[:, k, j * 128:(j + 1) * 128], wgb.t[:, k, :], k == 0, k == 7, [wgb.parts[k], hh], [pb])
            kb.op("act", lambda e, pb=pb: e.activation(out=gates.t[:].rearrange("p a b -> p (a b)"), in_=pb.t[:, 0:48], func=AF.Sigmoid), reads=[pb], writes=[gates])
            def cmpA(j, h, pp):
                qb = qs * 4 + j
                sub = slice(j * 128, (j + 1) * 128)
                ncv = min(8 * qb + 7, NCK)
                nlo = max(0, 8 * qb - 97); u0 = nlo - (8 * qb - 97)
                sbk = (banks[1], banks[2]) if pp == 0 else (banks[5], banks[6])
                for c0 in range(0, ncv, 512):
                    c1 = min(ncv, c0 + 512)
                    kb.mm(sbk[c0 // 512].t[:, 0:c1 - c0], qsel[h].t[0:64, sub], kcmpT.t[0:64, c0:c1], True, True, [qsel[h], kcmpT], [sbk[c0 // 512]])
                for c0 in range(0, ncv, 512):
                    c1 = min(ncv, c0 + 512)
                    pbk = sbk[c0 // 512]
                    kb.op("dve", lambda e, c0=c0, c1=c1, pbk=pbk: e.tensor_scalar(out=tmpc[pp].t[:, c0:c1], in0=pbk.t[:, 0:c1 - c0], scalar1=scale, scalar2=farb.t[:, h:h + 1],
                                                                             op0=ALU.mult, op1=ALU.add), reads=[pbk, farb], writes=[tmpc[pp]])
                    a0 = max(c0, nlo)
                    if a0 < c1:
                        kb.op("dve", lambda e, c0=c0, c1=c1, a0=a0, pbk=pbk: e.scalar_tensor_tensor(
                            out=tmpc[pp].t[:, a0:c1], in0=pbk.t[:, a0 - c0:c1 - c0], scalar=scale, in1=nearb.t[:, h, u0 + a0 - nlo:u0 + c1 - nlo],
                            op0=ALU.mult, op1=ALU.add), reads=[pbk, nearb, tmpc[pp]], writes=[tmpc[pp]])
                kb.op("act", lambda e: e.activation(out=ebuf[pp].t[:, 0:ncv], in_=tmpc[pp].t[:, 0:ncv], func=AF.Exp, accum_out=den[pp].t[:, 0:1]),
                      reads=[tmpc[pp]], writes=[tmpc[pp], den[pp]])

            def cmpB(j, h, pp):
                qb = qs * 4 + j
                ncv = min(8 * qb + 7, NCK)
                nct = (ncv + 127) // 128
                dn = den[pp]
                kb.op("dve", lambda e: e.tensor_scalar(out=dn.t[:, 1:2], in0=dn.t[:, 0:1], scalar1=1e-30, scalar2=None, op0=ALU.max), reads=[dn], writes=[dn])
                kb.op("dve", lambda e: e.reciprocal(out=dn.t[:, 1:2], in_=dn.t[:, 1:2]), reads=[dn], writes=[dn])
                if h == 0:
                    kb.op("dve", lambda e: e.tensor_scalar(out=pg.t[:, 0:ncv], in0=ebuf[pp].t[:, 0:ncv], scalar1=dn.t[:, 1:2], scalar2=None, op0=ALU.mult),
                          reads=[ebuf[pp], dn], writes=[pg])
                else:
                    kb.op("dve", lambda e: e.scalar_tensor_tensor(out=pg.t[:, 0:ncv], in0=ebuf[pp].t[:, 0:ncv], scalar=dn.t[:, 1:2], in1=pg.t[:, 0:ncv],
                                                                  op0=ALU.mult, op1=ALU.add), reads=[ebuf[pp], dn, pg], writes=[pg])
                kb.op("act", lambda e: e.activation(out=pbf[pp].t[:, 0:ncv], in_=ebuf[pp].t[:, 0:ncv], func=AF.Copy, scale=dn.t[:, 1:2]),
                      reads=[ebuf[pp], dn], writes=[pbf[pp]])
                ptk = banks[3] if pp == 0 else banks[0]
                ptb = ptk.t[:, :].bitcast(BF16)
                for ct in range(nct):
                    w_ = min(128, ncv - ct * 128)
                    kb.op("pe", lambda e, ct=ct, w_=w_: e.transpose(out=ptb[0:w_, ct * 128:(ct + 1) * 128], in_=pbf[pp].t[:, ct * 128:ct * 128 + w_],
                                                                    identity=identb.t[:]), reads=[pbf[pp], identb], writes=[ptk])
                hlf = (nct + 1) // 2
                for (ca, cb, en) in ((0, hlf, "act"), (hlf, nct, "dve")):
                    if cb <= ca:
                        continue
                    wl = min(128, ncv - (cb - 1) * 128)
                    if wl == 128 or cb - ca == 1:
                        w_ = wl if cb - ca == 1 else 128
                        src_ = ptb[0:w_, ca * 128:cb * 128].rearrange("p (c i) -> p c i", i=128)
                        dst_ = pTc[pp].t[0:w_, ca:cb, :]
                        if en == "act":
                            kb.op("act", lambda e, src_=src_, dst_=dst_: e.copy(out=dst_, in_=src_), reads=[ptk], writes=[pTc[pp]])
                        else:
                            kb.op("dve", lambda e, src_=src_, dst_=dst_: e.tensor_copy(out=dst_, in_=src_), reads=[ptk], writes=[pTc[pp]])
                    else:
                        for (a_, b_, w_) in ((ca, cb - 1, 128), (cb - 1, cb, wl)):
                            src_ = ptb[0:w_, a_ * 128:b_ * 128].rearrange("p (c i) -> p c i", i=128)
                            dst_ = pTc[pp].t[0:w_, a_:b_, :]
                            kb.op("act", lambda e, src_=src_, dst_=dst_: e.copy(out=dst_, in_=src_), reads=[ptk], writes=[pTc[pp]])
                for ct in range(nct):
                    w_ = min(128, ncv - ct * 128)
                    kb.mm(banks[4].t[:, 0:64], pTc[pp].t[0:w_, ct, :], vcmp.t[0:w_, ct, :], ct == 0, ct == nct - 1, [pTc[pp], vcmp], [banks[4]])
                kb.op("dve", lambda e: e.tensor_scalar(out=ofin[j].t[:, h, :], in0=banks[4].t[:, 0:64], scalar1=gates.t[:, j, h:h + 1], scalar2=None, op0=ALU.mult),
                      reads=[banks[4], gates], writes=[ofin[j]])

            def select(j):
                qb = qs * 4 + j
                sub = slice(j * 128, (j + 1) * 128)
                kb.op("dve", lambda e: e.tensor_tensor(out=imp.t[:, :], in0=pgv[:, :, 0], in1=pgv[:, :, 1], op=ALU.add), reads=[pg], writes=[imp])
                kb.op("dve", lambda e: e.tensor_tensor(out=imp.t[:, :], in0=imp.t[:, :], in1=pgv[:, :, 2], op=ALU.add), reads=[pg, imp], writes=[imp])
                kb.op("dve", lambda e: e.scalar_tensor_tensor(out=imp.t[:, :], in0=imp.t[:, :], scalar=2.0, in1=pgv[:, :, 3], op0=ALU.mult, op1=ALU.add),
                      reads=[pg, imp], writes=[imp])
                kb.op("dve", lambda e: e.tensor_tensor(out=imp.t[:, 1:256], in0=imp.t[:, 1:256], in1=pgv[:, 0:255, 3], op=ALU.add), reads=[pg, imp], writes=[imp])
                kb.op("dve", lambda e: e.tensor_tensor(out=imp.t[:, :], in0=imp.t[:, :], in1=Aw.t[:, 256 - 2 * qb:512 - 2 * qb], op=ALU.add), reads=[Aw, imp], writes=[imp])
                kb.op("dve", lambda e: e.memset(imp.t[:, 0:1], BIGF), reads=[imp], writes=[imp])
                kb.op("dve", lambda e: e.max(out=m8.t[:, 0:8], in_=imp.t[:, :]), reads=[imp], writes=[m8])
                kb.op("dve", lambda e: e.match_replace(out=sc2.t[:, :], in_to_replace=m8.t[:, 0:8], in_values=imp.t[:, :], imm_value=2 * BIGI), reads=[imp, m8], writes=[sc2])
                kb.op("dve", lambda e: e.max(out=m8.t[:, 8:16], in_=sc2.t[:, :]), reads=[sc2], writes=[m8])
                kb.op("dve", lambda e: e.tensor_scalar(out=m8.t[:, 15:16], in0=m8.t[:, 15:16], scalar1=0.1 * BIGI, scalar2=None, op0=ALU.max), reads=[m8], writes=[m8])
                kb.op("dve", lambda e: e.tensor_scalar(out=nsel.t[:, :], in0=imp.t[:, :], scalar1=m8.t[:, 15:16], scalar2=-1.0, op0=ALU.is_ge, op1=ALU.add),
                      reads=[imp, m8], writes=[nsel])
                ptb = banks[3].t[:, :].bitcast(BF16)
                for ch in range(2):
                    kb.op("pe", lambda e, ch=ch: e.transpose(out=ptb[:, ch * 128:(ch + 1) * 128], in_=nsel.t[:, ch * 128:(ch + 1) * 128], identity=identb.t[:]),
                          reads=[nsel, identb], writes=[banks[3]])
                kb.op("act", lambda e: e.copy(out=negselT.t[:, :, sub], in_=ptb[:, 0:256].rearrange("p (c i) -> p c i", c=2)), reads=[banks[3]], writes=[negselT])

            items = [(j, h) for j in range(4) for h in range(4)]
            cmpA(items[0][0], items[0][1], 0)
            for n_, (j, h) in enumerate(items):
                if n_ + 1 < len(items):
                    cmpA(items[n_ + 1][0], items[n_ + 1][1], (n_ + 1) % 2)
                cmpB(j, h, n_ % 2)
                if h == 3:
                    select(j)
            its = []
            for br in range(2):
                kt_lo = 0 if br == 0 else max(0, 4 * qs - 4)
                kt_hi = 4 * qs + 3
                for h in range(4):
                    for kt in range(kt_lo, kt_hi + 1):
                        its.append((br, h, kt, kt == kt_lo, kt == kt_hi))
            sbanks = (banks[5], banks[6], banks[1])
            pobanks = (banks[7], banks[2])

            def selA(n_):
                br, h, kt, first, last = its[n_]
                ps = sbanks[n_ % 3]
                qq = qsel[h] if br == 0 else qwin[h]
                wide = wsel if br == 0 else wwin
                dl = 4 * qs - kt
                kb.mm(ps.t[:, :], kswT.t[:, kt * 128:(kt + 1) * 128], qq.t[:, :], True, br == 1, [kswT, qq], [ps])
                if br == 0:
                    kb.mm(ps.t[:, :], Exp_.t[:, kt % 64, :], negselT.t[:, kt // 64, :], False, True, [Exp_, negselT], [ps])
                off = 384 + 128 * (min(dl, SEL_FARD) if br == 0 else dl)
                it = n_ % 3
                kb.op("dve", lambda e: e.scalar_tensor_tensor(out=tmp[it].t[:, :], in0=ps.t[:, :], scalar=scale, in1=wide.t[:, h, off:off + 512],
                                                              op0=ALU.mult, op1=ALU.add), reads=[ps, wide], writes=[tmp[it]])
                kb.op("act", lambda e: e.activation(out=pT[it].t[:, :], in_=tmp[it].t[:, :], func=AF.Exp), reads=[tmp[it]], writes=[pT[it]])

            def selB(n_, grp):
                br, h, kt, first, last = its[n_]
                it = n_ % 3
                Vt = Vs if br == 0 else Vw
                po = pobanks[grp % 2]
                kb.mm(po.t[0:65, :], Vt.t[:, kt, 0:65], pT[it].t[:, :], first, last, [pT[it], Vt], [po])
                if not last:
                    return
                pS = poS[grp % 2]
                kb.op("act", lambda e: e.copy(out=pS.t[:, :], in_=po.t[0:65, :]), reads=[po], writes=[pS])
                pt = banks[4]
                for j in range(4):
                    kb.op("pe", lambda e, j=j: e.transpose(out=pt.t[:, j * 65:(j + 1) * 65], in_=pS.t[0:65, j * 128:(j + 1) * 128], identity=identf.t[0:65, 0:65]),
                          reads=[pS, identf], writes=[pt])
                for j in range(4):
                    gcol = (1 + br) * 4 + h
                    kb.op("dve", lambda e, j=j: e.reciprocal(out=fcol.t[:, j:j + 1], in_=pt.t[:, j * 65 + 64:j * 65 + 65]), reads=[pt], writes=[fcol])
                    kb.op("dve", lambda e, j=j, gcol=gcol: e.tensor_tensor(out=fcol.t[:, 4 + j:5 + j], in0=fcol.t[:, j:j + 1], in1=gates.t[:, j, gcol:gcol + 1], op=ALU.mult),
                          reads=[fcol, gates], writes=[fcol])
                    kb.op("dve", lambda e, j=j: e.scalar_tensor_tensor(out=ofin[j].t[:, h, :], in0=pt.t[:, j * 65:j * 65 + 64], scalar=fcol.t[:, 4 + j:5 + j], in1=ofin[j].t[:, h, :],
                                                                       op0=ALU.mult, op1=ALU.add), reads=[pt, fcol, ofin[j]], writes=[ofin[j]])

            grp = 0
            if NSA_STAGE == 7:
                its = []
                continue
            selA(0)
            if len(its) > 1:
                selA(1)
            for n_ in range(len(its)):
                if n_ + 2 < len(its):
                    selA(n_ + 2)
                selB(n_, grp)
                if its[n_][4]:
                    grp += 1
            for j in range(4):
                kb.op("act", lambda e, j=j: e.copy(out=ogb.t[:, :], in_=ofin[j].t[:].rearrange("p a b -> p (a b)")), reads=[ofin[j]], writes=[ogb])
                ptb = banks[3].t[:, :].bitcast(BF16)
                for c in range(2):
                    kb.op("pe", lambda e, c=c, ptb=ptb: e.transpose(out=ptb[:, c * 128:(c + 1) * 128], in_=ogb.t[:, c * 128:(c + 1) * 128], identity=identb.t[:]),
                          reads=[ogb, identb], writes=[banks[3]])
                kb.op("act", lambda e, ptb=ptb: e.copy(out=oTt.t[:].rearrange("p a b -> p (a b)"), in_=ptb[:, 0:256]), reads=[banks[3]], writes=[oTt])
                t0 = qs * 512 + j * 128
                kb.dma("pool", ov[:, :, t0:t0 + 128], oTt.t[:], reads=[oTt], writes=[oT])
        kb.finish([oT])
    return nc


def nsa_inputs(hT_b, w_in, pe, w1, b1, w2, tab, g):
    q = w_in[:, 0:1024].reshape(D, 4, 4, 64)[:, g]
    wqd = np.concatenate([q, q], axis=2).reshape(D, 512)
    blk = lambda i: w_in[:, 1024 + i * 256 + g * 64:1024 + i * 256 + (g + 1) * 64]
    wkv = np.concatenate([blk(0), blk(1), blk(2), blk(4), blk(3), blk(5)], axis=1)
    gates = w_in[:, 2560:2608].reshape(D, 3, 4, 4)[:, :, g, :].reshape(D, 12)
    tb = np.full((33, 4), NEG, np.float32)
    tb[:32] = tab[:, g * 4:(g + 1) * 4]
    ohs, ohw, ohc = nsa_onehots()
    return {"hT": hT_b, "wqd": np.ascontiguousarray(wqd), "wkv": np.ascontiguousarray(wkv), "wg": np.ascontiguousarray(gates),
            "pe": np.ascontiguousarray(pe), "w1": np.ascontiguousarray(w1), "b1": np.ascontiguousarray(b1), "w2": np.ascontiguousarray(w2),
            "tab": tb, "ohs": ohs, "ohw": ohw, "ohc": ohc}


_NC_CACHE = {}


def _get(name, fn, *a):
    key = (name,) + a
    if key not in _NC_CACHE:
        _NC_CACHE[key] = fn(*a)
    return _NC_CACHE[key]


def _run(nc, in_maps):
    res = run_bass_kernel_spmd(nc, in_maps, core_ids=list(range(NCORES)))
    return res.results


def kernel(x, c, rel_table, mod_w, mod_b, ln_g, ln_b,
           gla_w_in, gla_w_a2, gla_b_a, gla_norm_g, gla_w_o,
           nsa_w_in, nsa_cmp_pe, nsa_cmp_w1, nsa_cmp_b1, nsa_cmp_w2, nsa_w_o,
           dil_w_in, dil_w_o,
           ffn_w_up, ffn_conv_w, ffn_conv_b, ffn_w_down):
    f32 = lambda a: np.ascontiguousarray(np.asarray(a, dtype=np.float32))
    x = f32(x); c = f32(c); rel_table = f32(rel_table); mod_w = f32(mod_w); mod_b = f32(mod_b)
    ln_g = f32(ln_g); ln_b = f32(ln_b)
    S = x.shape[1]
    T = S // 4
    dbg = globals().get("_DBG")
    res = _run(_get("mod", build_mod), [{"c": c, "w": f32(mod_w[s // 2, s % 2]), "b": f32(mod_b[s // 2, s % 2][None])} for s in range(8)])
    mod = [r["out"] for r in res]

    def shards():
        for k in range(NCORES):
            yield k, k // 4, (k % 4) * T

    res = _run(_get("prep", build_prep, T), [{"x": f32(x[b, t0:t0 + T]), "vec": f32(np.stack([mod[0][b, 0:D], mod[0][b, D:2 * D]]))} for k, b, t0 in shards()])
    hT = np.zeros((NB, D, S), ml_dtypes.bfloat16)
    for (k, b, t0), r in zip(shards(), res):
        hT[b, :, t0:t0 + T] = r["hT"]
    xcur = x
    for i in range(DEPTH):
        kind, j = i % 3, i // 3
        ins = []
        for k in range(NCORES):
            b, g = k // 4, k % 4
            hb = np.ascontiguousarray(hT[b])
            if kind == 0:
                w_in = f32(gla_w_in[j])
                wa2 = np.zeros((33, 128), np.float32)
                wa2[:16] = f32(gla_w_a2[j])[:, g * 128:(g + 1) * 128]
                wa2[32] = f32(gla_b_a[j])[g * 128:(g + 1) * 128]
                ins.append({"hT": hb, "wq": f32(w_in[:, g * 128:(g + 1) * 128]), "wk": f32(w_in[:, 512 + g * 128:512 + (g + 1) * 128]),
                            "wv": f32(w_in[:, 1024 + g * 256:1024 + (g + 1) * 256]), "wr": f32(w_in[:, 2048 + g * 256:2048 + (g + 1) * 256]),
                            "wa": f32(w_in[:, 3072:3088]), "wa2": wa2, "ng": f32(gla_norm_g[j])[None]})
            elif kind == 1:
                ins.append(nsa_inputs(hb, f32(nsa_w_in[j]), f32(nsa_cmp_pe[j]), f32(nsa_cmp_w1[j]), f32(nsa_cmp_b1[j]), f32(nsa_cmp_w2[j]), rel_table, g))
            else:
                wgd = f32(dil_w_in[j]).reshape(D, 3, 3, 1024)
                wqkv = np.stack([np.concatenate([wgd[:, gg, cc, g * 256:(g + 1) * 256] for cc in range(3)], axis=1) for gg in range(3)])
                tb = np.full((33, 4), NEG, np.float32)
                tb[:32] = rel_table[:, g * 4:(g + 1) * 4]
                ins.append({"hT": hb, "wqkv": f32(wqkv), "tab": tb, "oh": dil_onehots()})
        ncm = _get(("gla", "nsa", "dil")[kind], (build_gla, build_nsa, build_dil)[kind], S)
        res = _run(ncm, ins)
        oT = np.zeros((NB, D, S), ml_dtypes.bfloat16)
        for k in range(NCORES):
            oT[k // 4, (k % 4) * 256:(k % 4 + 1) * 256, :] = res[k]["oT"]
        w_o = f32((gla_w_o, nsa_w_o, dil_w_o)[kind][j])
        ins = []
        for k, b, t0 in shards():
            xh = np.zeros((T + 128, D), np.float32)
            oh_ = np.zeros((D, T + 128), ml_dtypes.bfloat16)
            xh[128:] = xcur[b, t0:t0 + T]
            oh_[:, 128:] = oT[b, :, t0:t0 + T]
            if t0 > 0:
                xh[:128] = xcur[b, t0 - 128:t0]
                oh_[:, :128] = oT[b, :, t0 - 128:t0]
            vec = np.zeros((10, D), np.float32)
            vec[0] = mod[2 * i][b, 2 * D:3 * D]
            vec[1] = mod[2 * i + 1][b, 0:D]; vec[2] = mod[2 * i + 1][b, D:2 * D]; vec[3] = mod[2 * i + 1][b, 2 * D:3 * D]
            if i + 1 < DEPTH:
                vec[4] = mod[2 * i + 2][b, 0:D]; vec[5] = mod[2 * i + 2][b, D:2 * D]
            vec[6] = ln_g[i, 0]; vec[7] = ln_b[i, 0]; vec[8] = ln_g[i, 1]; vec[9] = ln_b[i, 1]
            ins.append({"x": xh, "oT": oh_, "wo": w_o, "wup": f32(ffn_w_up[i]), "wdn": f32(ffn_w_down[i]), "convw": f32(ffn_conv_w[i]),
                        "convb": f32(ffn_conv_b[i]), "vec": vec, "flag": np.full((128, 1), 0.0 if t0 == 0 else 1.0, np.float32)})
        res = _run(_get("post", build_post, T), ins)
        xn = np.zeros((NB, S, D), np.float32)
        for (k, b, t0), r in zip(shards(), res):
            xn[b, t0:t0 + T] = r["xo"]
            hT[b, :, t0:t0 + T] = r["hT"]
        xcur = xn
        if dbg is not None:
            dbg.append(xn.copy())
    return xcur
```

```python
import math
from contextlib import ExitStack
import numpy as np
import ml_dtypes
import concourse.bass as bass
import concourse.mybir as mybir
from concourse.bass_utils import run_bass_kernel_spmd

F32 = mybir.dt.float32
BF16 = mybir.dt.bfloat16
AF = mybir.ActivationFunctionType
ALU = mybir.AluOpType
AX = mybir.AxisListType

D = 1024
SEQ = 16384
NB = 2
DEPTH = 4
D_FF = 2816
NFC = D_FF // 128
DN_ALPHA = (2 * DEPTH) ** 0.25
LN_EPS = 1e-5
NEG = -30000.0
NCORES = 8


class Buf:
    __slots__ = ("w", "r", "t", "parts")

    def __init__(self, t=None, nparts=0):
        self.w = None
        self.r = {}
        self.t = t
        self.parts = [Buf(t) for _ in range(nparts)]

    def __getitem__(self, k):
        return self.t[k]


class Eng:
    def __init__(self, name, h, sem):
        self.name, self.h, self.sem = name, h, sem
        self.count = 0
        self.waited = {}


class KB:
    def __init__(self, nc, es, ndma=40):
        self.nc, self.es = nc, es
        self.eng = {}
        for name, h in (("pe", nc.tensor), ("act", nc.scalar), ("dve", nc.vector), ("pool", nc.gpsimd), ("sp", nc.sync)):
            self.eng[name] = Eng(name, h, es.enter_context(nc.semaphore("sem_" + name)))
        self.dsl = [[es.enter_context(nc.semaphore("dsem%d" % i)), 0] for i in range(ndma)]
        self.di = {"sp": 0, "pool": 0, "act": 0}
        self.dq = {"sp": list(range(0, ndma // 2)), "pool": list(range(ndma // 2, ndma - 4)), "act": list(range(ndma - 4, ndma))}
        self.nt = 0

    def sb(self, shape, dt=F32, name=None, nparts=0, es=None):
        self.nt += 1
        return Buf((es or self.es).enter_context(self.nc.sbuf_tensor(name or "t%d" % self.nt, list(shape), dt)), nparts)

    def ps(self, shape=(128, 512), dt=F32, name=None):
        self.nt += 1
        return Buf(self.es.enter_context(self.nc.psum_tensor(name or "p%d" % self.nt, list(shape), dt)))

    def dram(self, name, shape, dt, kind="Internal"):
        return Buf(self.nc.dram_tensor(name, list(shape), dt, kind=kind).ap())

    def _semof(self, key):
        if isinstance(key, str):
            return self.eng[key].sem
        return self.dsl[key][0]

    def _wait(self, E, deps):
        need = {}
        for key, val in deps:
            if key == E.name and key in ("pe", "sp"):
                continue
            if val > need.get(key, 0):
                need[key] = val
        for key, val in need.items():
            if E.waited.get(key, 0) >= val:
                continue
            E.h.wait_ge(self._semof(key), val)
            E.waited[key] = val

    @staticmethod
    def _deps(reads, writes):
        deps = []
        for b in reads:
            if b.w is not None:
                deps.append(b.w)
        for b in writes:
            if b.w is not None:
                deps.append(b.w)
            deps.extend(b.r.items())
        return deps

    @staticmethod
    def _record(ev, reads, writes):
        key, val = ev
        for b in reads:
            if b.r.get(key, 0) < val:
                b.r[key] = val
        for b in writes:
            b.w = ev
            b.r = {}

    def op(self, en, fn, reads=(), writes=()):
        E = self.eng[en]
        self._wait(E, self._deps(reads, writes))
        ins = fn(E.h)
        E.count += 1
        ins.then_inc(E.sem, 1)
        self._record((en, E.count), reads, writes)

    def dma(self, qn, out, in_, reads=(), writes=(), **kw):
        Q = self.eng[qn]
        k = self.dq[qn][self.di[qn] % len(self.dq[qn])]
        self.di[qn] += 1
        slot = self.dsl[k]
        deps = self._deps(reads, writes)
        if slot[1] > 0:
            deps.append((k, slot[1]))
        self._wait(Q, deps)
        Q.h.dma_start(out=out, in_=in_, **kw).then_inc(slot[0], 16)
        slot[1] += 16
        self._record((k, slot[1]), reads, writes)

    def barrier(self):
        deps = [(n, E.count) for n, E in self.eng.items() if E.count > 0]
        deps += [(k, sl[1]) for k, sl in enumerate(self.dsl) if sl[1] > 0]
        for E in self.eng.values():
            self._wait(E, [d for d in deps if d[0] != E.name])

    def finish(self, outs):
        E = self.eng["sp"]
        self._wait(E, [b.w for b in outs if b.w is not None])

    def mm(self, out, lhsT, rhs, start, stop, reads, writes):
        self.op("pe", lambda e: e.matmul(out, lhsT=lhsT, rhs=rhs, start=start, stop=stop), reads, writes)


def new_nc():
    return bass.Bass("TRN2", target_bir_lowering=False)


def dr_in(nc, name, shape, dt=F32):
    return Buf(nc.dram_tensor(name, list(shape), dt, kind="ExternalInput").ap())


def dr_out(nc, name, shape, dt=F32):
    return Buf(nc.dram_tensor(name, list(shape), dt, kind="ExternalOutput").ap())


def load_bcast(kb, q, dst, src_row_ap, n=128):
    kb.dma(q, dst.t[0:n, :], src_row_ap.partition_broadcast(n), writes=[dst])


def make_ident(kb, dt=BF16):
    idf = kb.sb([128, 128], F32)
    kb.op("pool", lambda e: e.memset(idf[:], 0.0), writes=[idf])
    kb.op("pool", lambda e: e.affine_select(out=idf[:], in_=idf[:], pattern=[[-1, 128]], compare_op=ALU.not_equal,
                                             fill=1.0, base=0, channel_multiplier=1), reads=[idf], writes=[idf])
    if dt == F32:
        return idf
    idb = kb.sb([128, 128], dt)
    kb.op("dve", lambda e: e.tensor_copy(out=idb[:], in_=idf[:]), reads=[idf], writes=[idb])
    return idb


def gen_load_w_bf16(kb, dst, src_ap_fn, nk, ncols, stage, qs=("sp", "pool"), chunk=512, npart=128):
    i = 0
    for k in range(nk):
        for c0 in range(0, ncols, chunk):
            c1 = min(ncols, c0 + chunk)
            st = stage[i % len(stage)]
            kb.dma(qs[i % len(qs)], st.t[0:npart, 0:c1 - c0], src_ap_fn(k, c0, c1), writes=[st])
            if i % 2:
                kb.op("act", lambda e, st=st, k=k, c0=c0, c1=c1: e.copy(out=dst.t[0:npart, k, c0:c1], in_=st.t[0:npart, 0:c1 - c0]),
                      reads=[st], writes=[dst.parts[k]])
            else:
                kb.op("dve", lambda e, st=st, k=k, c0=c0, c1=c1: e.tensor_copy(out=dst.t[0:npart, k, c0:c1], in_=st.t[0:npart, 0:c1 - c0]),
                      reads=[st], writes=[dst.parts[k]])
            i += 1
            yield


def load_w_bf16(kb, dst, src_ap_fn, nk, ncols, stage, qs=("sp", "pool"), chunk=512, npart=128):
    i = 0
    for k in range(nk):
        for c0 in range(0, ncols, chunk):
            c1 = min(ncols, c0 + chunk)
            st = stage[i % len(stage)]
            kb.dma(qs[i % len(qs)], st.t[0:npart, 0:c1 - c0], src_ap_fn(k, c0, c1), writes=[st])
            en = "act" if i % 2 else "dve"
            if en == "act":
                kb.op("act", lambda e, st=st, k=k, c0=c0, c1=c1: e.copy(out=dst.t[0:npart, k, c0:c1], in_=st.t[0:npart, 0:c1 - c0]),
                      reads=[st], writes=[dst.parts[k]])
            else:
                kb.op("dve", lambda e, st=st, k=k, c0=c0, c1=c1: e.tensor_copy(out=dst.t[0:npart, k, c0:c1], in_=st.t[0:npart, 0:c1 - c0]),
                      reads=[st], writes=[dst.parts[k]])
            i += 1


def emit_ln(kb, z, st, mv, lng, lnb):
    for c in range(2):
        kb.op("dve", lambda e, c=c: e.bn_stats(out=st.t[:, c, :], in_=z.t[:, c * 512:(c + 1) * 512]), reads=[z], writes=[st])
    kb.op("dve", lambda e: e.bn_aggr(out=mv.t[:, 0:2], in_=st.t[:].rearrange("p a b -> p (a b)")), reads=[st], writes=[mv])
    kb.op("dve", lambda e: e.tensor_scalar_add(out=mv.t[:, 2:3], in0=mv.t[:, 1:2], scalar1=LN_EPS), reads=[mv], writes=[mv])
    kb.op("act", lambda e: e.sqrt(out=mv.t[:, 2:3], in_=mv.t[:, 2:3]), reads=[mv], writes=[mv])
    kb.op("dve", lambda e: e.reciprocal(out=mv.t[:, 2:3], in_=mv.t[:, 2:3]), reads=[mv], writes=[mv])
    kb.op("dve", lambda e: e.tensor_scalar(out=z.t[:], in0=z.t[:], scalar1=mv.t[:, 0:1], scalar2=mv.t[:, 2:3],
                                           op0=ALU.subtract, op1=ALU.mult), reads=[z, mv], writes=[z])
    kb.op("pool", lambda e: e.tensor_tensor(out=z.t[:], in0=z.t[:], in1=lng.t[:], op=ALU.mult), reads=[z, lng], writes=[z])
    kb.op("pool", lambda e: e.tensor_tensor(out=z.t[:], in0=z.t[:], in1=lnb.t[:], op=ALU.add), reads=[z, lnb], writes=[z])


def emit_modT(kb, x, pst, identf, scp, sh, ht, ncol=128, c0=0):
    for k in range(8):
        kb.op("pe", lambda e, k=k: e.transpose(out=pst.t[:, k * 128:(k + 1) * 128], in_=x.t[:, k * 128:(k + 1) * 128],
                                               identity=identf.t[:]), reads=[x, identf], writes=[pst])
    for k in range(8):
        kb.op("act", lambda e, k=k: e.activation(out=ht.t[:, k, c0:c0 + 128], in_=pst.t[:, k * 128:(k + 1) * 128],
                                                 func=AF.Identity, scale=scp.t[:, k:k + 1], bias=sh.t[:, k:k + 1]),
              reads=[pst, scp, sh], writes=[ht])


def load_pp(kb, q, dst, row_ap, nk=8):
    kb.dma(q, dst.t[:, 0:nk], row_ap.rearrange("(k p) -> p k", p=128), writes=[dst], allow_slow_non_contiguous=True)


def build_prep(T):
    nc = new_nc()
    x = dr_in(nc, "x", [T, D])
    vec = dr_in(nc, "vec", [2, D])
    hTo = dr_out(nc, "hT", [D, T], BF16)
    with ExitStack() as es:
        kb = KB(nc, es)
        identf = make_ident(kb, F32)
        sh = kb.sb([128, 8]); scp = kb.sb([128, 8])
        load_pp(kb, "sp", sh, vec.t[0, :]); load_pp(kb, "sp", scp, vec.t[1, :])
        kb.op("dve", lambda e: e.tensor_scalar_add(out=scp.t[:], in0=scp.t[:], scalar1=1.0), reads=[scp], writes=[scp])
        xs = [kb.sb([128, D]) for _ in range(3)]
        hts = [kb.sb([128, 8, 128], BF16) for _ in range(2)]
        psts = [kb.ps([128, 1024]) for _ in range(2)]
        hv = hTo.t.rearrange("(k p) t -> p k t", p=128)
        for i in range(T // 128):
            xt = xs[i % 3]; ht = hts[i % 2]; pst = psts[i % 2]
            kb.dma("sp", xt.t[:], x.t[i * 128:(i + 1) * 128, :], writes=[xt])
            emit_modT(kb, xt, pst, identf, scp, sh, ht)
            kb.dma("pool", hv[:, :, i * 128:(i + 1) * 128], ht.t[:], reads=[ht], writes=[hTo])
        kb.finish([hTo])
    return nc


def build_post(T):
    TT = 256
    nc = new_nc()
    x = dr_in(nc, "x", [T + 128, D])
    oT = dr_in(nc, "oT", [D, T + 128], BF16)
    wo = dr_in(nc, "wo", [D, D])
    wup = dr_in(nc, "wup", [D, 2 * D_FF])
    wdn = dr_in(nc, "wdn", [D_FF, D])
    convw = dr_in(nc, "convw", [3, D_FF])
    convb = dr_in(nc, "convb", [D_FF])
    vec = dr_in(nc, "vec", [10, D])
    flag = dr_in(nc, "flag", [128, 1])
    xo = dr_out(nc, "xo", [T, D])
    hTo = dr_out(nc, "hT", [D, T], BF16)
    x1s = Buf(nc.dram_tensor("x1s", [T, D], F32, kind="Internal").ap())
    h1s = Buf(nc.dram_tensor("h1s", [D, T], BF16, kind="Internal").ap())
    ntile = T // 128
    with ExitStack() as es:
        kb = KB(nc, es)
        identf = make_ident(kb, F32)
        banks = [kb.ps([128, 512]) for _ in range(4)]
        pst2 = [kb.ps([128, 1024]) for _ in range(2)]
        h1halo = kb.sb([128, 8, 128], BF16)
        pp = kb.sb([128, 6, 8])
        ppb = [Buf(pp.t) for _ in range(4)]
        for j, r in enumerate((1, 2, 4, 5)):
            kb.dma("sp", pp.t[:, j, :], vec.t[r, :].rearrange("(k p) -> p k", p=128), writes=[ppb[j]], allow_slow_non_contiguous=True)
        for j in (1, 3):
            kb.op("dve", lambda e, j=j: e.tensor_scalar_add(out=pp.t[:, j, :], in0=pp.t[:, j, :], scalar1=1.0), reads=[ppb[j]], writes=[ppb[j]])

        class PV:
            pass
        sh2 = Buf(pp.t[:, 0, :]); sc2 = Buf(pp.t[:, 1, :]); shn = Buf(pp.t[:, 2, :]); scn = Buf(pp.t[:, 3, :])
        for v, b in ((sh2, ppb[0]), (sc2, ppb[1]), (shn, ppb[2]), (scn, ppb[3])):
            v.w = b.w
        flg = kb.sb([128, 1])
        kb.dma("sp", flg.t[:], flag.t[:, :], writes=[flg])
        st = kb.sb([128, 2, 6]); mv = kb.sb([128, 4])
        zs = [kb.sb([128, D]) for _ in range(2)]
        xs = [kb.sb([128, D]) for _ in range(2)]
        hts = [kb.sb([128, 8, 128], BF16) for _ in range(2)]
        g1p = kb.sb([128, D]); lng = kb.sb([128, D]); lnb = kb.sb([128, D])

        def load_gl(gr, lgr, lbr):
            load_bcast(kb, "sp", g1p, vec.t[gr:gr + 1, :])
            load_bcast(kb, "sp", lng, vec.t[lgr:lgr + 1, :])
            load_bcast(kb, "sp", lnb, vec.t[lbr:lbr + 1, :])
            kb.op("pool", lambda e: e.tensor_scalar_add(out=g1p.t[:], in0=g1p.t[:], scalar1=1.0), reads=[g1p], writes=[g1p])

        load_gl(0, 6, 7)
        h1v = h1s.t.rearrange("(k p) t -> p k t", p=128)
        hov = hTo.t.rearrange("(k p) t -> p k t", p=128)
        oTv = oT.t.rearrange("(k p) t -> p k t", p=128)

        def resid_ln(psy, xt, z):
            for half in range(2):
                kb.op("dve", lambda e, half=half: e.tensor_tensor(out=z.t[:, half * 512:(half + 1) * 512], in0=psy[half].t[:, :],
                                                                   in1=g1p.t[:, half * 512:(half + 1) * 512], op=ALU.mult),
                      reads=[psy[half], g1p], writes=[z])
            kb.op("dve", lambda e: e.scalar_tensor_tensor(out=z.t[:], in0=xt.t[:], scalar=DN_ALPHA, in1=z.t[:],
                                                           op0=ALU.mult, op1=ALU.add), reads=[xt, z], writes=[z])
            emit_ln(kb, z, st, mv, lng, lnb)

        wub = kb.sb([128, 8, 2 * D_FF], BF16, nparts=8)
        wdb = kb.sb([128, NFC, D], BF16, nparts=NFC)
        stageB = [kb.sb([128, 512]) for _ in range(6)]

        def _wgen():
            yield from gen_load_w_bf16(kb, wub, lambda k, c0, c1: wup.t[k * 128:(k + 1) * 128, c0:c1], 8, 2 * D_FF, stageB, qs=("pool",))
            yield from gen_load_w_bf16(kb, wdb, lambda k, c0, c1: wdn.t[k * 128:(k + 1) * 128, c0:c1], NFC, D, stageB, qs=("pool",))
        wgen = _wgen()
        nchunks_w = 8 * 11 + NFC * 2
        per_tile = -(-nchunks_w // (ntile + 1))
        with ExitStack() as esA:
            wob = kb.sb([128, 8, D], BF16, nparts=8, es=esA)
            stage = [kb.sb([128, 512], es=esA) for _ in range(2)]
            ots = [kb.sb([128, 8, 128], BF16, es=esA) for _ in range(2)]
            load_w_bf16(kb, wob, lambda k, c0, c1: wo.t[k * 128:(k + 1) * 128, c0:c1], 8, D, stage)
            def loadA(i):
                kb.dma("sp", xs[i % 2].t[:], x.t[i * 128:(i + 1) * 128, :], writes=[xs[i % 2]])
                kb.dma("sp", ots[i % 2].t[:], oTv[:, :, i * 128:(i + 1) * 128], writes=[ots[i % 2]])

            loadA(0)
            for i in range(ntile + 1):
                xt = xs[i % 2]; z = zs[i % 2]; ot = ots[i % 2]; ht = hts[i % 2]
                psy = banks[2 * (i % 2):2 * (i % 2) + 2]
                for half in range(2):
                    for k in range(8):
                        kb.mm(psy[half].t[:, :], ot.t[:, k, :], wob.t[:, k, half * 512:(half + 1) * 512], k == 0, k == 7,
                              [ot, wob.parts[k]], [psy[half]])
                resid_ln(psy, xt, z)
                if i + 1 <= ntile:
                    loadA(i + 1)
                if i > 0:
                    kb.dma("sp", x1s.t[(i - 1) * 128:i * 128, :], z.t[:], reads=[z], writes=[x1s])
                emit_modT(kb, z, pst2[i % 2], identf, sc2, sh2, h1halo if i == 0 else ht)
                if i > 0:
                    kb.dma("sp", h1v[:, :, (i - 1) * 128:i * 128], ht.t[:], reads=[ht], writes=[h1s])
                for _ in range(per_tile):
                    next(wgen, None)
            for _ in wgen:
                pass

        kb.barrier()
        load_gl(3, 8, 9)
        with ExitStack() as esB:
            cw = kb.sb([128, 3, NFC], es=esB); cb = kb.sb([128, NFC], es=esB)
            for j in range(3):
                kb.dma("sp", cw.t[:, j, :], convw.t[j, :].rearrange("(k p) -> p k", p=128), writes=[cw], allow_slow_non_contiguous=True)
            kb.dma("sp", cb.t[:, :], convb.t[:].rearrange("(k p) -> p k", p=128), writes=[cb], allow_slow_non_contiguous=True)
            uprev = kb.sb([128, NFC, 2], nparts=NFC, es=esB)
            h1t = [kb.sb([128, 8, TT], BF16, es=esB) for _ in range(2)]
            aT = kb.sb([128, NFC, TT], BF16, nparts=NFC, es=esB)
            ubs = [kb.sb([128, TT + 2], es=esB) for _ in range(2)]
            cbs = [kb.sb([128, TT], es=esB) for _ in range(2)]
            sbs = [kb.sb([128, TT], es=esB) for _ in range(2)]
            pu = banks[0]
            for fc in range(NFC):
                for k in range(8):
                    kb.mm(pu.t[:, fc * 2:fc * 2 + 2], wub.t[:, k, fc * 128:(fc + 1) * 128], h1halo.t[:, k, 126:128], k == 0, k == 7,
                          [wub.parts[k], h1halo], [pu])
            kb.op("dve", lambda e: e.tensor_scalar_mul(out=uprev.t[:].rearrange("p a b -> p (a b)"), in0=pu.t[:, 0:2 * NFC],
                                                       scalar1=flg.t[:, 0:1]), reads=[pu, flg], writes=uprev.parts)
            nug = 0
            kb.dma("sp", h1t[0].t[:], h1v[:, :, 0:TT], reads=[h1s], writes=[h1t[0]])
            for it in range(T // TT):
                hh = h1t[it % 2]
                for sub in range(TT // 128):
                    i = it * (TT // 128) + sub
                    kb.dma("sp", xs[i % 2].t[:], x1s.t[i * 128:(i + 1) * 128, :], reads=[x1s], writes=[xs[i % 2]])
                if it + 1 < T // TT:
                    kb.dma("sp", h1t[(it + 1) % 2].t[:], h1v[:, :, (it + 1) * TT:(it + 2) * TT], reads=[h1s], writes=[h1t[(it + 1) % 2]])
                for fc in range(NFC):
                    pug = banks[nug % 2]; ub = ubs[nug % 2]; cbuf = cbs[nug % 2]; sbuf = sbs[nug % 2]; nug += 1
                    for k in range(8):
                        kb.mm(pug.t[:, 0:TT], wub.t[:, k, fc * 128:(fc + 1) * 128], hh.t[:, k, :], k == 0, k == 7, [wub.parts[k], hh], [pug])
                    for k in range(8):
                        kb.mm(pug.t[:, TT:2 * TT], wub.t[:, k, D_FF + fc * 128:D_FF + (fc + 1) * 128], hh.t[:, k, :], k == 0, k == 7,
                              [wub.parts[k], hh], [pug])
                    kb.op("pool", lambda e, ub=ub, fc=fc: e.tensor_copy(out=ub.t[:, 0:2], in_=uprev.t[:, fc, :]), reads=[uprev.parts[fc]], writes=[ub])
                    kb.op("act", lambda e, ub=ub, pug=pug: e.copy(out=ub.t[:, 2:TT + 2], in_=pug.t[:, 0:TT]), reads=[pug], writes=[ub])
                    kb.op("pool", lambda e, ub=ub, fc=fc: e.tensor_copy(out=uprev.t[:, fc, :], in_=ub.t[:, TT:TT + 2]), reads=[ub], writes=[uprev.parts[fc]])
                    kb.op("act", lambda e, ub=ub, cbuf=cbuf, fc=fc: e.activation(out=cbuf.t[:], in_=ub.t[:, 2:TT + 2], func=AF.Identity,
                                                                                 scale=cw.t[:, 2, fc:fc + 1], bias=cb.t[:, fc:fc + 1]),
                          reads=[ub, cw, cb], writes=[cbuf])
                    kb.op("dve", lambda e, ub=ub, cbuf=cbuf, fc=fc: e.scalar_tensor_tensor(out=cbuf.t[:], in0=ub.t[:, 1:TT + 1], scalar=cw.t[:, 1, fc:fc + 1],
                                                                                           in1=cbuf.t[:], op0=ALU.mult, op1=ALU.add),
                          reads=[ub, cw, cbuf], writes=[cbuf])
                    kb.op("dve", lambda e, ub=ub, cbuf=cbuf, fc=fc: e.scalar_tensor_tensor(out=cbuf.t[:], in0=ub.t[:, 0:TT], scalar=cw.t[:, 0, fc:fc + 1],
                                                                                           in1=cbuf.t[:], op0=ALU.mult, op1=ALU.add),
                          reads=[ub, cw, cbuf], writes=[cbuf])
                    kb.op("act", lambda e, cbuf=cbuf, sbuf=sbuf: e.activation(out=sbuf.t[:], in_=cbuf.t[:], func=AF.Silu), reads=[cbuf], writes=[sbuf])
                    kb.op("dve", lambda e, sbuf=sbuf, pug=pug, fc=fc: e.tensor_tensor(out=aT.t[:, fc, :], in0=pug.t[:, TT:2 * TT], in1=sbuf.t[:], op=ALU.mult),
                          reads=[pug, sbuf], writes=[aT.parts[fc]])
                for sub in range(TT // 128):
                    i = it * (TT // 128) + sub
                    psy = banks[2:4]
                    xt = xs[i % 2]; z = zs[i % 2]; ht = hts[i % 2]
                    for half in range(2):
                        for fc in range(NFC):
                            kb.mm(psy[half].t[:, :], aT.t[:, fc, sub * 128:(sub + 1) * 128], wdb.t[:, fc, half * 512:(half + 1) * 512],
                                  fc == 0, fc == NFC - 1, [aT.parts[fc], wdb.parts[fc]], [psy[half]])
                    resid_ln(psy, xt, z)
                    kb.dma("sp", xo.t[i * 128:(i + 1) * 128, :], z.t[:], reads=[z], writes=[xo])
                    emit_modT(kb, z, pst2[i % 2], identf, scn, shn, ht)
                    kb.dma("sp", hov[:, :, i * 128:(i + 1) * 128], ht.t[:], reads=[ht], writes=[hTo])
        kb.finish([xo, hTo])
    return nc


def build_mod():
    nc = new_nc()
    c = dr_in(nc, "c", [NB, D])
    w = dr_in(nc, "w", [D, 3 * D])
    b = dr_in(nc, "b", [1, 3 * D])
    out = dr_out(nc, "out", [NB, 3 * D])
    with ExitStack() as es:
        kb = KB(nc, es)
        cs = kb.sb([128, 8, NB])
        for bb in range(NB):
            kb.dma("sp", cs.t[:, :, bb], c.t[bb, :].rearrange("(k p) -> p k", p=128), writes=[cs], allow_slow_non_contiguous=True)
        kb.op("act", lambda e: e.activation(out=cs.t[:], in_=cs.t[:], func=AF.Silu), reads=[cs], writes=[cs])
        wt = kb.sb([128, 8, 3 * D], nparts=8)
        for k in range(8):
            kb.dma("sp" if k % 2 else "pool", wt.t[:, k, :], w.t[k * 128:(k + 1) * 128, :], writes=[wt.parts[k]])
        bt = kb.sb([NB, 3 * D])
        load_bcast(kb, "sp", bt, b.t[0:1, :], n=NB)
        ot = kb.sb([NB, 3 * D])
        banks = [kb.ps([128, 512]) for _ in range(6)]
        for n in range(6):
            for k in range(8):
                kb.mm(banks[n].t[0:NB, :], cs.t[:, k, :], wt.t[:, k, n * 512:(n + 1) * 512], k == 0, k == 7, [cs, wt.parts[k]], [banks[n]])
            kb.op("dve", lambda e, n=n: e.tensor_tensor(out=ot.t[:, n * 512:(n + 1) * 512], in0=banks[n].t[0:NB, :],
                                                         in1=bt.t[:, n * 512:(n + 1) * 512], op=ALU.add), reads=[banks[n], bt], writes=[ot])
        kb.dma("sp", out.t[:, :], ot.t[:], reads=[ot], writes=[out])
        kb.finish([out])
    return nc


GLA_DK, GLA_DV = 128, 256


def tri_const(kb, val, upper_strict):
    m = kb.sb([128, 128])
    kb.op("pool", lambda e: e.memset(m.t[:], val), writes=[m])
    if not upper_strict:
        kb.op("pool", lambda e: e.affine_select(out=m.t[:], in_=m.t[:], pattern=[[1, 128]], compare_op=ALU.is_ge, fill=0.0,
                                                 base=0, channel_multiplier=-1), reads=[m], writes=[m])
        kb.op("pool", lambda e: e.memset(m.t[0:64, 64:128], 0.0), reads=[m], writes=[m])
    else:
        kb.op("pool", lambda e: e.affine_select(out=m.t[:], in_=m.t[:], pattern=[[-1, 128]], compare_op=ALU.is_ge, fill=0.0,
                                                 base=-1, channel_multiplier=1), reads=[m], writes=[m])
        kb.op("pool", lambda e: e.memset(m.t[64:128, 0:64], 0.0), reads=[m], writes=[m])
    return m


GLA_STAGE = 7
GLA_SUB = 99


def build_gla(S):
    nc = new_nc()
    hT = dr_in(nc, "hT", [D, S], BF16)
    wq = dr_in(nc, "wq", [D, 128]); wk = dr_in(nc, "wk", [D, 128])
    wv = dr_in(nc, "wv", [D, 256]); wr = dr_in(nc, "wr", [D, 256]); wa = dr_in(nc, "wa", [D, 16])
    wa2 = dr_in(nc, "wa2", [33, 128])
    ng = dr_in(nc, "ng", [1, 256])
    oT = dr_out(nc, "oT", [256, S], BF16)
    with ExitStack() as es:
        kb = KB(nc, es)
        identb = make_ident(kb, BF16)
        Lneg = tri_const(kb, -1.0 / 16.0, False)
        Uneg = tri_const(kb, -1.0 / 16.0, True)
        M1 = tri_const(kb, 1.0, False)
        stage = [kb.sb([128, 512]) for _ in range(2)]
        wqb = kb.sb([128, 8, 128], BF16, nparts=8); wkb = kb.sb([128, 8, 128], BF16, nparts=8)
        wvb = kb.sb([128, 8, 256], BF16, nparts=8); wrb = kb.sb([128, 8, 256], BF16, nparts=8)
        wab = kb.sb([128, 8, 16], BF16, nparts=8)
        for dst, src, n in ((wqb, wq, 128), (wkb, wk, 128), (wvb, wv, 256), (wrb, wr, 256), (wab, wa, 16)):
            load_w_bf16(kb, dst, lambda k, c0, c1, src=src: src.t[k * 128:(k + 1) * 128, c0:c1], 8, n, stage)
        wa2b = kb.sb([33, 1, 128], BF16, nparts=1)
        load_w_bf16(kb, wa2b, lambda k, c0, c1: wa2.t[0:33, c0:c1], 1, 128, stage, npart=33)
        ngb = kb.sb([128, 256])
        load_bcast(kb, "sp", ngb, ng.t[0:1, :])
        alT = kb.sb([33, 512], BF16)
        kb.op("pool", lambda e: e.memset(alT.t[:], 0.0), writes=[alT])
        kb.op("pool", lambda e: e.memset(alT.t[32:33, :], 1.0), reads=[alT], writes=[alT])
        S32 = kb.sb([128, 256]); SbA = kb.sb([128, 256], BF16); SbB = kb.sb([128, 256], BF16)
        kb.op("pool", lambda e: e.memset(S32.t[:], 0.0), writes=[S32])
        kb.op("pool", lambda e: e.memset(SbA.t[:], 0.0), writes=[SbA])
        pq = kb.ps(); pk = kb.ps(); pal = kb.ps(); pD = kb.ps(); pE = kb.ps(); pF = kb.ps(); pG = kb.ps(); pH = kb.ps()
        pDk = pDl = pDt = pD
        pEv = pEr = pE
        pFb = pFr = pFa = pF
        pGa = pGb = pG
        pH0 = pH1 = pH
        pT = pD.t[:, 256:512].bitcast(BF16)
        hhs = [kb.sb([128, 8, 512], BF16) for _ in range(2)]
        R = 2
        ex = [kb.sb([128, 128]) for _ in range(R)]; ltm = [kb.sb([128, 128]) for _ in range(R)]
        ebT = [kb.sb([128, 128]) for _ in range(R)]; einvT = [kb.sb([128, 128]) for _ in range(R)]
        erem = [kb.sb([128, 128]) for _ in range(R)]
        qdT = [kb.sb([128, 128], BF16) for _ in range(R)]; kiT = [kb.sb([128, 128], BF16) for _ in range(R)]
        kdec = [kb.sb([128, 128], BF16) for _ in range(R)]; vsb = [kb.sb([128, 256], BF16) for _ in range(R)]
        kdec1 = [kb.sb([128, 128], BF16) for _ in range(R)]
        hm01 = kb.sb([128, 2])
        kb.op("pool", lambda e: e.memset(hm01.t[:], 0.0), writes=[hm01])
        kb.op("pool", lambda e: e.memset(hm01.t[0:64, 0:1], 1.0), reads=[hm01], writes=[hm01])
        kb.op("pool", lambda e: e.memset(hm01.t[64:128, 1:2], 1.0), reads=[hm01], writes=[hm01])
        sr = [kb.sb([128, 256]) for _ in range(R)]; gr = [kb.sb([128, 256]) for _ in range(R)]
        attm = [kb.sb([128, 128], BF16) for _ in range(R)]
        oint = [kb.sb([128, 256]) for _ in range(R)]; osb = [kb.sb([128, 256]) for _ in range(R)]
        junk = kb.sb([128, 256]); ss = [kb.sb([128, 2]) for _ in range(R)]
        og = [kb.sb([128, 256], BF16) for _ in range(R)]; oTt = [kb.sb([128, 2, 128], BF16) for _ in range(R)]
        hv = hT.t.rearrange("(k p) t -> p k t", p=128)
        ov = oT.t.rearrange("(c p) t -> p c t", p=128)
        scale = GLA_DK ** -0.5
        n = 0
        def superA(st):
            hh = hhs[st % 2]
            kb.dma("sp", hh.t[:], hv[:, :, st * 512:(st + 1) * 512], writes=[hh])
            for k in range(8):
                kb.mm(pq.t[:, :], wqb.t[:, k, :], hh.t[:, k, :], k == 0, k == 7, [wqb.parts[k], hh], [pq])
            for k in range(8):
                kb.mm(pk.t[:, :], wkb.t[:, k, :], hh.t[:, k, :], k == 0, k == 7, [wkb.parts[k], hh], [pk])
            for k in range(8):
                kb.mm(pal.t[0:16, :], wab.t[:, k, :], hh.t[:, k, :], k == 0, k == 7, [wab.parts[k], hh], [pal])
            kb.op("act", lambda e: e.copy(out=alT.t[0:16, :], in_=pal.t[0:16, :]), reads=[pal], writes=[alT])

        def tileA(st, j, r):
            hh = hhs[st % 2]
            sub = slice(j * 128, (j + 1) * 128)
            for k in range(8):
                kb.mm(pD.t[:, 0:128], hh.t[:, k, sub], wkb.t[:, k, :], k == 0, k == 7, [wkb.parts[k], hh], [pDk])
            kb.mm(pD.t[:, 128:256], alT.t[0:33, sub], wa2b.t[0:33, 0, :], True, True, [alT, wa2b.parts[0]], [pDl])
            for k in range(8):
                kb.mm(pE.t[:, 0:256], hh.t[:, k, sub], wvb.t[:, k, :], k == 0, k == 7, [wvb.parts[k], hh], [pEv])
            for k in range(8):
                kb.mm(pE.t[:, 256:512], hh.t[:, k, sub], wrb.t[:, k, :], k == 0, k == 7, [wrb.parts[k], hh], [pEr])
            if GLA_STAGE >= 2:
                kb.op("act", lambda e, r=r: e.activation(out=ex[r].t[:], in_=pD.t[:, 128:256], func=AF.Exp, scale=-1.0), reads=[pDl], writes=[ex[r]])
                kb.op("act", lambda e, r=r: e.activation(out=ltm[r].t[:], in_=ex[r].t[:], func=AF.Ln, bias=1.0), reads=[ex[r]], writes=[ltm[r]])
            if GLA_STAGE >= 3:
                kb.mm(pF.t[:, 0:128], ltm[r].t[:], Lneg.t[:], True, True, [ltm[r], Lneg], [pFb])
                kb.mm(pF.t[:, 128:256], Uneg.t[:], ltm[r].t[:], True, True, [ltm[r], Uneg], [pFr])
                kb.op("act", lambda e, r=r: e.activation(out=ebT[r].t[:], in_=pF.t[:, 0:128], func=AF.Exp), reads=[pFb], writes=[ebT[r]])
                kb.op("act", lambda e, r=r: e.activation(out=einvT[r].t[:], in_=pF.t[:, 0:128], func=AF.Exp, scale=-1.0), reads=[pFb], writes=[einvT[r]])
                kb.op("act", lambda e, r=r: e.activation(out=erem[r].t[:], in_=pF.t[:, 128:256], func=AF.Exp), reads=[pFr], writes=[erem[r]])
                kb.op("dve", lambda e, r=r, sub=sub: e.scalar_tensor_tensor(out=qdT[r].t[:], in0=pq.t[:, sub], scalar=scale, in1=ebT[r].t[:],
                                                                            op0=ALU.mult, op1=ALU.mult), reads=[pq, ebT[r]], writes=[qdT[r]])
                kb.op("dve", lambda e, r=r, sub=sub: e.tensor_tensor(out=kiT[r].t[:], in0=pk.t[:, sub], in1=einvT[r].t[:], op=ALU.mult),
                      reads=[pk, einvT[r]], writes=[kiT[r]])
                kb.op("dve", lambda e, r=r: e.scalar_tensor_tensor(out=kdec[r].t[:], in0=pD.t[:, 0:128], scalar=hm01.t[:, 0:1], in1=erem[r].t[:], op0=ALU.mult, op1=ALU.mult),
                      reads=[pDk, erem[r], hm01], writes=[kdec[r]])
                kb.op("dve", lambda e, r=r: e.scalar_tensor_tensor(out=kdec1[r].t[:], in0=pD.t[:, 0:128], scalar=hm01.t[:, 1:2], in1=erem[r].t[:], op0=ALU.mult, op1=ALU.mult),
                      reads=[pDk, erem[r], hm01], writes=[kdec1[r]])
                kb.op("act", lambda e, r=r: e.copy(out=vsb[r].t[:], in_=pE.t[:, 0:256]), reads=[pEv], writes=[vsb[r]])
                kb.op("act", lambda e, r=r: e.activation(out=sr[r].t[:], in_=pE.t[:, 256:512], func=AF.Silu), reads=[pEr], writes=[sr[r]])
                kb.op("pool", lambda e, r=r: e.tensor_tensor(out=gr[r].t[:], in0=sr[r].t[:], in1=ngb.t[:], op=ALU.mult), reads=[sr[r], ngb], writes=[gr[r]])

        def tileB(st, j, r):
            sub = slice(j * 128, (j + 1) * 128)
            if GLA_STAGE >= 4:
                kb.mm(pF.t[:, 256:384], kiT[r].t[:], qdT[r].t[:], True, True, [kiT[r], qdT[r]], [pFa])
                kb.op("dve", lambda e, r=r: e.tensor_tensor(out=attm[r].t[:], in0=pF.t[:, 256:384], in1=M1.t[:], op=ALU.mult),
                      reads=[pFa, M1], writes=[attm[r]])
                kb.mm(pG.t[:, 0:256], attm[r].t[:], vsb[r].t[:], True, True, [attm[r], vsb[r]], [pGa])
            if GLA_STAGE >= 5:
                if GLA_SUB >= 1:
                    kb.mm(pH.t[:, 0:256], kdec[r].t[:, :], vsb[r].t[:, :], True, True, [kdec[r], vsb[r]], [pH0])
                if GLA_SUB >= 2:
                    kb.mm(pH.t[:, 256:512], kdec1[r].t[:, :], vsb[r].t[:, :], True, True, [kdec1[r], vsb[r]], [pH1])
                if GLA_SUB >= 3:
                    kb.mm(pG.t[:, 256:512], qdT[r].t[:, :], SbA.t[:], True, True, [qdT[r], SbA], [pGb])
                if GLA_SUB >= 4:
                    kb.op("dve", lambda e, r=r: e.scalar_tensor_tensor(out=S32.t[:], in0=S32.t[:], scalar=ebT[r].t[:, 63:64], in1=pH.t[:, 0:256],
                                                                       op0=ALU.mult, op1=ALU.add), reads=[S32, ebT[r], pH0], writes=[S32])
                if GLA_SUB >= 5:
                    kb.op("act", lambda e: e.copy(out=SbB.t[:], in_=S32.t[:]), reads=[S32], writes=[SbB])
                if GLA_SUB >= 6:
                    kb.mm(pal.t[:, 0:256], qdT[r].t[:, :], SbB.t[:], True, True, [qdT[r], SbB], [pal])
                if GLA_SUB >= 7:
                    kb.op("dve", lambda e, r=r: e.scalar_tensor_tensor(out=S32.t[:], in0=S32.t[:], scalar=ebT[r].t[:, 127:128], in1=pH.t[:, 256:512],
                                                                       op0=ALU.mult, op1=ALU.add), reads=[S32, ebT[r], pH1], writes=[S32])
                if GLA_SUB >= 8:
                    kb.op("act", lambda e: e.copy(out=SbA.t[:], in_=S32.t[:]), reads=[S32], writes=[SbA])
            if GLA_STAGE >= 6:
                kb.op("act", lambda e, r=r: e.copy(out=oint[r].t[0:64, :], in_=pG.t[0:64, 256:512]), reads=[pGb], writes=[oint[r]])
                kb.op("act", lambda e, r=r: e.copy(out=oint[r].t[64:128, :], in_=pal.t[64:128, 0:256]), reads=[pal, oint[r]], writes=[oint[r]])
                kb.op("dve", lambda e, r=r: e.tensor_tensor(out=osb[r].t[:], in0=pG.t[:, 0:256], in1=oint[r].t[:], op=ALU.add),
                      reads=[pGa, oint[r]], writes=[osb[r]])
                kb.op("act", lambda e, r=r: e.activation(out=junk.t[:], in_=osb[r].t[:], func=AF.Square, accum_out=ss[r].t[:, 0:1]),
                      reads=[osb[r]], writes=[junk, ss[r]])
                kb.op("dve", lambda e, r=r: e.tensor_scalar(out=ss[r].t[:, 1:2], in0=ss[r].t[:, 0:1], scalar1=1.0 / GLA_DV, scalar2=LN_EPS,
                                                            op0=ALU.mult, op1=ALU.add), reads=[ss[r]], writes=[ss[r]])
                kb.op("act", lambda e, r=r: e.sqrt(out=ss[r].t[:, 1:2], in_=ss[r].t[:, 1:2]), reads=[ss[r]], writes=[ss[r]])
                kb.op("dve", lambda e, r=r: e.reciprocal(out=ss[r].t[:, 1:2], in_=ss[r].t[:, 1:2]), reads=[ss[r]], writes=[ss[r]])
                kb.op("dve", lambda e, r=r: e.scalar_tensor_tensor(out=og[r].t[:], in0=osb[r].t[:], scalar=ss[r].t[:, 1:2], in1=gr[r].t[:],
                                                                   op0=ALU.mult, op1=ALU.mult), reads=[osb[r], ss[r], gr[r]], writes=[og[r]])
            if GLA_STAGE >= 7:
                for c in range(2):
                    kb.op("pe", lambda e, r=r, c=c: e.transpose(out=pT[:, c * 128:(c + 1) * 128], in_=og[r].t[:, c * 128:(c + 1) * 128],
                                                                identity=identb.t[:]), reads=[og[r], identb], writes=[pDt])
                kb.op("act", lambda e, r=r: e.copy(out=oTt[r].t[:].rearrange("p a b -> p (a b)"), in_=pT[:, 0:256]), reads=[pDt], writes=[oTt[r]])
            t0 = st * 512 + j * 128
            kb.dma("pool", ov[:, :, t0:t0 + 128], oTt[r].t[:], reads=[oTt[r]], writes=[oT])

        gits = [(st, j) for st in range(S // 512) for j in range(4)]
        superA(0)
        tileA(0, 0, 0)
        for n_, (st, j) in enumerate(gits):
            if n_ + 1 < len(gits):
                st2, j2 = gits[n_ + 1]
                if j2 == 0:
                    superA(st2)
                tileA(st2, j2, (n_ + 1) % R)
            tileB(st, j, n_ % R)
        kb.finish([oT])
    return nc


REL_BUCKETS, REL_MAX_DIST = 32, 2048


def rel_bucket_np(n):
    n = np.maximum(n, 0)
    exact = REL_BUCKETS // 2
    logn = np.log(np.maximum(n, 1).astype(np.float32) / exact)
    large = exact + (logn / np.float32(math.log(REL_MAX_DIST / exact)) * (REL_BUCKETS - exact)).astype(np.int32)
    large = np.minimum(large, REL_BUCKETS - 1)
    return np.where(n < exact, n, large)


def onehot_struct(dists, valid):
    L = len(dists)
    oh = np.zeros((33, L), np.float32)
    b = rel_bucket_np(np.asarray(dists))
    for i in range(L):
        if valid[i]:
            oh[b[i], i] = 1.0
        else:
            oh[32, i] = 1.0
    return oh


def make_flipJ(kb):
    jm = kb.sb([128, 128])
    kb.op("pool", lambda e: e.memset(jm.t[:], 0.0), writes=[jm])
    kb.op("pool", lambda e: e.affine_select(out=jm.t[:], in_=jm.t[:], pattern=[[1, 128]], compare_op=ALU.not_equal, fill=1.0,
                                             base=-127, channel_multiplier=1), reads=[jm], writes=[jm])
    return jm


def build_wide_bias(kb, tab, oh_ap, nh, L, W, Fd, fd_row0, wides, wbuf, pbank, jm, es):
    oht = kb.sb([33, L], es=es)
    kb.dma("sp", oht.t[:], oh_ap, writes=[oht])
    fs = kb.sb([nh, L], es=es)
    for c0 in range(0, L, 512):
        c1 = min(L, c0 + 512)
        kb.mm(pbank.t[0:nh, 0:c1 - c0], tab.t[0:33, 0:nh], oht.t[0:33, c0:c1], True, True, [tab, oht], [pbank])
        kb.op("dve", lambda e, c0=c0, c1=c1: e.tensor_copy(out=fs.t[0:nh, c0:c1], in_=pbank.t[0:nh, 0:c1 - c0]), reads=[pbank], writes=[fs])
    kb.dma("sp", Fd.t[fd_row0:fd_row0 + nh, 0:L], fs.t[0:nh, :], reads=[fs], writes=[Fd])
    tp = kb.sb([128, W], es=es)
    for h in range(nh):
        src = bass.AP(Fd.t.tensor, (fd_row0 + h) * Fd.t.shape[1], [[1, 128], [1, W]])
        kb.dma("sp", tp.t[:, :], src, reads=[Fd], writes=[tp])
        for c0 in range(0, W, 512):
            c1 = min(W, c0 + 512)
            kb.mm(pbank.t[:, 0:c1 - c0], jm.t[:], tp.t[:, c0:c1], True, True, [jm, tp], [pbank])
            kb.op("dve", lambda e, h=h, c0=c0, c1=c1: e.tensor_copy(out=wides[h][:, c0:c1], in_=pbank.t[:, 0:c1 - c0]), reads=[pbank], writes=[wbuf])


DIL_GROUPS = ((128, 1), (512, 4), (2048, 16))
CH = 2048


def dil_onehots():
    ohs = []
    for (window, dil) in DIL_GROUPS:
        m = np.arange(383) - 127
        ohs.append(onehot_struct(m * dil, (m >= 0) & (m <= window // dil)))
    return np.stack(ohs)


def build_dil(S):
    nc = new_nc()
    hT = dr_in(nc, "hT", [D, S], BF16)
    wqkv = dr_in(nc, "wqkv", [3, D, 768])
    tabi = dr_in(nc, "tab", [33, 4])
    ohi = dr_in(nc, "oh", [3, 33, 383])
    oT = dr_out(nc, "oT", [256, S], BF16)
    Fd = Buf(nc.dram_tensor("Fd", [12, 383], F32, kind="Internal").ap())
    wbf = Buf(nc.dram_tensor("wbf", [3, 128, 8 * 768], BF16, kind="Internal").ap())
    nchunk = S // CH
    scale = 64 ** -0.5
    with ExitStack() as es:
        kb = KB(nc, es)
        banks = [kb.ps() for _ in range(8)]
        jm = make_flipJ(kb)
        tab = kb.sb([33, 4])
        kb.dma("sp", tab.t[:], tabi.t[:, :], writes=[tab])
        bias = [kb.sb([128, 4, 256]) for _ in range(3)]
        sel65 = kb.sb([128, 64])
        kb.op("pool", lambda e: e.memset(sel65.t[:], 0.0), writes=[sel65])
        kb.op("pool", lambda e: e.memset(sel65.t[64:65, :], 1.0), reads=[sel65], writes=[sel65])
        with ExitStack() as es0:
            for g in range(3):
                build_wide_bias(kb, tab, ohi.t[g], 4, 383, 256, Fd, 4 * g, [bias[g].t[:, h, :] for h in range(4)], bias[g], banks[0], jm, es0)
        kb.barrier()
        with ExitStack() as es1:
            stage = [kb.sb([128, 768], es=es1) for _ in range(2)]
            wtmp = [kb.sb([128, 8, 768], BF16, nparts=8, es=es1) for _ in range(1)]
            for g in range(3):
                load_w_bf16(kb, wtmp[0], lambda k, c0, c1, g=g: wqkv.t[g, k * 128:(k + 1) * 128, c0:c1], 8, 768, stage, chunk=768)
                kb.dma("sp", wbf.t[g].rearrange("p (k c) -> p k c", k=8), wtmp[0].t[:], reads=wtmp[0].parts, writes=[wbf])
                for p_ in wtmp[0].parts:
                    p_.r[kb.dq["sp"][(kb.di["sp"] - 1) % len(kb.dq["sp"])]] = kb.dsl[kb.dq["sp"][(kb.di["sp"] - 1) % len(kb.dq["sp"])]][1]
        kb.barrier()
        wg = [kb.sb([128, 8, 768], BF16) for _ in range(2)]
        hh = [kb.sb([128, 8, CH], BF16) for _ in range(2)]
        qTm = [[kb.sb([128, CH], BF16) for _ in range(2)] for _ in range(2)]
        for hp_ in range(2):
            for hd_ in range(2):
                kb.op("pool", lambda e, hp_=hp_, hd_=hd_: e.memset(qTm[hp_][hd_].t[:], 0.0), writes=[qTm[hp_][hd_]])
        kT = [kb.sb([128, 2 * CH], BF16) for _ in range(2)]
        V = kb.sb([128, 32, 4, 65], BF16)
        kb.op("pool", lambda e: e.memset(V.t[:].rearrange("p a b c -> p (a b c)"), 1.0), writes=[V])
        acc = kb.sb([65, 4, CH])
        tmp = [kb.sb([128, 512]) for _ in range(2)]
        pT = [kb.sb([128, 512], BF16) for _ in range(2)]
        oTt = [kb.sb([64, CH], BF16) for _ in range(2)]
        rdb = kb.sb([64, 512])
        hv = hT.t.rearrange("(k p) t -> p k t", p=128)
        nw = 0
        nit = 0
        for c in range(nchunk):
            cur = hh[c % 2]
            kb.dma("sp", cur.t[:], hv[:, :, c * CH:(c + 1) * CH], writes=[cur])
            slots = ([(hh[(c - 1) % 2], 0)] if c > 0 else []) + [(cur, 1)]
            for g, (window, d) in enumerate(DIL_GROUPS):
                w = wg[nw % 2]; nw += 1
                kb.dma("pool", w.t[:], wbf.t[g].rearrange("p (k c) -> p k c", k=8), reads=[wbf], writes=[w])
                nbk = 16 // d
                for hp in range(2):
                    for n4 in range(4):
                        pb = banks[(n4 + hp) % 2]
                        for k in range(8):
                            kb.mm(pb.t[:, :], w.t[:, k, hp * 128:(hp + 1) * 128], cur.t[:, k, n4 * 512:(n4 + 1) * 512], k == 0, k == 7, [w, cur], [pb])
                        kb.op("act", lambda e, pb=pb, hp=hp, n4=n4: e.copy(out=qTm[hp][0].t[0:64, n4 * 512:(n4 + 1) * 512], in_=pb.t[0:64, :]), reads=[pb], writes=[qTm[hp][0]])
                        kb.op("act", lambda e, pb=pb, hp=hp, n4=n4: e.copy(out=qTm[hp][1].t[64:128, n4 * 512:(n4 + 1) * 512], in_=pb.t[64:128, :]), reads=[pb], writes=[qTm[hp][1]])
                    for (hb, si) in slots:
                        for n4 in range(4):
                            pb = banks[(n4 + hp) % 2]
                            for k in range(8):
                                kb.mm(pb.t[:, :], w.t[:, k, 256 + hp * 128:256 + (hp + 1) * 128], hb.t[:, k, n4 * 512:(n4 + 1) * 512], k == 0, k == 7, [w, hb], [pb])
                            kb.op("dve", lambda e, pb=pb, hp=hp, n4=n4, si=si: e.tensor_copy(out=kT[hp].t[:, si * CH + n4 * 512:si * CH + (n4 + 1) * 512], in_=pb.t[:, :]),
                                  reads=[pb], writes=[kT[hp]])
                for (hb, si) in slots:
                    for r in range(d):
                        for bl in range(nbk):
                            ti = si * 16 + r * nbk + bl
                            pb = banks[2 + ti % 2]
                            st0 = r + d * 128 * bl
                            for k in range(8):
                                kb.mm(pb.t[:, 0:256], hb.t[:, k, st0:st0 + 127 * d + 1:d], w.t[:, k, 512:768], k == 0, k == 7, [w, hb], [pb])
                            kb.op("act" if ti % 2 else "dve",
                                  (lambda e, pb=pb, ti=ti: e.copy(out=V.t[:, ti, :, 0:64], in_=pb.t[:, 0:256].rearrange("p (h c) -> p h c", h=4))) if ti % 2 else
                                  (lambda e, pb=pb, ti=ti: e.tensor_copy(out=V.t[:, ti, :, 0:64], in_=pb.t[:, 0:256].rearrange("p (h c) -> p h c", h=4))),
                                  reads=[pb], writes=[V])
                aits = []
                for r in range(d):
                    for bl in range(nbk):
                        q0 = r + d * 128 * bl
                        keys = [(1, r, bl)]
                        if bl > 0:
                            keys.append((1, r, bl - 1))
                        elif c > 0:
                            keys.append((0, r, nbk - 1))
                        for hp in range(2):
                            aits.append((q0, keys, hp))

                def dilA(n_):
                    q0, keys, hp = aits[n_]
                    nd = len(keys)
                    it = (nit + n_) % 2
                    ps = banks[4 + it]
                    for hd in range(2):
                        for dl, (si, kr, kbl) in enumerate(keys):
                            k0 = si * CH + kr + d * 128 * kbl
                            kb.mm(ps.t[:, (hd * 2 + dl) * 128:(hd * 2 + dl + 1) * 128], kT[hp].t[:, k0:k0 + 127 * d + 1:d],
                                  qTm[hp][hd].t[:, q0:q0 + 127 * d + 1:d], True, True, [kT[hp], qTm[hp][hd]], [ps])
                    psv = ps.t[:, :].rearrange("p (h e i) -> p h e i", h=2, e=2)
                    tv = tmp[it].t[:, :].rearrange("p (h e i) -> p h e i", h=2, e=2)
                    pv = pT[it].t[:, :].rearrange("p (h e i) -> p h e i", h=2, e=2)
                    bv = bias[g].t[:, 2 * hp:2 * hp + 2, :].rearrange("p h (e i) -> p h e i", e=2)
                    kb.op("dve", lambda e: e.scalar_tensor_tensor(out=tv[:, :, 0:nd, :], in0=psv[:, :, 0:nd, :], scalar=scale,
                                                                  in1=bv[:, :, 0:nd, :], op0=ALU.mult, op1=ALU.add),
                          reads=[ps, bias[g]], writes=[tmp[it]])
                    kb.op("act", lambda e: e.activation(out=pv[:, :, 0:nd, :], in_=tv[:, :, 0:nd, :], func=AF.Exp),
                          reads=[tmp[it]], writes=[pT[it]])

                def dilB(n_):
                    q0, keys, hp = aits[n_]
                    nd = len(keys)
                    it = (nit + n_) % 2
                    po = banks[6 + it]
                    pv = pT[it].t[:, :].rearrange("p (h e i) -> p h e i", h=2, e=2)
                    for hd in range(2):
                        for dl, (si, kr, kbl) in enumerate(keys):
                            ti = si * 16 + kr * nbk + kbl
                            kb.mm(po.t[0:65, hd * 128:(hd + 1) * 128], V.t[:, ti, 2 * hp + hd, :], pv[:, hd, dl, :], dl == 0, dl == nd - 1, [V, pT[it]], [po])
                    av = acc.t[:, 2 * hp:2 * hp + 2, q0:q0 + 127 * d + 1:d]
                    pov = po.t[0:65, 0:256].rearrange("p (h i) -> p h i", h=2)
                    if g == 0:
                        kb.op("dve", lambda e: e.tensor_copy(out=av, in_=pov), reads=[po], writes=[acc])
                    else:
                        kb.op("dve", lambda e: e.tensor_tensor(out=av, in0=av, in1=pov, op=ALU.add), reads=[po, acc], writes=[acc])

                dilA(0)
                for n_ in range(len(aits)):
                    if n_ + 1 < len(aits):
                        dilA(n_ + 1)
                    dilB(n_)
                nit += len(aits)
            for h in range(4):
                ot = oTt[h % 2]
                for n4 in range(4):
                    pb = banks[n4 % 2]
                    kb.mm(pb.t[0:64, :], sel65.t[0:65, 0:64], acc.t[0:65, h, n4 * 512:(n4 + 1) * 512], True, True, [sel65, acc], [pb])
                    kb.op("dve", lambda e, pb=pb: e.reciprocal(out=rdb.t[:, :], in_=pb.t[0:64, :]), reads=[pb], writes=[rdb])
                    kb.op("dve", lambda e, h=h, n4=n4, ot=ot: e.tensor_tensor(out=ot.t[:, n4 * 512:(n4 + 1) * 512], in0=acc.t[0:64, h, n4 * 512:(n4 + 1) * 512],
                                                                            in1=rdb.t[:, :], op=ALU.mult), reads=[acc, rdb], writes=[ot])
                kb.dma("pool", oT.t[h * 64:(h + 1) * 64, c * CH:(c + 1) * CH], ot.t[:, :], reads=[ot], writes=[oT])
        kb.finish([oT])
    return nc


SELW, WINW = 2560, 1408
SEL_FARD = 13
BIGF, BIGI = 1.0e4, -1.0e6
EB = 30000.0


def nsa_onehots():
    m = np.arange(SELW + 127) - 127 - 384
    oh_sel = onehot_struct(m, m >= 0)
    m = np.arange(WINW + 127) - 127 - 384
    oh_win = onehot_struct(m, (m >= 0) & (m <= 511))
    m = np.arange(1776 + 16) - 127
    oh_cmp = onehot_struct(m, m >= 0)
    return oh_sel, oh_win, oh_cmp


NSA_STAGE = 99
NSA_SUB = 99


def build_nsa(S):
    nc = new_nc()
    NT = S // 128
    NCK = S // 16 - 1
    NCP = ((NCK + 127) // 128) * 128
    hT = dr_in(nc, "hT", [D, S], BF16)
    wqd = dr_in(nc, "wqd", [D, 512]); wkv = dr_in(nc, "wkv", [D, 384]); wgi = dr_in(nc, "wg", [D, 12])
    pei = dr_in(nc, "pe", [2, 32, 64]); w1i = dr_in(nc, "w1", [2, 2048, 256]); b1i = dr_in(nc, "b1", [2, 256]); w2i = dr_in(nc, "w2", [2, 256, 64])
    tabi = dr_in(nc, "tab", [33, 4])
    ohs = dr_in(nc, "ohs", [33, SELW + 127]); ohw = dr_in(nc, "ohw", [33, WINW + 127]); ohc = dr_in(nc, "ohc", [33, 1792])
    oT = dr_out(nc, "oT", [256, S], BF16)
    Fd = Buf(nc.dram_tensor("Fd", [12, SELW + 127], F32, kind="Internal").ap())
    scale = 64 ** -0.5
    with ExitStack() as es:
        kb = KB(nc, es)
        banks = [kb.ps() for _ in range(8)]
        identb = make_ident(kb, BF16)
        jm = make_flipJ(kb)
        tab = kb.sb([33, 4])
        kb.dma("sp", tab.t[:], tabi.t[:, :], writes=[tab])
        kswT = kb.sb([128, S], BF16)
        Vs = kb.sb([128, NT, 65], BF16); Vw = kb.sb([128, NT, 65], BF16)
        kb.op("pool", lambda e: e.memset(Vs.t[:].rearrange("p a b -> p (a b)"), 1.0), writes=[Vs])
        kb.op("pool", lambda e: e.memset(Vw.t[:].rearrange("p a b -> p (a b)"), 1.0), writes=[Vw])
        kcmpT = kb.sb([64, NCP], BF16); vcmp = kb.sb([128, NCP // 128, 64], BF16)
        if NSA_STAGE == 1.1:
            kb.finish([oT]); return nc
        stage = [kb.sb([128, 512]) for _ in range(2)]
        wqb = kb.sb([128, 8, 512], BF16, nparts=8); wkvb = kb.sb([128, 8, 384], BF16, nparts=8); wgb = kb.sb([128, 8, 12], BF16, nparts=8)
        for dst, src, n in ((wqb, wqd, 512), (wkvb, wkv, 384), (wgb, wgi, 12)):
            load_w_bf16(kb, dst, lambda k, c0, c1, src=src: src.t[k * 128:(k + 1) * 128, c0:c1], 8, n, stage)
        if NSA_STAGE == 1.2:
            kb.finish([oT]); return nc
        hhs = [kb.sb([128, 8, 512], BF16) for _ in range(2)]
        hv = hT.t.rearrange("(k p) t -> p k t", p=128)
        with ExitStack() as es1:
            kcvcT = kb.sb([128, S], BF16, es=es1)
            w1m = [kb.sb([128, 32, 256], BF16, es=es1) for _ in range(2)]
            for x_ in range(2):
                kb.op("pool", lambda e, x_=x_: e.memset(w1m[x_].t[:].rearrange("p a b -> p (a b)"), 0.0), writes=[w1m[x_]])
            w2b = kb.sb([128, 2, 2, 64], BF16, es=es1)
            st1 = [kb.sb([128, 8, 256], es=es1) for _ in range(1)]
            nst1 = 0
            for x_ in range(2):
                rws = slice(x_ * 64, (x_ + 1) * 64)
                for pc in range(4):
                    stg = st1[0]; nst1 += 1
                    kb.dma("sp", stg.t[rws, :, :], w1i.t[x_].rearrange("(p d) h -> d p h", d=64)[:, pc * 8:(pc + 1) * 8, :], writes=[stg])
                    kb.op("dve", lambda e, x_=x_, pc=pc, stg=stg, rws=rws: e.tensor_copy(out=w1m[x_].t[rws, pc * 8:(pc + 1) * 8, :], in_=stg.t[rws, :, :]),
                          reads=[stg], writes=[w1m[x_]])
            st2 = kb.sb([128, 2, 2, 64], es=es1)
            for x_ in range(2):
                kb.dma("sp", st2.t[:, x_, :, :], w2i.t[x_].rearrange("(a p) c -> p a c", p=128), writes=[st2])
            kb.op("dve", lambda e: e.tensor_copy(out=w2b.t[:].rearrange("p a b c -> p (a b c)"), in_=st2.t[:].rearrange("p a b c -> p (a b c)")),
                  reads=[st2], writes=[w2b])
            peT = kb.sb([128, 32], es=es1); peTb = kb.sb([128, 32], BF16, es=es1)
            for x_ in range(2):
                for p4 in range(4):
                    kb.dma("sp", peT.t[x_ * 64:(x_ + 1) * 64, p4 * 8:(p4 + 1) * 8], pei.t[x_, p4 * 8:(p4 + 1) * 8, :].rearrange("p d -> d p"), writes=[peT],
                           allow_slow_non_contiguous=True)
            kb.op("dve", lambda e: e.tensor_copy(out=peTb.t[:], in_=peT.t[:]), reads=[peT], writes=[peTb])
            b1t = kb.sb([128, 2, 2], es=es1)
            for x_ in range(2):
                kb.dma("sp", b1t.t[:, x_, :], b1i.t[x_, :].rearrange("(a p) -> p a", p=128), writes=[b1t], allow_slow_non_contiguous=True)
            if NSA_STAGE == 1.3:
                kb.finish([oT]); return nc
            for st in range(S // 512):
                hh = hhs[st % 2]
                kb.dma("sp", hh.t[:], hv[:, :, st * 512:(st + 1) * 512], writes=[hh])
                for (c0, dstT, eng) in ((0, kcvcT, "act"), (128, kswT, "dve")):
                    if NSA_STAGE == 1.5:
                        break
                    pb = banks[1 + (c0 // 128)]
                    for k in range(8):
                        kb.mm(pb.t[:, :], wkvb.t[:, k, c0:c0 + 128], hh.t[:, k, :], k == 0, k == 7, [wkvb.parts[k], hh], [pb])
                    if eng == "act":
                        kb.op("act", lambda e, pb=pb, dstT=dstT, st=st: e.copy(out=dstT.t[:, st * 512:(st + 1) * 512], in_=pb.t[:, :]), reads=[pb], writes=[dstT])
                    else:
                        kb.op("dve", lambda e, pb=pb, dstT=dstT, st=st: e.tensor_copy(out=dstT.t[:, st * 512:(st + 1) * 512], in_=pb.t[:, :]), reads=[pb], writes=[dstT])
                if NSA_STAGE == 1.4:
                    continue
                pb = banks[3 + st % 2]
                for j in range(4):
                    for k in range(8):
                        kb.mm(pb.t[:, j * 128:(j + 1) * 128], hh.t[:, k, j * 128:(j + 1) * 128], wkvb.t[:, k, 256:384], k == 0, k == 7, [wkvb.parts[k], hh], [pb])
                pv4 = pb.t[:, :].rearrange("p (j c) -> p j c", j=4)
                for j in range(4):
                    if NSA_SUB >= 1:
                        kb.op("act", lambda e, pb=pb, st=st, j=j: e.copy(out=Vs.t[:, st * 4 + j, 0:64], in_=pb.t[:, j * 128:j * 128 + 64]), reads=[pb], writes=[Vs])
                    if NSA_SUB >= 2:
                        kb.op("act", lambda e, pb=pb, st=st, j=j: e.copy(out=Vw.t[:, st * 4 + j, 0:64], in_=pb.t[:, j * 128 + 64:j * 128 + 128]), reads=[pb], writes=[Vw])
            if NSA_STAGE in (2, 1.4, 1.5):
                kb.finish([oT]); return nc
            hidT = kb.sb([128, 2, 2, NCP], BF16, es=es1)
            kb.op("pool", lambda e: e.memset(hidT.t[:].rearrange("p a b c -> p (a b c)"), 0.0), writes=[hidT])
            cbias = kb.sb([128, 2, 2], es=es1)
            for x_ in range(2):
                rows = slice(x_ * 64, (x_ + 1) * 64)
                for half in range(2):
                    pb = banks[1]
                    for p in range(32):
                        kb.mm(pb.t[:, 0:1], w1m[x_].t[:, p, half * 128:(half + 1) * 128], peTb.t[:, p:p + 1], p == 0, p == 31, [w1m[x_], peTb], [pb])
                    kb.op("dve", lambda e, x_=x_, half=half, pb=pb: e.tensor_tensor(out=cbias.t[:, x_, half:half + 1], in0=pb.t[:, 0:1], in1=b1t.t[:, x_, half:half + 1], op=ALU.add),
                          reads=[pb, b1t], writes=[cbias])
                    for n0 in range(0, NCK, 512):
                        n1 = min(NCK, n0 + 512)
                        pb2 = banks[2 + (n0 // 512) % 2]
                        for p in range(32):
                            kb.mm(pb2.t[:, 0:n1 - n0], w1m[x_].t[:, p, half * 128:(half + 1) * 128], kcvcT.t[:, 16 * n0 + p:16 * (n1 - 1) + p + 1:16],
                                  p == 0, p == 31, [w1m[x_], kcvcT], [pb2])
                        kb.op("act", lambda e, x_=x_, half=half, n0=n0, n1=n1, pb2=pb2: e.activation(out=hidT.t[:, x_, half, n0:n1], in_=pb2.t[:, 0:n1 - n0], func=AF.Silu,
                                                                                                     bias=cbias.t[:, x_, half:half + 1]), reads=[pb2, cbias], writes=[hidT])
            for n0 in range(0, NCP, 512):
                n1 = min(NCP, n0 + 512)
                pb = banks[1]
                for half in range(2):
                    kb.mm(pb.t[0:64, 0:n1 - n0], w2b.t[:, 0, half, :], hidT.t[:, 0, half, n0:n1], half == 0, half == 1, [w2b, hidT], [pb])
                kb.op("dve", lambda e, n0=n0, n1=n1, pb=pb: e.tensor_copy(out=kcmpT.t[:, n0:n1], in_=pb.t[0:64, 0:n1 - n0]), reads=[pb], writes=[kcmpT])
            for ct in range(NCP // 128):
                pb = banks[2 + ct % 2]
                for half in range(2):
                    kb.mm(pb.t[:, 0:64], hidT.t[:, 1, half, ct * 128:(ct + 1) * 128], w2b.t[:, 1, half, :], half == 0, half == 1, [w2b, hidT], [pb])
                kb.op("act", lambda e, ct=ct, pb=pb: e.copy(out=vcmp.t[:, ct, :], in_=pb.t[:, 0:64]), reads=[pb], writes=[vcmp])
        if NSA_STAGE == 3:
            kb.finish([oT]); return nc
        kb.barrier()
        wsel = kb.sb([128, 4, SELW], BF16); wwin = kb.sb([128, 4, WINW], BF16)
        nearb = kb.sb([128, 4, 104]); farb = kb.sb([128, 4])
        with ExitStack() as es0:
            wtmp = kb.sb([128, SELW], es=es0)
            for (oh, L, W, dst) in ((ohs, SELW + 127, SELW, wsel), (ohw, WINW + 127, WINW, wwin)):
                for h in range(4):
                    tabh = Buf(tab.t[:, h:h + 1]); tabh.w = tab.w
                    with ExitStack() as esx:
                        build_wide_bias(kb, tabh, oh.t[:, :], 1, L, W, Fd, h, [wtmp.t[:, 0:W]], wtmp, banks[0], jm, esx)
                        kb.op("act", lambda e, dst=dst, h=h, W=W: e.copy(out=dst.t[:, h, :], in_=wtmp.t[:, 0:W]), reads=[wtmp], writes=[dst])
                    kb.barrier()
            oht = kb.sb([33, 1792], es=es0)
            kb.dma("sp", oht.t[:], ohc.t[:, :], writes=[oht])
            fs = kb.sb([4, 1792], es=es0)
            for c0 in range(0, 1792, 512):
                c1 = min(1792, c0 + 512)
                kb.mm(banks[0].t[0:4, 0:c1 - c0], tab.t[0:33, 0:4], oht.t[0:33, c0:c1], True, True, [tab, oht], [banks[0]])
                kb.op("dve", lambda e, c0=c0, c1=c1: e.tensor_copy(out=fs.t[0:4, c0:c1], in_=banks[0].t[0:4, 0:c1 - c0]), reads=[banks[0]], writes=[fs])
            kb.dma("sp", Fd.t[8:12, 0:1792], fs.t[0:4, :], reads=[fs], writes=[Fd])
            jc = kb.sb([128, 104], es=es0)
            kb.op("pool", lambda e: e.memset(jc.t[:], 0.0), writes=[jc])
            kb.op("pool", lambda e: e.affine_select(out=jc.t[:], in_=jc.t[:], pattern=[[1, 104]], compare_op=ALU.not_equal, fill=1.0,
                                                     base=-103, channel_multiplier=1), reads=[jc], writes=[jc])
            xt = kb.sb([104, 128], es=es0)
            for h in range(4):
                src = bass.AP(Fd.t.tensor, (8 + h) * Fd.t.shape[1], [[16, 104], [1, 128]])
                kb.dma("sp", xt.t[:, :], src, reads=[Fd], writes=[xt])
                kb.mm(banks[0].t[:, 0:104], xt.t[0:104, :], jc.t[0:104, :], True, True, [xt, jc], [banks[0]])
                kb.op("dve", lambda e, h=h: e.tensor_copy(out=nearb.t[:, h, :], in_=banks[0].t[:, 0:104]), reads=[banks[0]], writes=[nearb])
            onesr = kb.sb([33, 128], es=es0)
            kb.op("pool", lambda e: e.memset(onesr.t[:], 0.0), writes=[onesr])
            kb.op("pool", lambda e: e.memset(onesr.t[0:1, :], 1.0), reads=[onesr], writes=[onesr])
            t31 = kb.sb([1, 4], es=es0)
            kb.dma("sp", t31.t[:, :], tabi.t[31:32, :], writes=[t31])
            kb.mm(banks[0].t[:, 0:4], onesr.t[0:1, :], t31.t[0:1, :], True, True, [onesr, t31], [banks[0]])
            kb.op("dve", lambda e: e.tensor_copy(out=farb.t[:, :], in_=banks[0].t[:, 0:4]), reads=[banks[0]], writes=[farb])
        kb.barrier()
        Aw = kb.sb([128, 512])
        kb.op("pool", lambda e: e.memset(Aw.t[:], 0.0), writes=[Aw])
        for (rows, c0) in ((slice(0, 64), 255), (slice(64, 128), 256)):
            kb.op("pool", lambda e, rows=rows, c0=c0: e.memset(Aw.t[rows, c0:c0 + 2], BIGF), reads=[Aw], writes=[Aw])
            kb.op("pool", lambda e, rows=rows, c0=c0: e.memset(Aw.t[rows, c0 + 2:512], BIGI), reads=[Aw], writes=[Aw])
        e2f = kb.sb([128, 64, 2])
        kb.op("pool", lambda e: e.memset(e2f.t[:].rearrange("p a b -> p (a b)"), 0.0), writes=[e2f])
        kb.op("pool", lambda e: e.affine_select(out=e2f.t[:], in_=e2f.t[:], pattern=[[-2, 64], [-1, 2]], compare_op=ALU.not_equal, fill=EB,
                                                 base=0, channel_multiplier=1), reads=[e2f], writes=[e2f])
        Exp_ = kb.sb([128, 64, 128], BF16)
        for half in range(2):
            kb.op("dve", lambda e, half=half: e.tensor_copy(out=Exp_.t[:, :, half * 64:(half + 1) * 64], in_=e2f.t[:, :, half:half + 1].to_broadcast([128, 64, 64])),
                  reads=[e2f], writes=[Exp_])
        qsel = [kb.sb([128, 512], BF16) for _ in range(4)]
        qwin = [kb.sb([128, 512], BF16) for _ in range(4)]
        for h_ in range(4):
            kb.op("pool", lambda e, h_=h_: e.memset(qsel[h_].t[:], 0.0), writes=[qsel[h_]])
            kb.op("pool", lambda e, h_=h_: e.memset(qwin[h_].t[:], 0.0), writes=[qwin[h_]])
        gates = kb.sb([128, 4, 12])
        negselT = kb.sb([128, 2, 512], BF16)
        kb.op("pool", lambda e: e.memset(negselT.t[:].rearrange("p a b -> p (a b)"), -1.0), writes=[negselT])
        tmpc = [kb.sb([128, 1024]) for _ in range(2)]; ebuf = tmpc; pg = kb.sb([128, 1024])
        pbf = [kb.sb([128, 1024], BF16) for _ in range(2)]
        kb.op("pool", lambda e: e.memset(pg.t[:], 0.0), writes=[pg])
        pTc = [kb.sb([128, NCP // 128, 128], BF16) for _ in range(2)]
        den = [kb.sb([128, 2]) for _ in range(2)]; imp = kb.sb([128, 256]); sc2 = kb.sb([128, 256]); m8 = kb.sb([128, 16]); nsel = kb.sb([128, 256], BF16)
        ofin = [kb.sb([128, 4, 64]) for _ in range(4)]
        tmp = [kb.sb([128, 512]) for _ in range(3)]; pT = [kb.sb([128, 512], BF16) for _ in range(3)]
        fcol = kb.sb([128, 8]); ogb = kb.sb([128, 256], BF16); oTt = kb.sb([128, 2, 128], BF16)
        poS = [kb.sb([65, 512]) for _ in range(2)]; identf = make_ident(kb, F32)
        ov = oT.t.rearrange("(c p) t -> p c t", p=128)
        pgv = pg.t[:, :].rearrange("p (b m) -> p b m", m=4)
        if NSA_STAGE == 4:
            kb.finish([oT]); return nc
        for qs in range(S // 512):
            hh = hhs[qs % 2]
            kb.dma("sp", hh.t[:], hv[:, :, qs * 512:(qs + 1) * 512], writes=[hh])
            for h in range(4):
                pb = banks[0]
                for k in range(8):
                    kb.mm(pb.t[:, :], wqb.t[:, k, h * 128:(h + 1) * 128], hh.t[:, k, :], k == 0, k == 7, [wqb.parts[k], hh], [pb])
                kb.op("act", lambda e, h=h, pb=pb: e.copy(out=qsel[h].t[0:64, :], in_=pb.t[0:64, :]), reads=[pb], writes=[qsel[h]])
                kb.op("dve", lambda e, h=h, pb=pb: e.tensor_copy(out=qwin[h].t[64:128, :], in_=pb.t[64:128, :]), reads=[pb], writes=[qwin[h]])
            pb = banks[0]
            for j in range(4):
                for k in range(8):
                    kb.mm(pb.t[:, j * 12:(j + 1) * 12], hh.t[:, k, j * 128:(j + 1) * 128], wgb.t[:, k, :], k == 0, k == 7, [wgb.parts[k], hh], [pb])
            kb.op("act", lambda e, pb=pb: e.activation(out=gates.t[:].rearrange("p a b -> p (a b)"), in_=pb.t[:, 0:48], func=AF.Sigmoid), reads=[pb], writes=[gates])
            def cmpA(j, h, pp):
                qb = qs * 4 + j
                sub = slice(j * 128, (j + 1) * 128)
                ncv = min(8 * qb + 7, NCK)
                nlo = max(0, 8 * qb - 97); u0 = nlo - (8 * qb - 97)
                sbk = (banks[1], banks[2]) if pp == 0 else (banks[5], banks[6])
                for c0 in range(0, ncv, 512):
                    c1 = min(ncv, c0 + 512)
                    kb.mm(sbk[c0 // 512].t[:, 0:c1 - c0], qsel[h].t[0:64, sub], kcmpT.t[0:64, c0:c1], True, True, [qsel[h], kcmpT], [sbk[c0 // 512]])
                for c0 in range(0, ncv, 512):
                    c1 = min(ncv, c0 + 512)
                    pbk = sbk[c0 // 512]
                    kb.op("dve", lambda e, c0=c0, c1=c1, pbk=pbk: e.tensor_scalar(out=tmpc[pp].t[:, c0:c1], in0=pbk.t[:, 0:c1 - c0], scalar1=scale, scalar2=farb.t[:, h:h + 1],
                                                                             op0=ALU.mult, op1=ALU.add), reads=[pbk, farb], writes=[tmpc[pp]])
                    a0 = max(c0, nlo)
                    if a0 < c1:
                        kb.op("dve", lambda e, c0=c0, c1=c1, a0=a0, pbk=pbk: e.scalar_tensor_tensor(
                            out=tmpc[pp].t[:, a0:c1], in0=pbk.t[:, a0 - c0:c1 - c0], scalar=scale, in1=nearb.t[:, h, u0 + a0 - nlo:u0 + c1 - nlo],
                            op0=ALU.mult, op1=ALU.add), reads=[pbk, nearb, tmpc[pp]], writes=[tmpc[pp]])
                kb.op("act", lambda e: e.activation(out=ebuf[pp].t[:, 0:ncv], in_=tmpc[pp].t[:, 0:ncv], func=AF.Exp, accum_out=den[pp].t[:, 0:1]),
                      reads=[tmpc[pp]], writes=[tmpc[pp], den[pp]])
                kb.op("act", lambda e: e.copy(out=pbf[pp].t[:, 0:ncv], in_=ebuf[pp].t[:, 0:ncv]), reads=[ebuf[pp]], writes=[pbf[pp]])

            def cmpB(j, h, pp):
                qb = qs * 4 + j
                ncv = min(8 * qb + 7, NCK)
                nct = (ncv + 127) // 128
                dn = den[pp]
                kb.op("dve", lambda e: e.tensor_scalar(out=dn.t[:, 1:2], in0=dn.t[:, 0:1], scalar1=1e-30, scalar2=None, op0=ALU.max), reads=[dn], writes=[dn])
                kb.op("dve", lambda e: e.reciprocal(out=dn.t[:, 1:2], in_=dn.t[:, 1:2]), reads=[dn], writes=[dn])
                if h == 0:
                    kb.op("dve", lambda e: e.tensor_scalar(out=pg.t[:, 0:ncv], in0=ebuf[pp].t[:, 0:ncv], scalar1=dn.t[:, 1:2], scalar2=None, op0=ALU.mult),
                          reads=[ebuf[pp], dn], writes=[pg])
                else:
                    kb.op("dve", lambda e: e.scalar_tensor_tensor(out=pg.t[:, 0:ncv], in0=ebuf[pp].t[:, 0:ncv], scalar=dn.t[:, 1:2], in1=pg.t[:, 0:ncv],
                                                                  op0=ALU.mult, op1=ALU.add), reads=[ebuf[pp], dn, pg], writes=[pg])
                ptk = banks[3] if pp == 0 else banks[0]
                ptb = ptk.t[:, :].bitcast(BF16)
                for ct in range(nct):
                    w_ = min(128, ncv - ct * 128)
                    kb.op("pe", lambda e, ct=ct, w_=w_: e.transpose(out=ptb[0:w_, ct * 128:(ct + 1) * 128], in_=pbf[pp].t[:, ct * 128:ct * 128 + w_],
                                                                    identity=identb.t[:]), reads=[pbf[pp], identb], writes=[ptk])
                hlf = (nct + 1) // 2
                for (ca, cb, en) in ((0, hlf, "act"), (hlf, nct, "dve")):
                    if cb <= ca:
                        continue
                    wl = min(128, ncv - (cb - 1) * 128)
                    if wl == 128 or cb - ca == 1:
                        w_ = wl if cb - ca == 1 else 128
                        src_ = ptb[0:w_, ca * 128:cb * 128].rearrange("p (c i) -> p c i", i=128)
                        dst_ = pTc[pp].t[0:w_, ca:cb, :]
                        if en == "act":
                            kb.op("act", lambda e, src_=src_, dst_=dst_: e.copy(out=dst_, in_=src_), reads=[ptk], writes=[pTc[pp]])
                        else:
                            kb.op("dve", lambda e, src_=src_, dst_=dst_: e.tensor_copy(out=dst_, in_=src_), reads=[ptk], writes=[pTc[pp]])
                    else:
                        for (a_, b_, w_) in ((ca, cb - 1, 128), (cb - 1, cb, wl)):
                            src_ = ptb[0:w_, a_ * 128:b_ * 128].rearrange("p (c i) -> p c i", i=128)
                            dst_ = pTc[pp].t[0:w_, a_:b_, :]
                            kb.op("act", lambda e, src_=src_, dst_=dst_: e.copy(out=dst_, in_=src_), reads=[ptk], writes=[pTc[pp]])
                for ct in range(nct):
                    w_ = min(128, ncv - ct * 128)
                    kb.mm(banks[4].t[:, 0:64], pTc[pp].t[0:w_, ct, :], vcmp.t[0:w_, ct, :], ct == 0, ct == nct - 1, [pTc[pp], vcmp], [banks[4]])
                kb.op("dve", lambda e: e.tensor_scalar(out=ofin[j].t[:, h, :], in0=banks[4].t[:, 0:64], scalar1=gates.t[:, j, h:h + 1], scalar2=dn.t[:, 1:2],
                                                       op0=ALU.mult, op1=ALU.mult), reads=[banks[4], gates, dn], writes=[ofin[j]])

            def select(j):
                qb = qs * 4 + j
                sub = slice(j * 128, (j + 1) * 128)
                kb.op("dve", lambda e: e.tensor_tensor(out=imp.t[:, :], in0=pgv[:, :, 0], in1=pgv[:, :, 1], op=ALU.add), reads=[pg], writes=[imp])
                kb.op("dve", lambda e: e.tensor_tensor(out=imp.t[:, :], in0=imp.t[:, :], in1=pgv[:, :, 2], op=ALU.add), reads=[pg, imp], writes=[imp])
                kb.op("dve", lambda e: e.scalar_tensor_tensor(out=imp.t[:, :], in0=imp.t[:, :], scalar=2.0, in1=pgv[:, :, 3], op0=ALU.mult, op1=ALU.add),
                      reads=[pg, imp], writes=[imp])
                kb.op("dve", lambda e: e.tensor_tensor(out=imp.t[:, 1:256], in0=imp.t[:, 1:256], in1=pgv[:, 0:255, 3], op=ALU.add), reads=[pg, imp], writes=[imp])
                kb.op("dve", lambda e: e.tensor_tensor(out=imp.t[:, :], in0=imp.t[:, :], in1=Aw.t[:, 256 - 2 * qb:512 - 2 * qb], op=ALU.add), reads=[Aw, imp], writes=[imp])
                kb.op("dve", lambda e: e.memset(imp.t[:, 0:1], BIGF), reads=[imp], writes=[imp])
                kb.op("dve", lambda e: e.max(out=m8.t[:, 0:8], in_=imp.t[:, :]), reads=[imp], writes=[m8])
                kb.op("dve", lambda e: e.match_replace(out=sc2.t[:, :], in_to_replace=m8.t[:, 0:8], in_values=imp.t[:, :], imm_value=2 * BIGI), reads=[imp, m8], writes=[sc2])
                kb.op("dve", lambda e: e.max(out=m8.t[:, 8:16], in_=sc2.t[:, :]), reads=[sc2], writes=[m8])
                kb.op("dve", lambda e: e.tensor_scalar(out=m8.t[:, 15:16], in0=m8.t[:, 15:16], scalar1=0.1 * BIGI, scalar2=None, op0=ALU.max), reads=[m8], writes=[m8])
                kb.op("dve", lambda e: e.tensor_scalar(out=nsel.t[:, :], in0=imp.t[:, :], scalar1=m8.t[:, 15:16], scalar2=-1.0, op0=ALU.is_ge, op1=ALU.add),
                      reads=[imp, m8], writes=[nsel])
                ptb = banks[3].t[:, :].bitcast(BF16)
                for ch in range(2):
                    kb.op("pe", lambda e, ch=ch: e.transpose(out=ptb[:, ch * 128:(ch + 1) * 128], in_=nsel.t[:, ch * 128:(ch + 1) * 128], identity=identb.t[:]),
                          reads=[nsel, identb], writes=[banks[3]])
                kb.op("act", lambda e: e.copy(out=negselT.t[:, :, sub], in_=ptb[:, 0:256].rearrange("p (c i) -> p c i", c=2)), reads=[banks[3]], writes=[negselT])

            items = [(j, h) for j in range(4) for h in range(4)]
            cmpA(items[0][0], items[0][1], 0)
            for n_, (j, h) in enumerate(items):
                if n_ + 1 < len(items):
                    cmpA(items[n_ + 1][0], items[n_ + 1][1], (n_ + 1) % 2)
                cmpB(j, h, n_ % 2)
                if h == 3:
                    select(j)
            its = []
            for br in range(2):
                kt_lo = 0 if br == 0 else max(0, 4 * qs - 4)
                kt_hi = 4 * qs + 3
                for h in range(4):
                    for kt in range(kt_lo, kt_hi + 1):
                        its.append((br, h, kt, kt == kt_lo, kt == kt_hi))
            sbanks = (banks[5], banks[6], banks[1])
            pobanks = (banks[7], banks[2])

            def selA(n_):
                br, h, kt, first, last = its[n_]
                ps = sbanks[n_ % 3]
                qq = qsel[h] if br == 0 else qwin[h]
                wide = wsel if br == 0 else wwin
                dl = 4 * qs - kt
                kb.mm(ps.t[:, :], kswT.t[:, kt * 128:(kt + 1) * 128], qq.t[:, :], True, br == 1, [kswT, qq], [ps])
                if br == 0:
                    kb.mm(ps.t[:, :], Exp_.t[:, kt % 64, :], negselT.t[:, kt // 64, :], False, True, [Exp_, negselT], [ps])
                off = 384 + 128 * (min(dl, SEL_FARD) if br == 0 else dl)
                it = n_ % 3
                kb.op("dve", lambda e: e.scalar_tensor_tensor(out=tmp[it].t[:, :], in0=ps.t[:, :], scalar=scale, in1=wide.t[:, h, off:off + 512],
                                                              op0=ALU.mult, op1=ALU.add), reads=[ps, wide], writes=[tmp[it]])
                kb.op("act", lambda e: e.activation(out=pT[it].t[:, :], in_=tmp[it].t[:, :], func=AF.Exp), reads=[tmp[it]], writes=[pT[it]])

            def selB(n_, grp):
                br, h, kt, first, last = its[n_]
                it = n_ % 3
                Vt = Vs if br == 0 else Vw
                po = pobanks[grp % 2]
                kb.mm(po.t[0:65, :], Vt.t[:, kt, 0:65], pT[it].t[:, :], first, last, [pT[it], Vt], [po])
                if not last:
                    return
                pS = poS[grp % 2]
                kb.op("act", lambda e: e.copy(out=pS.t[:, :], in_=po.t[0:65, :]), reads=[po], writes=[pS])
                pt = banks[4]
                for j in range(4):
                    kb.op("pe", lambda e, j=j: e.transpose(out=pt.t[:, j * 65:(j + 1) * 65], in_=pS.t[0:65, j * 128:(j + 1) * 128], identity=identf.t[0:65, 0:65]),
                          reads=[pS, identf], writes=[pt])
                for j in range(4):
                    gcol = (1 + br) * 4 + h
                    kb.op("dve", lambda e, j=j: e.reciprocal(out=fcol.t[:, j:j + 1], in_=pt.t[:, j * 65 + 64:j * 65 + 65]), reads=[pt], writes=[fcol])
                    kb.op("dve", lambda e, j=j, gcol=gcol: e.tensor_tensor(out=fcol.t[:, 4 + j:5 + j], in0=fcol.t[:, j:j + 1], in1=gates.t[:, j, gcol:gcol + 1], op=ALU.mult),
                          reads=[fcol, gates], writes=[fcol])
                    kb.op("dve", lambda e, j=j: e.scalar_tensor_tensor(out=ofin[j].t[:, h, :], in0=pt.t[:, j * 65:j * 65 + 64], scalar=fcol.t[:, 4 + j:5 + j], in1=ofin[j].t[:, h, :],
                                                                       op0=ALU.mult, op1=ALU.add), reads=[pt, fcol, ofin[j]], writes=[ofin[j]])

            grp = 0
            if NSA_STAGE == 7:
                its = []
                continue
            selA(0)
            if len(its) > 1:
                selA(1)
            for n_ in range(len(its)):
                if n_ + 2 < len(its):
                    selA(n_ + 2)
                selB(n_, grp)
                if its[n_][4]:
                    grp += 1
            for j in range(4):
                kb.op("act", lambda e, j=j: e.copy(out=ogb.t[:, :], in_=ofin[j].t[:].rearrange("p a b -> p (a b)")), reads=[ofin[j]], writes=[ogb])
                ptb = banks[3].t[:, :].bitcast(BF16)
                for c in range(2):
                    kb.op("pe", lambda e, c=c, ptb=ptb: e.transpose(out=ptb[:, c * 128:(c + 1) * 128], in_=ogb.t[:, c * 128:(c + 1) * 128], identity=identb.t[:]),
                          reads=[ogb, identb], writes=[banks[3]])
                kb.op("act", lambda e, ptb=ptb: e.copy(out=oTt.t[:].rearrange("p a b -> p (a b)"), in_=ptb[:, 0:256]), reads=[banks[3]], writes=[oTt])
                t0 = qs * 512 + j * 128
                kb.dma("pool", ov[:, :, t0:t0 + 128], oTt.t[:], reads=[oTt], writes=[oT])
        kb.finish([oT])
    return nc


def nsa_inputs(hT_b, w_in, pe, w1, b1, w2, tab, g):
    q = w_in[:, 0:1024].reshape(D, 4, 4, 64)[:, g]
    wqd = np.concatenate([q, q], axis=2).reshape(D, 512)
    blk = lambda i: w_in[:, 1024 + i * 256 + g * 64:1024 + i * 256 + (g + 1) * 64]
    wkv = np.concatenate([blk(0), blk(1), blk(2), blk(4), blk(3), blk(5)], axis=1)
    gates = w_in[:, 2560:2608].reshape(D, 3, 4, 4)[:, :, g, :].reshape(D, 12)
    tb = np.full((33, 4), NEG, np.float32)
    tb[:32] = tab[:, g * 4:(g + 1) * 4]
    ohs, ohw, ohc = nsa_onehots()
    return {"hT": hT_b, "wqd": np.ascontiguousarray(wqd), "wkv": np.ascontiguousarray(wkv), "wg": np.ascontiguousarray(gates),
            "pe": np.ascontiguousarray(pe), "w1": np.ascontiguousarray(w1), "b1": np.ascontiguousarray(b1), "w2": np.ascontiguousarray(w2),
            "tab": tb, "ohs": ohs, "ohw": ohw, "ohc": ohc}


_NC_CACHE = {}


def _get(name, fn, *a):
    key = (name,) + a
    if key not in _NC_CACHE:
        _NC_CACHE[key] = fn(*a)
    return _NC_CACHE[key]


def _run(nc, in_maps):
    res = run_bass_kernel_spmd(nc, in_maps, core_ids=list(range(NCORES)))
    return res.results


def kernel(x, c, rel_table, mod_w, mod_b, ln_g, ln_b,
           gla_w_in, gla_w_a2, gla_b_a, gla_norm_g, gla_w_o,
           nsa_w_in, nsa_cmp_pe, nsa_cmp_w1, nsa_cmp_b1, nsa_cmp_w2, nsa_w_o,
           dil_w_in, dil_w_o,
           ffn_w_up, ffn_conv_w, ffn_conv_b, ffn_w_down):
    f32 = lambda a: np.ascontiguousarray(np.asarray(a, dtype=np.float32))
    x = f32(x); c = f32(c); rel_table = f32(rel_table); mod_w = f32(mod_w); mod_b = f32(mod_b)
    ln_g = f32(ln_g); ln_b = f32(ln_b)
    S = x.shape[1]
    T = S // 4
    dbg = globals().get("_DBG")
    res = _run(_get("mod", build_mod), [{"c": c, "w": f32(mod_w[s // 2, s % 2]), "b": f32(mod_b[s // 2, s % 2][None])} for s in range(8)])
    mod = [r["out"] for r in res]

    def shards():
        for k in range(NCORES):
            yield k, k // 4, (k % 4) * T

    res = _run(_get("prep", build_prep, T), [{"x": f32(x[b, t0:t0 + T]), "vec": f32(np.stack([mod[0][b, 0:D], mod[0][b, D:2 * D]]))} for k, b, t0 in shards()])
    hT = np.zeros((NB, D, S), ml_dtypes.bfloat16)
    for (k, b, t0), r in zip(shards(), res):
        hT[b, :, t0:t0 + T] = r["hT"]
    xcur = x
    for i in range(DEPTH):
        kind, j = i % 3, i // 3
        ins = []
        for k in range(NCORES):
            b, g = k // 4, k % 4
            hb = np.ascontiguousarray(hT[b])
            if kind == 0:
                w_in = f32(gla_w_in[j])
                wa2 = np.zeros((33, 128), np.float32)
                wa2[:16] = f32(gla_w_a2[j])[:, g * 128:(g + 1) * 128]
                wa2[32] = f32(gla_b_a[j])[g * 128:(g + 1) * 128]
                ins.append({"hT": hb, "wq": f32(w_in[:, g * 128:(g + 1) * 128]), "wk": f32(w_in[:, 512 + g * 128:512 + (g + 1) * 128]),
                            "wv": f32(w_in[:, 1024 + g * 256:1024 + (g + 1) * 256]), "wr": f32(w_in[:, 2048 + g * 256:2048 + (g + 1) * 256]),
                            "wa": f32(w_in[:, 3072:3088]), "wa2": wa2, "ng": f32(gla_norm_g[j])[None]})
            elif kind == 1:
                ins.append(nsa_inputs(hb, f32(nsa_w_in[j]), f32(nsa_cmp_pe[j]), f32(nsa_cmp_w1[j]), f32(nsa_cmp_b1[j]), f32(nsa_cmp_w2[j]), rel_table, g))
            else:
                wgd = f32(dil_w_in[j]).reshape(D, 3, 3, 1024)
                wqkv = np.stack([np.concatenate([wgd[:, gg, cc, g * 256:(g + 1) * 256] for cc in range(3)], axis=1) for gg in range(3)])
                tb = np.full((33, 4), NEG, np.float32)
                tb[:32] = rel_table[:, g * 4:(g + 1) * 4]
                ins.append({"hT": hb, "wqkv": f32(wqkv), "tab": tb, "oh": dil_onehots()})
        ncm = _get(("gla", "nsa", "dil")[kind], (build_gla, build_nsa, build_dil)[kind], S)
        res = _run(ncm, ins)
        oT = np.zeros((NB, D, S), ml_dtypes.bfloat16)
        for k in range(NCORES):
            oT[k // 4, (k % 4) * 256:(k % 4 + 1) * 256, :] = res[k]["oT"]
        w_o = f32((gla_w_o, nsa_w_o, dil_w_o)[kind][j])
        ins = []
        for k, b, t0 in shards():
            xh = np.zeros((T + 128, D), np.float32)
            oh_ = np.zeros((D, T + 128), ml_dtypes.bfloat16)
            xh[128:] = xcur[b, t0:t0 + T]
            oh_[:, 128:] = oT[b, :, t0:t0 + T]
            if t0 > 0:
                xh[:128] = xcur[b, t0 - 128:t0]
                oh_[:, :128] = oT[b, :, t0 - 128:t0]
            vec = np.zeros((10, D), np.float32)
            vec[0] = mod[2 * i][b, 2 * D:3 * D]
            vec[1] = mod[2 * i + 1][b, 0:D]; vec[2] = mod[2 * i + 1][b, D:2 * D]; vec[3] = mod[2 * i + 1][b, 2 * D:3 * D]
            if i + 1 < DEPTH:
                vec[4] = mod[2 * i + 2][b, 0:D]; vec[5] = mod[2 * i + 2][b, D:2 * D]
            vec[6] = ln_g[i, 0]; vec[7] = ln_b[i, 0]; vec[8] = ln_g[i, 1]; vec[9] = ln_b[i, 1]
            ins.append({"x": xh, "oT": oh_, "wo": w_o, "wup": f32(ffn_w_up[i]), "wdn": f32(ffn_w_down[i]), "convw": f32(ffn_conv_w[i]),
                        "convb": f32(ffn_conv_b[i]), "vec": vec, "flag": np.full((128, 1), 0.0 if t0 == 0 else 1.0, np.float32)})
        res = _run(_get("post", build_post, T), ins)
        xn = np.zeros((NB, S, D), np.float32)
        for (k, b, t0), r in zip(shards(), res):
            xn[b, t0:t0 + T] = r["xo"]
            hT[b, :, t0:t0 + T] = r["hT"]
        xcur = xn
        if dbg is not None:
            dbg.append(xn.copy())
    return xcur
```
